# Optimizing a Trainium2 kernel written in Bass

```python
import math
import jax, jax.numpy as jnp
from jax import lax
import numpy as np

D_MODEL = 1024
BATCH = 4
SEQ = 4096
DEPTH = 4

CHUNK = 64
Q_BLOCK = 128
HEAD_DIM = 64
A_HEADS = 4
B_HEADS = 8
C_HEADS = 8
IDX_HEADS = 4
IDX_DIM = 64
TOPK_MAX = 256
N_BRANCHES = 3
D_FF = 4 * D_MODEL
PLE_DIM = 256
ROPE_THETA = 10000.0
EPS = 1e-6

A_QK = A_HEADS * 2 * HEAD_DIM
A_V = A_HEADS * 2 * HEAD_DIM
A_WIDTH = A_V
B_WIDTH = B_HEADS * HEAD_DIM
C_WIDTH = C_HEADS * HEAD_DIM
IN_SPLITS = (A_QK, A_QK, A_V, B_WIDTH, B_WIDTH, B_WIDTH, C_WIDTH, C_WIDTH, C_WIDTH,
             IDX_HEADS * IDX_DIM, IDX_DIM, IDX_HEADS, N_BRANCHES * D_MODEL)
W_IN_COLS = (2 * A_QK + A_V + 3 * B_WIDTH + 3 * C_WIDTH + IDX_HEADS * IDX_DIM + IDX_DIM
             + IDX_HEADS + N_BRANCHES * D_MODEL)

kernel_name = "hybrid_diff_stickbreak_dsa_block"


def rmsnorm(x, g):
    xf = x.astype(jnp.float32)
    y = xf * lax.rsqrt(jnp.mean(xf * xf, axis=-1, keepdims=True) + EPS)
    return (y * g.astype(jnp.float32)).astype(x.dtype)


def rope_tables(positions, dim):
    inv = ROPE_THETA ** (-jnp.arange(0, dim, 2, dtype=jnp.float32) / dim)
    ang = positions.astype(jnp.float32)[..., None] * inv
    return jnp.cos(ang), jnp.sin(ang)


def apply_rope(x, cos, sin):
    c = cos[:, :, None, :]
    s = sin[:, :, None, :]
    xf = x.astype(jnp.float32)
    x1, x2 = jnp.split(xf, 2, axis=-1)
    return jnp.concatenate([x1 * c - x2 * s, x2 * c + x1 * s], axis=-1).astype(x.dtype)


def sweep_query_blocks(fn, *arrays):
    b, s = arrays[0].shape[:2]
    nb = s // Q_BLOCK
    blocks = tuple(jnp.moveaxis(a.reshape((b, nb, Q_BLOCK) + a.shape[2:]), 1, 0) for a in arrays)
    out = lax.map(lambda args: fn(args[0], *args[1:]), (jnp.arange(nb), *blocks))
    return jnp.moveaxis(out, 0, 1).reshape((b, s) + out.shape[3:])


def block_positions(blk, n_keys):
    q_pos = blk * Q_BLOCK + jnp.arange(Q_BLOCK)
    k_pos = jnp.arange(n_keys)
    chunk_ok = (k_pos[None, :] // CHUNK) <= (q_pos[:, None] // CHUNK)
    return q_pos, k_pos, chunk_ok


def diff_attention(q, k, v, lam, lam_init, subln_g):
    n_keys = k.shape[1]
    scale = HEAD_DIM ** -0.5

    def block(blk, qb):
        _, _, chunk_ok = block_positions(blk, n_keys)
        s = jnp.einsum('bqhcd,bkhcd->bhcqk', qb, k).astype(jnp.float32) * scale
        s = jnp.where(chunk_ok, s, -jnp.inf)
        pr = jax.nn.softmax(s, axis=-1)
        attn = (pr[:, :, 0] - lam * pr[:, :, 1]).astype(v.dtype)
        return jnp.einsum('bhqk,bkhe->bqhe', attn, v)

    o = sweep_query_blocks(block, q)
    return rmsnorm(o, subln_g) * (1.0 - lam_init)


def stick_breaking_attention(q, k, v):
    n_keys = k.shape[1]
    scale = HEAD_DIM ** -0.5

    def block(blk, qb):
        q_pos, k_pos, _ = block_positions(blk, n_keys)
        strict = k_pos[None, :] < q_pos[:, None]
        z = jnp.einsum('bqhd,bkhd->bhqk', qb, k).astype(jnp.float32) * scale
        log_beta = jax.nn.log_sigmoid(z)
        log_one_minus = jnp.where(strict, jax.nn.log_sigmoid(-z), 0.0)
        key_axis = log_one_minus.ndim - 1
        later = lax.cumsum(log_one_minus, axis=key_axis, reverse=True) - log_one_minus
        a = jnp.where(strict, jnp.exp(log_beta + later), 0.0).astype(v.dtype)
        return jnp.einsum('bhqk,bkhd->bqhd', a, v)

    return sweep_query_blocks(block, q)


def dsa_attention(q, k, v, qi, ki, wi):
    n_keys = k.shape[1]
    topk = min(TOPK_MAX, n_keys // 4)
    scale = HEAD_DIM ** -0.5

    def block(blk, qb, qib, wib):
        q_pos, _, chunk_ok = block_positions(blk, n_keys)
        idx_s = jnp.einsum('bqhd,bkd->bqhk', qib, ki).astype(jnp.float32) * (IDX_DIM ** -0.5)
        score = jnp.einsum('bqh,bqhk->bqk', wib.astype(jnp.float32) * (IDX_HEADS ** -0.5),
                           jax.nn.relu(idx_s))
        score = jnp.where(chunk_ok, score, -jnp.inf)
        _, sel = lax.top_k(score, topk)
        ok = (sel // CHUNK) <= (q_pos[None, :, None] // CHUNK)
        kg = jax.vmap(lambda kb, ib: kb[ib])(k, sel)
        vg = jax.vmap(lambda vb, ib: vb[ib])(v, sel)
        s = jnp.einsum('bqhd,bqkhd->bhqk', qb, kg).astype(jnp.float32) * scale
        s = jnp.where(ok[:, None], s, -jnp.inf)
        pr = jax.nn.softmax(s, axis=-1).astype(v.dtype)
        return jnp.einsum('bhqk,bqkhd->bqhd', pr, vg)

    return sweep_query_blocks(block, q, qi, wi)


def setup_inputs(seed: int = 0) -> dict:
    key = jax.random.key(seed)
    ks = iter(jax.random.split(key, 32))

    def dense(shape, fan_in):
        return jax.random.normal(next(ks), shape, jnp.float32) * (fan_in ** -0.5)

    def gain(shape):
        return 1.0 + 0.05 * jax.random.normal(next(ks), shape, jnp.float32)

    x = jax.random.normal(next(ks), (BATCH, SEQ, D_MODEL), jnp.float32)
    p = jax.random.normal(next(ks), (DEPTH, BATCH, SEQ, PLE_DIM), jnp.float32)
    offset = jax.random.randint(next(ks), (BATCH, 1), 0, 16, dtype=jnp.int32) * CHUNK
    positions = (offset + jnp.arange(SEQ, dtype=jnp.int32)[None, :]).astype(jnp.int32)
    return {
        "x": x,
        "p": p,
        "positions": positions,
        "attn_norm": gain((DEPTH, D_MODEL)),
        "w_in": dense((DEPTH, D_MODEL, W_IN_COLS), D_MODEL),
        "a_q_norm": gain((DEPTH, HEAD_DIM)),
        "a_k_norm": gain((DEPTH, HEAD_DIM)),
        "a_lambda": 0.1 * jax.random.normal(next(ks), (DEPTH, 4, HEAD_DIM), jnp.float32),
        "a_subln": gain((DEPTH, 2 * HEAD_DIM)),
        "c_q_norm": gain((DEPTH, HEAD_DIM)),
        "c_k_norm": gain((DEPTH, HEAD_DIM)),
        "idx_k_norm": gain((DEPTH, IDX_DIM)),
        "w_br_a": dense((DEPTH, A_WIDTH, D_MODEL), A_WIDTH),
        "w_br_b": dense((DEPTH, B_WIDTH, D_MODEL), B_WIDTH),
        "w_br_c": dense((DEPTH, C_WIDTH, D_MODEL), C_WIDTH),
        "w_out": dense((DEPTH, D_MODEL, D_MODEL), D_MODEL),
        "mlp_norm": gain((DEPTH, D_MODEL)),
        "w_up": dense((DEPTH, D_MODEL, D_FF), D_MODEL),
        "w_down": dense((DEPTH, D_FF, D_MODEL), D_FF),
        "ple_norm": gain((DEPTH, D_MODEL)),
        "w_ple_gate": dense((DEPTH, D_MODEL, D_MODEL), D_MODEL),
        "w_ple_proj": dense((DEPTH, PLE_DIM, D_MODEL), PLE_DIM),
    }


def reference(x, p, positions, attn_norm, w_in, a_q_norm, a_k_norm, a_lambda, a_subln,
              c_q_norm, c_k_norm, idx_k_norm, w_br_a, w_br_b, w_br_c, w_out,
              mlp_norm, w_up, w_down, ple_norm, w_ple_gate, w_ple_proj):
    b, s = x.shape[:2]
    cos, sin = rope_tables(positions, HEAD_DIM)
    offsets = [int(o) for o in np.cumsum(IN_SPLITS)[:-1]]
    for i in range(DEPTH):
        lam_init = 0.8 - 0.6 * math.exp(-0.3 * i)
        h = rmsnorm(x, attn_norm[i])
        proj = h @ w_in[i]
        (aq, ak, av, bq, bk, bv, cq, ck, cv, iq, ik, iw, gl) = jnp.split(proj, offsets, axis=-1)

        aq = apply_rope(rmsnorm(aq.reshape(b, s, 2 * A_HEADS, HEAD_DIM), a_q_norm[i]), cos, sin)
        ak = apply_rope(rmsnorm(ak.reshape(b, s, 2 * A_HEADS, HEAD_DIM), a_k_norm[i]), cos, sin)
        lp = a_lambda[i].astype(jnp.float32)
        lam = jnp.exp(jnp.sum(lp[0] * lp[1])) - jnp.exp(jnp.sum(lp[2] * lp[3])) + lam_init
        y_a = diff_attention(aq.reshape(b, s, A_HEADS, 2, HEAD_DIM),
                             ak.reshape(b, s, A_HEADS, 2, HEAD_DIM),
                             av.reshape(b, s, A_HEADS, 2 * HEAD_DIM),
                             lam, lam_init, a_subln[i]).reshape(b, s, A_WIDTH)

        y_b = stick_breaking_attention(bq.reshape(b, s, B_HEADS, HEAD_DIM),
                                       bk.reshape(b, s, B_HEADS, HEAD_DIM),
                                       bv.reshape(b, s, B_HEADS, HEAD_DIM)).reshape(b, s, B_WIDTH)

        cq = apply_rope(rmsnorm(cq.reshape(b, s, C_HEADS, HEAD_DIM), c_q_norm[i]), cos, sin)
        ck = apply_rope(rmsnorm(ck.reshape(b, s, C_HEADS, HEAD_DIM), c_k_norm[i]), cos, sin)
        iq = apply_rope(iq.reshape(b, s, IDX_HEADS, IDX_DIM), cos, sin)
        ik = apply_rope(rmsnorm(ik, idx_k_norm[i])[:, :, None, :], cos, sin)[:, :, 0]
        y_c = dsa_attention(cq, ck, cv.reshape(b, s, C_HEADS, HEAD_DIM),
                            iq, ik, iw).reshape(b, s, C_WIDTH)

        gates = jax.nn.sigmoid(gl.reshape(b, s, N_BRANCHES, D_MODEL).astype(jnp.float32)).astype(x.dtype)
        merged = (gates[:, :, 0] * (y_a @ w_br_a[i])
                  + gates[:, :, 1] * (y_b @ w_br_b[i])
                  + gates[:, :, 2] * (y_c @ w_br_c[i]))
        x = x + merged @ w_out[i]

        hm = rmsnorm(x, mlp_norm[i])
        x = x + jnp.square(jax.nn.relu(hm @ w_up[i])) @ w_down[i]

        hp = rmsnorm(x, ple_norm[i])
        x = x + jax.nn.sigmoid(hp @ w_ple_gate[i]) * (p[i] @ w_ple_proj[i])
    return x
```

```python
import math
from contextlib import ExitStack

import numpy as np
import concourse.bass as bass
import concourse.mybir as mybir
from concourse.bass_utils import run_bass_kernel_spmd

F32 = mybir.dt.float32
BF16 = mybir.dt.bfloat16
I32 = mybir.dt.int32
AF = mybir.ActivationFunctionType
ALU = mybir.AluOpType
AX = mybir.AxisListType

DEPTH = 4
SL = 4096
D = 1024
NT = 32
NQ = 8
EPS = 1e-6
W_IN_COLS = 8004
TOPK = 256
NSTEP = 24
NEG = -1.0e30

SEM_EPOCH = 30000
N_DMA_SEMS = 24


class Sched:
    def __init__(self, nc, stack):
        self.nc = nc
        self.stack = stack
        self.engs = {"pe": nc.tensor, "act": nc.scalar, "dve": nc.vector,
                     "pool": nc.gpsimd, "sp": nc.sync}
        self.esem = {}
        self.ecount = {}
        self.n_sems = 0
        for e in self.engs:
            self._new_esem(e)
        self.waited = {e: {} for e in self.engs}
        self.dsems = [[self._sem(f"dma{i}"), 0] for i in range(N_DMA_SEMS)]
        self.dnext = 0
        self.lastw = {}
        self.reads = {}
        self.ninstr = {e: 0 for e in self.engs}

    def _sem(self, name):
        self.n_sems += 1
        return self.stack.enter_context(self.nc.semaphore(name))

    def _new_esem(self, e):
        self.esem[e] = self._sem(f"e_{e}_{self.n_sems}")
        self.ecount[e] = 0

    def _wait(self, e, dep):
        sem, val, tag = dep
        w = self.waited[e]
        k = id(sem)
        if w.get(k, 0) >= val:
            return
        w[k] = val
        self.engs[e].wait_ge(sem, val)
        self.ninstr[e] += 1

    def _collect(self, r, w):
        deps = []
        for k in list(r) + list(w):
            d = self.lastw.get(k)
            if d is not None:
                deps.append(d)
        for k in w:
            rd = self.reads.get(k)
            if rd:
                deps.extend(rd.values())
        return deps

    def _commit(self, r, w, dep):
        for k in w:
            self.lastw[k] = dep
            self.reads[k] = {}
        for k in r:
            if k in w:
                continue
            self.reads.setdefault(k, {})[dep[2]] = dep

    def op(self, e, fn, r=(), w=()):
        if self.ecount[e] >= SEM_EPOCH:
            self._new_esem(e)
        for d in self._collect(r, w):
            if e == "pe" and d[2] == "pe":
                continue
            self._wait(e, d)
        ins = fn()
        self.ecount[e] += 1
        ins.then_inc(self.esem[e], 1)
        dep = (self.esem[e], self.ecount[e], e)
        self.ninstr[e] += 1
        self._commit(r, w, dep)
        return ins

    def dma(self, q, out, in_, r=(), w=()):
        slot = self.dsems[self.dnext]
        self.dnext = (self.dnext + 1) % len(self.dsems)
        sem = slot[0]
        if slot[1] > 0:
            self._wait(q, (sem, slot[1], "dma"))
        for d in self._collect(r, w):
            self._wait(q, d)
        ins = self.engs[q].dma_start(out=out, in_=in_)
        slot[1] += 16
        ins.then_inc(sem, 16)
        dep = (sem, slot[1], f"dma{id(sem)}")
        self.ninstr[q] += 1
        self._commit(r, w, dep)
        return ins

    def barrier(self):
        for e in self.engs:
            for o in self.engs:
                if o != e and self.ecount[o] > 0:
                    self._wait(e, (self.esem[o], self.ecount[o], o))
            for sem, val in self.dsems:
                if val:
                    self._wait(e, (sem, val, "dma"))


def build_nc(depth=DEPTH, debug=False, phases=None):
    PH = set(phases) if phases is not None else {'p1', 'pa', 'pb', 'pc', 'pm', 'pf', 'pp'}
    nc = bass.Bass("TRN2", target_bir_lowering=False)

    def din(name, shape, dt=F32):
        return nc.dram_tensor(name, list(shape), dt, kind="ExternalInput").ap()

    def dscr(name, shape, dt):
        kind = "ExternalOutput" if debug else "Internal"
        return nc.dram_tensor(name, list(shape), dt, kind=kind).ap()

    x_in = din("x", [SL, D])
    p_in = din("p", [DEPTH, SL, 256])
    pos_in = din("pos", [128, NT], I32)
    invf_in = din("invf", [32])
    attn_norm = din("attn_norm_l", [128, DEPTH * 8])
    mlp_norm = din("mlp_norm_l", [128, DEPTH * 8])
    ple_norm = din("ple_norm_l", [128, DEPTH * 8])
    w_in = din("w_in", [DEPTH, D, W_IN_COLS])
    a_q_norm = din("a_q_norm", [DEPTH * 64])
    a_k_norm = din("a_k_norm", [DEPTH * 64])
    a_lambda = din("a_lambda", [DEPTH * 256])
    a_subln = din("a_subln", [DEPTH * 128])
    c_q_norm = din("c_q_norm", [DEPTH * 64])
    c_k_norm = din("c_k_norm", [DEPTH * 64])
    idx_k_norm = din("idx_k_norm", [DEPTH * 64])
    w_br = [din("w_br_a", [DEPTH, 512, D]), din("w_br_b", [DEPTH, 512, D]), din("w_br_c", [DEPTH, 512, D])]
    w_out = din("w_out", [DEPTH, D, D])
    w_up = din("w_up", [DEPTH, D, 4 * D])
    w_down = din("w_down", [DEPTH, 4 * D, D])
    w_ple_gate = din("w_ple_gate", [DEPTH, D, D])
    w_ple_proj = din("w_ple_proj", [DEPTH, 256, D])
    out_d = nc.dram_tensor("out", [SL, D], F32, kind="ExternalOutput").ap()

    xres = dscr("xres", [SL, D], F32)
    QT = {k: dscr("QT_" + k, [4, 128, SL], BF16) for k in "abc"}
    KT = {k: dscr("KT_" + k, [4, 128, SL], BF16) for k in "abc"}
    VX = {"a": dscr("V_a", [SL, 4 * 129], BF16), "b": dscr("V_b", [SL, 8 * 65], BF16),
          "c": dscr("V_c", [SL, 8 * 65], BF16)}
    IQT = dscr("IQT", [2, 128, SL], BF16)
    IKT = dscr("IKT", [128, SL], BF16)
    GT = dscr("GT", [24, 128, SL], BF16)
    YT = dscr("YT", [12, 128, SL], BF16)

    with ExitStack() as st:
        S = Sched(nc, st)
        E = st.enter_context
        V = nc.vector
        A = nc.scalar
        G = nc.gpsimd
        T = nc.tensor

        uid = [0]

        def sb(stack, name, shape, dt):
            uid[0] += 1
            return stack.enter_context(nc.sbuf_tensor(f"s{uid[0]}_{name}", list(shape), dt))

        def ps(stack, name, shape, dt):
            uid[0] += 1
            return stack.enter_context(nc.psum_tensor(f"q{uid[0]}_{name}", list(shape), dt))

        ident = sb(st, "ident", [128, 128], BF16)
        ntri = sb(st, "ntri", [128, 128], BF16)
        nones = sb(st, "nones", [128, 128], BF16)
        strict = sb(st, "strict", [128, 128], BF16)
        cos_t = sb(st, "cos_t", [128, NT, 32], F32)
        sin_t = sb(st, "sin_t", [128, NT, 32], F32)
        gA = sb(st, "gA", [128, DEPTH * 8], F32)
        gM = sb(st, "gM", [128, DEPTH * 8], F32)
        gP = sb(st, "gP", [128, DEPTH * 8], F32)
        aqn = sb(st, "aqn", [128, DEPTH * 64], F32)
        akn = sb(st, "akn", [128, DEPTH * 64], F32)
        cqn = sb(st, "cqn", [128, DEPTH * 64], F32)
        ckn = sb(st, "ckn", [128, DEPTH * 64], F32)
        ikn = sb(st, "ikn", [128, DEPTH * 64], F32)
        subg = sb(st, "subg", [128, DEPTH * 128], F32)
        lam_t = sb(st, "lam_t", [128, DEPTH], F32)
        nlam_t = sb(st, "nlam_t", [128, DEPTH], F32)
        iw_all = sb(st, "iw_all", [128, NT, 4], F32)

        CONST = ["const"]

        with ExitStack() as ph:
            posi = sb(ph, "posi", [128, NT], I32)
            posf = sb(ph, "posf", [128, NT], F32)
            invf = sb(ph, "invf", [128, 32], F32)
            ang = sb(ph, "ang", [128, NT, 32], F32)
            t0 = sb(ph, "t0", [128, NT, 32], F32)
            t1 = sb(ph, "t1", [128, NT, 32], F32)
            lpb = sb(ph, "lpb", [128, DEPTH * 256], F32)
            tmp64 = sb(ph, "tmp64", [128, 64], F32)
            s12 = sb(ph, "s12", [128, 2], F32)
            K0 = ["p0"]
            S.op("pool", lambda: G.memset(ident[:], 1.0), w=CONST)
            S.op("pool", lambda: G.affine_select(out=ident[:], in_=ident[:], pattern=[[-1, 128]],
                                                 compare_op=ALU.is_equal, fill=0.0, base=0, channel_multiplier=1),
                 r=CONST, w=CONST)
            S.op("pool", lambda: G.memset(nones[:], -1.0), w=CONST)
            S.op("pool", lambda: G.memset(ntri[:], -1.0), w=CONST)
            S.op("pool", lambda: G.affine_select(out=ntri[:], in_=ntri[:], pattern=[[-1, 128]],
                                                 compare_op=ALU.is_ge, fill=0.0, base=0, channel_multiplier=1),
                 r=CONST, w=CONST)
            S.op("pool", lambda: G.memset(strict[:], 1.0), w=CONST)
            S.op("pool", lambda: G.affine_select(out=strict[:], in_=strict[:], pattern=[[1, 128]],
                                                 compare_op=ALU.is_gt, fill=0.0, base=0, channel_multiplier=-1),
                 r=CONST, w=CONST)
            for dst, src in [(gA, attn_norm), (gM, mlp_norm), (gP, ple_norm)]:
                S.dma("sp", dst[:], src[:, :], w=CONST)
            for dst, src in [(aqn, a_q_norm), (akn, a_k_norm), (cqn, c_q_norm), (ckn, c_k_norm),
                             (ikn, idx_k_norm), (subg, a_subln), (lpb, a_lambda), (invf, invf_in)]:
                S.dma("sp", dst[:], src.partition_broadcast(128), w=CONST + K0)
            S.dma("sp", posi[:], pos_in[:, :], w=K0)
            S.op("dve", lambda: V.tensor_scalar(out=aqn[:], in0=aqn[:], scalar1=0.125, scalar2=None, op0=ALU.mult), r=CONST, w=CONST)
            S.op("dve", lambda: V.tensor_scalar(out=cqn[:], in0=cqn[:], scalar1=0.125, scalar2=None, op0=ALU.mult), r=CONST, w=CONST)
            for l in range(DEPTH):
                li = 0.8 - 0.6 * math.exp(-0.3 * l)
                S.op("dve", lambda: V.tensor_scalar(out=subg[:, l * 128:(l + 1) * 128], in0=subg[:, l * 128:(l + 1) * 128],
                                                    scalar1=float(1.0 - li), scalar2=None, op0=ALU.mult), r=CONST, w=CONST)
                for j in range(2):
                    S.op("dve", lambda: V.tensor_tensor(out=tmp64[:], in0=lpb[:, l * 256 + j * 128: l * 256 + j * 128 + 64],
                                                        in1=lpb[:, l * 256 + j * 128 + 64: l * 256 + j * 128 + 128], op=ALU.mult),
                         r=K0, w=["tmp64"])
                    S.op("dve", lambda: V.tensor_reduce(out=s12[:, j:j + 1], in_=tmp64[:], axis=AX.X, op=ALU.add),
                         r=["tmp64"], w=["s12"])
                S.op("act", lambda: A.activation(out=s12[:], in_=s12[:], func=AF.Exp), r=["s12"], w=["s12"])
                S.op("dve", lambda: V.tensor_tensor(out=lam_t[:, l:l + 1], in0=s12[:, 0:1], in1=s12[:, 1:2], op=ALU.subtract),
                     r=["s12"], w=CONST)
                S.op("dve", lambda: V.tensor_scalar(out=lam_t[:, l:l + 1], in0=lam_t[:, l:l + 1], scalar1=float(li), scalar2=None, op0=ALU.add),
                     r=CONST, w=CONST)
                S.op("dve", lambda: V.tensor_scalar(out=nlam_t[:, l:l + 1], in0=lam_t[:, l:l + 1], scalar1=-1.0, scalar2=None, op0=ALU.mult),
                     r=CONST, w=CONST)
            S.op("dve", lambda: V.tensor_copy(out=posf[:], in_=posi[:]), r=K0, w=["posf"])
            S.op("dve", lambda: V.tensor_tensor(out=ang[:], in0=posf[:].unsqueeze(2).to_broadcast([128, NT, 32]),
                                                in1=invf[:].unsqueeze(1).to_broadcast([128, NT, 32]), op=ALU.mult),
                 r=["posf"] + K0, w=["ang"])
            MAGIC = 12582912.0
            C1 = 6.28125
            C2 = 2.0 * math.pi - C1
            PI_LO = 3.1415925
            S.op("dve", lambda: V.tensor_scalar(out=t0[:], in0=ang[:], scalar1=float(1.0 / (2.0 * math.pi)), scalar2=None, op0=ALU.mult), r=["ang"], w=["t0"])
            S.op("dve", lambda: V.tensor_scalar(out=t0[:], in0=t0[:], scalar1=MAGIC, scalar2=None, op0=ALU.add), r=["t0"], w=["t0"])
            S.op("dve", lambda: V.tensor_scalar(out=t0[:], in0=t0[:], scalar1=-MAGIC, scalar2=None, op0=ALU.add), r=["t0"], w=["t0"])
            S.op("dve", lambda: V.scalar_tensor_tensor(out=t1[:], in0=t0[:], scalar=-C1, in1=ang[:], op0=ALU.mult, op1=ALU.add), r=["t0", "ang"], w=["t1"])
            S.op("dve", lambda: V.scalar_tensor_tensor(out=t1[:], in0=t0[:], scalar=-C2, in1=t1[:], op0=ALU.mult, op1=ALU.add), r=["t0", "t1"], w=["t1"])
            S.op("dve", lambda: V.tensor_scalar(out=t0[:], in0=t1[:], scalar1=PI_LO, scalar2=-PI_LO, op0=ALU.min, op1=ALU.max), r=["t1"], w=["t0"])
            S.op("act", lambda: A.activation(out=sin_t[:], in_=t0[:], func=AF.Sin), r=["t0"], w=CONST)
            S.op("dve", lambda: V.tensor_scalar(out=t1[:], in0=t1[:], scalar1=float(math.pi / 2), scalar2=None, op0=ALU.add), r=["t1"], w=["t1"])
            S.op("dve", lambda: V.tensor_scalar(out=t0[:], in0=t1[:], scalar1=float(math.pi), scalar2=float(-2.0 * math.pi), op0=ALU.is_gt, op1=ALU.mult), r=["t1", "const"], w=["t0"])
            S.op("dve", lambda: V.tensor_tensor(out=t1[:], in0=t1[:], in1=t0[:], op=ALU.add), r=["t0", "t1"], w=["t1"])
            S.op("dve", lambda: V.tensor_scalar(out=t1[:], in0=t1[:], scalar1=PI_LO, scalar2=-PI_LO, op0=ALU.min, op1=ALU.max), r=["t1"], w=["t1"])
            S.op("act", lambda: A.activation(out=cos_t[:], in_=t1[:], func=AF.Sin), r=["t1"], w=CONST)
            S.barrier()

        def rms_rows(xt, kx, ssq, rstd, junk, kpre):
            S.op("act", lambda: A.activation(out=junk[:], in_=xt, func=AF.Square, accum_out=ssq[:, 0:1]),
                 r=[kx], w=[kpre + "junk", kpre + "ssq"])
            S.op("act", lambda: A.activation(out=rstd[:, 0:1], in_=ssq[:, 0:1], func=AF.Sqrt, bias=EPS, scale=1.0 / D),
                 r=[kpre + "ssq"], w=[kpre + "rstd"])
            S.op("dve", lambda: V.reciprocal(out=rstd[:, 0:1], in_=rstd[:, 0:1]), r=[kpre + "rstd"], w=[kpre + "rstd"])

        def norm_transpose(xt, kx, gain, l, hb, pst2, dst_fn, kdst, ssq, rstd, junk, kpre):
            rms_rows(xt, kx, ssq, rstd, junk, kpre)
            S.op("dve", lambda: V.tensor_scalar(out=hb[:], in0=xt, scalar1=rstd[:, 0:1], scalar2=None, op0=ALU.mult),
                 r=[kx, kpre + "rstd"], w=[kpre + "hb"])
            for half in range(2):
                pt = pst2[half]
                for c4 in range(4):
                    c = half * 4 + c4
                    S.op("pe", lambda: T.transpose(out=pt[:, c4, :], in_=hb[:, c * 128:(c + 1) * 128], identity=ident[:]),
                         r=[kpre + "hb"], w=[kpre + f"pst{half}"])
                for c4 in range(4):
                    c = half * 4 + c4
                    if c4 % 2 == 0:
                        S.op("dve", lambda: V.tensor_scalar(out=dst_fn(c), in0=pt[:, c4, :], scalar1=gain[:, l * 8 + c:l * 8 + c + 1],
                                                            scalar2=None, op0=ALU.mult),
                             r=[kpre + f"pst{half}"], w=[kdst])
                    else:
                        S.op("act", lambda: A.activation(out=dst_fn(c), in_=pt[:, c4, :], func=AF.Copy,
                                                         scale=gain[:, l * 8 + c:l * 8 + c + 1]),
                             r=[kpre + f"pst{half}"], w=[kdst])

        def load_w_bf16(dst, src, key):
            S.dma("pool", dst, src, w=[key])

        for l in range(depth):
            xsrc = x_in if l == 0 else xres
            lam_init = 0.8 - 0.6 * math.exp(-0.3 * l)

            for ph in ([ExitStack()] if 'p1' in PH else []):
                hT = sb(ph, "hT", [128, 8, SL], BF16)
                xb2 = [sb(ph, f"p1x{i}", [128, D], F32) for i in range(2)]
                hb = sb(ph, "p1hb", [128, D], BF16)
                junk = sb(ph, "p1junk", [128, D], BF16)
                ssq = sb(ph, "p1ssq", [128, 8], F32)
                rstd = sb(ph, "p1rstd", [128, 8], F32)
                pst2 = [ps(ph, f"p1pst{i}", [128, 4, 128], BF16) for i in range(2)]
                psm = [ps(ph, f"p1ps{i}", [128, 512], F32) for i in range(2)]
                wblk = [sb(ph, f"p1w{i}", [128, 8, 512], BF16) for i in range(2)]
                sq = sb(ph, "p1sq", [128, 512], F32)
                xn = sb(ph, "p1xn", [128, 512], F32)
                xg = sb(ph, "p1xg", [128, 512], F32)
                ra = sb(ph, "p1ra", [128, 256], F32)
                rb = sb(ph, "p1rb", [128, 256], F32)
                ob = sb(ph, "p1ob", [128, 512], BF16)
                stg = [sb(ph, f"p1stg{i}", [128, 4, 512], BF16) for i in range(2)]
                vst = [sb(ph, f"p1vst{i}", [128, 520], BF16) for i in range(4)]
                gst = [sb(ph, f"p1gst{i}", [128, 512], BF16) for i in range(2)]

                for t in range(NT):
                    xt = xb2[t % 2]
                    kx = f"p1x{t % 2}"
                    S.dma("sp", xt[:], xsrc[t * 128:(t + 1) * 128, :], r=[f"xr{t}"], w=[kx])
                    norm_transpose(xt[:], kx, gA, l, hb, pst2,
                                   lambda c, t=t: hT[:, c, t * 128:(t + 1) * 128], "hT", ssq, rstd, junk, "p1")
                for i in range(4):
                    S.op("pool", lambda: G.memset(vst[i][:], 1.0), w=[f"vst{i}"])

                blocks = []
                names = ["aq", "ak", "av", "bq", "bk", "bv", "cq", "ck", "cv"]
                for i, nm in enumerate(names):
                    blocks.append((nm, i * 512, 512))
                blocks.append(("idx", 4608, 324))
                for i in range(6):
                    blocks.append((f"gl{i}", 4932 + i * 512, 512))
                w_l = w_in[l].rearrange("(c p) n -> p c n", p=128)

                def qk_post(pv, nh, gain, do_norm, do_rope, t, outv, scale, kout):
                    n = nh * 64
                    if do_norm:
                        sqv = sq[:, 0:n].rearrange("p (h d) -> p h d", d=64)
                        S.op("act", lambda: A.activation(out=sqv, in_=pv, func=AF.Square), r=["psm"], w=["sq"])
                        S.op("dve", lambda: V.tensor_reduce(out=ssq[:, 0:nh], in_=sqv, axis=AX.X, op=ALU.add), r=["sq"], w=["p1ssq"])
                        S.op("act", lambda: A.activation(out=rstd[:, 0:nh], in_=ssq[:, 0:nh], func=AF.Sqrt, bias=EPS, scale=1.0 / 64),
                             r=["p1ssq"], w=["p1rstd"])
                        S.op("dve", lambda: V.reciprocal(out=rstd[:, 0:nh], in_=rstd[:, 0:nh]), r=["p1rstd"], w=["p1rstd"])
                        xnv = xn[:, 0:n].rearrange("p (h d) -> p h d", d=64)
                        S.op("dve", lambda: V.tensor_tensor(out=xnv, in0=pv, in1=rstd[:, 0:nh].unsqueeze(2).to_broadcast([128, nh, 64]), op=ALU.mult),
                             r=["psm", "p1rstd"], w=["xn"])
                        xgv = xg[:, 0:n].rearrange("p (h d) -> p h d", d=64)
                        S.op("pool", lambda: G.tensor_tensor(out=xgv, in0=xnv, in1=gain.unsqueeze(1).to_broadcast([128, nh, 64]), op=ALU.mult),
                             r=["xn", "const"], w=["xg"])
                    else:
                        xgv = xg[:, 0:n].rearrange("p (h d) -> p h d", d=64)
                        S.op("act", lambda: A.activation(out=xgv, in_=pv, func=AF.Copy, scale=float(scale)), r=["psm"], w=["xg"])
                    if do_rope:
                        x1 = xgv[:, :, 0:32]
                        x2 = xgv[:, :, 32:64]
                        cb = cos_t[:, t, :].unsqueeze(1).to_broadcast([128, nh, 32])
                        sbb = sin_t[:, t, :].unsqueeze(1).to_broadcast([128, nh, 32])
                        rav = ra[:, 0:nh * 32].rearrange("p (h d) -> p h d", d=32)
                        rbv = rb[:, 0:nh * 32].rearrange("p (h d) -> p h d", d=32)
                        S.op("pool", lambda: G.tensor_tensor(out=rav, in0=x1, in1=cb, op=ALU.mult), r=["xg"], w=["ra"])
                        S.op("dve", lambda: V.tensor_tensor(out=rbv, in0=x2, in1=sbb, op=ALU.mult), r=["xg"], w=["rb"])
                        S.op("dve", lambda: V.tensor_tensor(out=outv[:, :, 0:32], in0=rav, in1=rbv, op=ALU.subtract), r=["ra", "rb"], w=[kout])
                        S.op("pool", lambda: G.tensor_tensor(out=rav, in0=x2, in1=cb, op=ALU.mult), r=["xg", "ra"], w=["ra"])
                        S.op("dve", lambda: V.tensor_tensor(out=rbv, in0=x1, in1=sbb, op=ALU.mult), r=["xg", "rb"], w=["rb"])
                        S.op("dve", lambda: V.tensor_tensor(out=outv[:, :, 32:64], in0=rav, in1=rbv, op=ALU.add), r=["ra", "rb"], w=[kout])
                    else:
                        S.op("dve", lambda: V.tensor_copy(out=outv, in_=xgv), r=["xg"], w=[kout])

                nblk = len(blocks)
                S.dma("pool", wblk[0][:, :, 0:blocks[0][2]], w_l[:, :, blocks[0][1]:blocks[0][1] + blocks[0][2]], w=["wblk0"])
                for bi, (nm, c0, n) in enumerate(blocks):
                    wb = wblk[bi % 2]
                    kw = f"wblk{bi % 2}"
                    if bi + 1 < nblk:
                        nn = blocks[bi + 1]
                        S.dma("pool", wblk[(bi + 1) % 2][:, :, 0:nn[2]], w_l[:, :, nn[1]:nn[1] + nn[2]], w=[f"wblk{(bi + 1) % 2}"])
                    if nm.startswith("gl"):
                        gi = int(nm[2:])
                        for Tq in range(NQ):
                            for j in range(4):
                                it = Tq * 4 + j
                                pm = psm[it % 2]
                                for c in range(8):
                                    S.op("pe", lambda: T.matmul(pm[:], lhsT=wb[:, c, j * 128:(j + 1) * 128], rhs=hT[:, c, Tq * 512:(Tq + 1) * 512],
                                                                start=(c == 0), stop=(c == 7)), r=[kw, "hT"], w=[f"psm{it % 2}"])
                                gs = gst[it % 2]
                                S.op("act", lambda: A.activation(out=gs[:], in_=pm[:], func=AF.Sigmoid), r=[f"psm{it % 2}"], w=[f"gst{it % 2}"])
                                S.dma("sp", GT[gi * 4 + j, :, Tq * 512:(Tq + 1) * 512], gs[:], r=[f"gst{it % 2}"], w=["GT"])
                        continue
                    for t in range(NT):
                        pm = psm[t % 2]
                        kp = f"psm{t % 2}"
                        for c in range(8):
                            S.op("pe", lambda: T.matmul(pm[:, 0:n], lhsT=hT[:, c, t * 128:(t + 1) * 128], rhs=wb[:, c, 0:n],
                                                        start=(c == 0), stop=(c == 7)), r=[kw, "hT"], w=[kp])
                        S.lastw["psm"] = S.lastw[kp]
                        S.reads["psm"] = S.reads[kp]
                        sg = stg[(t // 4) % 2]
                        ksg = f"stg{(t // 4) % 2}"
                        tt = t % 4
                        if nm in ("aq", "ak", "cq", "ck", "bq", "bk"):
                            pv = pm[:, 0:512].rearrange("p (h d) -> p h d", d=64)
                            obv = ob[:].rearrange("p (h d) -> p h d", d=64)
                            if nm[0] == "b":
                                qk_post(pv, 8, None, False, False, t, obv, 0.125 if nm == "bq" else 1.0, "ob")
                            else:
                                gain = {"aq": aqn, "ak": akn, "cq": cqn, "ck": ckn}[nm][:, l * 64:(l + 1) * 64]
                                qk_post(pv, 8, gain, True, True, t, obv, 1.0, "ob")
                            pt = pst2[t % 2]
                            for j in range(4):
                                S.op("pe", lambda: T.transpose(out=pt[:, j, :], in_=ob[:, j * 128:(j + 1) * 128], identity=ident[:]),
                                     r=["ob"], w=[f"p1pst{t % 2}"])
                            S.op("act", lambda: A.activation(out=sg[:, :, tt * 128:(tt + 1) * 128], in_=pt[:], func=AF.Copy),
                                 r=[f"p1pst{t % 2}"], w=[ksg])
                            if tt == 3:
                                dst = (QT if nm[1] == "q" else KT)[nm[0]]
                                t0_ = (t - 3) * 128
                                S.dma("sp", dst[:, :, t0_:t0_ + 512].rearrange("j p t -> p j t"), sg[:], r=[ksg], w=["QKT"])
                        elif nm in ("av", "bv", "cv"):
                            nh, dv = (4, 128) if nm == "av" else (8, 64)
                            vi = (t % 2) + (0 if nm == "av" else 2)
                            vs = vst[vi]
                            vv = vs[:, 0:nh * (dv + 1)].rearrange("p (h e) -> p h e", e=dv + 1)[:, :, 0:dv]
                            S.op("act", lambda: A.activation(out=vv, in_=pm[:, 0:512].rearrange("p (h e) -> p h e", e=dv), func=AF.Copy),
                                 r=[kp], w=[f"vst{vi}"])
                            S.dma("sp", VX[nm[0]][t * 128:(t + 1) * 128, :], vs[:, 0:nh * (dv + 1)], r=[f"vst{vi}"], w=["VX"])
                        else:
                            obv = ob[:, 0:256].rearrange("p (h d) -> p h d", d=64)
                            qk_post(pm[:, 0:256].rearrange("p (h d) -> p h d", d=64), 4, None, False, True, t, obv, 1.0, "ob")
                            obk = ob[:, 256:320].rearrange("p (h d) -> p h d", d=64)
                            qk_post(pm[:, 256:320].rearrange("p (h d) -> p h d", d=64), 1, ikn[:, l * 64:(l + 1) * 64], True, True, t, obk, 1.0, "ob")
                            S.op("dve", lambda: V.tensor_copy(out=ob[:, 320:384], in_=ob[:, 256:320]), r=["ob"], w=["ob"])
                            S.op("act", lambda: A.activation(out=iw_all[:, t, :], in_=pm[:, 320:324], func=AF.Copy), r=[kp], w=["iw_all"])
                            pt = pst2[t % 2]
                            for j in range(3):
                                S.op("pe", lambda: T.transpose(out=pt[:, j, :], in_=ob[:, j * 128:(j + 1) * 128], identity=ident[:]),
                                     r=["ob"], w=[f"p1pst{t % 2}"])
                            S.op("act", lambda: A.activation(out=sg[:, 0:3, tt * 128:(tt + 1) * 128], in_=pt[:, 0:3, :], func=AF.Copy),
                                 r=[f"p1pst{t % 2}"], w=[ksg])
                            if tt == 3:
                                t0_ = (t - 3) * 128
                                S.dma("sp", IQT[:, :, t0_:t0_ + 512].rearrange("j p t -> p j t"), sg[:, 0:2, :], r=[ksg], w=["QKT"])
                                S.dma("sp", IKT[:, t0_:t0_ + 512], sg[:, 2, :], r=[ksg], w=["QKT"])
                S.barrier()
                ph.close()

            for ph in ([ExitStack()] if 'pa' in PH else []):
                kt = sb(ph, "a_kt", [128, 4, SL], BF16)
                vx = sb(ph, "a_vx", [128, NT, 4 * 129], BF16)
                qt2 = [sb(ph, f"a_qt{i}", [128, 4, 512], BF16) for i in range(2)]
                pT2 = [sb(ph, f"a_pT{i}", [128, 512], BF16) for i in range(2)]
                o2 = [sb(ph, f"a_o{i}", [128, 4, 129], F32) for i in range(2)]
                rl = sb(ph, "a_rl", [128, 8], F32)
                y0 = sb(ph, "a_y0", [128, 4, 128], F32)
                y1 = sb(ph, "a_y1", [128, 4, 128], F32)
                ss = sb(ph, "a_ss", [128, 8], F32)
                yb = sb(ph, "a_yb", [128, 4, 128], BF16)
                yts = [sb(ph, f"a_yts{i}", [128, 512], BF16) for i in range(2)]
                pss = [ps(ph, f"a_pss{i}", [128, 512], F32) for i in range(2)]
                po = ps(ph, "a_po", [128, 4, 512], F32)
                ptr = ps(ph, "a_ptr", [128, 4, 128], BF16)
                S.dma("sp", kt[:], KT["a"].rearrange("j p t -> p j t"), r=["QKT"], w=["a_kt"])
                for v4 in range(4):
                    S.dma("sp", vx[:, v4 * 8:(v4 + 1) * 8, :], VX["a"][v4 * 1024:(v4 + 1) * 1024, :].rearrange("(t p) e -> p t e", p=128), r=["VX"], w=["a_vx"])
                it = 0
                for Q in range(NQ):
                    qt = qt2[Q % 2]
                    kq = f"a_qt{Q % 2}"
                    S.dma("sp", qt[:], QT["a"][:, :, Q * 512:(Q + 1) * 512].rearrange("j p t -> p j t"), r=["QKT"], w=[kq])
                    for h in range(4):
                        for c in range(2):
                            nj = 4 * Q + 4
                            for j in range(nj):
                                jj = j - 4 * Q
                                c0 = max(0, jj) * 128
                                b = it % 2
                                it += 1
                                S.op("pe", lambda: T.matmul(pss[b][:, c0:512], lhsT=kt[c * 64:(c + 1) * 64, h, j * 128:(j + 1) * 128],
                                                            rhs=qt[c * 64:(c + 1) * 64, h, c0:512], start=True, stop=True),
                                     r=["a_kt", kq], w=[f"a_pss{b}"])
                                S.op("act", lambda: A.activation(out=pT2[b][:, c0:512], in_=pss[b][:, c0:512], func=AF.Exp),
                                     r=[f"a_pss{b}"], w=[f"a_pT{b}"])
                                if jj >= 0:
                                    S.op("pool", lambda: G.memset(pT2[b][64:128, c0:c0 + 64], 0.0), w=[f"a_pT{b}"])
                                for qq in range(max(0, jj), 4):
                                    S.op("pe", lambda: T.matmul(po[:, qq, 0:129], lhsT=pT2[b][:, qq * 128:(qq + 1) * 128],
                                                                rhs=vx[:, j, h * 129:(h + 1) * 129], start=(j == 0), stop=(j == 4 * Q + qq)),
                                         r=[f"a_pT{b}", "a_vx"], w=["a_po"])
                            S.op("act", lambda: A.activation(out=o2[c][:], in_=po[:, :, 0:129], func=AF.Copy), r=["a_po"], w=[f"a_o{c}"])
                        S.op("dve", lambda: V.reciprocal(out=rl[:, 0:4], in_=o2[0][:, :, 128]), r=["a_o0"], w=["a_rl"])
                        S.op("dve", lambda: V.reciprocal(out=rl[:, 4:8], in_=o2[1][:, :, 128]), r=["a_o1"], w=["a_rl"])
                        S.op("dve", lambda: V.tensor_scalar(out=rl[:, 4:8], in0=rl[:, 4:8], scalar1=nlam_t[:, l:l + 1], scalar2=None, op0=ALU.mult),
                             r=["a_rl", "const"], w=["a_rl"])
                        S.op("dve", lambda: V.tensor_tensor(out=y0[:], in0=o2[0][:, :, 0:128], in1=rl[:, 0:4].unsqueeze(2).to_broadcast([128, 4, 128]), op=ALU.mult),
                             r=["a_o0", "a_rl"], w=["a_y0"])
                        S.op("pool", lambda: G.tensor_tensor(out=y1[:], in0=o2[1][:, :, 0:128], in1=rl[:, 4:8].unsqueeze(2).to_broadcast([128, 4, 128]), op=ALU.mult),
                             r=["a_o1", "a_rl"], w=["a_y1"])
                        S.op("dve", lambda: V.tensor_tensor(out=y0[:], in0=y0[:], in1=y1[:], op=ALU.add), r=["a_y0", "a_y1"], w=["a_y0"])
                        S.op("pool", lambda: G.tensor_tensor(out=y1[:], in0=y0[:], in1=y0[:], op=ALU.mult), r=["a_y0"], w=["a_y1"])
                        S.op("dve", lambda: V.tensor_reduce(out=ss[:, 0:4], in_=y1[:], axis=AX.X, op=ALU.add), r=["a_y1"], w=["a_ss"])
                        S.op("act", lambda: A.activation(out=ss[:, 0:4], in_=ss[:, 0:4], func=AF.Sqrt, bias=EPS, scale=1.0 / 128), r=["a_ss"], w=["a_ss"])
                        S.op("dve", lambda: V.reciprocal(out=ss[:, 0:4], in_=ss[:, 0:4]), r=["a_ss"], w=["a_ss"])
                        S.op("dve", lambda: V.tensor_tensor(out=y0[:], in0=y0[:], in1=ss[:, 0:4].unsqueeze(2).to_broadcast([128, 4, 128]), op=ALU.mult),
                             r=["a_y0", "a_ss"], w=["a_y0"])
                        S.op("dve", lambda: V.tensor_tensor(out=yb[:], in0=y0[:], in1=subg[:, l * 128:(l + 1) * 128].unsqueeze(1).to_broadcast([128, 4, 128]), op=ALU.mult),
                             r=["a_y0", "const"], w=["a_yb"])
                        for qq in range(4):
                            S.op("pe", lambda: T.transpose(out=ptr[:, qq, :], in_=yb[:, qq, :], identity=ident[:]), r=["a_yb"], w=["a_ptr"])
                        ys = yts[h % 2]
                        S.op("act", lambda: A.activation(out=ys[:], in_=ptr[:].rearrange("p a b -> p (a b)"), func=AF.Copy), r=["a_ptr"], w=[f"a_yts{h % 2}"])
                        S.dma("sp", YT[h, :, Q * 512:(Q + 1) * 512], ys[:], r=[f"a_yts{h % 2}"], w=["YT"])
                S.barrier()
                ph.close()

            for ph in ([ExitStack()] if 'pb' in PH else []):
                kt = sb(ph, "b_kt", [128, 4, SL], BF16)
                vx = sb(ph, "b_vx", [128, NT, 8 * 65], BF16)
                qt2 = [sb(ph, f"b_qt{i}", [128, 4, 512], BF16) for i in range(2)]
                e2 = [sb(ph, f"b_e{i}", [128, 512], F32) for i in range(2)]
                nl2 = [sb(ph, f"b_nl{i}", [128, 512], BF16) for i in range(2)]
                aT2 = [sb(ph, f"b_aT{i}", [128, 512], BF16) for i in range(2)]
                R = sb(ph, "b_R", [128, 512], BF16)
                qm = sb(ph, "b_qm", [128, 512], BF16)
                yq = sb(ph, "b_yq", [128, 4, 512], BF16)
                yts = [sb(ph, f"b_yts{i}", [128, 512], BF16) for i in range(2)]
                psz = [ps(ph, f"b_psz{i}", [128, 512], F32) for i in range(2)]
                psc = [ps(ph, f"b_psc{i}", [128, 512], F32) for i in range(2)]
                po2 = [ps(ph, f"b_po{i}", [128, 512], F32) for i in range(2)]
                ptr = ps(ph, "b_ptr", [128, 4, 128], BF16)
                S.dma("sp", kt[:], KT["b"].rearrange("j p t -> p j t"), r=["QKT"], w=["b_kt"])
                for v4 in range(4):
                    S.dma("sp", vx[:, v4 * 8:(v4 + 1) * 8, :], VX["b"][v4 * 1024:(v4 + 1) * 1024, :].rearrange("(t p) e -> p t e", p=128), r=["VX"], w=["b_vx"])
                it = 0
                ih = 0
                for Q in range(NQ):
                    qt = qt2[Q % 2]
                    kq = f"b_qt{Q % 2}"
                    S.dma("sp", qt[:], QT["b"][:, :, Q * 512:(Q + 1) * 512].rearrange("j p t -> p j t"), r=["QKT"], w=[kq])
                    for h in range(8):
                        pr, bs = h // 2, (h % 2) * 64
                        po = po2[ih % 2]
                        kpo = f"b_po{ih % 2}"
                        ih += 1
                        S.op("pool", lambda: G.memset(R[:], 0.0), w=["b_R"])
                        S.op("pool", lambda: G.memset(qm[:], 0.0), w=["b_qm"])
                        S.op("dve", lambda: V.tensor_copy(out=qm[bs:bs + 64, :], in_=qt[bs:bs + 64, pr, :]), r=[kq, "b_qm"], w=["b_qm"])
                        first_po = True
                        jtop = 4 * Q + 3
                        for j in range(jtop, -1, -1):
                            jj = j - 4 * Q
                            c0 = max(0, jj) * 128
                            b = it % 2
                            it += 1
                            kT_ = kt[bs:bs + 64, pr, j * 128:(j + 1) * 128]
                            q_ = qt[bs:bs + 64, pr, c0:512]
                            S.op("pe", lambda: T.matmul(psz[b][:, c0:512], lhsT=kT_, rhs=q_, start=True, stop=True),
                                 r=["b_kt", kq], w=[f"b_psz{b}"])
                            S.op("act", lambda: A.activation(out=e2[b][:, c0:512], in_=psz[b][:, c0:512], func=AF.Exp),
                                 r=[f"b_psz{b}"], w=[f"b_e{b}"])
                            S.op("act", lambda: A.activation(out=nl2[b][:, c0:512], in_=e2[b][:, c0:512], func=AF.Ln, bias=1.0),
                                 r=[f"b_e{b}"], w=[f"b_nl{b}"])
                            if jj >= 0:
                                S.op("dve", lambda: V.tensor_tensor(out=nl2[b][:, c0:c0 + 128], in0=nl2[b][:, c0:c0 + 128], in1=strict[:], op=ALU.mult),
                                     r=[f"b_nl{b}", "const"], w=[f"b_nl{b}"])
                            S.op("pe", lambda: T.matmul(psc[b][:, c0:512], lhsT=kt[:, pr, j * 128:(j + 1) * 128], rhs=qm[:, c0:512], start=True, stop=False),
                                 r=["b_kt", "b_qm"], w=[f"b_psc{b}"])
                            last = (j == jtop)
                            S.op("pe", lambda: T.matmul(psc[b][:, c0:512], lhsT=ntri[:], rhs=nl2[b][:, c0:512], start=False, stop=last),
                                 r=[f"b_nl{b}", "const"], w=[f"b_psc{b}"])
                            if not last:
                                S.op("pe", lambda: T.matmul(psc[b][:, c0:512], lhsT=nones[:], rhs=R[:, c0:512], start=False, stop=True),
                                     r=["b_R", "const"], w=[f"b_psc{b}"])
                            S.op("act", lambda: A.activation(out=aT2[b][:, c0:512], in_=psc[b][:, c0:512], func=AF.Exp),
                                 r=[f"b_psc{b}"], w=[f"b_aT{b}"])
                            if jj >= 0:
                                S.op("dve", lambda: V.tensor_tensor(out=aT2[b][:, c0:c0 + 128], in0=aT2[b][:, c0:c0 + 128], in1=strict[:], op=ALU.mult),
                                     r=[f"b_aT{b}", "const"], w=[f"b_aT{b}"])
                            for qq in range(max(0, jj), 4):
                                S.op("pe", lambda: T.matmul(po[:, qq * 65:(qq + 1) * 65], lhsT=aT2[b][:, qq * 128:(qq + 1) * 128],
                                                            rhs=vx[:, j, h * 65:(h + 1) * 65], start=first_po, stop=(j == 0),
                                                            skip_group_check=True),
                                     r=[f"b_aT{b}", "b_vx"], w=[kpo])
                                first_po = False
                            if j > 0:
                                S.op("dve", lambda: V.tensor_tensor(out=R[:, c0:512], in0=R[:, c0:512], in1=nl2[b][:, c0:512], op=ALU.add),
                                     r=["b_R", f"b_nl{b}"], w=["b_R"])
                        S.op("act", lambda: A.activation(out=yq[:, :, h * 64:(h + 1) * 64],
                                                         in_=po[:, 0:260].rearrange("p (a e) -> p a e", e=65)[:, :, 0:64], func=AF.Copy),
                             r=[kpo], w=["b_yq"])
                    for cch in range(4):
                        for qq in range(4):
                            S.op("pe", lambda: T.transpose(out=ptr[:, qq, :], in_=yq[:, qq, cch * 128:(cch + 1) * 128], identity=ident[:]),
                                 r=["b_yq"], w=["b_ptr"])
                        ys = yts[cch % 2]
                        S.op("act", lambda: A.activation(out=ys[:], in_=ptr[:].rearrange("p a b -> p (a b)"), func=AF.Copy), r=["b_ptr"], w=[f"b_yts{cch % 2}"])
                        S.dma("sp", YT[4 + cch, :, Q * 512:(Q + 1) * 512], ys[:], r=[f"b_yts{cch % 2}"], w=["YT"])
                S.barrier()
                ph.close()

            for ph in ([ExitStack()] if 'pc' in PH else []):
                kt = sb(ph, "c_kt", [128, 4, SL], BF16)
                vx = sb(ph, "c_vx", [128, NT, 8 * 65], BF16)
                ikt = sb(ph, "c_ikt", [128, SL], BF16)
                qt2 = [sb(ph, f"c_qt{i}", [128, 4, 512], BF16) for i in range(2)]
                iqt2 = [sb(ph, f"c_iqt{i}", [128, 2, 512], BF16) for i in range(2)]
                score = sb(ph, "c_score", [128, SL], F32)
                maskq = sb(ph, "c_maskq", [128, SL], BF16)
                maskT = sb(ph, "c_maskT", [128, NT, 512], BF16)
                rb2 = [sb(ph, f"c_r{i}", [128, 512], F32) for i in range(2)]
                pT2 = [sb(ph, f"c_pT{i}", [128, 512], BF16) for i in range(2)]
                bs_ = sb(ph, "c_bis", [128, 8], F32)
                osb = sb(ph, "c_o", [128, 4, 65], F32)
                rl = sb(ph, "c_rl", [128, 4], F32)
                yq = sb(ph, "c_yq", [128, 4, 512], BF16)
                yts = [sb(ph, f"c_yts{i}", [128, 512], BF16) for i in range(2)]
                psi = [ps(ph, f"c_psi{i}", [128, 512], F32) for i in range(2)]
                pss = [ps(ph, f"c_pss{i}", [128, 512], F32) for i in range(2)]
                po2 = [ps(ph, f"c_po{i}", [128, 512], F32) for i in range(2)]
                ptr = ps(ph, "c_ptr", [128, 4, 128], BF16)
                S.dma("sp", kt[:], KT["c"].rearrange("j p t -> p j t"), r=["QKT"], w=["c_kt"])
                for v4 in range(4):
                    S.dma("sp", vx[:, v4 * 8:(v4 + 1) * 8, :], VX["c"][v4 * 1024:(v4 + 1) * 1024, :].rearrange("(t p) e -> p t e", p=128), r=["VX"], w=["c_vx"])
                S.dma("sp", ikt[:], IKT[:, :], r=["QKT"], w=["c_ikt"])
                it = 0
                ih = 0
                ir = 0
                for Q in range(NQ):
                    qt = qt2[Q % 2]
                    iqt = iqt2[Q % 2]
                    kq = f"c_qt{Q % 2}"
                    S.dma("sp", qt[:], QT["c"][:, :, Q * 512:(Q + 1) * 512].rearrange("j p t -> p j t"), r=["QKT"], w=[kq])
                    S.dma("sp", iqt[:], IQT[:, :, Q * 512:(Q + 1) * 512].rearrange("j p t -> p j t"), r=["QKT"], w=[kq])
                    for qq in range(4):
                        qb = 4 * Q + qq
                        nadm = (qb + 1) * 128
                        nkt = (nadm + 511) // 512
                        for k5 in range(nkt):
                            wd = min(512, nadm - k5 * 512)
                            for h in range(4):
                                b = ir % 2
                                ir += 1
                                hb_ = (h % 2) * 64
                                S.op("pe", lambda: T.matmul(psi[b][:, 0:wd], lhsT=iqt[hb_:hb_ + 64, h // 2, qq * 128:(qq + 1) * 128],
                                                            rhs=ikt[hb_:hb_ + 64, k5 * 512:k5 * 512 + wd], start=True, stop=True),
                                     r=[kq, "c_ikt"], w=[f"c_psi{b}"])
                                S.op("act", lambda: A.activation(out=rb2[b][:, 0:wd], in_=psi[b][:, 0:wd], func=AF.Relu),
                                     r=[f"c_psi{b}"], w=[f"c_r{b}"])
                                sc = score[:, k5 * 512:k5 * 512 + wd]
                                if h == 0:
                                    S.op("dve", lambda: V.tensor_scalar(out=sc, in0=rb2[b][:, 0:wd], scalar1=iw_all[:, qb, 0:1], scalar2=None, op0=ALU.mult),
                                         r=[f"c_r{b}", "iw_all"], w=["c_score"])
                                else:
                                    S.op("dve", lambda: V.scalar_tensor_tensor(out=sc, in0=rb2[b][:, 0:wd], scalar=iw_all[:, qb, h:h + 1], in1=sc,
                                                                               op0=ALU.mult, op1=ALU.add),
                                         r=[f"c_r{b}", "iw_all", "c_score"], w=["c_score"])
                        S.op("dve", lambda: V.memset(score[0:64, qb * 128 + 64:(qb + 1) * 128], NEG), r=["c_score"], w=["c_score"])
                        KB = ["c_bis"]
                        if qb >= 2:
                            nfull = qb * 128 + 64
                            S.op("dve", lambda: V.tensor_reduce(out=bs_[:, 5:6], in_=score[:, 0:nadm], axis=AX.X, op=ALU.max), r=["c_score"], w=KB)
                            S.op("dve", lambda: V.tensor_reduce(out=bs_[:, 0:1], in_=score[:, 0:nfull], axis=AX.X, op=ALU.min), r=["c_score"], w=KB)
                            S.op("dve", lambda: V.tensor_scalar(out=bs_[:, 0:1], in0=bs_[:, 0:1], scalar1=-1.0, scalar2=None, op0=ALU.add), r=KB, w=KB)
                            S.op("dve", lambda: V.tensor_tensor(out=bs_[:, 1:2], in0=bs_[:, 5:6], in1=bs_[:, 0:1], op=ALU.subtract), r=KB, w=KB)
                            for i in range(NSTEP):
                                f = 2.0 ** -(i + 1)
                                S.op("dve", lambda: V.scalar_tensor_tensor(out=bs_[:, 2:3], in0=bs_[:, 1:2], scalar=float(f), in1=bs_[:, 0:1],
                                                                           op0=ALU.mult, op1=ALU.add), r=KB, w=KB)
                                S.op("dve", lambda: V.tensor_scalar(out=maskq[:, 0:nadm], in0=score[:, 0:nadm], scalar1=bs_[:, 2:3], scalar2=None,
                                                                    op0=ALU.is_gt, op1=ALU.add, accum_out=bs_[:, 3:4]),
                                     r=KB + ["c_score", "c_maskq"], w=KB + ["c_maskq"])
                                S.op("dve", lambda: V.tensor_scalar(out=bs_[:, 4:5], in0=bs_[:, 3:4], scalar1=float(TOPK), scalar2=float(f),
                                                                    op0=ALU.is_ge, op1=ALU.mult), r=KB, w=KB)
                                S.op("dve", lambda: V.scalar_tensor_tensor(out=bs_[:, 0:1], in0=bs_[:, 4:5], scalar=bs_[:, 1:2], in1=bs_[:, 0:1],
                                                                           op0=ALU.mult, op1=ALU.add), r=KB, w=KB)
                        else:
                            S.op("dve", lambda: V.memset(bs_[:, 0:1], -1.0e29), r=KB, w=KB)
                        S.op("dve", lambda: V.tensor_scalar(out=maskq[:, 0:nadm], in0=score[:, 0:nadm], scalar1=bs_[:, 0:1], scalar2=None, op0=ALU.is_gt),
                             r=KB + ["c_score", "c_maskq"], w=["c_maskq"])
                        for k1 in range(0, qb + 1, 4):
                            nk = min(4, qb + 1 - k1)
                            for u in range(nk):
                                S.op("pe", lambda: T.transpose(out=ptr[:, u, :], in_=maskq[:, (k1 + u) * 128:(k1 + u + 1) * 128], identity=ident[:]),
                                     r=["c_maskq"], w=["c_ptr"])
                            S.op("act", lambda: A.activation(out=maskT[:, k1:k1 + nk, qq * 128:(qq + 1) * 128], in_=ptr[:, 0:nk, :], func=AF.Copy),
                                 r=["c_ptr"], w=["c_maskT"])
                    for h in range(8):
                        pr, bs = h // 2, (h % 2) * 64
                        po = po2[ih % 2]
                        kpo = f"c_po{ih % 2}"
                        ih += 1
                        first_po = True
                        nj = 4 * Q + 4
                        for j in range(nj):
                            jj = j - 4 * Q
                            c0 = max(0, jj) * 128
                            b = it % 2
                            it += 1
                            S.op("pe", lambda: T.matmul(pss[b][:, c0:512], lhsT=kt[bs:bs + 64, pr, j * 128:(j + 1) * 128],
                                                        rhs=qt[bs:bs + 64, pr, c0:512], start=True, stop=True),
                                 r=["c_kt", kq], w=[f"c_pss{b}"])
                            S.op("act", lambda: A.activation(out=pT2[b][:, c0:512], in_=pss[b][:, c0:512], func=AF.Exp),
                                 r=[f"c_pss{b}"], w=[f"c_pT{b}"])
                            S.op("dve", lambda: V.tensor_tensor(out=pT2[b][:, c0:512], in0=pT2[b][:, c0:512], in1=maskT[:, j, c0:512], op=ALU.mult),
                                 r=[f"c_pT{b}", "c_maskT"], w=[f"c_pT{b}"])
                            for q4 in range(max(0, jj), 4):
                                S.op("pe", lambda: T.matmul(po[:, q4 * 65:(q4 + 1) * 65], lhsT=pT2[b][:, q4 * 128:(q4 + 1) * 128],
                                                            rhs=vx[:, j, h * 65:(h + 1) * 65], start=first_po, stop=(j == nj - 1),
                                                            skip_group_check=True),
                                     r=[f"c_pT{b}", "c_vx"], w=[kpo])
                                first_po = False
                        S.op("act", lambda: A.activation(out=osb[:], in_=po[:, 0:260].rearrange("p (a e) -> p a e", e=65), func=AF.Copy),
                             r=[kpo], w=["c_o"])
                        S.op("dve", lambda: V.reciprocal(out=rl[:], in_=osb[:, :, 64]), r=["c_o"], w=["c_rl"])
                        S.op("dve", lambda: V.tensor_tensor(out=yq[:, :, h * 64:(h + 1) * 64], in0=osb[:, :, 0:64],
                                                            in1=rl[:].unsqueeze(2).to_broadcast([128, 4, 64]), op=ALU.mult),
                             r=["c_o", "c_rl"], w=["c_yq"])
                    for cch in range(4):
                        for q4 in range(4):
                            S.op("pe", lambda: T.transpose(out=ptr[:, q4, :], in_=yq[:, q4, cch * 128:(cch + 1) * 128], identity=ident[:]),
                                 r=["c_yq"], w=["c_ptr"])
                        ys = yts[cch % 2]
                        S.op("act", lambda: A.activation(out=ys[:], in_=ptr[:].rearrange("p a b -> p (a b)"), func=AF.Copy), r=["c_ptr"], w=[f"c_yts{cch % 2}"])
                        S.dma("sp", YT[8 + cch, :, Q * 512:(Q + 1) * 512], ys[:], r=[f"c_yts{cch % 2}"], w=["YT"])
                S.barrier()
                ph.close()

            for ph in ([ExitStack()] if 'pm' in PH else []):
                wbr = [sb(ph, f"m_wbr{i}", [128, 4, D], BF16) for i in range(3)]
                wo = sb(ph, "m_wo", [128, 8, D], BF16)
                yt2 = [sb(ph, f"m_yt{i}", [128, 12, 512], BF16) for i in range(2)]
                gt2 = [sb(ph, f"m_gt{i}", [128, 24, 512], BF16) for i in range(2)]
                m0 = sb(ph, "m_m0", [128, 512], F32)
                m1 = sb(ph, "m_m1", [128, 512], F32)
                m2 = sb(ph, "m_m2", [128, 512], F32)
                mT = sb(ph, "m_mT", [128, 8, 512], BF16)
                xb2 = [sb(ph, f"m_x{i}", [128, D], F32) for i in range(2)]
                ps3 = [ps(ph, f"m_ps{i}", [128, 512], F32) for i in range(3)]
                pso = [ps(ph, f"m_pso{i}", [128, 512], F32) for i in range(2)]
                for i in range(3):
                    load_w_bf16(wbr[i][:], w_br[i][l].rearrange("(c p) n -> p c n", p=128), "m_w")
                load_w_bf16(wo[:], w_out[l].rearrange("(c p) n -> p c n", p=128), "m_w")
                io = 0
                for Tq in range(NQ):
                    yt = yt2[Tq % 2]
                    gt = gt2[Tq % 2]
                    kin = f"m_in{Tq % 2}"
                    for b3 in range(3):
                        S.dma("sp", yt[:, b3 * 4:(b3 + 1) * 4, :], YT[b3 * 4:(b3 + 1) * 4, :, Tq * 512:(Tq + 1) * 512].rearrange("j p t -> p j t"), r=["YT"], w=[kin])
                        S.dma("sp", gt[:, b3 * 8:(b3 + 1) * 8, :], GT[b3 * 8:(b3 + 1) * 8, :, Tq * 512:(Tq + 1) * 512].rearrange("j p t -> p j t"), r=["GT"], w=[kin])
                    for fc in range(8):
                        for br in range(3):
                            for kc in range(4):
                                S.op("pe", lambda: T.matmul(ps3[br][:], lhsT=wbr[br][:, kc, fc * 128:(fc + 1) * 128], rhs=yt[:, br * 4 + kc, :],
                                                            start=(kc == 0), stop=(kc == 3)), r=["m_w", kin], w=[f"m_ps{br}"])
                        S.op("dve", lambda: V.tensor_tensor(out=m0[:], in0=ps3[0][:], in1=gt[:, fc, :], op=ALU.mult), r=["m_ps0", kin], w=["m_m0"])
                        S.op("dve", lambda: V.tensor_tensor(out=m1[:], in0=ps3[1][:], in1=gt[:, 8 + fc, :], op=ALU.mult), r=["m_ps1", kin], w=["m_m1"])
                        S.op("dve", lambda: V.tensor_tensor(out=m2[:], in0=ps3[2][:], in1=gt[:, 16 + fc, :], op=ALU.mult), r=["m_ps2", kin], w=["m_m2"])
                        S.op("pool", lambda: G.tensor_tensor(out=m0[:], in0=m0[:], in1=m1[:], op=ALU.add), r=["m_m0", "m_m1"], w=["m_m0"])
                        S.op("pool", lambda: G.tensor_tensor(out=mT[:, fc, :], in0=m0[:], in1=m2[:], op=ALU.add), r=["m_m0", "m_m2"], w=["m_mT"])
                    for tt in range(4):
                        t = Tq * 4 + tt
                        xt = xb2[t % 2]
                        kx = f"m_x{t % 2}"
                        S.dma("sp", xt[:], xsrc[t * 128:(t + 1) * 128, :], r=[f"xr{t}"], w=[kx])
                        for nb in range(2):
                            pq = pso[io % 2]
                            kpq = f"m_pso{io % 2}"
                            io += 1
                            for fc in range(8):
                                S.op("pe", lambda: T.matmul(pq[:], lhsT=mT[:, fc, tt * 128:(tt + 1) * 128], rhs=wo[:, fc, nb * 512:(nb + 1) * 512],
                                                            start=(fc == 0), stop=(fc == 7)), r=["m_mT", "m_w"], w=[kpq])
                            S.op("dve", lambda: V.tensor_tensor(out=xt[:, nb * 512:(nb + 1) * 512], in0=pq[:], in1=xt[:, nb * 512:(nb + 1) * 512], op=ALU.add),
                                 r=[kpq, kx], w=[kx])
                        S.dma("sp", xres[t * 128:(t + 1) * 128, :], xt[:], r=[kx], w=[f"xr{t}"])
                S.barrier()
                ph.close()

            for ph in ([ExitStack()] if 'pf' in PH else []):
                wu = sb(ph, "f_wu", [128, 8, 4 * D], BF16)
                wd = sb(ph, "f_wd", [128, 32, D], BF16)
                xb2 = [sb(ph, f"f_x{i}", [128, D], F32) for i in range(2)]
                hb = sb(ph, "f_hb", [128, D], BF16)
                junk = sb(ph, "f_junk", [128, D], BF16)
                ssq = sb(ph, "f_ssq", [128, 8], F32)
                rstd = sb(ph, "f_rstd", [128, 8], F32)
                hmT = sb(ph, "f_hmT", [128, 8, 256], BF16)
                rr2 = [sb(ph, f"f_r{i}", [128, 256], F32) for i in range(2)]
                uT = sb(ph, "f_uT", [128, 32, 256], BF16)
                pst2 = [ps(ph, f"f_pst{i}", [128, 4, 128], BF16) for i in range(2)]
                psu = [ps(ph, f"f_psu{i}", [128, 256], F32) for i in range(2)]
                psd = [ps(ph, f"f_psd{i}", [128, 512], F32) for i in range(2)]
                for c in range(8):
                    load_w_bf16(wu[:, c, :], w_up[l, c * 128:(c + 1) * 128, :], "f_w")
                for c4 in range(4):
                    load_w_bf16(wd[:, c4 * 8:(c4 + 1) * 8, :], w_down[l, c4 * 1024:(c4 + 1) * 1024, :].rearrange("(c p) n -> p c n", p=128), "f_w")
                io = 0
                for tg in range(NT // 2):
                    for t2 in range(2):
                        t = tg * 2 + t2
                        xt = xb2[t2]
                        kx = f"f_x{t2}"
                        S.dma("sp", xt[:], xres[t * 128:(t + 1) * 128, :], r=[f"xr{t}"], w=[kx])
                        norm_transpose(xt[:], kx, gM, l, hb, pst2, lambda c, t2=t2: hmT[:, c, t2 * 128:(t2 + 1) * 128], "f_hmT", ssq, rstd, junk, "f")
                    for fc in range(32):
                        pu = psu[fc % 2]
                        for c in range(8):
                            S.op("pe", lambda: T.matmul(pu[:], lhsT=wu[:, c, fc * 128:(fc + 1) * 128], rhs=hmT[:, c, :], start=(c == 0), stop=(c == 7)),
                                 r=["f_w", "f_hmT"], w=[f"f_psu{fc % 2}"])
                        rr = rr2[fc % 2]
                        S.op("act", lambda: A.activation(out=rr[:], in_=pu[:], func=AF.Relu), r=[f"f_psu{fc % 2}"], w=[f"f_r{fc % 2}"])
                        if fc % 2 == 0:
                            S.op("pool", lambda: G.tensor_tensor(out=uT[:, fc, :], in0=rr[:], in1=rr[:], op=ALU.mult), r=[f"f_r{fc % 2}"], w=["f_uT"])
                        else:
                            S.op("dve", lambda: V.tensor_tensor(out=uT[:, fc, :], in0=rr[:], in1=rr[:], op=ALU.mult), r=[f"f_r{fc % 2}"], w=["f_uT"])
                    for t2 in range(2):
                        t = tg * 2 + t2
                        xt = xb2[t2]
                        kx = f"f_x{t2}"
                        for nb in range(2):
                            pq = psd[io % 2]
                            kpq = f"f_psd{io % 2}"
                            io += 1
                            for fc in range(32):
                                S.op("pe", lambda: T.matmul(pq[:], lhsT=uT[:, fc, t2 * 128:(t2 + 1) * 128], rhs=wd[:, fc, nb * 512:(nb + 1) * 512],
                                                            start=(fc == 0), stop=(fc == 31)), r=["f_uT", "f_w"], w=[kpq])
                            S.op("dve", lambda: V.tensor_tensor(out=xt[:, nb * 512:(nb + 1) * 512], in0=pq[:], in1=xt[:, nb * 512:(nb + 1) * 512], op=ALU.add),
                                 r=[kpq, kx], w=[kx])
                        S.dma("sp", xres[t * 128:(t + 1) * 128, :], xt[:], r=[kx], w=[f"xr{t}"])
                S.barrier()
                ph.close()

            for ph in ([ExitStack()] if 'pp' in PH else []):
                wg = sb(ph, "e_wg", [128, 8, D], BF16)
                wp = sb(ph, "e_wp", [128, 2, D], BF16)
                xb2 = [sb(ph, f"e_x{i}", [128, D], F32) for i in range(2)]
                pb2 = [sb(ph, f"e_p{i}", [128, 256], F32) for i in range(2)]
                pbb = sb(ph, "e_pbb", [128, 256], BF16)
                hb = sb(ph, "e_hb", [128, D], BF16)
                junk = sb(ph, "e_junk", [128, D], BF16)
                ssq = sb(ph, "e_ssq", [128, 8], F32)
                rstd = sb(ph, "e_rstd", [128, 8], F32)
                hpT = sb(ph, "e_hpT", [128, 8, 128], BF16)
                pT = sb(ph, "e_pT", [128, 2, 128], BF16)
                sg = sb(ph, "e_sg", [128, 512], F32)
                tm = sb(ph, "e_tm", [128, 512], F32)
                pst2 = [ps(ph, f"e_pst{i}", [128, 4, 128], BF16) for i in range(2)]
                ptp = ps(ph, "e_ptp", [128, 2, 128], BF16)
                psg = [ps(ph, f"e_psg{i}", [128, 512], F32) for i in range(2)]
                psp = [ps(ph, f"e_psp{i}", [128, 512], F32) for i in range(2)]
                load_w_bf16(wg[:], w_ple_gate[l].rearrange("(c p) n -> p c n", p=128), "e_w")
                load_w_bf16(wp[:], w_ple_proj[l].rearrange("(c p) n -> p c n", p=128), "e_w")
                dst = out_d if l == depth - 1 else xres
                kdst = "out"
                io = 0
                for t in range(NT):
                    xt = xb2[t % 2]
                    kx = f"e_x{t % 2}"
                    pb = pb2[t % 2]
                    S.dma("sp", xt[:], xres[t * 128:(t + 1) * 128, :], r=[f"xr{t}"], w=[kx])
                    S.dma("sp", pb[:], p_in[l, t * 128:(t + 1) * 128, :], w=[f"e_p{t % 2}"])
                    norm_transpose(xt[:], kx, gP, l, hb, pst2, lambda c: hpT[:, c, :], "e_hpT", ssq, rstd, junk, "e")
                    S.op("pool", lambda: G.tensor_copy(out=pbb[:], in_=pb[:]), r=[f"e_p{t % 2}"], w=["e_pbb"])
                    for c in range(2):
                        S.op("pe", lambda: T.transpose(out=ptp[:, c, :], in_=pbb[:, c * 128:(c + 1) * 128], identity=ident[:]), r=["e_pbb"], w=["e_ptp"])
                    S.op("act", lambda: A.activation(out=pT[:], in_=ptp[:], func=AF.Copy), r=["e_ptp"], w=["e_pT"])
                    for nb in range(2):
                        b = io % 2
                        io += 1
                        for c in range(8):
                            S.op("pe", lambda: T.matmul(psg[b][:], lhsT=hpT[:, c, :], rhs=wg[:, c, nb * 512:(nb + 1) * 512], start=(c == 0), stop=(c == 7)),
                                 r=["e_hpT", "e_w"], w=[f"e_psg{b}"])
                        for c in range(2):
                            S.op("pe", lambda: T.matmul(psp[b][:], lhsT=pT[:, c, :], rhs=wp[:, c, nb * 512:(nb + 1) * 512], start=(c == 0), stop=(c == 1)),
                                 r=["e_pT", "e_w"], w=[f"e_psp{b}"])
                        S.op("act", lambda: A.activation(out=sg[:], in_=psg[b][:], func=AF.Sigmoid), r=[f"e_psg{b}"], w=["e_sg"])
                        S.op("dve", lambda: V.tensor_tensor(out=tm[:], in0=psp[b][:], in1=sg[:], op=ALU.mult), r=[f"e_psp{b}", "e_sg"], w=["e_tm"])
                        S.op("pool", lambda: G.tensor_tensor(out=xt[:, nb * 512:(nb + 1) * 512], in0=xt[:, nb * 512:(nb + 1) * 512], in1=tm[:], op=ALU.add),
                             r=["e_tm", kx], w=[kx])
                    S.dma("sp", dst[t * 128:(t + 1) * 128, :], xt[:], r=[kx], w=[kdst if l == depth - 1 else f"xr{t}"])
                S.barrier()
                ph.close()
        S.barrier()
        build_nc.ninstr = dict(S.ninstr)
    return nc


def _host_layout(inputs, b):
    f32 = np.float32
    m = {}
    m["x"] = np.ascontiguousarray(inputs["x"][b], dtype=f32)
    m["p"] = np.ascontiguousarray(inputs["p"][:, b], dtype=f32)
    pos = np.asarray(inputs["positions"][b]).astype(np.int32)
    m["pos"] = np.ascontiguousarray(pos.reshape(NT, 128).T)
    m["invf"] = (10000.0 ** (-np.arange(0, 64, 2, dtype=np.float32) / 64)).astype(f32)
    for k in ("attn_norm", "mlp_norm", "ple_norm"):
        a = np.asarray(inputs[k], dtype=f32)
        m[k + "_l"] = np.ascontiguousarray(a.reshape(DEPTH, 8, 128).transpose(2, 0, 1).reshape(128, DEPTH * 8))
    for k in ("a_q_norm", "a_k_norm", "a_lambda", "a_subln", "c_q_norm", "c_k_norm", "idx_k_norm"):
        m[k] = np.ascontiguousarray(np.asarray(inputs[k], dtype=f32).reshape(-1))
    for k in ("w_in", "w_br_a", "w_br_b", "w_br_c", "w_out", "w_up", "w_down", "w_ple_gate", "w_ple_proj"):
        m[k] = np.ascontiguousarray(inputs[k], dtype=f32)
    return m


def kernel(**inputs):
    nc = build_nc()
    in_maps = [_host_layout(inputs, c % 4) for c in range(4)]
    in_maps = in_maps + in_maps
    res = run_bass_kernel_spmd(nc, in_maps, core_ids=list(range(8)))
    out = np.stack([np.asarray(res.results[b]["out"], dtype=np.float32) for b in range(4)], axis=0)
    return out
```

```python
import math
from contextlib import ExitStack

import numpy as np
import concourse.bass as bass
import concourse.mybir as mybir
from concourse.bass_utils import run_bass_kernel_spmd

F32 = mybir.dt.float32
BF16 = mybir.dt.bfloat16
I32 = mybir.dt.int32
AF = mybir.ActivationFunctionType
ALU = mybir.AluOpType
AX = mybir.AxisListType

DEPTH = 4
SL = 4096
D = 1024
NT = 32
NQ = 8
EPS = 1e-6
W_IN_COLS = 8004
TOPK = 256
NSTEP = 24
NEG = -1.0e30

SEM_EPOCH = 30000
N_DMA_SEMS = 24


class Sched:
    def __init__(self, nc, stack):
        self.nc = nc
        self.stack = stack
        self.engs = {"pe": nc.tensor, "act": nc.scalar, "dve": nc.vector,
                     "pool": nc.gpsimd, "sp": nc.sync}
        self.esem = {}
        self.ecount = {}
        self.n_sems = 0
        for e in self.engs:
            self._new_esem(e)
        self.waited = {e: {} for e in self.engs}
        self.dsems = [[self._sem(f"dma{i}"), 0] for i in range(N_DMA_SEMS)]
        self.dnext = 0
        self.lastw = {}
        self.reads = {}
        self.ninstr = {e: 0 for e in self.engs}

    def _sem(self, name):
        self.n_sems += 1
        return self.stack.enter_context(self.nc.semaphore(name))

    def _new_esem(self, e):
        self.esem[e] = self._sem(f"e_{e}_{self.n_sems}")
        self.ecount[e] = 0

    def _wait(self, e, dep):
        sem, val, tag = dep
        w = self.waited[e]
        k = id(sem)
        if w.get(k, 0) >= val:
            return
        w[k] = val
        self.engs[e].wait_ge(sem, val)
        self.ninstr[e] += 1

    def _collect(self, r, w):
        deps = []
        for k in list(r) + list(w):
            d = self.lastw.get(k)
            if d is not None:
                deps.append(d)
        for k in w:
            rd = self.reads.get(k)
            if rd:
                deps.extend(rd.values())
        return deps

    def _commit(self, r, w, dep):
        for k in w:
            self.lastw[k] = dep
            self.reads[k] = {}
        for k in r:
            if k in w:
                continue
            self.reads.setdefault(k, {})[dep[2]] = dep

    def op(self, e, fn, r=(), w=()):
        if self.ecount[e] >= SEM_EPOCH:
            self._new_esem(e)
        for d in self._collect(r, w):
            if e == "pe" and d[2] == "pe":
                continue
            self._wait(e, d)
        ins = fn()
        self.ecount[e] += 1
        ins.then_inc(self.esem[e], 1)
        dep = (self.esem[e], self.ecount[e], e)
        self.ninstr[e] += 1
        self._commit(r, w, dep)
        return ins

    def dma(self, q, out, in_, r=(), w=()):
        slot = self.dsems[self.dnext]
        self.dnext = (self.dnext + 1) % len(self.dsems)
        sem = slot[0]
        if slot[1] > 0:
            self._wait(q, (sem, slot[1], "dma"))
        for d in self._collect(r, w):
            self._wait(q, d)
        ins = self.engs[q].dma_start(out=out, in_=in_)
        slot[1] += 16
        ins.then_inc(sem, 16)
        dep = (sem, slot[1], f"dma{id(sem)}")
        self.ninstr[q] += 1
        self._commit(r, w, dep)
        return ins

    def barrier(self):
        for e in self.engs:
            for o in self.engs:
                if o != e and self.ecount[o] > 0:
                    self._wait(e, (self.esem[o], self.ecount[o], o))
            for sem, val in self.dsems:
                if val:
                    self._wait(e, (sem, val, "dma"))


def run_pipeline(items, nstage):
    n = len(items)
    for step in range(n + nstage - 1):
        for k in range(nstage - 1, -1, -1):
            i = step - k
            if 0 <= i < n and items[i][k] is not None:
                items[i][k]()


def build_nc(depth=DEPTH, debug=False, phases=None):
    PH = set(phases) if phases is not None else {'p1', 'pa', 'pb', 'pc', 'pm', 'pf', 'pp'}
    nc = bass.Bass("TRN2", target_bir_lowering=False)

    def din(name, shape, dt=F32):
        return nc.dram_tensor(name, list(shape), dt, kind="ExternalInput").ap()

    def dscr(name, shape, dt):
        kind = "ExternalOutput" if debug else "Internal"
        return nc.dram_tensor(name, list(shape), dt, kind=kind).ap()

    x_in = din("x", [SL, D])
    p_in = din("p", [DEPTH, SL, 256])
    pos_in = din("pos", [128, NT], I32)
    invf_in = din("invf", [32])
    attn_norm = din("attn_norm_l", [128, DEPTH * 8])
    mlp_norm = din("mlp_norm_l", [128, DEPTH * 8])
    ple_norm = din("ple_norm_l", [128, DEPTH * 8])
    w_in = din("w_in", [DEPTH, D, W_IN_COLS])
    a_q_norm = din("a_q_norm", [DEPTH * 64])
    a_k_norm = din("a_k_norm", [DEPTH * 64])
    a_lambda = din("a_lambda", [DEPTH * 256])
    a_subln = din("a_subln", [DEPTH * 128])
    c_q_norm = din("c_q_norm", [DEPTH * 64])
    c_k_norm = din("c_k_norm", [DEPTH * 64])
    idx_k_norm = din("idx_k_norm", [DEPTH * 64])
    w_br = [din("w_br_a", [DEPTH, 512, D]), din("w_br_b", [DEPTH, 512, D]), din("w_br_c", [DEPTH, 512, D])]
    w_out = din("w_out", [DEPTH, D, D])
    w_up = din("w_up", [DEPTH, D, 4 * D])
    w_down = din("w_down", [DEPTH, 4 * D, D])
    w_ple_gate = din("w_ple_gate", [DEPTH, D, D])
    w_ple_proj = din("w_ple_proj", [DEPTH, 256, D])
    out_d = nc.dram_tensor("out", [SL, D], F32, kind="ExternalOutput").ap()

    xres = dscr("xres", [SL, D], F32)
    QT = {k: dscr("QT_" + k, [4, 128, SL], BF16) for k in "abc"}
    KT = {k: dscr("KT_" + k, [4, 128, SL], BF16) for k in "abc"}
    VX = {"a": dscr("V_a", [SL, 4 * 129], BF16), "b": dscr("V_b", [SL, 8 * 65], BF16),
          "c": dscr("V_c", [SL, 8 * 65], BF16)}
    IQT = dscr("IQT", [2, 128, SL], BF16)
    IKT = dscr("IKT", [128, SL], BF16)
    GT = dscr("GT", [24, 128, SL], BF16)
    YT = dscr("YT", [12, 128, SL], BF16)

    with ExitStack() as st:
        S = Sched(nc, st)
        E = st.enter_context
        V = nc.vector
        A = nc.scalar
        G = nc.gpsimd
        T = nc.tensor

        uid = [0]

        def sb(stack, name, shape, dt):
            uid[0] += 1
            return stack.enter_context(nc.sbuf_tensor(f"s{uid[0]}_{name}", list(shape), dt))

        def ps(stack, name, shape, dt):
            uid[0] += 1
            return stack.enter_context(nc.psum_tensor(f"q{uid[0]}_{name}", list(shape), dt))

        ident = sb(st, "ident", [128, 128], BF16)
        ntri = sb(st, "ntri", [128, 128], BF16)
        nones = sb(st, "nones", [128, 128], BF16)
        strict = sb(st, "strict", [128, 128], BF16)
        cos_t = sb(st, "cos_t", [128, NT, 32], F32)
        sin_t = sb(st, "sin_t", [128, NT, 32], F32)
        gA = sb(st, "gA", [128, DEPTH * 8], F32)
        gM = sb(st, "gM", [128, DEPTH * 8], F32)
        gP = sb(st, "gP", [128, DEPTH * 8], F32)
        aqn = sb(st, "aqn", [128, DEPTH * 64], F32)
        akn = sb(st, "akn", [128, DEPTH * 64], F32)
        cqn = sb(st, "cqn", [128, DEPTH * 64], F32)
        ckn = sb(st, "ckn", [128, DEPTH * 64], F32)
        ikn = sb(st, "ikn", [128, DEPTH * 64], F32)
        subg = sb(st, "subg", [128, DEPTH * 128], F32)
        lam_t = sb(st, "lam_t", [128, DEPTH], F32)
        nlam_t = sb(st, "nlam_t", [128, DEPTH], F32)
        iw_all = sb(st, "iw_all", [128, NT, 4], F32)

        CONST = ["const"]

        with ExitStack() as ph:
            posi = sb(ph, "posi", [128, NT], I32)
            posf = sb(ph, "posf", [128, NT], F32)
            invf = sb(ph, "invf", [128, 32], F32)
            ang = sb(ph, "ang", [128, NT, 32], F32)
            t0 = sb(ph, "t0", [128, NT, 32], F32)
            t1 = sb(ph, "t1", [128, NT, 32], F32)
            lpb = sb(ph, "lpb", [128, DEPTH * 256], F32)
            tmp64 = sb(ph, "tmp64", [128, 64], F32)
            s12 = sb(ph, "s12", [128, 2], F32)
            K0 = ["p0"]
            S.op("pool", lambda: G.memset(ident[:], 1.0), w=CONST)
            S.op("pool", lambda: G.affine_select(out=ident[:], in_=ident[:], pattern=[[-1, 128]],
                                                 compare_op=ALU.is_equal, fill=0.0, base=0, channel_multiplier=1),
                 r=CONST, w=CONST)
            S.op("pool", lambda: G.memset(nones[:], -1.0), w=CONST)
            S.op("pool", lambda: G.memset(ntri[:], -1.0), w=CONST)
            S.op("pool", lambda: G.affine_select(out=ntri[:], in_=ntri[:], pattern=[[-1, 128]],
                                                 compare_op=ALU.is_ge, fill=0.0, base=0, channel_multiplier=1),
                 r=CONST, w=CONST)
            S.op("pool", lambda: G.memset(strict[:], 1.0), w=CONST)
            S.op("pool", lambda: G.affine_select(out=strict[:], in_=strict[:], pattern=[[1, 128]],
                                                 compare_op=ALU.is_gt, fill=0.0, base=0, channel_multiplier=-1),
                 r=CONST, w=CONST)
            for dst, src in [(gA, attn_norm), (gM, mlp_norm), (gP, ple_norm)]:
                S.dma("sp", dst[:], src[:, :], w=CONST)
            for dst, src in [(aqn, a_q_norm), (akn, a_k_norm), (cqn, c_q_norm), (ckn, c_k_norm),
                             (ikn, idx_k_norm), (subg, a_subln), (lpb, a_lambda), (invf, invf_in)]:
                S.dma("sp", dst[:], src.partition_broadcast(128), w=CONST + K0)
            S.dma("sp", posi[:], pos_in[:, :], w=K0)
            S.op("dve", lambda: V.tensor_scalar(out=aqn[:], in0=aqn[:], scalar1=0.125, scalar2=None, op0=ALU.mult), r=CONST, w=CONST)
            S.op("dve", lambda: V.tensor_scalar(out=cqn[:], in0=cqn[:], scalar1=0.125, scalar2=None, op0=ALU.mult), r=CONST, w=CONST)
            for l in range(DEPTH):
                li = 0.8 - 0.6 * math.exp(-0.3 * l)
                S.op("dve", lambda: V.tensor_scalar(out=subg[:, l * 128:(l + 1) * 128], in0=subg[:, l * 128:(l + 1) * 128],
                                                    scalar1=float(1.0 - li), scalar2=None, op0=ALU.mult), r=CONST, w=CONST)
                for j in range(2):
                    S.op("dve", lambda: V.tensor_tensor(out=tmp64[:], in0=lpb[:, l * 256 + j * 128: l * 256 + j * 128 + 64],
                                                        in1=lpb[:, l * 256 + j * 128 + 64: l * 256 + j * 128 + 128], op=ALU.mult),
                         r=K0, w=["tmp64"])
                    S.op("dve", lambda: V.tensor_reduce(out=s12[:, j:j + 1], in_=tmp64[:], axis=AX.X, op=ALU.add),
                         r=["tmp64"], w=["s12"])
                S.op("act", lambda: A.activation(out=s12[:], in_=s12[:], func=AF.Exp), r=["s12"], w=["s12"])
                S.op("dve", lambda: V.tensor_tensor(out=lam_t[:, l:l + 1], in0=s12[:, 0:1], in1=s12[:, 1:2], op=ALU.subtract),
                     r=["s12"], w=CONST)
                S.op("dve", lambda: V.tensor_scalar(out=lam_t[:, l:l + 1], in0=lam_t[:, l:l + 1], scalar1=float(li), scalar2=None, op0=ALU.add),
                     r=CONST, w=CONST)
                S.op("dve", lambda: V.tensor_scalar(out=nlam_t[:, l:l + 1], in0=lam_t[:, l:l + 1], scalar1=-1.0, scalar2=None, op0=ALU.mult),
                     r=CONST, w=CONST)
            S.op("dve", lambda: V.tensor_copy(out=posf[:], in_=posi[:]), r=K0, w=["posf"])
            S.op("dve", lambda: V.tensor_tensor(out=ang[:], in0=posf[:].unsqueeze(2).to_broadcast([128, NT, 32]),
                                                in1=invf[:].unsqueeze(1).to_broadcast([128, NT, 32]), op=ALU.mult),
                 r=["posf"] + K0, w=["ang"])
            MAGIC = 12582912.0
            C1 = 6.28125
            C2 = 2.0 * math.pi - C1
            PI_LO = 3.1415925
            S.op("dve", lambda: V.tensor_scalar(out=t0[:], in0=ang[:], scalar1=float(1.0 / (2.0 * math.pi)), scalar2=None, op0=ALU.mult), r=["ang"], w=["t0"])
            S.op("dve", lambda: V.tensor_scalar(out=t0[:], in0=t0[:], scalar1=MAGIC, scalar2=None, op0=ALU.add), r=["t0"], w=["t0"])
            S.op("dve", lambda: V.tensor_scalar(out=t0[:], in0=t0[:], scalar1=-MAGIC, scalar2=None, op0=ALU.add), r=["t0"], w=["t0"])
            S.op("dve", lambda: V.scalar_tensor_tensor(out=t1[:], in0=t0[:], scalar=-C1, in1=ang[:], op0=ALU.mult, op1=ALU.add), r=["t0", "ang"], w=["t1"])
            S.op("dve", lambda: V.scalar_tensor_tensor(out=t1[:], in0=t0[:], scalar=-C2, in1=t1[:], op0=ALU.mult, op1=ALU.add), r=["t0", "t1"], w=["t1"])
            S.op("dve", lambda: V.tensor_scalar(out=t0[:], in0=t1[:], scalar1=PI_LO, scalar2=-PI_LO, op0=ALU.min, op1=ALU.max), r=["t1"], w=["t0"])
            S.op("act", lambda: A.activation(out=sin_t[:], in_=t0[:], func=AF.Sin), r=["t0"], w=CONST)
            S.op("dve", lambda: V.tensor_scalar(out=t1[:], in0=t1[:], scalar1=float(math.pi / 2), scalar2=None, op0=ALU.add), r=["t1"], w=["t1"])
            S.op("dve", lambda: V.tensor_scalar(out=t0[:], in0=t1[:], scalar1=float(math.pi), scalar2=float(-2.0 * math.pi), op0=ALU.is_gt, op1=ALU.mult), r=["t1", "const"], w=["t0"])
            S.op("dve", lambda: V.tensor_tensor(out=t1[:], in0=t1[:], in1=t0[:], op=ALU.add), r=["t0", "t1"], w=["t1"])
            S.op("dve", lambda: V.tensor_scalar(out=t1[:], in0=t1[:], scalar1=PI_LO, scalar2=-PI_LO, op0=ALU.min, op1=ALU.max), r=["t1"], w=["t1"])
            S.op("act", lambda: A.activation(out=cos_t[:], in_=t1[:], func=AF.Sin), r=["t1"], w=CONST)
            S.barrier()

        def rms_rows(xt, kx, ssq, rstd, junk, kpre):
            S.op("act", lambda: A.activation(out=junk[:], in_=xt, func=AF.Square, accum_out=ssq[:, 0:1]),
                 r=[kx], w=[kpre + "junk", kpre + "ssq"])
            S.op("act", lambda: A.activation(out=rstd[:, 0:1], in_=ssq[:, 0:1], func=AF.Sqrt, bias=EPS, scale=1.0 / D),
                 r=[kpre + "ssq"], w=[kpre + "rstd"])
            S.op("dve", lambda: V.reciprocal(out=rstd[:, 0:1], in_=rstd[:, 0:1]), r=[kpre + "rstd"], w=[kpre + "rstd"])

        def norm_transpose(xt, kx, gain, l, hb, pst2, dst_fn, kdst, ssq, rstd, junk, kpre):
            rms_rows(xt, kx, ssq, rstd, junk, kpre)
            S.op("dve", lambda: V.tensor_scalar(out=hb[:], in0=xt, scalar1=rstd[:, 0:1], scalar2=None, op0=ALU.mult),
                 r=[kx, kpre + "rstd"], w=[kpre + "hb"])
            for half in range(2):
                pt = pst2[half]
                for c4 in range(4):
                    c = half * 4 + c4
                    S.op("pe", lambda: T.transpose(out=pt[:, c4, :], in_=hb[:, c * 128:(c + 1) * 128], identity=ident[:]),
                         r=[kpre + "hb"], w=[kpre + f"pst{half}"])
                for c4 in range(4):
                    c = half * 4 + c4
                    if c4 % 2 == 0:
                        S.op("dve", lambda: V.tensor_scalar(out=dst_fn(c), in0=pt[:, c4, :], scalar1=gain[:, l * 8 + c:l * 8 + c + 1],
                                                            scalar2=None, op0=ALU.mult),
                             r=[kpre + f"pst{half}"], w=[kdst])
                    else:
                        S.op("act", lambda: A.activation(out=dst_fn(c), in_=pt[:, c4, :], func=AF.Copy,
                                                         scale=gain[:, l * 8 + c:l * 8 + c + 1]),
                             r=[kpre + f"pst{half}"], w=[kdst])

        def load_w_bf16(dst, src, key):
            S.dma("pool", dst, src, w=[key])

        for l in range(depth):
            xsrc = x_in if l == 0 else xres
            lam_init = 0.8 - 0.6 * math.exp(-0.3 * l)

            for ph in ([ExitStack()] if 'p1' in PH else []):
                hT = sb(ph, "hT", [128, 8, SL], BF16)
                xb2 = [sb(ph, f"p1x{i}", [128, D], F32) for i in range(2)]
                hb = sb(ph, "p1hb", [128, D], BF16)
                junk = sb(ph, "p1junk", [128, D], BF16)
                ssq = sb(ph, "p1ssq", [128, 8], F32)
                rstd = sb(ph, "p1rstd", [128, 8], F32)
                pst2 = [ps(ph, f"p1pst{i}", [128, 4, 128], BF16) for i in range(2)]
                psm = [ps(ph, f"p1ps{i}", [128, 512], F32) for i in range(2)]
                wblk = [sb(ph, f"p1w{i}", [128, 8, 512], BF16) for i in range(2)]
                sq = sb(ph, "p1sq", [128, 512], F32)
                xn = sb(ph, "p1xn", [128, 512], F32)
                xg = sb(ph, "p1xg", [128, 512], F32)
                ra = sb(ph, "p1ra", [128, 256], F32)
                rb = sb(ph, "p1rb", [128, 256], F32)
                ob = sb(ph, "p1ob", [128, 512], BF16)
                stg = [sb(ph, f"p1stg{i}", [128, 4, 512], BF16) for i in range(2)]
                vst = [sb(ph, f"p1vst{i}", [128, 520], BF16) for i in range(4)]
                gst = [sb(ph, f"p1gst{i}", [128, 512], BF16) for i in range(2)]

                for t in range(NT):
                    xt = xb2[t % 2]
                    kx = f"p1x{t % 2}"
                    S.dma("sp", xt[:], xsrc[t * 128:(t + 1) * 128, :], r=[f"xr{t}"], w=[kx])
                    norm_transpose(xt[:], kx, gA, l, hb, pst2,
                                   lambda c, t=t: hT[:, c, t * 128:(t + 1) * 128], "hT", ssq, rstd, junk, "p1")
                for i in range(4):
                    S.op("pool", lambda: G.memset(vst[i][:], 1.0), w=[f"vst{i}"])

                blocks = []
                names = ["aq", "ak", "av", "bq", "bk", "bv", "cq", "ck", "cv"]
                for i, nm in enumerate(names):
                    blocks.append((nm, i * 512, 512))
                blocks.append(("idx", 4608, 324))
                for i in range(6):
                    blocks.append((f"gl{i}", 4932 + i * 512, 512))
                w_l = w_in[l].rearrange("(c p) n -> p c n", p=128)

                def qk_post(pv, nh, gain, do_norm, do_rope, t, outv, scale, kout):
                    n = nh * 64
                    if do_norm:
                        sqv = sq[:, 0:n].rearrange("p (h d) -> p h d", d=64)
                        S.op("act", lambda: A.activation(out=sqv, in_=pv, func=AF.Square), r=["psm"], w=["sq"])
                        S.op("dve", lambda: V.tensor_reduce(out=ssq[:, 0:nh], in_=sqv, axis=AX.X, op=ALU.add), r=["sq"], w=["p1ssq"])
                        S.op("act", lambda: A.activation(out=rstd[:, 0:nh], in_=ssq[:, 0:nh], func=AF.Sqrt, bias=EPS, scale=1.0 / 64),
                             r=["p1ssq"], w=["p1rstd"])
                        S.op("dve", lambda: V.reciprocal(out=rstd[:, 0:nh], in_=rstd[:, 0:nh]), r=["p1rstd"], w=["p1rstd"])
                        xnv = xn[:, 0:n].rearrange("p (h d) -> p h d", d=64)
                        S.op("dve", lambda: V.tensor_tensor(out=xnv, in0=pv, in1=rstd[:, 0:nh].unsqueeze(2).to_broadcast([128, nh, 64]), op=ALU.mult),
                             r=["psm", "p1rstd"], w=["xn"])
                        xgv = xg[:, 0:n].rearrange("p (h d) -> p h d", d=64)
                        S.op("pool", lambda: G.tensor_tensor(out=xgv, in0=xnv, in1=gain.unsqueeze(1).to_broadcast([128, nh, 64]), op=ALU.mult),
                             r=["xn", "const"], w=["xg"])
                    else:
                        xgv = xg[:, 0:n].rearrange("p (h d) -> p h d", d=64)
                        S.op("act", lambda: A.activation(out=xgv, in_=pv, func=AF.Copy, scale=float(scale)), r=["psm"], w=["xg"])
                    if do_rope:
                        x1 = xgv[:, :, 0:32]
                        x2 = xgv[:, :, 32:64]
                        cb = cos_t[:, t, :].unsqueeze(1).to_broadcast([128, nh, 32])
                        sbb = sin_t[:, t, :].unsqueeze(1).to_broadcast([128, nh, 32])
                        rav = ra[:, 0:nh * 32].rearrange("p (h d) -> p h d", d=32)
                        rbv = rb[:, 0:nh * 32].rearrange("p (h d) -> p h d", d=32)
                        S.op("pool", lambda: G.tensor_tensor(out=rav, in0=x1, in1=cb, op=ALU.mult), r=["xg"], w=["ra"])
                        S.op("dve", lambda: V.tensor_tensor(out=rbv, in0=x2, in1=sbb, op=ALU.mult), r=["xg"], w=["rb"])
                        S.op("dve", lambda: V.tensor_tensor(out=outv[:, :, 0:32], in0=rav, in1=rbv, op=ALU.subtract), r=["ra", "rb"], w=[kout])
                        S.op("pool", lambda: G.tensor_tensor(out=rav, in0=x2, in1=cb, op=ALU.mult), r=["xg", "ra"], w=["ra"])
                        S.op("dve", lambda: V.tensor_tensor(out=rbv, in0=x1, in1=sbb, op=ALU.mult), r=["xg", "rb"], w=["rb"])
                        S.op("dve", lambda: V.tensor_tensor(out=outv[:, :, 32:64], in0=rav, in1=rbv, op=ALU.add), r=["ra", "rb"], w=[kout])
                    else:
                        S.op("dve", lambda: V.tensor_copy(out=outv, in_=xgv), r=["xg"], w=[kout])

                nblk = len(blocks)
                S.dma("pool", wblk[0][:, :, 0:blocks[0][2]], w_l[:, :, blocks[0][1]:blocks[0][1] + blocks[0][2]], w=["wblk0"])
                for bi, (nm, c0, n) in enumerate(blocks):
                    wb = wblk[bi % 2]
                    kw = f"wblk{bi % 2}"
                    if bi + 1 < nblk:
                        nn = blocks[bi + 1]
                        S.dma("pool", wblk[(bi + 1) % 2][:, :, 0:nn[2]], w_l[:, :, nn[1]:nn[1] + nn[2]], w=[f"wblk{(bi + 1) % 2}"])
                    if nm.startswith("gl"):
                        gi = int(nm[2:])
                        for Tq in range(NQ):
                            for j in range(4):
                                it = Tq * 4 + j
                                pm = psm[it % 2]
                                for c in range(8):
                                    S.op("pe", lambda: T.matmul(pm[:], lhsT=wb[:, c, j * 128:(j + 1) * 128], rhs=hT[:, c, Tq * 512:(Tq + 1) * 512],
                                                                start=(c == 0), stop=(c == 7)), r=[kw, "hT"], w=[f"psm{it % 2}"])
                                gs = gst[it % 2]
                                S.op("act", lambda: A.activation(out=gs[:], in_=pm[:], func=AF.Sigmoid), r=[f"psm{it % 2}"], w=[f"gst{it % 2}"])
                                S.dma("sp", GT[gi * 4 + j, :, Tq * 512:(Tq + 1) * 512], gs[:], r=[f"gst{it % 2}"], w=["GT"])
                        continue
                    for t in range(NT):
                        pm = psm[t % 2]
                        kp = f"psm{t % 2}"
                        for c in range(8):
                            S.op("pe", lambda: T.matmul(pm[:, 0:n], lhsT=hT[:, c, t * 128:(t + 1) * 128], rhs=wb[:, c, 0:n],
                                                        start=(c == 0), stop=(c == 7)), r=[kw, "hT"], w=[kp])
                        S.lastw["psm"] = S.lastw[kp]
                        S.reads["psm"] = S.reads[kp]
                        sg = stg[(t // 4) % 2]
                        ksg = f"stg{(t // 4) % 2}"
                        tt = t % 4
                        if nm in ("aq", "ak", "cq", "ck", "bq", "bk"):
                            pv = pm[:, 0:512].rearrange("p (h d) -> p h d", d=64)
                            obv = ob[:].rearrange("p (h d) -> p h d", d=64)
                            if nm[0] == "b":
                                qk_post(pv, 8, None, False, False, t, obv, 0.125 if nm == "bq" else 1.0, "ob")
                            else:
                                gain = {"aq": aqn, "ak": akn, "cq": cqn, "ck": ckn}[nm][:, l * 64:(l + 1) * 64]
                                qk_post(pv, 8, gain, True, True, t, obv, 1.0, "ob")
                            pt = pst2[t % 2]
                            for j in range(4):
                                S.op("pe", lambda: T.transpose(out=pt[:, j, :], in_=ob[:, j * 128:(j + 1) * 128], identity=ident[:]),
                                     r=["ob"], w=[f"p1pst{t % 2}"])
                            S.op("act", lambda: A.activation(out=sg[:, :, tt * 128:(tt + 1) * 128], in_=pt[:], func=AF.Copy),
                                 r=[f"p1pst{t % 2}"], w=[ksg])
                            if tt == 3:
                                dst = (QT if nm[1] == "q" else KT)[nm[0]]
                                t0_ = (t - 3) * 128
                                S.dma("sp", dst[:, :, t0_:t0_ + 512].rearrange("j p t -> p j t"), sg[:], r=[ksg], w=["QKT"])
                        elif nm in ("av", "bv", "cv"):
                            nh, dv = (4, 128) if nm == "av" else (8, 64)
                            vi = (t % 2) + (0 if nm == "av" else 2)
                            vs = vst[vi]
                            vv = vs[:, 0:nh * (dv + 1)].rearrange("p (h e) -> p h e", e=dv + 1)[:, :, 0:dv]
                            S.op("act", lambda: A.activation(out=vv, in_=pm[:, 0:512].rearrange("p (h e) -> p h e", e=dv), func=AF.Copy),
                                 r=[kp], w=[f"vst{vi}"])
                            S.dma("sp", VX[nm[0]][t * 128:(t + 1) * 128, :], vs[:, 0:nh * (dv + 1)], r=[f"vst{vi}"], w=["VX"])
                        else:
                            obv = ob[:, 0:256].rearrange("p (h d) -> p h d", d=64)
                            qk_post(pm[:, 0:256].rearrange("p (h d) -> p h d", d=64), 4, None, False, True, t, obv, 1.0, "ob")
                            obk = ob[:, 256:320].rearrange("p (h d) -> p h d", d=64)
                            qk_post(pm[:, 256:320].rearrange("p (h d) -> p h d", d=64), 1, ikn[:, l * 64:(l + 1) * 64], True, True, t, obk, 1.0, "ob")
                            S.op("dve", lambda: V.tensor_copy(out=ob[:, 320:384], in_=ob[:, 256:320]), r=["ob"], w=["ob"])
                            S.op("act", lambda: A.activation(out=iw_all[:, t, :], in_=pm[:, 320:324], func=AF.Copy), r=[kp], w=["iw_all"])
                            pt = pst2[t % 2]
                            for j in range(3):
                                S.op("pe", lambda: T.transpose(out=pt[:, j, :], in_=ob[:, j * 128:(j + 1) * 128], identity=ident[:]),
                                     r=["ob"], w=[f"p1pst{t % 2}"])
                            S.op("act", lambda: A.activation(out=sg[:, 0:3, tt * 128:(tt + 1) * 128], in_=pt[:, 0:3, :], func=AF.Copy),
                                 r=[f"p1pst{t % 2}"], w=[ksg])
                            if tt == 3:
                                t0_ = (t - 3) * 128
                                S.dma("sp", IQT[:, :, t0_:t0_ + 512].rearrange("j p t -> p j t"), sg[:, 0:2, :], r=[ksg], w=["QKT"])
                                S.dma("sp", IKT[:, t0_:t0_ + 512], sg[:, 2, :], r=[ksg], w=["QKT"])
                S.barrier()
                ph.close()

            for ph in ([ExitStack()] if 'pa' in PH else []):
                kt = sb(ph, "a_kt", [128, 4, SL], BF16)
                vx = sb(ph, "a_vx", [128, NT, 4 * 129], BF16)
                qt2 = [sb(ph, f"a_qt{i}", [128, 4, 512], BF16) for i in range(2)]
                pT2 = [sb(ph, f"a_pT{i}", [128, 512], BF16) for i in range(2)]
                o2 = [sb(ph, f"a_o{i}", [128, 4, 129], F32) for i in range(2)]
                rl = sb(ph, "a_rl", [128, 8], F32)
                y0 = sb(ph, "a_y0", [128, 4, 128], F32)
                y1 = sb(ph, "a_y1", [128, 4, 128], F32)
                ss = sb(ph, "a_ss", [128, 8], F32)
                yb2 = [sb(ph, f"a_yb{i}", [128, 4, 128], BF16) for i in range(2)]
                yts = [sb(ph, f"a_yts{i}", [128, 512], BF16) for i in range(2)]
                pss = [ps(ph, f"a_pss{i}", [128, 512], F32) for i in range(2)]
                po2 = [ps(ph, f"a_po{i}", [128, 2, 512], F32) for i in range(2)]
                ptr = ps(ph, "a_ptr", [128, 4, 128], BF16)
                S.dma("sp", kt[:], KT["a"].rearrange("j p t -> p j t"), r=["QKT"], w=["a_kt"])
                for v4 in range(4):
                    S.dma("sp", vx[:, v4 * 8:(v4 + 1) * 8, :], VX["a"][v4 * 1024:(v4 + 1) * 1024, :].rearrange("(t p) e -> p t e", p=128), r=["VX"], w=["a_vx"])

                def a_load_q(Q):
                    S.dma("sp", qt2[Q % 2][:], QT["a"][:, :, Q * 512:(Q + 1) * 512].rearrange("j p t -> p j t"), r=["QKT"], w=[f"a_qt{Q % 2}"])

                a_load_q(0)
                items = []
                deferred = []

                def a_epilogue(Q, h, c, po, kpo):
                    for bk in range(2):
                        S.op("act", lambda: A.activation(out=o2[c][:, 2 * bk:2 * bk + 2, :], in_=po[:, bk, 0:258].rearrange("p (r e) -> p r e", e=129), func=AF.Copy),
                             r=[kpo], w=[f"a_o{c}"])
                    if c == 0:
                        return
                    yb = yb2[h % 2]
                    kyb = f"a_yb{h % 2}"
                    S.op("dve", lambda: V.reciprocal(out=rl[:, 0:4], in_=o2[0][:, :, 128]), r=["a_o0"], w=["a_rl"])
                    S.op("dve", lambda: V.reciprocal(out=rl[:, 4:8], in_=o2[1][:, :, 128]), r=["a_o1"], w=["a_rl"])
                    S.op("dve", lambda: V.tensor_scalar(out=rl[:, 4:8], in0=rl[:, 4:8], scalar1=nlam_t[:, l:l + 1], scalar2=None, op0=ALU.mult),
                         r=["a_rl", "const"], w=["a_rl"])
                    S.op("dve", lambda: V.tensor_tensor(out=y0[:], in0=o2[0][:, :, 0:128], in1=rl[:, 0:4].unsqueeze(2).to_broadcast([128, 4, 128]), op=ALU.mult),
                         r=["a_o0", "a_rl"], w=["a_y0"])
                    S.op("pool", lambda: G.tensor_tensor(out=y1[:], in0=o2[1][:, :, 0:128], in1=rl[:, 4:8].unsqueeze(2).to_broadcast([128, 4, 128]), op=ALU.mult),
                         r=["a_o1", "a_rl"], w=["a_y1"])
                    S.op("dve", lambda: V.tensor_tensor(out=y0[:], in0=y0[:], in1=y1[:], op=ALU.add), r=["a_y0", "a_y1"], w=["a_y0"])
                    S.op("pool", lambda: G.tensor_tensor(out=y1[:], in0=y0[:], in1=y0[:], op=ALU.mult), r=["a_y0"], w=["a_y1"])
                    S.op("dve", lambda: V.tensor_reduce(out=ss[:, 0:4], in_=y1[:], axis=AX.X, op=ALU.add), r=["a_y1"], w=["a_ss"])
                    S.op("act", lambda: A.activation(out=ss[:, 0:4], in_=ss[:, 0:4], func=AF.Sqrt, bias=EPS, scale=1.0 / 128), r=["a_ss"], w=["a_ss"])
                    S.op("dve", lambda: V.reciprocal(out=ss[:, 0:4], in_=ss[:, 0:4]), r=["a_ss"], w=["a_ss"])
                    S.op("dve", lambda: V.tensor_tensor(out=y0[:], in0=y0[:], in1=ss[:, 0:4].unsqueeze(2).to_broadcast([128, 4, 128]), op=ALU.mult),
                         r=["a_y0", "a_ss"], w=["a_y0"])
                    S.op("dve", lambda: V.tensor_tensor(out=yb[:], in0=y0[:], in1=subg[:, l * 128:(l + 1) * 128].unsqueeze(1).to_broadcast([128, 4, 128]), op=ALU.mult),
                         r=["a_y0", "const"], w=[kyb])

                    def fin():
                        for qq in range(4):
                            S.op("pe", lambda: T.transpose(out=ptr[:, qq, :], in_=yb[:, qq, :], identity=ident[:]), r=[kyb], w=["a_ptr"])
                        ys = yts[h % 2]
                        S.op("act", lambda: A.activation(out=ys[:], in_=ptr[:].rearrange("p a b -> p (a b)"), func=AF.Copy), r=["a_ptr"], w=[f"a_yts{h % 2}"])
                        S.dma("sp", YT[h, :, Q * 512:(Q + 1) * 512], ys[:], r=[f"a_yts{h % 2}"], w=["YT"])
                    deferred.append([4, fin])

                def run_deferred(force=False):
                    keep = []
                    for dly in deferred:
                        dly[0] -= 1
                        if dly[0] <= 0 or force:
                            dly[1]()
                        else:
                            keep.append(dly)
                    deferred[:] = keep

                gidx = 0
                for Q in range(NQ):
                    for h in range(4):
                        for c in range(2):
                            g = gidx
                            gidx += 1
                            nj = 4 * Q + 4
                            for j in range(nj):
                                b = len(items) % 2
                                jj = j - 4 * Q
                                c0 = max(0, jj) * 128

                                def s0(Q=Q, h=h, c=c, j=j, c0=c0, b=b, pre=(h == 0 and c == 0 and j == 0)):
                                    if pre and Q + 1 < NQ:
                                        a_load_q(Q + 1)
                                    qt = qt2[Q % 2]
                                    S.op("pe", lambda: T.matmul(pss[b][:, c0:512], lhsT=kt[c * 64:(c + 1) * 64, h, j * 128:(j + 1) * 128],
                                                                rhs=qt[c * 64:(c + 1) * 64, h, c0:512], start=True, stop=True),
                                         r=["a_kt", f"a_qt{Q % 2}"], w=[f"a_pss{b}"])

                                def s1(c0=c0, b=b, jj=jj):
                                    S.op("act", lambda: A.activation(out=pT2[b][:, c0:512], in_=pss[b][:, c0:512], func=AF.Exp),
                                         r=[f"a_pss{b}"], w=[f"a_pT{b}"])
                                    if jj >= 0:
                                        S.op("pool", lambda: G.memset(pT2[b][64:128, c0:c0 + 64], 0.0), w=[f"a_pT{b}"])

                                def s2(Q=Q, h=h, c=c, j=j, jj=jj, b=b, g=g, nj=nj):
                                    po = po2[g % 2]
                                    kpo = f"a_po{g % 2}"
                                    for qq in range(max(0, jj), 4):
                                        off = (qq % 2) * 129
                                        S.op("pe", lambda: T.matmul(po[:, qq // 2, off:off + 129], lhsT=pT2[b][:, qq * 128:(qq + 1) * 128],
                                                                    rhs=vx[:, j, h * 129:(h + 1) * 129], start=(j == 0 and qq % 2 == 0), stop=(j == 4 * Q + qq),
                                                                    skip_group_check=True),
                                             r=[f"a_pT{b}", "a_vx"], w=[kpo])
                                    if j == nj - 1:
                                        a_epilogue(Q, h, c, po, kpo)
                                    run_deferred()
                                items.append([s0, s1, s2])
                run_pipeline(items, 3)
                run_deferred(force=True)
                S.barrier()
                ph.close()

            for ph in ([ExitStack()] if 'pb' in PH else []):
                kt = sb(ph, "b_kt", [128, 4, SL], BF16)
                vx = sb(ph, "b_vx", [128, NT, 8 * 65], BF16)
                qt2 = [sb(ph, f"b_qt{i}", [128, 4, 512], BF16) for i in range(2)]
                e2 = [sb(ph, f"b_e{i}", [128, 512], F32) for i in range(2)]
                nl2 = [sb(ph, f"b_nl{i}", [128, 512], BF16) for i in range(2)]
                aT2 = [sb(ph, f"b_aT{i}", [128, 512], BF16) for i in range(2)]
                R = sb(ph, "b_R", [128, 512], BF16)
                qm2 = [sb(ph, f"b_qm{i}", [128, 512], BF16) for i in range(2)]
                yq2 = [sb(ph, f"b_yq{i}", [128, 4, 512], BF16) for i in range(2)]
                yts = [sb(ph, f"b_yts{i}", [128, 512], BF16) for i in range(2)]
                psz = [ps(ph, f"b_psz{i}", [128, 512], F32) for i in range(2)]
                psc = [ps(ph, f"b_psc{i}", [128, 512], F32) for i in range(2)]
                po2 = [ps(ph, f"b_po{i}", [128, 512], F32) for i in range(2)]
                ptr = ps(ph, "b_ptr", [128, 4, 128], BF16)
                S.dma("sp", kt[:], KT["b"].rearrange("j p t -> p j t"), r=["QKT"], w=["b_kt"])
                for v4 in range(4):
                    S.dma("sp", vx[:, v4 * 8:(v4 + 1) * 8, :], VX["b"][v4 * 1024:(v4 + 1) * 1024, :].rearrange("(t p) e -> p t e", p=128), r=["VX"], w=["b_vx"])

                def b_load_q(Q):
                    S.dma("sp", qt2[Q % 2][:], QT["b"][:, :, Q * 512:(Q + 1) * 512].rearrange("j p t -> p j t"), r=["QKT"], w=[f"b_qt{Q % 2}"])

                b_load_q(0)
                items = []
                gidx = 0
                for Q in range(NQ):
                    for h in range(8):
                        g = gidx
                        gidx += 1
                        pr, bs = h // 2, (h % 2) * 64
                        jtop = 4 * Q + 3
                        gstate = {"first": True}
                        for j in range(jtop, -1, -1):
                            b = len(items) % 2
                            jj = j - 4 * Q
                            c0 = max(0, jj) * 128

                            def s0(Q=Q, h=h, j=j, c0=c0, b=b, pr=pr, bs=bs, pre=(h == 0 and j == jtop)):
                                if pre and Q + 1 < NQ:
                                    b_load_q(Q + 1)
                                qt = qt2[Q % 2]
                                S.op("pe", lambda: T.matmul(psz[b][:, c0:512], lhsT=kt[bs:bs + 64, pr, j * 128:(j + 1) * 128], rhs=qt[bs:bs + 64, pr, c0:512],
                                                            start=True, stop=True), r=["b_kt", f"b_qt{Q % 2}"], w=[f"b_psz{b}"])

                            def s1(c0=c0, b=b, jj=jj):
                                S.op("act", lambda: A.activation(out=e2[b][:, c0:512], in_=psz[b][:, c0:512], func=AF.Exp),
                                     r=[f"b_psz{b}"], w=[f"b_e{b}"])
                                S.op("act", lambda: A.activation(out=nl2[b][:, c0:512], in_=e2[b][:, c0:512], func=AF.Ln, bias=1.0),
                                     r=[f"b_e{b}"], w=[f"b_nl{b}"])
                                if jj >= 0:
                                    S.op("dve", lambda: V.tensor_tensor(out=nl2[b][:, c0:c0 + 128], in0=nl2[b][:, c0:c0 + 128], in1=strict[:], op=ALU.mult),
                                         r=[f"b_nl{b}", "const"], w=[f"b_nl{b}"])

                            def s2(Q=Q, j=j, c0=c0, b=b, pr=pr, bs=bs, g=g, jtop=jtop):
                                qm = qm2[g % 2]
                                kqm = f"b_qm{g % 2}"
                                qt = qt2[Q % 2]
                                if j == jtop:
                                    S.op("pool", lambda: G.memset(R[:], 0.0), w=["b_R"])
                                    S.op("pool", lambda: G.memset(qm[:], 0.0), w=[kqm])
                                    S.op("dve", lambda: V.tensor_copy(out=qm[bs:bs + 64, :], in_=qt[bs:bs + 64, pr, :]), r=[f"b_qt{Q % 2}", kqm], w=[kqm])
                                S.op("pe", lambda: T.matmul(psc[b][:, c0:512], lhsT=kt[:, pr, j * 128:(j + 1) * 128], rhs=qm[:, c0:512], start=True, stop=False),
                                     r=["b_kt", kqm], w=[f"b_psc{b}"])
                                last = (j == jtop)
                                S.op("pe", lambda: T.matmul(psc[b][:, c0:512], lhsT=ntri[:], rhs=nl2[b][:, c0:512], start=False, stop=last),
                                     r=[f"b_nl{b}", "const"], w=[f"b_psc{b}"])
                                if not last:
                                    S.op("pe", lambda: T.matmul(psc[b][:, c0:512], lhsT=nones[:], rhs=R[:, c0:512], start=False, stop=True),
                                         r=["b_R", "const"], w=[f"b_psc{b}"])
                                if j > 0:
                                    S.op("dve", lambda: V.tensor_tensor(out=R[:, c0:512], in0=R[:, c0:512], in1=nl2[b][:, c0:512], op=ALU.add),
                                         r=["b_R", f"b_nl{b}"], w=["b_R"])

                            def s3(c0=c0, b=b, jj=jj):
                                S.op("act", lambda: A.activation(out=aT2[b][:, c0:512], in_=psc[b][:, c0:512], func=AF.Exp),
                                     r=[f"b_psc{b}"], w=[f"b_aT{b}"])
                                if jj >= 0:
                                    S.op("dve", lambda: V.tensor_tensor(out=aT2[b][:, c0:c0 + 128], in0=aT2[b][:, c0:c0 + 128], in1=strict[:], op=ALU.mult),
                                         r=[f"b_aT{b}", "const"], w=[f"b_aT{b}"])

                            def s4(Q=Q, h=h, j=j, jj=jj, b=b, g=g, gstate=gstate):
                                po = po2[g % 2]
                                kpo = f"b_po{g % 2}"
                                yq = yq2[Q % 2]
                                kyq = f"b_yq{Q % 2}"
                                for qq in range(max(0, jj), 4):
                                    S.op("pe", lambda: T.matmul(po[:, qq * 65:(qq + 1) * 65], lhsT=aT2[b][:, qq * 128:(qq + 1) * 128],
                                                                rhs=vx[:, j, h * 65:(h + 1) * 65], start=gstate["first"], stop=(j == 0),
                                                                skip_group_check=True),
                                         r=[f"b_aT{b}", "b_vx"], w=[kpo])
                                    gstate["first"] = False
                                if j == 0:
                                    S.op("act", lambda: A.activation(out=yq[:, :, h * 64:(h + 1) * 64],
                                                                     in_=po[:, 0:260].rearrange("p (a e) -> p a e", e=65)[:, :, 0:64], func=AF.Copy),
                                         r=[kpo], w=[kyq])
                                    if h == 7:
                                        for cch in range(4):
                                            for qq in range(4):
                                                S.op("pe", lambda: T.transpose(out=ptr[:, qq, :], in_=yq[:, qq, cch * 128:(cch + 1) * 128], identity=ident[:]),
                                                     r=[kyq], w=["b_ptr"])
                                            ys = yts[cch % 2]
                                            S.op("act", lambda: A.activation(out=ys[:], in_=ptr[:].rearrange("p a b -> p (a b)"), func=AF.Copy), r=["b_ptr"], w=[f"b_yts{cch % 2}"])
                                            S.dma("sp", YT[4 + cch, :, Q * 512:(Q + 1) * 512], ys[:], r=[f"b_yts{cch % 2}"], w=["YT"])
                            items.append([s0, s1, s2, s3, s4])
                run_pipeline(items, 5)
                S.barrier()
                ph.close()

            for ph in ([ExitStack()] if 'pc' in PH else []):
                kt = sb(ph, "c_kt", [128, 4, SL], BF16)
                vx = sb(ph, "c_vx", [128, NT, 8 * 65], BF16)
                ikt = sb(ph, "c_ikt", [128, SL], BF16)
                qt2 = [sb(ph, f"c_qt{i}", [128, 4, 512], BF16) for i in range(2)]
                iqt2 = [sb(ph, f"c_iqt{i}", [128, 2, 512], BF16) for i in range(2)]
                score = sb(ph, "c_score", [128, SL], F32)
                maskq = sb(ph, "c_maskq", [128, SL], BF16)
                maskT = sb(ph, "c_maskT", [128, NT, 512], BF16)
                rb2 = [sb(ph, f"c_r{i}", [128, 512], F32) for i in range(2)]
                pT2 = [sb(ph, f"c_pT{i}", [128, 512], BF16) for i in range(2)]
                bs_ = sb(ph, "c_bis", [128, 8], F32)
                osb2 = [sb(ph, f"c_o{i}", [128, 4, 65], F32) for i in range(2)]
                rl2 = [sb(ph, f"c_rl{i}", [128, 4], F32) for i in range(2)]
                yq2 = [sb(ph, f"c_yq{i}", [128, 4, 512], BF16) for i in range(2)]
                yts = [sb(ph, f"c_yts{i}", [128, 512], BF16) for i in range(2)]
                psi = [ps(ph, f"c_psi{i}", [128, 512], F32) for i in range(2)]
                pss = [ps(ph, f"c_pss{i}", [128, 512], F32) for i in range(2)]
                po2 = [ps(ph, f"c_po{i}", [128, 512], F32) for i in range(2)]
                ptr = ps(ph, "c_ptr", [128, 4, 128], BF16)
                S.dma("sp", kt[:], KT["c"].rearrange("j p t -> p j t"), r=["QKT"], w=["c_kt"])
                for v4 in range(4):
                    S.dma("sp", vx[:, v4 * 8:(v4 + 1) * 8, :], VX["c"][v4 * 1024:(v4 + 1) * 1024, :].rearrange("(t p) e -> p t e", p=128), r=["VX"], w=["c_vx"])
                S.dma("sp", ikt[:], IKT[:, :], r=["QKT"], w=["c_ikt"])

                def c_load_q(Q):
                    kq_ = f"c_qt{Q % 2}"
                    S.dma("sp", qt2[Q % 2][:], QT["c"][:, :, Q * 512:(Q + 1) * 512].rearrange("j p t -> p j t"), r=["QKT"], w=[kq_])
                    S.dma("sp", iqt2[Q % 2][:], IQT[:, :, Q * 512:(Q + 1) * 512].rearrange("j p t -> p j t"), r=["QKT"], w=[kq_])

                c_load_q(0)
                ir = 0
                gidx = 0
                for Q in range(NQ):
                    qt = qt2[Q % 2]
                    iqt = iqt2[Q % 2]
                    kq = f"c_qt{Q % 2}"
                    if Q + 1 < NQ:
                        c_load_q(Q + 1)
                    for qq in range(4):
                        qb = 4 * Q + qq
                        nadm = (qb + 1) * 128
                        nkt = (nadm + 511) // 512
                        for k5 in range(nkt):
                            wd = min(512, nadm - k5 * 512)
                            for h in range(4):
                                b = ir % 2
                                ir += 1
                                hb_ = (h % 2) * 64
                                S.op("pe", lambda: T.matmul(psi[b][:, 0:wd], lhsT=iqt[hb_:hb_ + 64, h // 2, qq * 128:(qq + 1) * 128],
                                                            rhs=ikt[hb_:hb_ + 64, k5 * 512:k5 * 512 + wd], start=True, stop=True),
                                     r=[kq, "c_ikt"], w=[f"c_psi{b}"])
                                S.op("act", lambda: A.activation(out=rb2[b][:, 0:wd], in_=psi[b][:, 0:wd], func=AF.Relu),
                                     r=[f"c_psi{b}"], w=[f"c_r{b}"])
                                sc = score[:, k5 * 512:k5 * 512 + wd]
                                if h == 0:
                                    S.op("dve", lambda: V.tensor_scalar(out=sc, in0=rb2[b][:, 0:wd], scalar1=iw_all[:, qb, 0:1], scalar2=None, op0=ALU.mult),
                                         r=[f"c_r{b}", "iw_all"], w=["c_score"])
                                else:
                                    S.op("dve", lambda: V.scalar_tensor_tensor(out=sc, in0=rb2[b][:, 0:wd], scalar=iw_all[:, qb, h:h + 1], in1=sc,
                                                                               op0=ALU.mult, op1=ALU.add),
                                         r=[f"c_r{b}", "iw_all", "c_score"], w=["c_score"])
                        S.op("dve", lambda: V.memset(score[0:64, qb * 128 + 64:(qb + 1) * 128], NEG), r=["c_score"], w=["c_score"])
                        KB = ["c_bis"]
                        if qb >= 2:
                            nfull = qb * 128 + 64
                            S.op("dve", lambda: V.tensor_reduce(out=bs_[:, 5:6], in_=score[:, 0:nadm], axis=AX.X, op=ALU.max), r=["c_score"], w=KB)
                            S.op("dve", lambda: V.tensor_reduce(out=bs_[:, 0:1], in_=score[:, 0:nfull], axis=AX.X, op=ALU.min), r=["c_score"], w=KB)
                            S.op("dve", lambda: V.tensor_scalar(out=bs_[:, 0:1], in0=bs_[:, 0:1], scalar1=-1.0, scalar2=None, op0=ALU.add), r=KB, w=KB)
                            S.op("dve", lambda: V.tensor_tensor(out=bs_[:, 1:2], in0=bs_[:, 5:6], in1=bs_[:, 0:1], op=ALU.subtract), r=KB, w=KB)
                            for i in range(NSTEP):
                                f = 2.0 ** -(i + 1)
                                S.op("dve", lambda: V.scalar_tensor_tensor(out=bs_[:, 2:3], in0=bs_[:, 1:2], scalar=float(f), in1=bs_[:, 0:1],
                                                                           op0=ALU.mult, op1=ALU.add), r=KB, w=KB)
                                S.op("dve", lambda: V.tensor_scalar(out=maskq[:, 0:nadm], in0=score[:, 0:nadm], scalar1=bs_[:, 2:3], scalar2=None,
                                                                    op0=ALU.is_gt, op1=ALU.add, accum_out=bs_[:, 3:4]),
                                     r=KB + ["c_score", "c_maskq"], w=KB + ["c_maskq"])
                                S.op("dve", lambda: V.tensor_scalar(out=bs_[:, 4:5], in0=bs_[:, 3:4], scalar1=float(TOPK), scalar2=float(f),
                                                                    op0=ALU.is_ge, op1=ALU.mult), r=KB, w=KB)
                                S.op("dve", lambda: V.scalar_tensor_tensor(out=bs_[:, 0:1], in0=bs_[:, 4:5], scalar=bs_[:, 1:2], in1=bs_[:, 0:1],
                                                                           op0=ALU.mult, op1=ALU.add), r=KB, w=KB)
                        else:
                            S.op("dve", lambda: V.memset(bs_[:, 0:1], -1.0e29), r=KB, w=KB)
                        S.op("dve", lambda: V.tensor_scalar(out=maskq[:, 0:nadm], in0=score[:, 0:nadm], scalar1=bs_[:, 0:1], scalar2=None, op0=ALU.is_gt),
                             r=KB + ["c_score", "c_maskq"], w=["c_maskq"])
                        for k1 in range(0, qb + 1, 4):
                            nk = min(4, qb + 1 - k1)
                            for u in range(nk):
                                S.op("pe", lambda: T.transpose(out=ptr[:, u, :], in_=maskq[:, (k1 + u) * 128:(k1 + u + 1) * 128], identity=ident[:]),
                                     r=["c_maskq"], w=["c_ptr"])
                            S.op("act", lambda: A.activation(out=maskT[:, k1:k1 + nk, qq * 128:(qq + 1) * 128], in_=ptr[:, 0:nk, :], func=AF.Copy),
                                 r=["c_ptr"], w=["c_maskT"])
                    items = []
                    yq = yq2[Q % 2]
                    kyq = f"c_yq{Q % 2}"
                    for h in range(8):
                        g = gidx
                        gidx += 1
                        pr, bs = h // 2, (h % 2) * 64
                        nj = 4 * Q + 4
                        gstate = {"first": True}
                        for j in range(nj):
                            b = len(items) % 2
                            jj = j - 4 * Q
                            c0 = max(0, jj) * 128

                            def s0(j=j, c0=c0, b=b, pr=pr, bs=bs):
                                S.op("pe", lambda: T.matmul(pss[b][:, c0:512], lhsT=kt[bs:bs + 64, pr, j * 128:(j + 1) * 128],
                                                            rhs=qt[bs:bs + 64, pr, c0:512], start=True, stop=True),
                                     r=["c_kt", kq], w=[f"c_pss{b}"])

                            def s1(j=j, c0=c0, b=b):
                                S.op("act", lambda: A.activation(out=pT2[b][:, c0:512], in_=pss[b][:, c0:512], func=AF.Exp),
                                     r=[f"c_pss{b}"], w=[f"c_pT{b}"])
                                if b == 0:
                                    S.op("dve", lambda: V.tensor_tensor(out=pT2[b][:, c0:512], in0=pT2[b][:, c0:512], in1=maskT[:, j, c0:512], op=ALU.mult),
                                         r=[f"c_pT{b}", "c_maskT"], w=[f"c_pT{b}"])
                                else:
                                    S.op("pool", lambda: G.tensor_tensor(out=pT2[b][:, c0:512], in0=pT2[b][:, c0:512], in1=maskT[:, j, c0:512], op=ALU.mult),
                                         r=[f"c_pT{b}", "c_maskT"], w=[f"c_pT{b}"])

                            def s2(h=h, j=j, jj=jj, b=b, g=g, gstate=gstate, nj=nj):
                                po = po2[g % 2]
                                kpo = f"c_po{g % 2}"
                                osb = osb2[g % 2]
                                rl = rl2[g % 2]
                                for q4 in range(max(0, jj), 4):
                                    S.op("pe", lambda: T.matmul(po[:, q4 * 65:(q4 + 1) * 65], lhsT=pT2[b][:, q4 * 128:(q4 + 1) * 128],
                                                                rhs=vx[:, j, h * 65:(h + 1) * 65], start=gstate["first"], stop=(j == nj - 1),
                                                                skip_group_check=True),
                                         r=[f"c_pT{b}", "c_vx"], w=[kpo])
                                    gstate["first"] = False
                                if j == nj - 1:
                                    S.op("act", lambda: A.activation(out=osb[:], in_=po[:, 0:260].rearrange("p (a e) -> p a e", e=65), func=AF.Copy),
                                         r=[kpo], w=[f"c_o{g % 2}"])
                                    S.op("dve", lambda: V.reciprocal(out=rl[:], in_=osb[:, :, 64]), r=[f"c_o{g % 2}"], w=[f"c_rl{g % 2}"])
                                    S.op("dve", lambda: V.tensor_tensor(out=yq[:, :, h * 64:(h + 1) * 64], in0=osb[:, :, 0:64],
                                                                        in1=rl[:].unsqueeze(2).to_broadcast([128, 4, 64]), op=ALU.mult),
                                         r=[f"c_o{g % 2}", f"c_rl{g % 2}"], w=[kyq])
                            items.append([s0, s1, s2])
                    run_pipeline(items, 3)
                    for cch in range(4):
                        for q4 in range(4):
                            S.op("pe", lambda: T.transpose(out=ptr[:, q4, :], in_=yq[:, q4, cch * 128:(cch + 1) * 128], identity=ident[:]),
                                 r=[kyq], w=["c_ptr"])
                        ys = yts[cch % 2]
                        S.op("act", lambda: A.activation(out=ys[:], in_=ptr[:].rearrange("p a b -> p (a b)"), func=AF.Copy), r=["c_ptr"], w=[f"c_yts{cch % 2}"])
                        S.dma("sp", YT[8 + cch, :, Q * 512:(Q + 1) * 512], ys[:], r=[f"c_yts{cch % 2}"], w=["YT"])
                S.barrier()
                ph.close()

            for ph in ([ExitStack()] if 'pm' in PH else []):
                wbr = [sb(ph, f"m_wbr{i}", [128, 4, D], BF16) for i in range(3)]
                wo = sb(ph, "m_wo", [128, 8, D], BF16)
                yt2 = [sb(ph, f"m_yt{i}", [128, 12, 512], BF16) for i in range(2)]
                gt2 = [sb(ph, f"m_gt{i}", [128, 24, 512], BF16) for i in range(2)]
                m0 = sb(ph, "m_m0", [128, 512], F32)
                m1 = sb(ph, "m_m1", [128, 512], F32)
                m2 = sb(ph, "m_m2", [128, 512], F32)
                mT = sb(ph, "m_mT", [128, 8, 512], BF16)
                xb2 = [sb(ph, f"m_x{i}", [128, D], F32) for i in range(2)]
                ps3 = [ps(ph, f"m_ps{i}", [128, 512], F32) for i in range(3)]
                pso = [ps(ph, f"m_pso{i}", [128, 512], F32) for i in range(2)]
                for i in range(3):
                    load_w_bf16(wbr[i][:], w_br[i][l].rearrange("(c p) n -> p c n", p=128), "m_w")
                load_w_bf16(wo[:], w_out[l].rearrange("(c p) n -> p c n", p=128), "m_w")
                io = 0
                for Tq in range(NQ):
                    yt = yt2[Tq % 2]
                    gt = gt2[Tq % 2]
                    kin = f"m_in{Tq % 2}"
                    for b3 in range(3):
                        S.dma("sp", yt[:, b3 * 4:(b3 + 1) * 4, :], YT[b3 * 4:(b3 + 1) * 4, :, Tq * 512:(Tq + 1) * 512].rearrange("j p t -> p j t"), r=["YT"], w=[kin])
                        S.dma("sp", gt[:, b3 * 8:(b3 + 1) * 8, :], GT[b3 * 8:(b3 + 1) * 8, :, Tq * 512:(Tq + 1) * 512].rearrange("j p t -> p j t"), r=["GT"], w=[kin])
                    for fc in range(8):
                        for br in range(3):
                            for kc in range(4):
                                S.op("pe", lambda: T.matmul(ps3[br][:], lhsT=wbr[br][:, kc, fc * 128:(fc + 1) * 128], rhs=yt[:, br * 4 + kc, :],
                                                            start=(kc == 0), stop=(kc == 3)), r=["m_w", kin], w=[f"m_ps{br}"])
                        S.op("dve", lambda: V.tensor_tensor(out=m0[:], in0=ps3[0][:], in1=gt[:, fc, :], op=ALU.mult), r=["m_ps0", kin], w=["m_m0"])
                        S.op("dve", lambda: V.tensor_tensor(out=m1[:], in0=ps3[1][:], in1=gt[:, 8 + fc, :], op=ALU.mult), r=["m_ps1", kin], w=["m_m1"])
                        S.op("dve", lambda: V.tensor_tensor(out=m2[:], in0=ps3[2][:], in1=gt[:, 16 + fc, :], op=ALU.mult), r=["m_ps2", kin], w=["m_m2"])
                        S.op("pool", lambda: G.tensor_tensor(out=m0[:], in0=m0[:], in1=m1[:], op=ALU.add), r=["m_m0", "m_m1"], w=["m_m0"])
                        S.op("pool", lambda: G.tensor_tensor(out=mT[:, fc, :], in0=m0[:], in1=m2[:], op=ALU.add), r=["m_m0", "m_m2"], w=["m_mT"])
                    for tt in range(4):
                        t = Tq * 4 + tt
                        xt = xb2[t % 2]
                        kx = f"m_x{t % 2}"
                        S.dma("sp", xt[:], xsrc[t * 128:(t + 1) * 128, :], r=[f"xr{t}"], w=[kx])
                        for nb in range(2):
                            pq = pso[io % 2]
                            kpq = f"m_pso{io % 2}"
                            io += 1
                            for fc in range(8):
                                S.op("pe", lambda: T.matmul(pq[:], lhsT=mT[:, fc, tt * 128:(tt + 1) * 128], rhs=wo[:, fc, nb * 512:(nb + 1) * 512],
                                                            start=(fc == 0), stop=(fc == 7)), r=["m_mT", "m_w"], w=[kpq])
                            S.op("dve", lambda: V.tensor_tensor(out=xt[:, nb * 512:(nb + 1) * 512], in0=pq[:], in1=xt[:, nb * 512:(nb + 1) * 512], op=ALU.add),
                                 r=[kpq, kx], w=[kx])
                        S.dma("sp", xres[t * 128:(t + 1) * 128, :], xt[:], r=[kx], w=[f"xr{t}"])
                S.barrier()
                ph.close()

            for ph in ([ExitStack()] if 'pf' in PH else []):
                wu = sb(ph, "f_wu", [128, 8, 4 * D], BF16)
                wd = sb(ph, "f_wd", [128, 32, D], BF16)
                xb2 = [sb(ph, f"f_x{i}", [128, D], F32) for i in range(2)]
                hb = sb(ph, "f_hb", [128, D], BF16)
                junk = sb(ph, "f_junk", [128, D], BF16)
                ssq = sb(ph, "f_ssq", [128, 8], F32)
                rstd = sb(ph, "f_rstd", [128, 8], F32)
                hmT = sb(ph, "f_hmT", [128, 8, 256], BF16)
                rr2 = [sb(ph, f"f_r{i}", [128, 256], F32) for i in range(2)]
                uT = sb(ph, "f_uT", [128, 32, 256], BF16)
                pst2 = [ps(ph, f"f_pst{i}", [128, 4, 128], BF16) for i in range(2)]
                psu = [ps(ph, f"f_psu{i}", [128, 256], F32) for i in range(2)]
                psd = [ps(ph, f"f_psd{i}", [128, 512], F32) for i in range(2)]
                for c in range(8):
                    load_w_bf16(wu[:, c, :], w_up[l, c * 128:(c + 1) * 128, :], "f_w")
                for c4 in range(4):
                    load_w_bf16(wd[:, c4 * 8:(c4 + 1) * 8, :], w_down[l, c4 * 1024:(c4 + 1) * 1024, :].rearrange("(c p) n -> p c n", p=128), "f_w")
                io = 0
                for tg in range(NT // 2):
                    for t2 in range(2):
                        t = tg * 2 + t2
                        xt = xb2[t2]
                        kx = f"f_x{t2}"
                        S.dma("sp", xt[:], xres[t * 128:(t + 1) * 128, :], r=[f"xr{t}"], w=[kx])
                        norm_transpose(xt[:], kx, gM, l, hb, pst2, lambda c, t2=t2: hmT[:, c, t2 * 128:(t2 + 1) * 128], "f_hmT", ssq, rstd, junk, "f")
                    for fc in range(32):
                        pu = psu[fc % 2]
                        for c in range(8):
                            S.op("pe", lambda: T.matmul(pu[:], lhsT=wu[:, c, fc * 128:(fc + 1) * 128], rhs=hmT[:, c, :], start=(c == 0), stop=(c == 7)),
                                 r=["f_w", "f_hmT"], w=[f"f_psu{fc % 2}"])
                        rr = rr2[fc % 2]
                        S.op("act", lambda: A.activation(out=rr[:], in_=pu[:], func=AF.Relu), r=[f"f_psu{fc % 2}"], w=[f"f_r{fc % 2}"])
                        if fc % 2 == 0:
                            S.op("pool", lambda: G.tensor_tensor(out=uT[:, fc, :], in0=rr[:], in1=rr[:], op=ALU.mult), r=[f"f_r{fc % 2}"], w=["f_uT"])
                        else:
                            S.op("dve", lambda: V.tensor_tensor(out=uT[:, fc, :], in0=rr[:], in1=rr[:], op=ALU.mult), r=[f"f_r{fc % 2}"], w=["f_uT"])
                    for t2 in range(2):
                        t = tg * 2 + t2
                        xt = xb2[t2]
                        kx = f"f_x{t2}"
                        for nb in range(2):
                            pq = psd[io % 2]
                            kpq = f"f_psd{io % 2}"
                            io += 1
                            for fc in range(32):
                                S.op("pe", lambda: T.matmul(pq[:], lhsT=uT[:, fc, t2 * 128:(t2 + 1) * 128], rhs=wd[:, fc, nb * 512:(nb + 1) * 512],
                                                            start=(fc == 0), stop=(fc == 31)), r=["f_uT", "f_w"], w=[kpq])
                            S.op("dve", lambda: V.tensor_tensor(out=xt[:, nb * 512:(nb + 1) * 512], in0=pq[:], in1=xt[:, nb * 512:(nb + 1) * 512], op=ALU.add),
                                 r=[kpq, kx], w=[kx])
                        S.dma("sp", xres[t * 128:(t + 1) * 128, :], xt[:], r=[kx], w=[f"xr{t}"])
                S.barrier()
                ph.close()

            for ph in ([ExitStack()] if 'pp' in PH else []):
                wg = sb(ph, "e_wg", [128, 8, D], BF16)
                wp = sb(ph, "e_wp", [128, 2, D], BF16)
                xb2 = [sb(ph, f"e_x{i}", [128, D], F32) for i in range(2)]
                pb2 = [sb(ph, f"e_p{i}", [128, 256], F32) for i in range(2)]
                pbb = sb(ph, "e_pbb", [128, 256], BF16)
                hb = sb(ph, "e_hb", [128, D], BF16)
                junk = sb(ph, "e_junk", [128, D], BF16)
                ssq = sb(ph, "e_ssq", [128, 8], F32)
                rstd = sb(ph, "e_rstd", [128, 8], F32)
                hpT = sb(ph, "e_hpT", [128, 8, 128], BF16)
                pT = sb(ph, "e_pT", [128, 2, 128], BF16)
                sg = sb(ph, "e_sg", [128, 512], F32)
                tm = sb(ph, "e_tm", [128, 512], F32)
                pst2 = [ps(ph, f"e_pst{i}", [128, 4, 128], BF16) for i in range(2)]
                ptp = ps(ph, "e_ptp", [128, 2, 128], BF16)
                psg = [ps(ph, f"e_psg{i}", [128, 512], F32) for i in range(2)]
                psp = [ps(ph, f"e_psp{i}", [128, 512], F32) for i in range(2)]
                load_w_bf16(wg[:], w_ple_gate[l].rearrange("(c p) n -> p c n", p=128), "e_w")
                load_w_bf16(wp[:], w_ple_proj[l].rearrange("(c p) n -> p c n", p=128), "e_w")
                dst = out_d if l == depth - 1 else xres
                kdst = "out"
                io = 0
                for t in range(NT):
                    xt = xb2[t % 2]
                    kx = f"e_x{t % 2}"
                    pb = pb2[t % 2]
                    S.dma("sp", xt[:], xres[t * 128:(t + 1) * 128, :], r=[f"xr{t}"], w=[kx])
                    S.dma("sp", pb[:], p_in[l, t * 128:(t + 1) * 128, :], w=[f"e_p{t % 2}"])
                    norm_transpose(xt[:], kx, gP, l, hb, pst2, lambda c: hpT[:, c, :], "e_hpT", ssq, rstd, junk, "e")
                    S.op("pool", lambda: G.tensor_copy(out=pbb[:], in_=pb[:]), r=[f"e_p{t % 2}"], w=["e_pbb"])
                    for c in range(2):
                        S.op("pe", lambda: T.transpose(out=ptp[:, c, :], in_=pbb[:, c * 128:(c + 1) * 128], identity=ident[:]), r=["e_pbb"], w=["e_ptp"])
                    S.op("act", lambda: A.activation(out=pT[:], in_=ptp[:], func=AF.Copy), r=["e_ptp"], w=["e_pT"])
                    for nb in range(2):
                        b = io % 2
                        io += 1
                        for c in range(8):
                            S.op("pe", lambda: T.matmul(psg[b][:], lhsT=hpT[:, c, :], rhs=wg[:, c, nb * 512:(nb + 1) * 512], start=(c == 0), stop=(c == 7)),
                                 r=["e_hpT", "e_w"], w=[f"e_psg{b}"])
                        for c in range(2):
                            S.op("pe", lambda: T.matmul(psp[b][:], lhsT=pT[:, c, :], rhs=wp[:, c, nb * 512:(nb + 1) * 512], start=(c == 0), stop=(c == 1)),
                                 r=["e_pT", "e_w"], w=[f"e_psp{b}"])
                        S.op("act", lambda: A.activation(out=sg[:], in_=psg[b][:], func=AF.Sigmoid), r=[f"e_psg{b}"], w=["e_sg"])
                        S.op("dve", lambda: V.tensor_tensor(out=tm[:], in0=psp[b][:], in1=sg[:], op=ALU.mult), r=[f"e_psp{b}", "e_sg"], w=["e_tm"])
                        S.op("pool", lambda: G.tensor_tensor(out=xt[:, nb * 512:(nb + 1) * 512], in0=xt[:, nb * 512:(nb + 1) * 512], in1=tm[:], op=ALU.add),
                             r=["e_tm", kx], w=[kx])
                    S.dma("sp", dst[t * 128:(t + 1) * 128, :], xt[:], r=[kx], w=[kdst if l == depth - 1 else f"xr{t}"])
                S.barrier()
                ph.close()
        S.barrier()
        build_nc.ninstr = dict(S.ninstr)
    return nc


def _host_layout(inputs, b):
    f32 = np.float32
    m = {}
    m["x"] = np.ascontiguousarray(inputs["x"][b], dtype=f32)
    m["p"] = np.ascontiguousarray(inputs["p"][:, b], dtype=f32)
    pos = np.asarray(inputs["positions"][b]).astype(np.int32)
    m["pos"] = np.ascontiguousarray(pos.reshape(NT, 128).T)
    m["invf"] = (10000.0 ** (-np.arange(0, 64, 2, dtype=np.float32) / 64)).astype(f32)
    for k in ("attn_norm", "mlp_norm", "ple_norm"):
        a = np.asarray(inputs[k], dtype=f32)
        m[k + "_l"] = np.ascontiguousarray(a.reshape(DEPTH, 8, 128).transpose(2, 0, 1).reshape(128, DEPTH * 8))
    for k in ("a_q_norm", "a_k_norm", "a_lambda", "a_subln", "c_q_norm", "c_k_norm", "idx_k_norm"):
        m[k] = np.ascontiguousarray(np.asarray(inputs[k], dtype=f32).reshape(-1))
    for k in ("w_in", "w_br_a", "w_br_b", "w_br_c", "w_out", "w_up", "w_down", "w_ple_gate", "w_ple_proj"):
        m[k] = np.ascontiguousarray(inputs[k], dtype=f32)
    return m


def kernel(**inputs):
    nc = build_nc()
    in_maps = [_host_layout(inputs, c % 4) for c in range(4)]
    in_maps = in_maps + in_maps
    res = run_bass_kernel_spmd(nc, in_maps, core_ids=list(range(8)))
    out = np.stack([np.asarray(res.results[b]["out"], dtype=np.float32) for b in range(4)], axis=0)
    return out
```

```python
import math
from contextlib import ExitStack

import numpy as np
import concourse.bass as bass
import concourse.mybir as mybir
from concourse.bass_utils import run_bass_kernel_spmd

F32 = mybir.dt.float32
BF16 = mybir.dt.bfloat16
I32 = mybir.dt.int32
AF = mybir.ActivationFunctionType
ALU = mybir.AluOpType
AX = mybir.AxisListType

DEPTH = 4
SL = 4096
D = 1024
NT = 32
NQ = 8
EPS = 1e-6
W_IN_COLS = 8004
TOPK = 256
NSTEP = 20
NEG = -1.0e30

SEM_EPOCH = 30000
N_DMA_SEMS = 24


class Sched:
    def __init__(self, nc, stack):
        self.nc = nc
        self.stack = stack
        self.engs = {"pe": nc.tensor, "act": nc.scalar, "dve": nc.vector,
                     "pool": nc.gpsimd, "sp": nc.sync}
        self.esem = {}
        self.ecount = {}
        self.n_sems = 0
        for e in self.engs:
            self._new_esem(e)
        self.waited = {e: {} for e in self.engs}
        self.dsems = [[self._sem(f"dma{i}"), 0] for i in range(N_DMA_SEMS)]
        self.dnext = 0
        self.lastw = {}
        self.reads = {}
        self.ninstr = {e: 0 for e in self.engs}

    def _sem(self, name):
        self.n_sems += 1
        return self.stack.enter_context(self.nc.semaphore(name))

    def _new_esem(self, e):
        self.esem[e] = self._sem(f"e_{e}_{self.n_sems}")
        self.ecount[e] = 0

    def _wait(self, e, dep):
        sem, val, tag = dep
        w = self.waited[e]
        k = id(sem)
        if w.get(k, 0) >= val:
            return
        w[k] = val
        self.engs[e].wait_ge(sem, val)
        self.ninstr[e] += 1

    def _collect(self, r, w):
        deps = []
        for k in list(r) + list(w):
            deps.extend(self.lastw.get(k, ()))
        for k in w:
            rd = self.reads.get(k)
            if rd:
                deps.extend(rd.values())
        return deps

    def _commit(self, r, w, dep, merge=()):
        for k in w:
            if k in merge:
                self.lastw[k].append(dep)
            else:
                self.lastw[k] = [dep]
                self.reads[k] = {}
        for k in r:
            if k in w:
                continue
            self.reads.setdefault(k, {})[dep[2]] = dep

    def op(self, e, fn, r=(), w=()):
        if self.ecount[e] >= SEM_EPOCH:
            self._new_esem(e)
        for d in self._collect(r, w):
            if e == "pe" and d[2] == "pe":
                continue
            self._wait(e, d)
        ins = fn()
        self.ecount[e] += 1
        ins.then_inc(self.esem[e], 1)
        dep = (self.esem[e], self.ecount[e], e)
        self.ninstr[e] += 1
        self._commit(r, w, dep)
        return ins

    def dma(self, q, out, in_, r=(), w=()):
        slot = self.dsems[self.dnext]
        self.dnext = (self.dnext + 1) % len(self.dsems)
        sem = slot[0]
        if slot[1] > 0:
            self._wait(q, (sem, slot[1], "dma"))
        deps = []
        merge = set()
        for k in r:
            deps.extend(self.lastw.get(k, ()))
        for k in w:
            lw = self.lastw.get(k, ())
            rd = self.reads.get(k) or {}
            if lw and not rd and all(d[2].startswith("dma") for d in lw):
                merge.add(k)
            else:
                deps.extend(lw)
                deps.extend(rd.values())
        for d in deps:
            self._wait(q, d)
        ins = self.engs[q].dma_start(out=out, in_=in_)
        slot[1] += 16
        ins.then_inc(sem, 16)
        dep = (sem, slot[1], f"dma{id(sem)}")
        self.ninstr[q] += 1
        self._commit(r, w, dep, merge)
        return ins

    def barrier(self):
        for e in self.engs:
            for o in self.engs:
                if o != e and self.ecount[o] > 0:
                    self._wait(e, (self.esem[o], self.ecount[o], o))
            for sem, val in self.dsems:
                if val:
                    self._wait(e, (sem, val, "dma"))


def run_pipeline(items, nstage):
    n = len(items)
    for step in range(n + nstage - 1):
        for k in range(nstage - 1, -1, -1):
            i = step - k
            if 0 <= i < n and items[i][k] is not None:
                items[i][k]()


def build_nc(depth=DEPTH, debug=False, phases=None):
    PH = set(phases) if phases is not None else {'p1', 'pa', 'pb', 'pc', 'pm', 'pf', 'pp'}
    nc = bass.Bass("TRN2", target_bir_lowering=False)

    def din(name, shape, dt=F32):
        return nc.dram_tensor(name, list(shape), dt, kind="ExternalInput").ap()

    def dscr(name, shape, dt):
        kind = "ExternalOutput" if debug else "Internal"
        return nc.dram_tensor(name, list(shape), dt, kind=kind).ap()

    x_in = din("x", [SL, D])
    p_in = din("p", [DEPTH, SL, 256])
    pos_in = din("pos", [128, NT], I32)
    invf_in = din("invf", [32])
    attn_norm = din("attn_norm_l", [128, DEPTH * 8])
    mlp_norm = din("mlp_norm_l", [128, DEPTH * 8])
    ple_norm = din("ple_norm_l", [128, DEPTH * 8])
    w_in = din("w_in", [DEPTH, D, W_IN_COLS])
    a_q_norm = din("a_q_norm", [DEPTH * 64])
    a_k_norm = din("a_k_norm", [DEPTH * 64])
    a_lambda = din("a_lambda", [DEPTH * 256])
    a_subln = din("a_subln", [DEPTH * 128])
    c_q_norm = din("c_q_norm", [DEPTH * 64])
    c_k_norm = din("c_k_norm", [DEPTH * 64])
    idx_k_norm = din("idx_k_norm", [DEPTH * 64])
    w_br = [din("w_br_a", [DEPTH, 512, D]), din("w_br_b", [DEPTH, 512, D]), din("w_br_c", [DEPTH, 512, D])]
    w_out = din("w_out", [DEPTH, D, D])
    w_up = din("w_up", [DEPTH, D, 4 * D])
    w_down = din("w_down", [DEPTH, 4 * D, D])
    w_ple_gate = din("w_ple_gate", [DEPTH, D, D])
    w_ple_proj = din("w_ple_proj", [DEPTH, 256, D])
    out_d = nc.dram_tensor("out", [SL, D], F32, kind="ExternalOutput").ap()

    xres = dscr("xres", [SL, D], F32)
    QT = {k: dscr("QT_" + k, [4, 128, SL], BF16) for k in "abc"}
    KT = {k: dscr("KT_" + k, [4, 128, SL], BF16) for k in "abc"}
    VX = {"a": dscr("V_a", [SL, 4 * 129], BF16), "b": dscr("V_b", [SL, 8 * 65], BF16),
          "c": dscr("V_c", [SL, 8 * 65], BF16)}
    IQT = dscr("IQT", [2, 128, SL], BF16)
    IKT = dscr("IKT", [128, SL], BF16)
    GT = dscr("GT", [24, 128, SL], BF16)
    YT = dscr("YT", [12, 128, SL], BF16)

    with ExitStack() as st:
        S = Sched(nc, st)
        E = st.enter_context
        V = nc.vector
        A = nc.scalar
        G = nc.gpsimd
        T = nc.tensor

        uid = [0]

        def sb(stack, name, shape, dt):
            uid[0] += 1
            return stack.enter_context(nc.sbuf_tensor(f"s{uid[0]}_{name}", list(shape), dt))

        def ps(stack, name, shape, dt):
            uid[0] += 1
            return stack.enter_context(nc.psum_tensor(f"q{uid[0]}_{name}", list(shape), dt))

        ident = sb(st, "ident", [128, 128], BF16)
        ntri = sb(st, "ntri", [128, 128], BF16)
        nones = sb(st, "nones", [128, 128], BF16)
        strict = sb(st, "strict", [128, 128], BF16)
        cos_t = sb(st, "cos_t", [128, NT, 32], F32)
        sin_t = sb(st, "sin_t", [128, NT, 32], F32)
        gA = sb(st, "gA", [128, DEPTH * 8], F32)
        gM = sb(st, "gM", [128, DEPTH * 8], F32)
        gP = sb(st, "gP", [128, DEPTH * 8], F32)
        aqn = sb(st, "aqn", [128, DEPTH * 64], F32)
        akn = sb(st, "akn", [128, DEPTH * 64], F32)
        cqn = sb(st, "cqn", [128, DEPTH * 64], F32)
        ckn = sb(st, "ckn", [128, DEPTH * 64], F32)
        ikn = sb(st, "ikn", [128, DEPTH * 64], F32)
        subg = sb(st, "subg", [128, DEPTH * 128], F32)
        lam_t = sb(st, "lam_t", [128, DEPTH], F32)
        nlam_t = sb(st, "nlam_t", [128, DEPTH], F32)
        iw_all = sb(st, "iw_all", [128, NT, 4], F32)
        pw2 = sb(st, "pw2", [128, NSTEP + 1], F32)

        CONST = ["const"]

        with ExitStack() as ph:
            posi = sb(ph, "posi", [128, NT], I32)
            posf = sb(ph, "posf", [128, NT], F32)
            invf = sb(ph, "invf", [128, 32], F32)
            ang = sb(ph, "ang", [128, NT, 32], F32)
            t0 = sb(ph, "t0", [128, NT, 32], F32)
            t1 = sb(ph, "t1", [128, NT, 32], F32)
            lpb = sb(ph, "lpb", [128, DEPTH * 256], F32)
            tmp64 = sb(ph, "tmp64", [128, 64], F32)
            s12 = sb(ph, "s12", [128, 2], F32)
            K0 = ["p0"]
            S.op("pool", lambda: G.memset(ident[:], 1.0), w=CONST)
            S.op("pool", lambda: G.affine_select(out=ident[:], in_=ident[:], pattern=[[-1, 128]],
                                                 compare_op=ALU.is_equal, fill=0.0, base=0, channel_multiplier=1),
                 r=CONST, w=CONST)
            S.op("pool", lambda: G.memset(nones[:], -1.0), w=CONST)
            for i_ in range(NSTEP + 1):
                S.op("pool", lambda: G.memset(pw2[:, i_:i_ + 1], float(2.0 ** -(i_ + 1))), w=CONST)
            S.op("pool", lambda: G.memset(ntri[:], -1.0), w=CONST)
            S.op("pool", lambda: G.affine_select(out=ntri[:], in_=ntri[:], pattern=[[-1, 128]],
                                                 compare_op=ALU.is_ge, fill=0.0, base=0, channel_multiplier=1),
                 r=CONST, w=CONST)
            S.op("pool", lambda: G.memset(strict[:], 1.0), w=CONST)
            S.op("pool", lambda: G.affine_select(out=strict[:], in_=strict[:], pattern=[[1, 128]],
                                                 compare_op=ALU.is_gt, fill=0.0, base=0, channel_multiplier=-1),
                 r=CONST, w=CONST)
            for dst, src in [(gA, attn_norm), (gM, mlp_norm), (gP, ple_norm)]:
                S.dma("sp", dst[:], src[:, :], w=CONST)
            for dst, src in [(aqn, a_q_norm), (akn, a_k_norm), (cqn, c_q_norm), (ckn, c_k_norm),
                             (ikn, idx_k_norm), (subg, a_subln), (lpb, a_lambda), (invf, invf_in)]:
                S.dma("sp", dst[:], src.partition_broadcast(128), w=CONST + K0)
            S.dma("sp", posi[:], pos_in[:, :], w=K0)
            S.op("dve", lambda: V.tensor_scalar(out=aqn[:], in0=aqn[:], scalar1=0.125, scalar2=None, op0=ALU.mult), r=CONST, w=CONST)
            S.op("dve", lambda: V.tensor_scalar(out=cqn[:], in0=cqn[:], scalar1=0.125, scalar2=None, op0=ALU.mult), r=CONST, w=CONST)
            for l in range(DEPTH):
                li = 0.8 - 0.6 * math.exp(-0.3 * l)
                S.op("dve", lambda: V.tensor_scalar(out=subg[:, l * 128:(l + 1) * 128], in0=subg[:, l * 128:(l + 1) * 128],
                                                    scalar1=float(1.0 - li), scalar2=None, op0=ALU.mult), r=CONST, w=CONST)
                for j in range(2):
                    S.op("dve", lambda: V.tensor_tensor(out=tmp64[:], in0=lpb[:, l * 256 + j * 128: l * 256 + j * 128 + 64],
                                                        in1=lpb[:, l * 256 + j * 128 + 64: l * 256 + j * 128 + 128], op=ALU.mult),
                         r=K0, w=["tmp64"])
                    S.op("dve", lambda: V.tensor_reduce(out=s12[:, j:j + 1], in_=tmp64[:], axis=AX.X, op=ALU.add),
                         r=["tmp64"], w=["s12"])
                S.op("act", lambda: A.activation(out=s12[:], in_=s12[:], func=AF.Exp), r=["s12"], w=["s12"])
                S.op("dve", lambda: V.tensor_tensor(out=lam_t[:, l:l + 1], in0=s12[:, 0:1], in1=s12[:, 1:2], op=ALU.subtract),
                     r=["s12"], w=CONST)
                S.op("dve", lambda: V.tensor_scalar(out=lam_t[:, l:l + 1], in0=lam_t[:, l:l + 1], scalar1=float(li), scalar2=None, op0=ALU.add),
                     r=CONST, w=CONST)
                S.op("dve", lambda: V.tensor_scalar(out=nlam_t[:, l:l + 1], in0=lam_t[:, l:l + 1], scalar1=-1.0, scalar2=None, op0=ALU.mult),
                     r=CONST, w=CONST)
            S.op("dve", lambda: V.tensor_copy(out=posf[:], in_=posi[:]), r=K0, w=["posf"])
            S.op("dve", lambda: V.tensor_tensor(out=ang[:], in0=posf[:].unsqueeze(2).to_broadcast([128, NT, 32]),
                                                in1=invf[:].unsqueeze(1).to_broadcast([128, NT, 32]), op=ALU.mult),
                 r=["posf"] + K0, w=["ang"])
            MAGIC = 12582912.0
            C1 = 6.28125
            C2 = 2.0 * math.pi - C1
            PI_LO = 3.1415925
            S.op("dve", lambda: V.tensor_scalar(out=t0[:], in0=ang[:], scalar1=float(1.0 / (2.0 * math.pi)), scalar2=None, op0=ALU.mult), r=["ang"], w=["t0"])
            S.op("dve", lambda: V.tensor_scalar(out=t0[:], in0=t0[:], scalar1=MAGIC, scalar2=None, op0=ALU.add), r=["t0"], w=["t0"])
            S.op("dve", lambda: V.tensor_scalar(out=t0[:], in0=t0[:], scalar1=-MAGIC, scalar2=None, op0=ALU.add), r=["t0"], w=["t0"])
            S.op("dve", lambda: V.scalar_tensor_tensor(out=t1[:], in0=t0[:], scalar=-C1, in1=ang[:], op0=ALU.mult, op1=ALU.add), r=["t0", "ang"], w=["t1"])
            S.op("dve", lambda: V.scalar_tensor_tensor(out=t1[:], in0=t0[:], scalar=-C2, in1=t1[:], op0=ALU.mult, op1=ALU.add), r=["t0", "t1"], w=["t1"])
            S.op("dve", lambda: V.tensor_scalar(out=t0[:], in0=t1[:], scalar1=PI_LO, scalar2=-PI_LO, op0=ALU.min, op1=ALU.max), r=["t1"], w=["t0"])
            S.op("act", lambda: A.activation(out=sin_t[:], in_=t0[:], func=AF.Sin), r=["t0"], w=CONST)
            S.op("dve", lambda: V.tensor_scalar(out=t1[:], in0=t1[:], scalar1=float(math.pi / 2), scalar2=None, op0=ALU.add), r=["t1"], w=["t1"])
            S.op("dve", lambda: V.tensor_scalar(out=t0[:], in0=t1[:], scalar1=float(math.pi), scalar2=float(-2.0 * math.pi), op0=ALU.is_gt, op1=ALU.mult), r=["t1", "const"], w=["t0"])
            S.op("dve", lambda: V.tensor_tensor(out=t1[:], in0=t1[:], in1=t0[:], op=ALU.add), r=["t0", "t1"], w=["t1"])
            S.op("dve", lambda: V.tensor_scalar(out=t1[:], in0=t1[:], scalar1=PI_LO, scalar2=-PI_LO, op0=ALU.min, op1=ALU.max), r=["t1"], w=["t1"])
            S.op("act", lambda: A.activation(out=cos_t[:], in_=t1[:], func=AF.Sin), r=["t1"], w=CONST)
            S.barrier()

        def rms_rows(xt, kx, ssq, rstd, junk, kpre):
            S.op("act", lambda: A.activation(out=junk[:], in_=xt, func=AF.Square, accum_out=ssq[:, 0:1]),
                 r=[kx], w=[kpre + "junk", kpre + "ssq"])
            S.op("act", lambda: A.activation(out=rstd[:, 0:1], in_=ssq[:, 0:1], func=AF.Sqrt, bias=EPS, scale=1.0 / D),
                 r=[kpre + "ssq"], w=[kpre + "rstd"])
            S.op("dve", lambda: V.reciprocal(out=rstd[:, 0:1], in_=rstd[:, 0:1]), r=[kpre + "rstd"], w=[kpre + "rstd"])

        def norm_transpose(xt, kx, gain, l, hb, pst2, dst_fn, kdst, ssq, rstd, junk, kpre):
            rms_rows(xt, kx, ssq, rstd, junk, kpre)
            S.op("dve", lambda: V.tensor_scalar(out=hb[:], in0=xt, scalar1=rstd[:, 0:1], scalar2=None, op0=ALU.mult),
                 r=[kx, kpre + "rstd"], w=[kpre + "hb"])
            for half in range(2):
                pt = pst2[half]
                for c4 in range(4):
                    c = half * 4 + c4
                    S.op("pe", lambda: T.transpose(out=pt[:, c4, :], in_=hb[:, c * 128:(c + 1) * 128], identity=ident[:]),
                         r=[kpre + "hb"], w=[kpre + f"pst{half}"])
                for c4 in range(4):
                    c = half * 4 + c4
                    if c4 % 2 == 0:
                        S.op("dve", lambda: V.tensor_scalar(out=dst_fn(c), in0=pt[:, c4, :], scalar1=gain[:, l * 8 + c:l * 8 + c + 1],
                                                            scalar2=None, op0=ALU.mult),
                             r=[kpre + f"pst{half}"], w=[kdst])
                    else:
                        S.op("act", lambda: A.activation(out=dst_fn(c), in_=pt[:, c4, :], func=AF.Copy,
                                                         scale=gain[:, l * 8 + c:l * 8 + c + 1]),
                             r=[kpre + f"pst{half}"], w=[kdst])

        def load_w_bf16(dst, src, key):
            S.dma("pool", dst, src, w=[key])

        for l in range(depth):
            xsrc = x_in if l == 0 else xres
            lam_init = 0.8 - 0.6 * math.exp(-0.3 * l)

            for ph in ([ExitStack()] if 'p1' in PH else []):
                hT = sb(ph, "hT", [128, 8, SL], BF16)
                xb2 = [sb(ph, f"p1x{i}", [128, D], F32) for i in range(2)]
                hb = sb(ph, "p1hb", [128, D], BF16)
                junk = sb(ph, "p1junk", [128, D], BF16)
                ssq = sb(ph, "p1ssq", [128, 8], F32)
                rstd = sb(ph, "p1rstd", [128, 8], F32)
                pst2 = [ps(ph, f"p1pst{i}", [128, 4, 128], BF16) for i in range(2)]
                psm = [ps(ph, f"p1ps{i}", [128, 512], F32) for i in range(2)]
                wblk = [sb(ph, f"p1w{i}", [128, 8, 512], BF16) for i in range(2)]
                sq_2 = [sb(ph, f"p1sq{i}", [128, 512], F32) for i in range(2)]
                xn_2 = [sb(ph, f"p1xn{i}", [128, 512], F32) for i in range(2)]
                xg_2 = [sb(ph, f"p1xg{i}", [128, 512], F32) for i in range(2)]
                ra_2 = [sb(ph, f"p1ra{i}", [128, 256], F32) for i in range(2)]
                rb_2 = [sb(ph, f"p1rb{i}", [128, 256], F32) for i in range(2)]
                ob_2 = [sb(ph, f"p1ob{i}", [128, 512], BF16) for i in range(2)]
                ssq_2 = [sb(ph, f"p1ssq{i}", [128, 8], F32) for i in range(2)]
                rstd_2 = [sb(ph, f"p1rstd{i}", [128, 8], F32) for i in range(2)]
                stg = [sb(ph, f"p1stg{i}", [128, 4, 512], BF16) for i in range(2)]
                vst = [sb(ph, f"p1vst{i}", [128, 520], BF16) for i in range(4)]
                gst = [sb(ph, f"p1gst{i}", [128, 512], BF16) for i in range(2)]

                for t in range(NT):
                    xt = xb2[t % 2]
                    kx = f"p1x{t % 2}"
                    S.dma("sp", xt[:], xsrc[t * 128:(t + 1) * 128, :], r=[f"xr{t}"], w=[kx])
                    norm_transpose(xt[:], kx, gA, l, hb, pst2,
                                   lambda c, t=t: hT[:, c, t * 128:(t + 1) * 128], "hT", ssq, rstd, junk, "p1")
                for i in range(4):
                    S.op("pool", lambda: G.memset(vst[i][:], 1.0), w=[f"vst{i}"])

                blocks = []
                names = ["aq", "ak", "av", "bq", "bk", "bv", "cq", "ck", "cv"]
                for i, nm in enumerate(names):
                    blocks.append((nm, i * 512, 512))
                blocks.append(("idx", 4608, 324))
                for i in range(6):
                    blocks.append((f"gl{i}", 4932 + i * 512, 512))
                w_l = w_in[l].rearrange("(c p) n -> p c n", p=128)

                def qk_post(pv, nh, gain, do_norm, do_rope, t, outv, scale, kout):
                    n = nh * 64
                    pz = t % 2
                    sq, xn, xg, ra, rb = sq_2[pz], xn_2[pz], xg_2[pz], ra_2[pz], rb_2[pz]
                    ssq, rstd = ssq_2[pz], rstd_2[pz]
                    ksq, kxn, kxg, kra, krb, kss, krs = (f"sq{pz}", f"xn{pz}", f"xg{pz}", f"ra{pz}", f"rb{pz}", f"qssq{pz}", f"qrstd{pz}")
                    if do_norm:
                        sqv = sq[:, 0:n].rearrange("p (h d) -> p h d", d=64)
                        S.op("act", lambda: A.activation(out=sqv, in_=pv, func=AF.Square), r=["psm"], w=[ksq])
                        S.op("dve", lambda: V.tensor_reduce(out=ssq[:, 0:nh], in_=sqv, axis=AX.X, op=ALU.add), r=[ksq], w=[kss])
                        S.op("act", lambda: A.activation(out=rstd[:, 0:nh], in_=ssq[:, 0:nh], func=AF.Sqrt, bias=EPS, scale=1.0 / 64),
                             r=[kss], w=[krs])
                        S.op("dve", lambda: V.reciprocal(out=rstd[:, 0:nh], in_=rstd[:, 0:nh]), r=[krs], w=[krs])
                        xnv = xn[:, 0:n].rearrange("p (h d) -> p h d", d=64)
                        S.op("dve", lambda: V.tensor_tensor(out=xnv, in0=pv, in1=rstd[:, 0:nh].unsqueeze(2).to_broadcast([128, nh, 64]), op=ALU.mult),
                             r=["psm", krs], w=[kxn])
                        xgv = xg[:, 0:n].rearrange("p (h d) -> p h d", d=64)
                        S.op("pool", lambda: G.tensor_tensor(out=xgv, in0=xnv, in1=gain.unsqueeze(1).to_broadcast([128, nh, 64]), op=ALU.mult),
                             r=[kxn, "const"], w=[kxg])
                    else:
                        xgv = xg[:, 0:n].rearrange("p (h d) -> p h d", d=64)
                        S.op("act", lambda: A.activation(out=xgv, in_=pv, func=AF.Copy, scale=float(scale)), r=["psm"], w=[kxg])
                    if do_rope:
                        x1 = xgv[:, :, 0:32]
                        x2 = xgv[:, :, 32:64]
                        cb = cos_t[:, t, :].unsqueeze(1).to_broadcast([128, nh, 32])
                        sbb = sin_t[:, t, :].unsqueeze(1).to_broadcast([128, nh, 32])
                        rav = ra[:, 0:nh * 32].rearrange("p (h d) -> p h d", d=32)
                        rbv = rb[:, 0:nh * 32].rearrange("p (h d) -> p h d", d=32)
                        S.op("pool", lambda: G.tensor_tensor(out=rav, in0=x1, in1=cb, op=ALU.mult), r=[kxg], w=[kra])
                        S.op("dve", lambda: V.tensor_tensor(out=rbv, in0=x2, in1=sbb, op=ALU.mult), r=[kxg], w=[krb])
                        S.op("dve", lambda: V.tensor_tensor(out=outv[:, :, 0:32], in0=rav, in1=rbv, op=ALU.subtract), r=[kra, krb], w=[kout])
                        S.op("pool", lambda: G.tensor_tensor(out=rav, in0=x2, in1=cb, op=ALU.mult), r=[kxg, kra], w=[kra])
                        S.op("dve", lambda: V.tensor_tensor(out=rbv, in0=x1, in1=sbb, op=ALU.mult), r=[kxg, krb], w=[krb])
                        S.op("dve", lambda: V.tensor_tensor(out=outv[:, :, 32:64], in0=rav, in1=rbv, op=ALU.add), r=[kra, krb], w=[kout])
                    else:
                        S.op("dve", lambda: V.tensor_copy(out=outv, in_=xgv), r=[kxg], w=[kout])

                nblk = len(blocks)
                S.dma("pool", wblk[0][:, :, 0:blocks[0][2]], w_l[:, :, blocks[0][1]:blocks[0][1] + blocks[0][2]], w=["wblk0"])
                items = []

                def prefetch_w(bi):
                    if bi + 1 < nblk:
                        nn = blocks[bi + 1]
                        S.dma("pool", wblk[(bi + 1) % 2][:, :, 0:nn[2]], w_l[:, :, nn[1]:nn[1] + nn[2]], w=[f"wblk{(bi + 1) % 2}"])

                for bi, (nm, c0, n) in enumerate(blocks):
                    wb = wblk[bi % 2]
                    kw = f"wblk{bi % 2}"
                    if nm.startswith("gl"):
                        gi = int(nm[2:])
                        for Tq in range(NQ):
                            for j in range(4):
                                pb_ = len(items) % 2

                                def g0(bi=bi, wb=wb, kw=kw, Tq=Tq, j=j, pb_=pb_, first=(Tq == 0 and j == 0)):
                                    if first:
                                        prefetch_w(bi)
                                    pm = psm[pb_]
                                    for c in range(8):
                                        S.op("pe", lambda: T.matmul(pm[:], lhsT=wb[:, c, j * 128:(j + 1) * 128], rhs=hT[:, c, Tq * 512:(Tq + 1) * 512],
                                                                    start=(c == 0), stop=(c == 7)), r=[kw, "hT"], w=[f"psm{pb_}"])

                                def g1(gi=gi, Tq=Tq, j=j, pb_=pb_):
                                    pm = psm[pb_]
                                    gs = gst[pb_]
                                    S.op("act", lambda: A.activation(out=gs[:], in_=pm[:], func=AF.Sigmoid), r=[f"psm{pb_}"], w=[f"gst{pb_}"])
                                    S.dma("act", GT[gi * 4 + j, :, Tq * 512:(Tq + 1) * 512], gs[:], r=[f"gst{pb_}"], w=["GT"])
                                items.append([g0, g1, None])
                        continue
                    for t in range(NT):
                        pb_ = len(items) % 2

                        def s0(bi=bi, wb=wb, kw=kw, t=t, n=n, pb_=pb_):
                            if t == 0:
                                prefetch_w(bi)
                            pm = psm[pb_]
                            for c in range(8):
                                S.op("pe", lambda: T.matmul(pm[:, 0:n], lhsT=hT[:, c, t * 128:(t + 1) * 128], rhs=wb[:, c, 0:n],
                                                            start=(c == 0), stop=(c == 7)), r=[kw, "hT"], w=[f"psm{pb_}"])

                        def s1(nm=nm, t=t, pb_=pb_):
                            pm = psm[pb_]
                            kp = f"psm{pb_}"
                            S.lastw["psm"] = S.lastw[kp]
                            S.reads["psm"] = S.reads[kp]
                            ob = ob_2[t % 2]
                            kob = f"ob{t % 2}"
                            if nm in ("aq", "ak", "cq", "ck", "bq", "bk"):
                                pv = pm[:, 0:512].rearrange("p (h d) -> p h d", d=64)
                                obv = ob[:].rearrange("p (h d) -> p h d", d=64)
                                if nm[0] == "b":
                                    qk_post(pv, 8, None, False, False, t, obv, 0.125 if nm == "bq" else 1.0, kob)
                                else:
                                    gain = {"aq": aqn, "ak": akn, "cq": cqn, "ck": ckn}[nm][:, l * 64:(l + 1) * 64]
                                    qk_post(pv, 8, gain, True, True, t, obv, 1.0, kob)
                            elif nm in ("av", "bv", "cv"):
                                nh, dv = (4, 128) if nm == "av" else (8, 64)
                                vi = (t % 2) + (0 if nm == "av" else 2)
                                vs = vst[vi]
                                vv = vs[:, 0:nh * (dv + 1)].rearrange("p (h e) -> p h e", e=dv + 1)[:, :, 0:dv]
                                S.op("act", lambda: A.activation(out=vv, in_=pm[:, 0:512].rearrange("p (h e) -> p h e", e=dv), func=AF.Copy),
                                     r=[kp], w=[f"vst{vi}"])
                                S.dma("act", VX[nm[0]][t * 128:(t + 1) * 128, :], vs[:, 0:nh * (dv + 1)], r=[f"vst{vi}"], w=["VX"])
                            else:
                                obv = ob[:, 0:256].rearrange("p (h d) -> p h d", d=64)
                                qk_post(pm[:, 0:256].rearrange("p (h d) -> p h d", d=64), 4, None, False, True, t, obv, 1.0, kob)
                                obk = ob[:, 256:320].rearrange("p (h d) -> p h d", d=64)
                                qk_post(pm[:, 256:320].rearrange("p (h d) -> p h d", d=64), 1, ikn[:, l * 64:(l + 1) * 64], True, True, t, obk, 1.0, kob)
                                S.op("dve", lambda: V.tensor_copy(out=ob[:, 320:384], in_=ob[:, 256:320]), r=[kob], w=[kob])
                                S.op("act", lambda: A.activation(out=iw_all[:, t, :], in_=pm[:, 320:324], func=AF.Copy), r=[kp], w=["iw_all"])

                        def s2(nm=nm, t=t):
                            if nm in ("av", "bv", "cv"):
                                return
                            ob = ob_2[t % 2]
                            kob = f"ob{t % 2}"
                            sg = stg[(t // 4) % 2]
                            ksg = f"stg{(t // 4) % 2}"
                            tt = t % 4
                            pt = pst2[t % 2]
                            nj_ = 3 if nm == "idx" else 4
                            for j in range(nj_):
                                S.op("pe", lambda: T.transpose(out=pt[:, j, :], in_=ob[:, j * 128:(j + 1) * 128], identity=ident[:]),
                                     r=[kob], w=[f"p1pst{t % 2}"])
                            S.op("act", lambda: A.activation(out=sg[:, 0:nj_, tt * 128:(tt + 1) * 128], in_=pt[:, 0:nj_, :], func=AF.Copy),
                                 r=[f"p1pst{t % 2}"], w=[ksg])
                            if tt == 3:
                                t0_ = (t - 3) * 128
                                if nm == "idx":
                                    S.dma("act", IQT[:, :, t0_:t0_ + 512].rearrange("j p t -> p j t"), sg[:, 0:2, :], r=[ksg], w=["QKT"])
                                    S.dma("act", IKT[:, t0_:t0_ + 512], sg[:, 2, :], r=[ksg], w=["QKT"])
                                else:
                                    dst = (QT if nm[1] == "q" else KT)[nm[0]]
                                    S.dma("act", dst[:, :, t0_:t0_ + 512].rearrange("j p t -> p j t"), sg[:], r=[ksg], w=["QKT"])
                        items.append([s0, s1, s2])
                run_pipeline(items, 3)
                S.barrier()
                ph.close()

            for ph in ([ExitStack()] if 'pa' in PH else []):
                kt = sb(ph, "a_kt", [128, 4, SL], BF16)
                vx = sb(ph, "a_vx", [128, NT, 4 * 129], BF16)
                qt2 = [sb(ph, f"a_qt{i}", [128, 4, 512], BF16) for i in range(2)]
                pT2 = [sb(ph, f"a_pT{i}", [128, 512], BF16) for i in range(2)]
                o2 = [sb(ph, f"a_o{i}", [128, 4, 129], F32) for i in range(2)]
                rl = sb(ph, "a_rl", [128, 8], F32)
                y0 = sb(ph, "a_y0", [128, 4, 128], F32)
                y1 = sb(ph, "a_y1", [128, 4, 128], F32)
                ss = sb(ph, "a_ss", [128, 8], F32)
                yb2 = [sb(ph, f"a_yb{i}", [128, 4, 128], BF16) for i in range(2)]
                yts = [sb(ph, f"a_yts{i}", [128, 512], BF16) for i in range(2)]
                pss = [ps(ph, f"a_pss{i}", [128, 512], F32) for i in range(2)]
                po2 = [ps(ph, f"a_po{i}", [128, 2, 512], F32) for i in range(2)]
                ptr = ps(ph, "a_ptr", [128, 4, 128], BF16)
                S.dma("sp", kt[:], KT["a"].rearrange("j p t -> p j t"), r=["QKT"], w=["a_kt"])
                for v4 in range(4):
                    S.dma("sp", vx[:, v4 * 8:(v4 + 1) * 8, :], VX["a"][v4 * 1024:(v4 + 1) * 1024, :].rearrange("(t p) e -> p t e", p=128), r=["VX"], w=["a_vx"])

                def a_load_q(Q):
                    S.dma("sp", qt2[Q % 2][:], QT["a"][:, :, Q * 512:(Q + 1) * 512].rearrange("j p t -> p j t"), r=["QKT"], w=[f"a_qt{Q % 2}"])

                a_load_q(0)
                items = []
                deferred = []

                def a_epilogue(Q, h, c, po, kpo):
                    for bk in range(2):
                        S.op("act", lambda: A.activation(out=o2[c][:, 2 * bk:2 * bk + 2, :], in_=po[:, bk, 0:258].rearrange("p (r e) -> p r e", e=129), func=AF.Copy),
                             r=[kpo], w=[f"a_o{c}"])
                    if c == 0:
                        return
                    yb = yb2[h % 2]
                    kyb = f"a_yb{h % 2}"
                    S.op("dve", lambda: V.reciprocal(out=rl[:, 0:4], in_=o2[0][:, :, 128]), r=["a_o0"], w=["a_rl"])
                    S.op("dve", lambda: V.reciprocal(out=rl[:, 4:8], in_=o2[1][:, :, 128]), r=["a_o1"], w=["a_rl"])
                    S.op("dve", lambda: V.tensor_scalar(out=rl[:, 4:8], in0=rl[:, 4:8], scalar1=nlam_t[:, l:l + 1], scalar2=None, op0=ALU.mult),
                         r=["a_rl", "const"], w=["a_rl"])
                    S.op("dve", lambda: V.tensor_tensor(out=y0[:], in0=o2[0][:, :, 0:128], in1=rl[:, 0:4].unsqueeze(2).to_broadcast([128, 4, 128]), op=ALU.mult),
                         r=["a_o0", "a_rl"], w=["a_y0"])
                    S.op("pool", lambda: G.tensor_tensor(out=y1[:], in0=o2[1][:, :, 0:128], in1=rl[:, 4:8].unsqueeze(2).to_broadcast([128, 4, 128]), op=ALU.mult),
                         r=["a_o1", "a_rl"], w=["a_y1"])
                    S.op("dve", lambda: V.tensor_tensor(out=y0[:], in0=y0[:], in1=y1[:], op=ALU.add), r=["a_y0", "a_y1"], w=["a_y0"])
                    S.op("pool", lambda: G.tensor_tensor(out=y1[:], in0=y0[:], in1=y0[:], op=ALU.mult), r=["a_y0"], w=["a_y1"])
                    S.op("dve", lambda: V.tensor_reduce(out=ss[:, 0:4], in_=y1[:], axis=AX.X, op=ALU.add), r=["a_y1"], w=["a_ss"])
                    S.op("act", lambda: A.activation(out=ss[:, 0:4], in_=ss[:, 0:4], func=AF.Sqrt, bias=EPS, scale=1.0 / 128), r=["a_ss"], w=["a_ss"])
                    S.op("dve", lambda: V.reciprocal(out=ss[:, 0:4], in_=ss[:, 0:4]), r=["a_ss"], w=["a_ss"])
                    S.op("dve", lambda: V.tensor_tensor(out=y0[:], in0=y0[:], in1=ss[:, 0:4].unsqueeze(2).to_broadcast([128, 4, 128]), op=ALU.mult),
                         r=["a_y0", "a_ss"], w=["a_y0"])
                    S.op("dve", lambda: V.tensor_tensor(out=yb[:], in0=y0[:], in1=subg[:, l * 128:(l + 1) * 128].unsqueeze(1).to_broadcast([128, 4, 128]), op=ALU.mult),
                         r=["a_y0", "const"], w=[kyb])

                    def fin():
                        for qq in range(4):
                            S.op("pe", lambda: T.transpose(out=ptr[:, qq, :], in_=yb[:, qq, :], identity=ident[:]), r=[kyb], w=["a_ptr"])
                        ys = yts[h % 2]
                        S.op("act", lambda: A.activation(out=ys[:], in_=ptr[:].rearrange("p a b -> p (a b)"), func=AF.Copy), r=["a_ptr"], w=[f"a_yts{h % 2}"])
                        S.dma("act", YT[h, :, Q * 512:(Q + 1) * 512], ys[:], r=[f"a_yts{h % 2}"], w=["YT"])
                    deferred.append([4, fin])

                def run_deferred(force=False):
                    keep = []
                    for dly in deferred:
                        dly[0] -= 1
                        if dly[0] <= 0 or force:
                            dly[1]()
                        else:
                            keep.append(dly)
                    deferred[:] = keep

                gidx = 0
                for Q in range(NQ):
                    for h in range(4):
                        for c in range(2):
                            g = gidx
                            gidx += 1
                            nj = 4 * Q + 4
                            for j in range(nj):
                                b = len(items) % 2
                                jj = j - 4 * Q
                                c0 = max(0, jj) * 128

                                def s0(Q=Q, h=h, c=c, j=j, c0=c0, b=b, pre=(h == 0 and c == 0 and j == 0)):
                                    if pre and Q + 1 < NQ:
                                        a_load_q(Q + 1)
                                    qt = qt2[Q % 2]
                                    S.op("pe", lambda: T.matmul(pss[b][:, c0:512], lhsT=kt[c * 64:(c + 1) * 64, h, j * 128:(j + 1) * 128],
                                                                rhs=qt[c * 64:(c + 1) * 64, h, c0:512], start=True, stop=True),
                                         r=["a_kt", f"a_qt{Q % 2}"], w=[f"a_pss{b}"])

                                def s1(c0=c0, b=b, jj=jj):
                                    S.op("act", lambda: A.activation(out=pT2[b][:, c0:512], in_=pss[b][:, c0:512], func=AF.Exp),
                                         r=[f"a_pss{b}"], w=[f"a_pT{b}"])
                                    if jj >= 0:
                                        S.op("pool", lambda: G.memset(pT2[b][64:128, c0:c0 + 64], 0.0), w=[f"a_pT{b}"])

                                def s2(Q=Q, h=h, c=c, j=j, jj=jj, b=b, g=g, nj=nj):
                                    po = po2[g % 2]
                                    kpo = f"a_po{g % 2}"
                                    for qq in range(max(0, jj), 4):
                                        off = (qq % 2) * 129
                                        S.op("pe", lambda: T.matmul(po[:, qq // 2, off:off + 129], lhsT=pT2[b][:, qq * 128:(qq + 1) * 128],
                                                                    rhs=vx[:, j, h * 129:(h + 1) * 129], start=(j == 0 and qq % 2 == 0), stop=(j == 4 * Q + qq),
                                                                    skip_group_check=True),
                                             r=[f"a_pT{b}", "a_vx"], w=[kpo])
                                    if j == nj - 1:
                                        a_epilogue(Q, h, c, po, kpo)
                                    run_deferred()
                                items.append([s0, s1, s2])
                run_pipeline(items, 3)
                run_deferred(force=True)
                S.barrier()
                ph.close()

            for ph in ([ExitStack()] if 'pb' in PH else []):
                kt = sb(ph, "b_kt", [128, 4, SL], BF16)
                vx = sb(ph, "b_vx", [128, NT, 8 * 65], BF16)
                qt2 = [sb(ph, f"b_qt{i}", [128, 4, 512], BF16) for i in range(2)]
                e2 = [sb(ph, f"b_e{i}", [128, 512], F32) for i in range(2)]
                nl2 = [sb(ph, f"b_nl{i}", [128, 512], BF16) for i in range(2)]
                aT2 = [sb(ph, f"b_aT{i}", [128, 512], BF16) for i in range(2)]
                R = sb(ph, "b_R", [128, 512], BF16)
                qm2 = [sb(ph, f"b_qm{i}", [128, 512], BF16) for i in range(2)]
                yq2 = [sb(ph, f"b_yq{i}", [128, 4, 512], BF16) for i in range(2)]
                yts = [sb(ph, f"b_yts{i}", [128, 512], BF16) for i in range(2)]
                psz = [ps(ph, f"b_psz{i}", [128, 512], F32) for i in range(2)]
                psc = [ps(ph, f"b_psc{i}", [128, 512], F32) for i in range(2)]
                po2 = [ps(ph, f"b_po{i}", [128, 512], F32) for i in range(2)]
                ptr = ps(ph, "b_ptr", [128, 4, 128], BF16)
                S.dma("sp", kt[:], KT["b"].rearrange("j p t -> p j t"), r=["QKT"], w=["b_kt"])
                for v4 in range(4):
                    S.dma("sp", vx[:, v4 * 8:(v4 + 1) * 8, :], VX["b"][v4 * 1024:(v4 + 1) * 1024, :].rearrange("(t p) e -> p t e", p=128), r=["VX"], w=["b_vx"])

                def b_load_q(Q):
                    S.dma("sp", qt2[Q % 2][:], QT["b"][:, :, Q * 512:(Q + 1) * 512].rearrange("j p t -> p j t"), r=["QKT"], w=[f"b_qt{Q % 2}"])

                b_load_q(0)
                items = []
                gidx = 0
                for Q in range(NQ):
                    for h in range(8):
                        g = gidx
                        gidx += 1
                        pr, bs = h // 2, (h % 2) * 64
                        jtop = 4 * Q + 3
                        gstate = {"first": True}
                        for j in range(jtop, -1, -1):
                            b = len(items) % 2
                            jj = j - 4 * Q
                            c0 = max(0, jj) * 128

                            def s0(Q=Q, h=h, j=j, c0=c0, b=b, pr=pr, bs=bs, pre=(h == 0 and j == jtop)):
                                if pre and Q + 1 < NQ:
                                    b_load_q(Q + 1)
                                qt = qt2[Q % 2]
                                S.op("pe", lambda: T.matmul(psz[b][:, c0:512], lhsT=kt[bs:bs + 64, pr, j * 128:(j + 1) * 128], rhs=qt[bs:bs + 64, pr, c0:512],
                                                            start=True, stop=True), r=["b_kt", f"b_qt{Q % 2}"], w=[f"b_psz{b}"])

                            def s1(c0=c0, b=b, jj=jj):
                                S.op("act", lambda: A.activation(out=e2[b][:, c0:512], in_=psz[b][:, c0:512], func=AF.Exp),
                                     r=[f"b_psz{b}"], w=[f"b_e{b}"])
                                S.op("act", lambda: A.activation(out=nl2[b][:, c0:512], in_=e2[b][:, c0:512], func=AF.Ln, bias=1.0),
                                     r=[f"b_e{b}"], w=[f"b_nl{b}"])
                                if jj >= 0:
                                    S.op("dve", lambda: V.tensor_tensor(out=nl2[b][:, c0:c0 + 128], in0=nl2[b][:, c0:c0 + 128], in1=strict[:], op=ALU.mult),
                                         r=[f"b_nl{b}", "const"], w=[f"b_nl{b}"])

                            def s2(Q=Q, j=j, c0=c0, b=b, pr=pr, bs=bs, g=g, jtop=jtop):
                                qm = qm2[g % 2]
                                kqm = f"b_qm{g % 2}"
                                qt = qt2[Q % 2]
                                if j == jtop:
                                    S.op("pool", lambda: G.memset(R[:], 0.0), w=["b_R"])
                                    S.op("pool", lambda: G.memset(qm[:], 0.0), w=[kqm])
                                    S.op("dve", lambda: V.tensor_copy(out=qm[bs:bs + 64, :], in_=qt[bs:bs + 64, pr, :]), r=[f"b_qt{Q % 2}", kqm], w=[kqm])
                                S.op("pe", lambda: T.matmul(psc[b][:, c0:512], lhsT=kt[:, pr, j * 128:(j + 1) * 128], rhs=qm[:, c0:512], start=True, stop=False),
                                     r=["b_kt", kqm], w=[f"b_psc{b}"])
                                last = (j == jtop)
                                S.op("pe", lambda: T.matmul(psc[b][:, c0:512], lhsT=ntri[:], rhs=nl2[b][:, c0:512], start=False, stop=last),
                                     r=[f"b_nl{b}", "const"], w=[f"b_psc{b}"])
                                if not last:
                                    S.op("pe", lambda: T.matmul(psc[b][:, c0:512], lhsT=nones[:], rhs=R[:, c0:512], start=False, stop=True),
                                         r=["b_R", "const"], w=[f"b_psc{b}"])
                                if j > 0:
                                    S.op("dve", lambda: V.tensor_tensor(out=R[:, c0:512], in0=R[:, c0:512], in1=nl2[b][:, c0:512], op=ALU.add),
                                         r=["b_R", f"b_nl{b}"], w=["b_R"])

                            def s3(c0=c0, b=b, jj=jj):
                                S.op("act", lambda: A.activation(out=aT2[b][:, c0:512], in_=psc[b][:, c0:512], func=AF.Exp),
                                     r=[f"b_psc{b}"], w=[f"b_aT{b}"])
                                if jj >= 0:
                                    S.op("dve", lambda: V.tensor_tensor(out=aT2[b][:, c0:c0 + 128], in0=aT2[b][:, c0:c0 + 128], in1=strict[:], op=ALU.mult),
                                         r=[f"b_aT{b}", "const"], w=[f"b_aT{b}"])

                            def s4(Q=Q, h=h, j=j, jj=jj, b=b, g=g, gstate=gstate):
                                po = po2[g % 2]
                                kpo = f"b_po{g % 2}"
                                yq = yq2[Q % 2]
                                kyq = f"b_yq{Q % 2}"
                                for qq in range(max(0, jj), 4):
                                    S.op("pe", lambda: T.matmul(po[:, qq * 65:(qq + 1) * 65], lhsT=aT2[b][:, qq * 128:(qq + 1) * 128],
                                                                rhs=vx[:, j, h * 65:(h + 1) * 65], start=gstate["first"], stop=(j == 0),
                                                                skip_group_check=True),
                                         r=[f"b_aT{b}", "b_vx"], w=[kpo])
                                    gstate["first"] = False
                                if j == 0:
                                    S.op("act", lambda: A.activation(out=yq[:, :, h * 64:(h + 1) * 64],
                                                                     in_=po[:, 0:260].rearrange("p (a e) -> p a e", e=65)[:, :, 0:64], func=AF.Copy),
                                         r=[kpo], w=[kyq])
                                    if h == 7:
                                        for cch in range(4):
                                            for qq in range(4):
                                                S.op("pe", lambda: T.transpose(out=ptr[:, qq, :], in_=yq[:, qq, cch * 128:(cch + 1) * 128], identity=ident[:]),
                                                     r=[kyq], w=["b_ptr"])
                                            ys = yts[cch % 2]
                                            S.op("act", lambda: A.activation(out=ys[:], in_=ptr[:].rearrange("p a b -> p (a b)"), func=AF.Copy), r=["b_ptr"], w=[f"b_yts{cch % 2}"])
                                            S.dma("act", YT[4 + cch, :, Q * 512:(Q + 1) * 512], ys[:], r=[f"b_yts{cch % 2}"], w=["YT"])
                            items.append([s0, s1, s2, s3, s4])
                run_pipeline(items, 5)
                S.barrier()
                ph.close()

            for ph in ([ExitStack()] if 'pc' in PH else []):
                kt = sb(ph, "c_kt", [128, 4, SL], BF16)
                vx = sb(ph, "c_vx", [128, NT, 8 * 65], BF16)
                ikt = sb(ph, "c_ikt", [128, SL], BF16)
                qt2 = [sb(ph, f"c_qt{i}", [128, 4, 512], BF16) for i in range(2)]
                iqt2 = [sb(ph, f"c_iqt{i}", [128, 2, 512], BF16) for i in range(2)]
                score2 = [sb(ph, f"c_score{i}", [128, SL], F32) for i in range(2)]
                maskq2 = [sb(ph, f"c_maskq{i}", [128, SL], BF16) for i in range(2)]
                bst = [sb(ph, f"c_bst{i}", [128, 8], F32) for i in range(2)]
                dft = [sb(ph, f"c_dft{i}", [128, NSTEP + 1], F32) for i in range(2)]
                maskT = sb(ph, "c_maskT", [128, NT, 512], BF16)
                rb2 = [sb(ph, f"c_r{i}", [128, 512], F32) for i in range(2)]
                pT2 = [sb(ph, f"c_pT{i}", [128, 512], BF16) for i in range(2)]
                bs_ = sb(ph, "c_bis", [128, 8], F32)
                osb2 = [sb(ph, f"c_o{i}", [128, 4, 65], F32) for i in range(2)]
                rl2 = [sb(ph, f"c_rl{i}", [128, 4], F32) for i in range(2)]
                yq2 = [sb(ph, f"c_yq{i}", [128, 4, 512], BF16) for i in range(2)]
                yts = [sb(ph, f"c_yts{i}", [128, 512], BF16) for i in range(2)]
                psi = [ps(ph, f"c_psi{i}", [128, 512], F32) for i in range(2)]
                pss = [ps(ph, f"c_pss{i}", [128, 512], F32) for i in range(2)]
                po2 = [ps(ph, f"c_po{i}", [128, 512], F32) for i in range(2)]
                ptr = ps(ph, "c_ptr", [128, 4, 128], BF16)
                S.dma("sp", kt[:], KT["c"].rearrange("j p t -> p j t"), r=["QKT"], w=["c_kt"])
                for v4 in range(4):
                    S.dma("sp", vx[:, v4 * 8:(v4 + 1) * 8, :], VX["c"][v4 * 1024:(v4 + 1) * 1024, :].rearrange("(t p) e -> p t e", p=128), r=["VX"], w=["c_vx"])
                S.dma("sp", ikt[:], IKT[:, :], r=["QKT"], w=["c_ikt"])

                def c_load_q(Q):
                    kq_ = f"c_qt{Q % 2}"
                    S.dma("sp", qt2[Q % 2][:], QT["c"][:, :, Q * 512:(Q + 1) * 512].rearrange("j p t -> p j t"), r=["QKT"], w=[kq_])
                    S.dma("sp", iqt2[Q % 2][:], IQT[:, :, Q * 512:(Q + 1) * 512].rearrange("j p t -> p j t"), r=["QKT"], w=[kq_])

                c_load_q(0)
                ir = 0
                gidx = 0
                for Q in range(NQ):
                    qt = qt2[Q % 2]
                    iqt = iqt2[Q % 2]
                    kq = f"c_qt{Q % 2}"
                    if Q + 1 < NQ:
                        c_load_q(Q + 1)
                    for pair in range(2):
                        info = []
                        for ci in range(2):
                            qq = 2 * pair + ci
                            qb = 4 * Q + qq
                            nadm = (qb + 1) * 128
                            info.append((qq, qb, nadm))
                            sc_t = score2[ci]
                            ksc = f"c_score{ci}"
                            nkt = (nadm + 511) // 512
                            for k5 in range(nkt):
                                wd = min(512, nadm - k5 * 512)
                                for h in range(4):
                                    b = ir % 2
                                    ir += 1
                                    hb_ = (h % 2) * 64
                                    S.op("pe", lambda: T.matmul(psi[b][:, 0:wd], lhsT=iqt[hb_:hb_ + 64, h // 2, qq * 128:(qq + 1) * 128],
                                                                rhs=ikt[hb_:hb_ + 64, k5 * 512:k5 * 512 + wd], start=True, stop=True),
                                         r=[kq, "c_ikt"], w=[f"c_psi{b}"])
                                    S.op("act", lambda: A.activation(out=rb2[b][:, 0:wd], in_=psi[b][:, 0:wd], func=AF.Relu),
                                         r=[f"c_psi{b}"], w=[f"c_r{b}"])
                                    sc = sc_t[:, k5 * 512:k5 * 512 + wd]
                                    if h == 0:
                                        S.op("dve", lambda: V.tensor_scalar(out=sc, in0=rb2[b][:, 0:wd], scalar1=iw_all[:, qb, 0:1], scalar2=None, op0=ALU.mult),
                                             r=[f"c_r{b}", "iw_all"], w=[ksc])
                                    else:
                                        S.op("dve", lambda: V.scalar_tensor_tensor(out=sc, in0=rb2[b][:, 0:wd], scalar=iw_all[:, qb, h:h + 1], in1=sc,
                                                                                   op0=ALU.mult, op1=ALU.add),
                                             r=[f"c_r{b}", "iw_all", ksc], w=[ksc])
                            S.op("dve", lambda: V.memset(sc_t[0:64, qb * 128 + 64:(qb + 1) * 128], NEG), r=[ksc], w=[ksc])
                        bisect = info[0][1] >= 2
                        for ci in range(2):
                            qq, qb, nadm = info[ci]
                            sc_t = score2[ci]
                            ksc = f"c_score{ci}"
                            stt = bst[ci]
                            df = dft[ci]
                            KB = [f"c_bis{ci}"]
                            if not bisect:
                                S.op("dve", lambda: V.memset(stt[:, 5:6], -1.0e29), r=KB, w=KB)
                                continue
                            nfull = qb * 128 + 64
                            S.op("dve", lambda: V.tensor_reduce(out=stt[:, 4:5], in_=sc_t[:, 0:nadm], axis=AX.X, op=ALU.max), r=[ksc], w=KB)
                            S.op("dve", lambda: V.tensor_reduce(out=stt[:, 5:6], in_=sc_t[:, 0:nfull], axis=AX.X, op=ALU.min), r=[ksc], w=KB)
                            S.op("dve", lambda: V.tensor_scalar(out=stt[:, 5:6], in0=stt[:, 5:6], scalar1=-1.0, scalar2=None, op0=ALU.add), r=KB, w=KB)
                            S.op("dve", lambda: V.tensor_tensor(out=stt[:, 1:2], in0=stt[:, 4:5], in1=stt[:, 5:6], op=ALU.subtract), r=KB, w=KB)
                            S.op("dve", lambda: V.tensor_scalar(out=df[:], in0=pw2[:], scalar1=stt[:, 1:2], scalar2=None, op0=ALU.mult), r=KB + ["const"], w=KB)
                            if ci == 0:
                                S.op("dve", lambda: V.tensor_tensor(out=stt[:, 0:1], in0=stt[:, 5:6], in1=df[:, 0:1], op=ALU.add), r=KB, w=KB)
                            else:
                                S.op("dve", lambda: V.scalar_tensor_tensor(out=stt[:, 0:1], in0=stt[:, 5:6], scalar=-1.0, in1=df[:, 0:1],
                                                                           op0=ALU.mult, op1=ALU.subtract), r=KB, w=KB)
                        if bisect:
                            for i in range(NSTEP):
                                lastst = (i == NSTEP - 1)
                                qq, qb, nadm = info[0]
                                stt, df, KB, ksc = bst[0], dft[0], ["c_bis0"], "c_score0"
                                S.op("dve", lambda: V.tensor_scalar(out=maskq2[0][:, 0:nadm], in0=score2[0][:, 0:nadm], scalar1=stt[:, 0:1], scalar2=None,
                                                                    op0=ALU.is_gt, op1=ALU.add, accum_out=stt[:, 2:3]),
                                     r=KB + [ksc, "c_maskq0"], w=KB + ["c_maskq0"])
                                S.op("dve", lambda: V.tensor_scalar(out=stt[:, 3:4], in0=stt[:, 2:3], scalar1=float(TOPK), scalar2=df[:, i:i + 1],
                                                                    op0=ALU.is_ge, op1=ALU.mult), r=KB, w=KB)
                                if not lastst:
                                    S.op("dve", lambda: V.scalar_tensor_tensor(out=stt[:, 0:1], in0=stt[:, 3:4], scalar=df[:, i + 1:i + 2], in1=stt[:, 0:1],
                                                                               op0=ALU.subtract, op1=ALU.add), r=KB, w=KB)
                                else:
                                    S.op("dve", lambda: V.scalar_tensor_tensor(out=stt[:, 5:6], in0=stt[:, 3:4], scalar=df[:, i:i + 1], in1=stt[:, 0:1],
                                                                               op0=ALU.subtract, op1=ALU.add), r=KB, w=KB)
                                qq, qb, nadm = info[1]
                                stt, df, KB, ksc = bst[1], dft[1], ["c_bis1"], "c_score1"
                                S.op("act", lambda: A.activation(out=maskq2[1][:, 0:nadm], in_=score2[1][:, 0:nadm], func=AF.Sign, bias=stt[:, 0:1],
                                                                 accum_out=stt[:, 2:3]),
                                     r=KB + [ksc, "c_maskq1"], w=KB + ["c_maskq1"])
                                S.op("pool", lambda: G.tensor_scalar(out=stt[:, 3:4], in0=stt[:, 2:3], scalar1=float(2 * TOPK - nadm), scalar2=df[:, i:i + 1],
                                                                     op0=ALU.is_ge, op1=ALU.mult), r=KB, w=KB)
                                S.op("pool", lambda: G.tensor_tensor(out=stt[:, 0:1], in0=stt[:, 0:1], in1=stt[:, 3:4], op=ALU.subtract), r=KB, w=KB)
                                if not lastst:
                                    S.op("pool", lambda: G.tensor_tensor(out=stt[:, 0:1], in0=stt[:, 0:1], in1=df[:, i + 1:i + 2], op=ALU.add), r=KB, w=KB)
                                else:
                                    S.op("pool", lambda: G.tensor_tensor(out=stt[:, 0:1], in0=stt[:, 0:1], in1=df[:, i:i + 1], op=ALU.add), r=KB, w=KB)
                                    S.op("pool", lambda: G.tensor_scalar(out=stt[:, 5:6], in0=stt[:, 0:1], scalar1=-1.0, scalar2=None, op0=ALU.mult), r=KB, w=KB)
                        for ci in range(2):
                            qq, qb, nadm = info[ci]
                            mq = maskq2[ci]
                            kmq = f"c_maskq{ci}"
                            S.op("dve", lambda: V.tensor_scalar(out=mq[:, 0:nadm], in0=score2[ci][:, 0:nadm], scalar1=bst[ci][:, 5:6], scalar2=None, op0=ALU.is_gt),
                                 r=[f"c_bis{ci}", f"c_score{ci}", kmq], w=[kmq])
                            for k1 in range(0, qb + 1, 4):
                                nk = min(4, qb + 1 - k1)
                                for u in range(nk):
                                    S.op("pe", lambda: T.transpose(out=ptr[:, u, :], in_=mq[:, (k1 + u) * 128:(k1 + u + 1) * 128], identity=ident[:]),
                                         r=[kmq], w=["c_ptr"])
                                S.op("act", lambda: A.activation(out=maskT[:, k1:k1 + nk, qq * 128:(qq + 1) * 128], in_=ptr[:, 0:nk, :], func=AF.Copy),
                                     r=["c_ptr"], w=["c_maskT"])
                    items = []
                    yq = yq2[Q % 2]
                    kyq = f"c_yq{Q % 2}"
                    for h in range(8):
                        g = gidx
                        gidx += 1
                        pr, bs = h // 2, (h % 2) * 64
                        nj = 4 * Q + 4
                        gstate = {"first": True}
                        for j in range(nj):
                            b = len(items) % 2
                            jj = j - 4 * Q
                            c0 = max(0, jj) * 128

                            def s0(j=j, c0=c0, b=b, pr=pr, bs=bs):
                                S.op("pe", lambda: T.matmul(pss[b][:, c0:512], lhsT=kt[bs:bs + 64, pr, j * 128:(j + 1) * 128],
                                                            rhs=qt[bs:bs + 64, pr, c0:512], start=True, stop=True),
                                     r=["c_kt", kq], w=[f"c_pss{b}"])

                            def s1(j=j, c0=c0, b=b):
                                S.op("act", lambda: A.activation(out=pT2[b][:, c0:512], in_=pss[b][:, c0:512], func=AF.Exp),
                                     r=[f"c_pss{b}"], w=[f"c_pT{b}"])
                                if b == 0:
                                    S.op("dve", lambda: V.tensor_tensor(out=pT2[b][:, c0:512], in0=pT2[b][:, c0:512], in1=maskT[:, j, c0:512], op=ALU.mult),
                                         r=[f"c_pT{b}", "c_maskT"], w=[f"c_pT{b}"])
                                else:
                                    S.op("pool", lambda: G.tensor_tensor(out=pT2[b][:, c0:512], in0=pT2[b][:, c0:512], in1=maskT[:, j, c0:512], op=ALU.mult),
                                         r=[f"c_pT{b}", "c_maskT"], w=[f"c_pT{b}"])

                            def s2(h=h, j=j, jj=jj, b=b, g=g, gstate=gstate, nj=nj):
                                po = po2[g % 2]
                                kpo = f"c_po{g % 2}"
                                osb = osb2[g % 2]
                                rl = rl2[g % 2]
                                for q4 in range(max(0, jj), 4):
                                    S.op("pe", lambda: T.matmul(po[:, q4 * 65:(q4 + 1) * 65], lhsT=pT2[b][:, q4 * 128:(q4 + 1) * 128],
                                                                rhs=vx[:, j, h * 65:(h + 1) * 65], start=gstate["first"], stop=(j == nj - 1),
                                                                skip_group_check=True),
                                         r=[f"c_pT{b}", "c_vx"], w=[kpo])
                                    gstate["first"] = False
                                if j == nj - 1:
                                    S.op("act", lambda: A.activation(out=osb[:], in_=po[:, 0:260].rearrange("p (a e) -> p a e", e=65), func=AF.Copy),
                                         r=[kpo], w=[f"c_o{g % 2}"])
                                    S.op("dve", lambda: V.reciprocal(out=rl[:], in_=osb[:, :, 64]), r=[f"c_o{g % 2}"], w=[f"c_rl{g % 2}"])
                                    S.op("dve", lambda: V.tensor_tensor(out=yq[:, :, h * 64:(h + 1) * 64], in0=osb[:, :, 0:64],
                                                                        in1=rl[:].unsqueeze(2).to_broadcast([128, 4, 64]), op=ALU.mult),
                                         r=[f"c_o{g % 2}", f"c_rl{g % 2}"], w=[kyq])
                            items.append([s0, s1, s2])
                    run_pipeline(items, 3)
                    for cch in range(4):
                        for q4 in range(4):
                            S.op("pe", lambda: T.transpose(out=ptr[:, q4, :], in_=yq[:, q4, cch * 128:(cch + 1) * 128], identity=ident[:]),
                                 r=[kyq], w=["c_ptr"])
                        ys = yts[cch % 2]
                        S.op("act", lambda: A.activation(out=ys[:], in_=ptr[:].rearrange("p a b -> p (a b)"), func=AF.Copy), r=["c_ptr"], w=[f"c_yts{cch % 2}"])
                        S.dma("act", YT[8 + cch, :, Q * 512:(Q + 1) * 512], ys[:], r=[f"c_yts{cch % 2}"], w=["YT"])
                S.barrier()
                ph.close()

            for ph in ([ExitStack()] if 'pm' in PH else []):
                wbr = [sb(ph, f"m_wbr{i}", [128, 4, D], BF16) for i in range(3)]
                wo = sb(ph, "m_wo", [128, 8, D], BF16)
                yt2 = [sb(ph, f"m_yt{i}", [128, 12, 512], BF16) for i in range(2)]
                gt2 = [sb(ph, f"m_gt{i}", [128, 24, 512], BF16) for i in range(2)]
                m0 = sb(ph, "m_m0", [128, 512], F32)
                m1 = sb(ph, "m_m1", [128, 512], F32)
                m2 = sb(ph, "m_m2", [128, 512], F32)
                mT = sb(ph, "m_mT", [128, 8, 512], BF16)
                xb2 = [sb(ph, f"m_x{i}", [128, D], F32) for i in range(2)]
                ps3 = [ps(ph, f"m_ps{i}", [128, 512], F32) for i in range(3)]
                pso = [ps(ph, f"m_pso{i}", [128, 512], F32) for i in range(2)]
                for i in range(3):
                    load_w_bf16(wbr[i][:], w_br[i][l].rearrange("(c p) n -> p c n", p=128), "m_w")
                load_w_bf16(wo[:], w_out[l].rearrange("(c p) n -> p c n", p=128), "m_w")
                io = 0
                for Tq in range(NQ):
                    yt = yt2[Tq % 2]
                    gt = gt2[Tq % 2]
                    kin = f"m_in{Tq % 2}"
                    for b3 in range(3):
                        S.dma("sp", yt[:, b3 * 4:(b3 + 1) * 4, :], YT[b3 * 4:(b3 + 1) * 4, :, Tq * 512:(Tq + 1) * 512].rearrange("j p t -> p j t"), r=["YT"], w=[kin])
                        S.dma("sp", gt[:, b3 * 8:(b3 + 1) * 8, :], GT[b3 * 8:(b3 + 1) * 8, :, Tq * 512:(Tq + 1) * 512].rearrange("j p t -> p j t"), r=["GT"], w=[kin])
                    for fc in range(8):
                        for br in range(3):
                            for kc in range(4):
                                S.op("pe", lambda: T.matmul(ps3[br][:], lhsT=wbr[br][:, kc, fc * 128:(fc + 1) * 128], rhs=yt[:, br * 4 + kc, :],
                                                            start=(kc == 0), stop=(kc == 3)), r=["m_w", kin], w=[f"m_ps{br}"])
                        S.op("dve", lambda: V.tensor_tensor(out=m0[:], in0=ps3[0][:], in1=gt[:, fc, :], op=ALU.mult), r=["m_ps0", kin], w=["m_m0"])
                        S.op("dve", lambda: V.tensor_tensor(out=m1[:], in0=ps3[1][:], in1=gt[:, 8 + fc, :], op=ALU.mult), r=["m_ps1", kin], w=["m_m1"])
                        S.op("dve", lambda: V.tensor_tensor(out=m2[:], in0=ps3[2][:], in1=gt[:, 16 + fc, :], op=ALU.mult), r=["m_ps2", kin], w=["m_m2"])
                        S.op("pool", lambda: G.tensor_tensor(out=m0[:], in0=m0[:], in1=m1[:], op=ALU.add), r=["m_m0", "m_m1"], w=["m_m0"])
                        S.op("pool", lambda: G.tensor_tensor(out=mT[:, fc, :], in0=m0[:], in1=m2[:], op=ALU.add), r=["m_m0", "m_m2"], w=["m_mT"])
                    for tt in range(4):
                        t = Tq * 4 + tt
                        xt = xb2[t % 2]
                        kx = f"m_x{t % 2}"
                        S.dma("sp", xt[:], xsrc[t * 128:(t + 1) * 128, :], r=[f"xr{t}"], w=[kx])
                        for nb in range(2):
                            pq = pso[io % 2]
                            kpq = f"m_pso{io % 2}"
                            io += 1
                            for fc in range(8):
                                S.op("pe", lambda: T.matmul(pq[:], lhsT=mT[:, fc, tt * 128:(tt + 1) * 128], rhs=wo[:, fc, nb * 512:(nb + 1) * 512],
                                                            start=(fc == 0), stop=(fc == 7)), r=["m_mT", "m_w"], w=[kpq])
                            S.op("dve", lambda: V.tensor_tensor(out=xt[:, nb * 512:(nb + 1) * 512], in0=pq[:], in1=xt[:, nb * 512:(nb + 1) * 512], op=ALU.add),
                                 r=[kpq, kx], w=[kx])
                        S.dma("sp", xres[t * 128:(t + 1) * 128, :], xt[:], r=[kx], w=[f"xr{t}"])
                S.barrier()
                ph.close()

            for ph in ([ExitStack()] if 'pf' in PH else []):
                wu = sb(ph, "f_wu", [128, 8, 4 * D], BF16)
                wd = sb(ph, "f_wd", [128, 32, D], BF16)
                xb2 = [sb(ph, f"f_x{i}", [128, D], F32) for i in range(2)]
                hb = sb(ph, "f_hb", [128, D], BF16)
                junk = sb(ph, "f_junk", [128, D], BF16)
                ssq = sb(ph, "f_ssq", [128, 8], F32)
                rstd = sb(ph, "f_rstd", [128, 8], F32)
                hmT = sb(ph, "f_hmT", [128, 8, 256], BF16)
                rr2 = [sb(ph, f"f_r{i}", [128, 256], F32) for i in range(2)]
                uT = sb(ph, "f_uT", [128, 32, 256], BF16)
                pst2 = [ps(ph, f"f_pst{i}", [128, 4, 128], BF16) for i in range(2)]
                psu = [ps(ph, f"f_psu{i}", [128, 256], F32) for i in range(2)]
                psd = [ps(ph, f"f_psd{i}", [128, 512], F32) for i in range(2)]
                for c in range(8):
                    load_w_bf16(wu[:, c, :], w_up[l, c * 128:(c + 1) * 128, :], "f_w")
                for c4 in range(4):
                    load_w_bf16(wd[:, c4 * 8:(c4 + 1) * 8, :], w_down[l, c4 * 1024:(c4 + 1) * 1024, :].rearrange("(c p) n -> p c n", p=128), "f_w")
                io = 0
                for tg in range(NT // 2):
                    for t2 in range(2):
                        t = tg * 2 + t2
                        xt = xb2[t2]
                        kx = f"f_x{t2}"
                        S.dma("sp", xt[:], xres[t * 128:(t + 1) * 128, :], r=[f"xr{t}"], w=[kx])
                        norm_transpose(xt[:], kx, gM, l, hb, pst2, lambda c, t2=t2: hmT[:, c, t2 * 128:(t2 + 1) * 128], "f_hmT", ssq, rstd, junk, "f")
                    for fc in range(32):
                        pu = psu[fc % 2]
                        for c in range(8):
                            S.op("pe", lambda: T.matmul(pu[:], lhsT=wu[:, c, fc * 128:(fc + 1) * 128], rhs=hmT[:, c, :], start=(c == 0), stop=(c == 7)),
                                 r=["f_w", "f_hmT"], w=[f"f_psu{fc % 2}"])
                        rr = rr2[fc % 2]
                        S.op("act", lambda: A.activation(out=rr[:], in_=pu[:], func=AF.Relu), r=[f"f_psu{fc % 2}"], w=[f"f_r{fc % 2}"])
                        if fc % 2 == 0:
                            S.op("pool", lambda: G.tensor_tensor(out=uT[:, fc, :], in0=rr[:], in1=rr[:], op=ALU.mult), r=[f"f_r{fc % 2}"], w=["f_uT"])
                        else:
                            S.op("dve", lambda: V.tensor_tensor(out=uT[:, fc, :], in0=rr[:], in1=rr[:], op=ALU.mult), r=[f"f_r{fc % 2}"], w=["f_uT"])
                    for t2 in range(2):
                        t = tg * 2 + t2
                        xt = xb2[t2]
                        kx = f"f_x{t2}"
                        for nb in range(2):
                            pq = psd[io % 2]
                            kpq = f"f_psd{io % 2}"
                            io += 1
                            for fc in range(32):
                                S.op("pe", lambda: T.matmul(pq[:], lhsT=uT[:, fc, t2 * 128:(t2 + 1) * 128], rhs=wd[:, fc, nb * 512:(nb + 1) * 512],
                                                            start=(fc == 0), stop=(fc == 31)), r=["f_uT", "f_w"], w=[kpq])
                            S.op("dve", lambda: V.tensor_tensor(out=xt[:, nb * 512:(nb + 1) * 512], in0=pq[:], in1=xt[:, nb * 512:(nb + 1) * 512], op=ALU.add),
                                 r=[kpq, kx], w=[kx])
                        S.dma("sp", xres[t * 128:(t + 1) * 128, :], xt[:], r=[kx], w=[f"xr{t}"])
                S.barrier()
                ph.close()

            for ph in ([ExitStack()] if 'pp' in PH else []):
                wg = sb(ph, "e_wg", [128, 8, D], BF16)
                wp = sb(ph, "e_wp", [128, 2, D], BF16)
                xb2 = [sb(ph, f"e_x{i}", [128, D], F32) for i in range(2)]
                pb2 = [sb(ph, f"e_p{i}", [128, 256], F32) for i in range(2)]
                pbb = sb(ph, "e_pbb", [128, 256], BF16)
                hb = sb(ph, "e_hb", [128, D], BF16)
                junk = sb(ph, "e_junk", [128, D], BF16)
                ssq = sb(ph, "e_ssq", [128, 8], F32)
                rstd = sb(ph, "e_rstd", [128, 8], F32)
                hpT = sb(ph, "e_hpT", [128, 8, 128], BF16)
                pT = sb(ph, "e_pT", [128, 2, 128], BF16)
                sg = sb(ph, "e_sg", [128, 512], F32)
                tm = sb(ph, "e_tm", [128, 512], F32)
                pst2 = [ps(ph, f"e_pst{i}", [128, 4, 128], BF16) for i in range(2)]
                ptp = ps(ph, "e_ptp", [128, 2, 128], BF16)
                psg = [ps(ph, f"e_psg{i}", [128, 512], F32) for i in range(2)]
                psp = [ps(ph, f"e_psp{i}", [128, 512], F32) for i in range(2)]
                load_w_bf16(wg[:], w_ple_gate[l].rearrange("(c p) n -> p c n", p=128), "e_w")
                load_w_bf16(wp[:], w_ple_proj[l].rearrange("(c p) n -> p c n", p=128), "e_w")
                dst = out_d if l == depth - 1 else xres
                kdst = "out"
                io = 0
                for t in range(NT):
                    xt = xb2[t % 2]
                    kx = f"e_x{t % 2}"
                    pb = pb2[t % 2]
                    S.dma("sp", xt[:], xres[t * 128:(t + 1) * 128, :], r=[f"xr{t}"], w=[kx])
                    S.dma("sp", pb[:], p_in[l, t * 128:(t + 1) * 128, :], w=[f"e_p{t % 2}"])
                    norm_transpose(xt[:], kx, gP, l, hb, pst2, lambda c: hpT[:, c, :], "e_hpT", ssq, rstd, junk, "e")
                    S.op("pool", lambda: G.tensor_copy(out=pbb[:], in_=pb[:]), r=[f"e_p{t % 2}"], w=["e_pbb"])
                    for c in range(2):
                        S.op("pe", lambda: T.transpose(out=ptp[:, c, :], in_=pbb[:, c * 128:(c + 1) * 128], identity=ident[:]), r=["e_pbb"], w=["e_ptp"])
                    S.op("act", lambda: A.activation(out=pT[:], in_=ptp[:], func=AF.Copy), r=["e_ptp"], w=["e_pT"])
                    for nb in range(2):
                        b = io % 2
                        io += 1
                        for c in range(8):
                            S.op("pe", lambda: T.matmul(psg[b][:], lhsT=hpT[:, c, :], rhs=wg[:, c, nb * 512:(nb + 1) * 512], start=(c == 0), stop=(c == 7)),
                                 r=["e_hpT", "e_w"], w=[f"e_psg{b}"])
                        for c in range(2):
                            S.op("pe", lambda: T.matmul(psp[b][:], lhsT=pT[:, c, :], rhs=wp[:, c, nb * 512:(nb + 1) * 512], start=(c == 0), stop=(c == 1)),
                                 r=["e_pT", "e_w"], w=[f"e_psp{b}"])
                        S.op("act", lambda: A.activation(out=sg[:], in_=psg[b][:], func=AF.Sigmoid), r=[f"e_psg{b}"], w=["e_sg"])
                        S.op("dve", lambda: V.tensor_tensor(out=tm[:], in0=psp[b][:], in1=sg[:], op=ALU.mult), r=[f"e_psp{b}", "e_sg"], w=["e_tm"])
                        S.op("pool", lambda: G.tensor_tensor(out=xt[:, nb * 512:(nb + 1) * 512], in0=xt[:, nb * 512:(nb + 1) * 512], in1=tm[:], op=ALU.add),
                             r=["e_tm", kx], w=[kx])
                    S.dma("sp", dst[t * 128:(t + 1) * 128, :], xt[:], r=[kx], w=[kdst if l == depth - 1 else f"xr{t}"])
                S.barrier()
                ph.close()
        S.barrier()
        build_nc.ninstr = dict(S.ninstr)
    return nc


def _host_layout(inputs, b):
    f32 = np.float32
    m = {}
    m["x"] = np.ascontiguousarray(inputs["x"][b], dtype=f32)
    m["p"] = np.ascontiguousarray(inputs["p"][:, b], dtype=f32)
    pos = np.asarray(inputs["positions"][b]).astype(np.int32)
    m["pos"] = np.ascontiguousarray(pos.reshape(NT, 128).T)
    m["invf"] = (10000.0 ** (-np.arange(0, 64, 2, dtype=np.float32) / 64)).astype(f32)
    for k in ("attn_norm", "mlp_norm", "ple_norm"):
        a = np.asarray(inputs[k], dtype=f32)
        m[k + "_l"] = np.ascontiguousarray(a.reshape(DEPTH, 8, 128).transpose(2, 0, 1).reshape(128, DEPTH * 8))
    for k in ("a_q_norm", "a_k_norm", "a_lambda", "a_subln", "c_q_norm", "c_k_norm", "idx_k_norm"):
        m[k] = np.ascontiguousarray(np.asarray(inputs[k], dtype=f32).reshape(-1))
    for k in ("w_in", "w_br_a", "w_br_b", "w_br_c", "w_out", "w_up", "w_down", "w_ple_gate", "w_ple_proj"):
        m[k] = np.ascontiguousarray(inputs[k], dtype=f32)
    return m


def kernel(**inputs):
    nc = build_nc()
    in_maps = [_host_layout(inputs, c % 4) for c in range(4)]
    in_maps = in_maps + in_maps
    res = run_bass_kernel_spmd(nc, in_maps, core_ids=list(range(8)))
    out = np.stack([np.asarray(res.results[b]["out"], dtype=np.float32) for b in range(4)], axis=0)
    return out
```

```python
import math
from contextlib import ExitStack

import numpy as np
import concourse.bass as bass
import concourse.mybir as mybir
from concourse.bass_utils import run_bass_kernel_spmd

F32 = mybir.dt.float32
BF16 = mybir.dt.bfloat16
I32 = mybir.dt.int32
AF = mybir.ActivationFunctionType
ALU = mybir.AluOpType
AX = mybir.AxisListType

DEPTH = 4
SL = 4096
D = 1024
NT = 32
NQ = 8
EPS = 1e-6
W_IN_COLS = 8004
TOPK = 256
NSTEP = 20
NEG = -1.0e30

SEM_EPOCH = 30000
N_DMA_SEMS = 24


class Sched:
    def __init__(self, nc, stack):
        self.nc = nc
        self.stack = stack
        self.engs = {"pe": nc.tensor, "act": nc.scalar, "dve": nc.vector,
                     "pool": nc.gpsimd, "sp": nc.sync}
        self.esem = {}
        self.ecount = {}
        self.n_sems = 0
        for e in self.engs:
            self._new_esem(e)
        self.waited = {e: {} for e in self.engs}
        self.dsems = [[self._sem(f"dma{i}"), 0] for i in range(N_DMA_SEMS)]
        self.dnext = 0
        self.lastw = {}
        self.reads = {}
        self.ninstr = {e: 0 for e in self.engs}

    def _sem(self, name):
        self.n_sems += 1
        return self.stack.enter_context(self.nc.semaphore(name))

    def _new_esem(self, e):
        self.esem[e] = self._sem(f"e_{e}_{self.n_sems}")
        self.ecount[e] = 0

    def _wait(self, e, dep):
        sem, val, tag = dep
        w = self.waited[e]
        k = id(sem)
        if w.get(k, 0) >= val:
            return
        w[k] = val
        self.engs[e].wait_ge(sem, val)
        self.ninstr[e] += 1

    def _collect(self, r, w):
        deps = []
        for k in list(r) + list(w):
            deps.extend(self.lastw.get(k, ()))
        for k in w:
            rd = self.reads.get(k)
            if rd:
                deps.extend(rd.values())
        return deps

    def _commit(self, r, w, dep, merge=()):
        for k in w:
            if k in merge:
                self.lastw[k].append(dep)
            else:
                self.lastw[k] = [dep]
                self.reads[k] = {}
        for k in r:
            if k in w:
                continue
            self.reads.setdefault(k, {})[dep[2]] = dep

    def op(self, e, fn, r=(), w=()):
        if self.ecount[e] >= SEM_EPOCH:
            self._new_esem(e)
        for d in self._collect(r, w):
            if e == "pe" and d[2] == "pe":
                continue
            self._wait(e, d)
        ins = fn()
        self.ecount[e] += 1
        ins.then_inc(self.esem[e], 1)
        dep = (self.esem[e], self.ecount[e], e)
        self.ninstr[e] += 1
        self._commit(r, w, dep)
        return ins

    def dma(self, q, out, in_, r=(), w=()):
        slot = self.dsems[self.dnext]
        self.dnext = (self.dnext + 1) % len(self.dsems)
        sem = slot[0]
        if slot[1] > 0:
            self._wait(q, (sem, slot[1], "dma"))
        deps = []
        merge = set()
        for k in r:
            deps.extend(self.lastw.get(k, ()))
        for k in w:
            lw = self.lastw.get(k, ())
            rd = self.reads.get(k) or {}
            if lw and not rd and all(d[2].startswith("dma") for d in lw):
                merge.add(k)
            else:
                deps.extend(lw)
                deps.extend(rd.values())
        for d in deps:
            self._wait(q, d)
        ins = self.engs[q].dma_start(out=out, in_=in_)
        slot[1] += 16
        ins.then_inc(sem, 16)
        dep = (sem, slot[1], f"dma{id(sem)}")
        self.ninstr[q] += 1
        self._commit(r, w, dep, merge)
        return ins

    def barrier(self):
        for e in self.engs:
            for o in self.engs:
                if o != e and self.ecount[o] > 0:
                    self._wait(e, (self.esem[o], self.ecount[o], o))
            for sem, val in self.dsems:
                if val:
                    self._wait(e, (sem, val, "dma"))


def run_pipeline(items, nstage):
    n = len(items)
    for step in range(n + nstage - 1):
        for k in range(nstage - 1, -1, -1):
            i = step - k
            if 0 <= i < n and items[i][k] is not None:
                items[i][k]()


def build_nc(depth=DEPTH, debug=False, phases=None):
    PH = set(phases) if phases is not None else {'p1', 'pa', 'pb', 'pc', 'pm', 'pf', 'pp'}
    nc = bass.Bass("TRN2", target_bir_lowering=False)

    def din(name, shape, dt=F32):
        return nc.dram_tensor(name, list(shape), dt, kind="ExternalInput").ap()

    def dscr(name, shape, dt):
        kind = "ExternalOutput" if debug else "Internal"
        return nc.dram_tensor(name, list(shape), dt, kind=kind).ap()

    x_in = din("x", [SL, D])
    p_in = din("p", [DEPTH, SL, 256])
    pos_in = din("pos", [128, NT], I32)
    invf_in = din("invf", [32])
    attn_norm = din("attn_norm_l", [128, DEPTH * 8])
    mlp_norm = din("mlp_norm_l", [128, DEPTH * 8])
    ple_norm = din("ple_norm_l", [128, DEPTH * 8])
    w_in = din("w_in", [DEPTH, D, W_IN_COLS])
    a_q_norm = din("a_q_norm", [DEPTH * 64])
    a_k_norm = din("a_k_norm", [DEPTH * 64])
    a_lambda = din("a_lambda", [DEPTH * 256])
    a_subln = din("a_subln", [DEPTH * 128])
    c_q_norm = din("c_q_norm", [DEPTH * 64])
    c_k_norm = din("c_k_norm", [DEPTH * 64])
    idx_k_norm = din("idx_k_norm", [DEPTH * 64])
    w_br = [din("w_br_a", [DEPTH, 512, D]), din("w_br_b", [DEPTH, 512, D]), din("w_br_c", [DEPTH, 512, D])]
    w_out = din("w_out", [DEPTH, D, D])
    w_up = din("w_up", [DEPTH, D, 4 * D])
    w_down = din("w_down", [DEPTH, 4 * D, D])
    w_ple_gate = din("w_ple_gate", [DEPTH, D, D])
    w_ple_proj = din("w_ple_proj", [DEPTH, 256, D])
    out_d = nc.dram_tensor("out", [SL, D], F32, kind="ExternalOutput").ap()

    xres = dscr("xres", [SL, D], F32)
    QT = {k: dscr("QT_" + k, [4, 128, SL], BF16) for k in "abc"}
    KT = {k: dscr("KT_" + k, [4, 128, SL], BF16) for k in "abc"}
    VX = {"a": dscr("V_a", [SL, 4 * 129], BF16), "b": dscr("V_b", [SL, 8 * 65], BF16),
          "c": dscr("V_c", [SL, 8 * 65], BF16)}
    IQT = dscr("IQT", [2, 128, SL], BF16)
    IKT = dscr("IKT", [128, SL], BF16)
    GT = dscr("GT", [24, 128, SL], BF16)
    YT = dscr("YT", [12, 128, SL], BF16)

    with ExitStack() as st:
        S = Sched(nc, st)
        E = st.enter_context
        V = nc.vector
        A = nc.scalar
        G = nc.gpsimd
        T = nc.tensor

        uid = [0]

        def sb(stack, name, shape, dt):
            uid[0] += 1
            return stack.enter_context(nc.sbuf_tensor(f"s{uid[0]}_{name}", list(shape), dt))

        def ps(stack, name, shape, dt):
            uid[0] += 1
            return stack.enter_context(nc.psum_tensor(f"q{uid[0]}_{name}", list(shape), dt))

        ident = sb(st, "ident", [128, 128], BF16)
        ntri = sb(st, "ntri", [128, 128], BF16)
        nones = sb(st, "nones", [128, 128], BF16)
        strict = sb(st, "strict", [128, 128], BF16)
        cos_t = sb(st, "cos_t", [128, NT, 32], F32)
        sin_t = sb(st, "sin_t", [128, NT, 32], F32)
        gA = sb(st, "gA", [128, DEPTH * 8], F32)
        gM = sb(st, "gM", [128, DEPTH * 8], F32)
        gP = sb(st, "gP", [128, DEPTH * 8], F32)
        aqn = sb(st, "aqn", [128, DEPTH * 64], F32)
        akn = sb(st, "akn", [128, DEPTH * 64], F32)
        cqn = sb(st, "cqn", [128, DEPTH * 64], F32)
        ckn = sb(st, "ckn", [128, DEPTH * 64], F32)
        ikn = sb(st, "ikn", [128, DEPTH * 64], F32)
        subg = sb(st, "subg", [128, DEPTH * 128], F32)
        lam_t = sb(st, "lam_t", [128, DEPTH], F32)
        nlam_t = sb(st, "nlam_t", [128, DEPTH], F32)
        iw_all = sb(st, "iw_all", [128, NT, 4], F32)
        pw2 = sb(st, "pw2", [128, NSTEP + 1], F32)

        CONST = ["const"]

        with ExitStack() as ph:
            posi = sb(ph, "posi", [128, NT], I32)
            posf = sb(ph, "posf", [128, NT], F32)
            invf = sb(ph, "invf", [128, 32], F32)
            ang = sb(ph, "ang", [128, NT, 32], F32)
            t0 = sb(ph, "t0", [128, NT, 32], F32)
            t1 = sb(ph, "t1", [128, NT, 32], F32)
            lpb = sb(ph, "lpb", [128, DEPTH * 256], F32)
            tmp64 = sb(ph, "tmp64", [128, 64], F32)
            s12 = sb(ph, "s12", [128, 2], F32)
            K0 = ["p0"]
            S.op("pool", lambda: G.memset(ident[:], 1.0), w=CONST)
            S.op("pool", lambda: G.affine_select(out=ident[:], in_=ident[:], pattern=[[-1, 128]],
                                                 compare_op=ALU.is_equal, fill=0.0, base=0, channel_multiplier=1),
                 r=CONST, w=CONST)
            S.op("pool", lambda: G.memset(nones[:], -1.0), w=CONST)
            for i_ in range(NSTEP + 1):
                S.op("pool", lambda: G.memset(pw2[:, i_:i_ + 1], float(2.0 ** -(i_ + 1))), w=CONST)
            S.op("pool", lambda: G.memset(ntri[:], -1.0), w=CONST)
            S.op("pool", lambda: G.affine_select(out=ntri[:], in_=ntri[:], pattern=[[-1, 128]],
                                                 compare_op=ALU.is_ge, fill=0.0, base=0, channel_multiplier=1),
                 r=CONST, w=CONST)
            S.op("pool", lambda: G.memset(strict[:], 1.0), w=CONST)
            S.op("pool", lambda: G.affine_select(out=strict[:], in_=strict[:], pattern=[[1, 128]],
                                                 compare_op=ALU.is_gt, fill=0.0, base=0, channel_multiplier=-1),
                 r=CONST, w=CONST)
            for dst, src in [(gA, attn_norm), (gM, mlp_norm), (gP, ple_norm)]:
                S.dma("sp", dst[:], src[:, :], w=CONST)
            for dst, src in [(aqn, a_q_norm), (akn, a_k_norm), (cqn, c_q_norm), (ckn, c_k_norm),
                             (ikn, idx_k_norm), (subg, a_subln), (lpb, a_lambda), (invf, invf_in)]:
                S.dma("sp", dst[:], src.partition_broadcast(128), w=CONST + K0)
            S.dma("sp", posi[:], pos_in[:, :], w=K0)
            S.op("dve", lambda: V.tensor_scalar(out=aqn[:], in0=aqn[:], scalar1=0.125, scalar2=None, op0=ALU.mult), r=CONST, w=CONST)
            S.op("dve", lambda: V.tensor_scalar(out=cqn[:], in0=cqn[:], scalar1=0.125, scalar2=None, op0=ALU.mult), r=CONST, w=CONST)
            for l in range(DEPTH):
                li = 0.8 - 0.6 * math.exp(-0.3 * l)
                S.op("dve", lambda: V.tensor_scalar(out=subg[:, l * 128:(l + 1) * 128], in0=subg[:, l * 128:(l + 1) * 128],
                                                    scalar1=float(1.0 - li), scalar2=None, op0=ALU.mult), r=CONST, w=CONST)
                for j in range(2):
                    S.op("dve", lambda: V.tensor_tensor(out=tmp64[:], in0=lpb[:, l * 256 + j * 128: l * 256 + j * 128 + 64],
                                                        in1=lpb[:, l * 256 + j * 128 + 64: l * 256 + j * 128 + 128], op=ALU.mult),
                         r=K0, w=["tmp64"])
                    S.op("dve", lambda: V.tensor_reduce(out=s12[:, j:j + 1], in_=tmp64[:], axis=AX.X, op=ALU.add),
                         r=["tmp64"], w=["s12"])
                S.op("act", lambda: A.activation(out=s12[:], in_=s12[:], func=AF.Exp), r=["s12"], w=["s12"])
                S.op("dve", lambda: V.tensor_tensor(out=lam_t[:, l:l + 1], in0=s12[:, 0:1], in1=s12[:, 1:2], op=ALU.subtract),
                     r=["s12"], w=CONST)
                S.op("dve", lambda: V.tensor_scalar(out=lam_t[:, l:l + 1], in0=lam_t[:, l:l + 1], scalar1=float(li), scalar2=None, op0=ALU.add),
                     r=CONST, w=CONST)
                S.op("dve", lambda: V.tensor_scalar(out=nlam_t[:, l:l + 1], in0=lam_t[:, l:l + 1], scalar1=-1.0, scalar2=None, op0=ALU.mult),
                     r=CONST, w=CONST)
            S.op("dve", lambda: V.tensor_copy(out=posf[:], in_=posi[:]), r=K0, w=["posf"])
            S.op("dve", lambda: V.tensor_tensor(out=ang[:], in0=posf[:].unsqueeze(2).to_broadcast([128, NT, 32]),
                                                in1=invf[:].unsqueeze(1).to_broadcast([128, NT, 32]), op=ALU.mult),
                 r=["posf"] + K0, w=["ang"])
            MAGIC = 12582912.0
            C1 = 6.28125
            C2 = 2.0 * math.pi - C1
            PI_LO = 3.1415925
            S.op("dve", lambda: V.tensor_scalar(out=t0[:], in0=ang[:], scalar1=float(1.0 / (2.0 * math.pi)), scalar2=None, op0=ALU.mult), r=["ang"], w=["t0"])
            S.op("dve", lambda: V.tensor_scalar(out=t0[:], in0=t0[:], scalar1=MAGIC, scalar2=None, op0=ALU.add), r=["t0"], w=["t0"])
            S.op("dve", lambda: V.tensor_scalar(out=t0[:], in0=t0[:], scalar1=-MAGIC, scalar2=None, op0=ALU.add), r=["t0"], w=["t0"])
            S.op("dve", lambda: V.scalar_tensor_tensor(out=t1[:], in0=t0[:], scalar=-C1, in1=ang[:], op0=ALU.mult, op1=ALU.add), r=["t0", "ang"], w=["t1"])
            S.op("dve", lambda: V.scalar_tensor_tensor(out=t1[:], in0=t0[:], scalar=-C2, in1=t1[:], op0=ALU.mult, op1=ALU.add), r=["t0", "t1"], w=["t1"])
            S.op("dve", lambda: V.tensor_scalar(out=t0[:], in0=t1[:], scalar1=PI_LO, scalar2=-PI_LO, op0=ALU.min, op1=ALU.max), r=["t1"], w=["t0"])
            S.op("act", lambda: A.activation(out=sin_t[:], in_=t0[:], func=AF.Sin), r=["t0"], w=CONST)
            S.op("dve", lambda: V.tensor_scalar(out=t1[:], in0=t1[:], scalar1=float(math.pi / 2), scalar2=None, op0=ALU.add), r=["t1"], w=["t1"])
            S.op("dve", lambda: V.tensor_scalar(out=t0[:], in0=t1[:], scalar1=float(math.pi), scalar2=float(-2.0 * math.pi), op0=ALU.is_gt, op1=ALU.mult), r=["t1", "const"], w=["t0"])
            S.op("dve", lambda: V.tensor_tensor(out=t1[:], in0=t1[:], in1=t0[:], op=ALU.add), r=["t0", "t1"], w=["t1"])
            S.op("dve", lambda: V.tensor_scalar(out=t1[:], in0=t1[:], scalar1=PI_LO, scalar2=-PI_LO, op0=ALU.min, op1=ALU.max), r=["t1"], w=["t1"])
            S.op("act", lambda: A.activation(out=cos_t[:], in_=t1[:], func=AF.Sin), r=["t1"], w=CONST)
            S.barrier()

        def rms_rows(xt, kx, ssq, rstd, junk, kpre):
            S.op("act", lambda: A.activation(out=junk[:], in_=xt, func=AF.Square, accum_out=ssq[:, 0:1]),
                 r=[kx], w=[kpre + "junk", kpre + "ssq"])
            S.op("act", lambda: A.activation(out=rstd[:, 0:1], in_=ssq[:, 0:1], func=AF.Sqrt, bias=EPS, scale=1.0 / D),
                 r=[kpre + "ssq"], w=[kpre + "rstd"])
            S.op("dve", lambda: V.reciprocal(out=rstd[:, 0:1], in_=rstd[:, 0:1]), r=[kpre + "rstd"], w=[kpre + "rstd"])

        def norm_transpose(xt, kx, gain, l, hb, pst2, dst_fn, kdst, ssq, rstd, junk, kpre):
            rms_rows(xt, kx, ssq, rstd, junk, kpre)
            S.op("dve", lambda: V.tensor_scalar(out=hb[:], in0=xt, scalar1=rstd[:, 0:1], scalar2=None, op0=ALU.mult),
                 r=[kx, kpre + "rstd"], w=[kpre + "hb"])
            for half in range(2):
                pt = pst2[half]
                for c4 in range(4):
                    c = half * 4 + c4
                    S.op("pe", lambda: T.transpose(out=pt[:, c4, :], in_=hb[:, c * 128:(c + 1) * 128], identity=ident[:]),
                         r=[kpre + "hb"], w=[kpre + f"pst{half}"])
                for c4 in range(4):
                    c = half * 4 + c4
                    if c4 % 2 == 0:
                        S.op("dve", lambda: V.tensor_scalar(out=dst_fn(c), in0=pt[:, c4, :], scalar1=gain[:, l * 8 + c:l * 8 + c + 1],
                                                            scalar2=None, op0=ALU.mult),
                             r=[kpre + f"pst{half}"], w=[kdst])
                    else:
                        S.op("act", lambda: A.activation(out=dst_fn(c), in_=pt[:, c4, :], func=AF.Copy,
                                                         scale=gain[:, l * 8 + c:l * 8 + c + 1]),
                             r=[kpre + f"pst{half}"], w=[kdst])

        def load_w_bf16(dst, src, key):
            S.dma("pool", dst, src, w=[key])

        for l in range(depth):
            xsrc = x_in if l == 0 else xres
            lam_init = 0.8 - 0.6 * math.exp(-0.3 * l)

            for ph in ([ExitStack()] if 'p1' in PH else []):
                hT = sb(ph, "hT", [128, 8, SL], BF16)
                xb2 = [sb(ph, f"p1x{i}", [128, D], F32) for i in range(2)]
                hb = sb(ph, "p1hb", [128, D], BF16)
                junk = sb(ph, "p1junk", [128, D], BF16)
                ssq = sb(ph, "p1ssq", [128, 8], F32)
                rstd = sb(ph, "p1rstd", [128, 8], F32)
                pst2 = [ps(ph, f"p1pst{i}", [128, 4, 128], BF16) for i in range(2)]
                psm = [ps(ph, f"p1ps{i}", [128, 512], F32) for i in range(2)]
                wblk = [sb(ph, f"p1w{i}", [128, 8, 512], BF16) for i in range(2)]
                sq_2 = [sb(ph, f"p1sq{i}", [128, 512], F32) for i in range(2)]
                xn_2 = [sb(ph, f"p1xn{i}", [128, 512], F32) for i in range(2)]
                xg_2 = [sb(ph, f"p1xg{i}", [128, 512], F32) for i in range(2)]
                ra_2 = [sb(ph, f"p1ra{i}", [128, 256], F32) for i in range(2)]
                rb_2 = [sb(ph, f"p1rb{i}", [128, 256], F32) for i in range(2)]
                ob_2 = [sb(ph, f"p1ob{i}", [128, 512], BF16) for i in range(2)]
                ssq_2 = [sb(ph, f"p1ssq{i}", [128, 8], F32) for i in range(2)]
                rstd_2 = [sb(ph, f"p1rstd{i}", [128, 8], F32) for i in range(2)]
                stg = [sb(ph, f"p1stg{i}", [128, 4, 512], BF16) for i in range(2)]
                vst = [sb(ph, f"p1vst{i}", [128, 520], BF16) for i in range(4)]
                gst = [sb(ph, f"p1gst{i}", [128, 512], BF16) for i in range(2)]

                for t in range(NT):
                    xt = xb2[t % 2]
                    kx = f"p1x{t % 2}"
                    S.dma("sp", xt[:], xsrc[t * 128:(t + 1) * 128, :], r=[f"xr{t}"], w=[kx])
                    norm_transpose(xt[:], kx, gA, l, hb, pst2,
                                   lambda c, t=t: hT[:, c, t * 128:(t + 1) * 128], "hT", ssq, rstd, junk, "p1")
                for i in range(4):
                    S.op("pool", lambda: G.memset(vst[i][:], 1.0), w=[f"vst{i}"])

                blocks = []
                names = ["aq", "ak", "av", "bq", "bk", "bv", "cq", "ck", "cv"]
                for i, nm in enumerate(names):
                    blocks.append((nm, i * 512, 512))
                blocks.append(("idx", 4608, 324))
                for i in range(6):
                    blocks.append((f"gl{i}", 4932 + i * 512, 512))
                w_l = w_in[l].rearrange("(c p) n -> p c n", p=128)

                def qk_post(pv, nh, gain, do_norm, do_rope, t, outv, scale, kout):
                    n = nh * 64
                    pz = t % 2
                    sq, xn, xg, ra, rb = sq_2[pz], xn_2[pz], xg_2[pz], ra_2[pz], rb_2[pz]
                    ssq, rstd = ssq_2[pz], rstd_2[pz]
                    ksq, kxn, kxg, kra, krb, kss, krs = (f"sq{pz}", f"xn{pz}", f"xg{pz}", f"ra{pz}", f"rb{pz}", f"qssq{pz}", f"qrstd{pz}")
                    if do_norm:
                        sqv = sq[:, 0:n].rearrange("p (h d) -> p h d", d=64)
                        S.op("act", lambda: A.activation(out=sqv, in_=pv, func=AF.Square), r=["psm"], w=[ksq])
                        S.op("dve", lambda: V.tensor_reduce(out=ssq[:, 0:nh], in_=sqv, axis=AX.X, op=ALU.add), r=[ksq], w=[kss])
                        S.op("act", lambda: A.activation(out=rstd[:, 0:nh], in_=ssq[:, 0:nh], func=AF.Sqrt, bias=EPS, scale=1.0 / 64),
                             r=[kss], w=[krs])
                        S.op("dve", lambda: V.reciprocal(out=rstd[:, 0:nh], in_=rstd[:, 0:nh]), r=[krs], w=[krs])
                        xnv = xn[:, 0:n].rearrange("p (h d) -> p h d", d=64)
                        S.op("dve", lambda: V.tensor_tensor(out=xnv, in0=pv, in1=rstd[:, 0:nh].unsqueeze(2).to_broadcast([128, nh, 64]), op=ALU.mult),
                             r=["psm", krs], w=[kxn])
                        xgv = xg[:, 0:n].rearrange("p (h d) -> p h d", d=64)
                        S.op("pool", lambda: G.tensor_tensor(out=xgv, in0=xnv, in1=gain.unsqueeze(1).to_broadcast([128, nh, 64]), op=ALU.mult),
                             r=[kxn, "const"], w=[kxg])
                    else:
                        xgv = xg[:, 0:n].rearrange("p (h d) -> p h d", d=64)
                        S.op("act", lambda: A.activation(out=xgv, in_=pv, func=AF.Copy, scale=float(scale)), r=["psm"], w=[kxg])
                    if do_rope:
                        x1 = xgv[:, :, 0:32]
                        x2 = xgv[:, :, 32:64]
                        cb = cos_t[:, t, :].unsqueeze(1).to_broadcast([128, nh, 32])
                        sbb = sin_t[:, t, :].unsqueeze(1).to_broadcast([128, nh, 32])
                        rav = ra[:, 0:nh * 32].rearrange("p (h d) -> p h d", d=32)
                        rbv = rb[:, 0:nh * 32].rearrange("p (h d) -> p h d", d=32)
                        S.op("pool", lambda: G.tensor_tensor(out=rav, in0=x1, in1=cb, op=ALU.mult), r=[kxg], w=[kra])
                        S.op("dve", lambda: V.tensor_tensor(out=rbv, in0=x2, in1=sbb, op=ALU.mult), r=[kxg], w=[krb])
                        S.op("dve", lambda: V.tensor_tensor(out=outv[:, :, 0:32], in0=rav, in1=rbv, op=ALU.subtract), r=[kra, krb], w=[kout])
                        S.op("pool", lambda: G.tensor_tensor(out=rav, in0=x2, in1=cb, op=ALU.mult), r=[kxg, kra], w=[kra])
                        S.op("dve", lambda: V.tensor_tensor(out=rbv, in0=x1, in1=sbb, op=ALU.mult), r=[kxg, krb], w=[krb])
                        S.op("dve", lambda: V.tensor_tensor(out=outv[:, :, 32:64], in0=rav, in1=rbv, op=ALU.add), r=[kra, krb], w=[kout])
                    else:
                        S.op("dve", lambda: V.tensor_copy(out=outv, in_=xgv), r=[kxg], w=[kout])

                nblk = len(blocks)
                S.dma("pool", wblk[0][:, :, 0:blocks[0][2]], w_l[:, :, blocks[0][1]:blocks[0][1] + blocks[0][2]], w=["wblk0"])
                items = []

                def prefetch_w(bi):
                    if bi + 1 < nblk:
                        nn = blocks[bi + 1]
                        S.dma("pool", wblk[(bi + 1) % 2][:, :, 0:nn[2]], w_l[:, :, nn[1]:nn[1] + nn[2]], w=[f"wblk{(bi + 1) % 2}"])

                for bi, (nm, c0, n) in enumerate(blocks):
                    wb = wblk[bi % 2]
                    kw = f"wblk{bi % 2}"
                    if nm.startswith("gl"):
                        gi = int(nm[2:])
                        for Tq in range(NQ):
                            for j in range(4):
                                pb_ = len(items) % 2

                                def g0(bi=bi, wb=wb, kw=kw, Tq=Tq, j=j, pb_=pb_, first=(Tq == 0 and j == 0)):
                                    if first:
                                        prefetch_w(bi)
                                    pm = psm[pb_]
                                    for c in range(8):
                                        S.op("pe", lambda: T.matmul(pm[:], lhsT=wb[:, c, j * 128:(j + 1) * 128], rhs=hT[:, c, Tq * 512:(Tq + 1) * 512],
                                                                    start=(c == 0), stop=(c == 7)), r=[kw, "hT"], w=[f"psm{pb_}"])

                                def g1(gi=gi, Tq=Tq, j=j, pb_=pb_):
                                    pm = psm[pb_]
                                    gs = gst[pb_]
                                    S.op("act", lambda: A.activation(out=gs[:], in_=pm[:], func=AF.Sigmoid), r=[f"psm{pb_}"], w=[f"gst{pb_}"])
                                    S.dma("act", GT[gi * 4 + j, :, Tq * 512:(Tq + 1) * 512], gs[:], r=[f"gst{pb_}"], w=["GT"])
                                items.append([g0, g1, None])
                        continue
                    for t in range(NT):
                        pb_ = len(items) % 2

                        def s0(bi=bi, wb=wb, kw=kw, t=t, n=n, pb_=pb_):
                            if t == 0:
                                prefetch_w(bi)
                            pm = psm[pb_]
                            for c in range(8):
                                S.op("pe", lambda: T.matmul(pm[:, 0:n], lhsT=hT[:, c, t * 128:(t + 1) * 128], rhs=wb[:, c, 0:n],
                                                            start=(c == 0), stop=(c == 7)), r=[kw, "hT"], w=[f"psm{pb_}"])

                        def s1(nm=nm, t=t, pb_=pb_):
                            pm = psm[pb_]
                            kp = f"psm{pb_}"
                            S.lastw["psm"] = S.lastw[kp]
                            S.reads["psm"] = S.reads[kp]
                            ob = ob_2[t % 2]
                            kob = f"ob{t % 2}"
                            if nm in ("aq", "ak", "cq", "ck", "bq", "bk"):
                                pv = pm[:, 0:512].rearrange("p (h d) -> p h d", d=64)
                                obv = ob[:].rearrange("p (h d) -> p h d", d=64)
                                if nm[0] == "b":
                                    qk_post(pv, 8, None, False, False, t, obv, 0.125 if nm == "bq" else 1.0, kob)
                                else:
                                    gain = {"aq": aqn, "ak": akn, "cq": cqn, "ck": ckn}[nm][:, l * 64:(l + 1) * 64]
                                    qk_post(pv, 8, gain, True, True, t, obv, 1.0, kob)
                            elif nm in ("av", "bv", "cv"):
                                nh, dv = (4, 128) if nm == "av" else (8, 64)
                                vi = (t % 2) + (0 if nm == "av" else 2)
                                vs = vst[vi]
                                vv = vs[:, 0:nh * (dv + 1)].rearrange("p (h e) -> p h e", e=dv + 1)[:, :, 0:dv]
                                S.op("act", lambda: A.activation(out=vv, in_=pm[:, 0:512].rearrange("p (h e) -> p h e", e=dv), func=AF.Copy),
                                     r=[kp], w=[f"vst{vi}"])
                                S.dma("act", VX[nm[0]][t * 128:(t + 1) * 128, :], vs[:, 0:nh * (dv + 1)], r=[f"vst{vi}"], w=["VX"])
                            else:
                                obv = ob[:, 0:256].rearrange("p (h d) -> p h d", d=64)
                                qk_post(pm[:, 0:256].rearrange("p (h d) -> p h d", d=64), 4, None, False, True, t, obv, 1.0, kob)
                                obk = ob[:, 256:320].rearrange("p (h d) -> p h d", d=64)
                                qk_post(pm[:, 256:320].rearrange("p (h d) -> p h d", d=64), 1, ikn[:, l * 64:(l + 1) * 64], True, True, t, obk, 1.0, kob)
                                S.op("dve", lambda: V.tensor_copy(out=ob[:, 320:384], in_=ob[:, 256:320]), r=[kob], w=[kob])
                                S.op("act", lambda: A.activation(out=iw_all[:, t, :], in_=pm[:, 320:324], func=AF.Copy), r=[kp], w=["iw_all"])

                        def s2(nm=nm, t=t):
                            if nm in ("av", "bv", "cv"):
                                return
                            ob = ob_2[t % 2]
                            kob = f"ob{t % 2}"
                            sg = stg[(t // 4) % 2]
                            ksg = f"stg{(t // 4) % 2}"
                            tt = t % 4
                            pt = pst2[t % 2]
                            nj_ = 3 if nm == "idx" else 4
                            for j in range(nj_):
                                S.op("pe", lambda: T.transpose(out=pt[:, j, :], in_=ob[:, j * 128:(j + 1) * 128], identity=ident[:]),
                                     r=[kob], w=[f"p1pst{t % 2}"])
                            S.op("act", lambda: A.activation(out=sg[:, 0:nj_, tt * 128:(tt + 1) * 128], in_=pt[:, 0:nj_, :], func=AF.Copy),
                                 r=[f"p1pst{t % 2}"], w=[ksg])
                            if tt == 3:
                                t0_ = (t - 3) * 128
                                if nm == "idx":
                                    S.dma("act", IQT[:, :, t0_:t0_ + 512].rearrange("j p t -> p j t"), sg[:, 0:2, :], r=[ksg], w=["QKT"])
                                    S.dma("act", IKT[:, t0_:t0_ + 512], sg[:, 2, :], r=[ksg], w=["QKT"])
                                else:
                                    dst = (QT if nm[1] == "q" else KT)[nm[0]]
                                    S.dma("act", dst[:, :, t0_:t0_ + 512].rearrange("j p t -> p j t"), sg[:], r=[ksg], w=["QKT"])
                        items.append([s0, s1, s2])
                run_pipeline(items, 3)
                S.barrier()
                ph.close()

            for ph in ([ExitStack()] if 'pa' in PH else []):
                kt = sb(ph, "a_kt", [128, 4, SL], BF16)
                vx = sb(ph, "a_vx", [128, NT, 4 * 129], BF16)
                qt2 = [sb(ph, f"a_qt{i}", [128, 4, 512], BF16) for i in range(2)]
                pT2 = [sb(ph, f"a_pT{i}", [128, 512], BF16) for i in range(2)]
                o2 = [sb(ph, f"a_o{i}", [128, 4, 129], F32) for i in range(2)]
                rl = sb(ph, "a_rl", [128, 8], F32)
                y0 = sb(ph, "a_y0", [128, 4, 128], F32)
                y1 = sb(ph, "a_y1", [128, 4, 128], F32)
                ss = sb(ph, "a_ss", [128, 8], F32)
                yb2 = [sb(ph, f"a_yb{i}", [128, 4, 128], BF16) for i in range(2)]
                yts = [sb(ph, f"a_yts{i}", [128, 512], BF16) for i in range(2)]
                pss = [ps(ph, f"a_pss{i}", [128, 512], F32) for i in range(2)]
                po2 = [ps(ph, f"a_po{i}", [128, 2, 512], F32) for i in range(2)]
                ptr = ps(ph, "a_ptr", [128, 4, 128], BF16)
                S.dma("sp", kt[:], KT["a"].rearrange("j p t -> p j t"), r=["QKT"], w=["a_kt"])
                for v4 in range(4):
                    S.dma("sp", vx[:, v4 * 8:(v4 + 1) * 8, :], VX["a"][v4 * 1024:(v4 + 1) * 1024, :].rearrange("(t p) e -> p t e", p=128), r=["VX"], w=["a_vx"])

                def a_load_q(Q):
                    S.dma("sp", qt2[Q % 2][:], QT["a"][:, :, Q * 512:(Q + 1) * 512].rearrange("j p t -> p j t"), r=["QKT"], w=[f"a_qt{Q % 2}"])

                a_load_q(0)
                items = []
                deferred = []

                def a_epilogue(Q, h, c, po, kpo):
                    for bk in range(2):
                        S.op("act", lambda: A.activation(out=o2[c][:, 2 * bk:2 * bk + 2, :], in_=po[:, bk, 0:258].rearrange("p (r e) -> p r e", e=129), func=AF.Copy),
                             r=[kpo], w=[f"a_o{c}"])
                    if c == 0:
                        return
                    yb = yb2[h % 2]
                    kyb = f"a_yb{h % 2}"
                    S.op("dve", lambda: V.reciprocal(out=rl[:, 0:4], in_=o2[0][:, :, 128]), r=["a_o0"], w=["a_rl"])
                    S.op("dve", lambda: V.reciprocal(out=rl[:, 4:8], in_=o2[1][:, :, 128]), r=["a_o1"], w=["a_rl"])
                    S.op("dve", lambda: V.tensor_scalar(out=rl[:, 4:8], in0=rl[:, 4:8], scalar1=nlam_t[:, l:l + 1], scalar2=None, op0=ALU.mult),
                         r=["a_rl", "const"], w=["a_rl"])
                    S.op("dve", lambda: V.tensor_tensor(out=y0[:], in0=o2[0][:, :, 0:128], in1=rl[:, 0:4].unsqueeze(2).to_broadcast([128, 4, 128]), op=ALU.mult),
                         r=["a_o0", "a_rl"], w=["a_y0"])
                    S.op("pool", lambda: G.tensor_tensor(out=y1[:], in0=o2[1][:, :, 0:128], in1=rl[:, 4:8].unsqueeze(2).to_broadcast([128, 4, 128]), op=ALU.mult),
                         r=["a_o1", "a_rl"], w=["a_y1"])
                    S.op("dve", lambda: V.tensor_tensor(out=y0[:], in0=y0[:], in1=y1[:], op=ALU.add), r=["a_y0", "a_y1"], w=["a_y0"])
                    S.op("pool", lambda: G.tensor_tensor(out=y1[:], in0=y0[:], in1=y0[:], op=ALU.mult), r=["a_y0"], w=["a_y1"])
                    S.op("dve", lambda: V.tensor_reduce(out=ss[:, 0:4], in_=y1[:], axis=AX.X, op=ALU.add), r=["a_y1"], w=["a_ss"])
                    S.op("act", lambda: A.activation(out=ss[:, 0:4], in_=ss[:, 0:4], func=AF.Sqrt, bias=EPS, scale=1.0 / 128), r=["a_ss"], w=["a_ss"])
                    S.op("dve", lambda: V.reciprocal(out=ss[:, 0:4], in_=ss[:, 0:4]), r=["a_ss"], w=["a_ss"])
                    S.op("dve", lambda: V.tensor_tensor(out=y0[:], in0=y0[:], in1=ss[:, 0:4].unsqueeze(2).to_broadcast([128, 4, 128]), op=ALU.mult),
                         r=["a_y0", "a_ss"], w=["a_y0"])
                    S.op("dve", lambda: V.tensor_tensor(out=yb[:], in0=y0[:], in1=subg[:, l * 128:(l + 1) * 128].unsqueeze(1).to_broadcast([128, 4, 128]), op=ALU.mult),
                         r=["a_y0", "const"], w=[kyb])

                    def fin():
                        for qq in range(4):
                            S.op("pe", lambda: T.transpose(out=ptr[:, qq, :], in_=yb[:, qq, :], identity=ident[:]), r=[kyb], w=["a_ptr"])
                        ys = yts[h % 2]
                        S.op("act", lambda: A.activation(out=ys[:], in_=ptr[:].rearrange("p a b -> p (a b)"), func=AF.Copy), r=["a_ptr"], w=[f"a_yts{h % 2}"])
                        S.dma("act", YT[h, :, Q * 512:(Q + 1) * 512], ys[:], r=[f"a_yts{h % 2}"], w=["YT"])
                    deferred.append([4, fin])

                def run_deferred(force=False):
                    keep = []
                    for dly in deferred:
                        dly[0] -= 1
                        if dly[0] <= 0 or force:
                            dly[1]()
                        else:
                            keep.append(dly)
                    deferred[:] = keep

                gidx = 0
                for Q in range(NQ):
                    for h in range(4):
                        for c in range(2):
                            g = gidx
                            gidx += 1
                            nj = 4 * Q + 4
                            for j in range(nj):
                                b = len(items) % 2
                                jj = j - 4 * Q
                                c0 = max(0, jj) * 128

                                def s0(Q=Q, h=h, c=c, j=j, c0=c0, b=b, pre=(h == 0 and c == 0 and j == 0)):
                                    if pre and Q + 1 < NQ:
                                        a_load_q(Q + 1)
                                    qt = qt2[Q % 2]
                                    S.op("pe", lambda: T.matmul(pss[b][:, c0:512], lhsT=kt[c * 64:(c + 1) * 64, h, j * 128:(j + 1) * 128],
                                                                rhs=qt[c * 64:(c + 1) * 64, h, c0:512], start=True, stop=True),
                                         r=["a_kt", f"a_qt{Q % 2}"], w=[f"a_pss{b}"])

                                def s1(c0=c0, b=b, jj=jj):
                                    S.op("act", lambda: A.activation(out=pT2[b][:, c0:512], in_=pss[b][:, c0:512], func=AF.Exp),
                                         r=[f"a_pss{b}"], w=[f"a_pT{b}"])
                                    if jj >= 0:
                                        S.op("pool", lambda: G.memset(pT2[b][64:128, c0:c0 + 64], 0.0), w=[f"a_pT{b}"])

                                def s2(Q=Q, h=h, c=c, j=j, jj=jj, b=b, g=g, nj=nj):
                                    po = po2[g % 2]
                                    kpo = f"a_po{g % 2}"
                                    for qq in range(max(0, jj), 4):
                                        off = (qq % 2) * 129
                                        S.op("pe", lambda: T.matmul(po[:, qq // 2, off:off + 129], lhsT=pT2[b][:, qq * 128:(qq + 1) * 128],
                                                                    rhs=vx[:, j, h * 129:(h + 1) * 129], start=(j == 0 and qq % 2 == 0), stop=(j == 4 * Q + qq),
                                                                    skip_group_check=True),
                                             r=[f"a_pT{b}", "a_vx"], w=[kpo])
                                    if j == nj - 1:
                                        a_epilogue(Q, h, c, po, kpo)
                                    run_deferred()
                                items.append([s0, s1, s2])
                run_pipeline(items, 3)
                run_deferred(force=True)
                S.barrier()
                ph.close()

            for ph in ([ExitStack()] if 'pb' in PH else []):
                kt = sb(ph, "b_kt", [128, 4, SL], BF16)
                vx = sb(ph, "b_vx", [128, NT, 8 * 65], BF16)
                qt2 = [sb(ph, f"b_qt{i}", [128, 4, 512], BF16) for i in range(2)]
                e2 = [sb(ph, f"b_e{i}", [128, 512], F32) for i in range(2)]
                nl2 = [sb(ph, f"b_nl{i}", [128, 512], BF16) for i in range(2)]
                aT2 = [sb(ph, f"b_aT{i}", [128, 512], BF16) for i in range(2)]
                R = sb(ph, "b_R", [128, 512], BF16)
                qm2 = [sb(ph, f"b_qm{i}", [128, 512], BF16) for i in range(2)]
                yq2 = [sb(ph, f"b_yq{i}", [128, 4, 512], BF16) for i in range(2)]
                yts = [sb(ph, f"b_yts{i}", [128, 512], BF16) for i in range(2)]
                psz = [ps(ph, f"b_psz{i}", [128, 512], F32) for i in range(2)]
                psc = [ps(ph, f"b_psc{i}", [128, 512], F32) for i in range(2)]
                po2 = [ps(ph, f"b_po{i}", [128, 512], F32) for i in range(2)]
                ptr = ps(ph, "b_ptr", [128, 4, 128], BF16)
                S.dma("sp", kt[:], KT["b"].rearrange("j p t -> p j t"), r=["QKT"], w=["b_kt"])
                for v4 in range(4):
                    S.dma("sp", vx[:, v4 * 8:(v4 + 1) * 8, :], VX["b"][v4 * 1024:(v4 + 1) * 1024, :].rearrange("(t p) e -> p t e", p=128), r=["VX"], w=["b_vx"])

                def b_load_q(Q):
                    S.dma("sp", qt2[Q % 2][:], QT["b"][:, :, Q * 512:(Q + 1) * 512].rearrange("j p t -> p j t"), r=["QKT"], w=[f"b_qt{Q % 2}"])

                b_load_q(0)
                items = []
                gidx = 0
                for Q in range(NQ):
                    for h in range(8):
                        g = gidx
                        gidx += 1
                        pr, bs = h // 2, (h % 2) * 64
                        jtop = 4 * Q + 3
                        gstate = {"first": True}
                        for j in range(jtop, -1, -1):
                            b = len(items) % 2
                            jj = j - 4 * Q
                            c0 = max(0, jj) * 128

                            def s0(Q=Q, h=h, j=j, c0=c0, b=b, pr=pr, bs=bs, pre=(h == 0 and j == jtop)):
                                if pre and Q + 1 < NQ:
                                    b_load_q(Q + 1)
                                qt = qt2[Q % 2]
                                S.op("pe", lambda: T.matmul(psz[b][:, c0:512], lhsT=kt[bs:bs + 64, pr, j * 128:(j + 1) * 128], rhs=qt[bs:bs + 64, pr, c0:512],
                                                            start=True, stop=True), r=["b_kt", f"b_qt{Q % 2}"], w=[f"b_psz{b}"])

                            def s1(c0=c0, b=b, jj=jj):
                                S.op("act", lambda: A.activation(out=e2[b][:, c0:512], in_=psz[b][:, c0:512], func=AF.Exp),
                                     r=[f"b_psz{b}"], w=[f"b_e{b}"])
                                S.op("act", lambda: A.activation(out=nl2[b][:, c0:512], in_=e2[b][:, c0:512], func=AF.Ln, bias=1.0),
                                     r=[f"b_e{b}"], w=[f"b_nl{b}"])
                                if jj >= 0:
                                    S.op("dve", lambda: V.tensor_tensor(out=nl2[b][:, c0:c0 + 128], in0=nl2[b][:, c0:c0 + 128], in1=strict[:], op=ALU.mult),
                                         r=[f"b_nl{b}", "const"], w=[f"b_nl{b}"])

                            def s2(Q=Q, j=j, c0=c0, b=b, pr=pr, bs=bs, g=g, jtop=jtop):
                                qm = qm2[g % 2]
                                kqm = f"b_qm{g % 2}"
                                qt = qt2[Q % 2]
                                if j == jtop:
                                    S.op("pool", lambda: G.memset(R[:], 0.0), w=["b_R"])
                                    S.op("pool", lambda: G.memset(qm[:], 0.0), w=[kqm])
                                    S.op("dve", lambda: V.tensor_copy(out=qm[bs:bs + 64, :], in_=qt[bs:bs + 64, pr, :]), r=[f"b_qt{Q % 2}", kqm], w=[kqm])
                                S.op("pe", lambda: T.matmul(psc[b][:, c0:512], lhsT=kt[:, pr, j * 128:(j + 1) * 128], rhs=qm[:, c0:512], start=True, stop=False),
                                     r=["b_kt", kqm], w=[f"b_psc{b}"])
                                last = (j == jtop)
                                S.op("pe", lambda: T.matmul(psc[b][:, c0:512], lhsT=ntri[:], rhs=nl2[b][:, c0:512], start=False, stop=last),
                                     r=[f"b_nl{b}", "const"], w=[f"b_psc{b}"])
                                if not last:
                                    S.op("pe", lambda: T.matmul(psc[b][:, c0:512], lhsT=nones[:], rhs=R[:, c0:512], start=False, stop=True),
                                         r=["b_R", "const"], w=[f"b_psc{b}"])
                                if j > 0:
                                    S.op("dve", lambda: V.tensor_tensor(out=R[:, c0:512], in0=R[:, c0:512], in1=nl2[b][:, c0:512], op=ALU.add),
                                         r=["b_R", f"b_nl{b}"], w=["b_R"])

                            def s3(c0=c0, b=b, jj=jj):
                                S.op("act", lambda: A.activation(out=aT2[b][:, c0:512], in_=psc[b][:, c0:512], func=AF.Exp),
                                     r=[f"b_psc{b}"], w=[f"b_aT{b}"])
                                if jj >= 0:
                                    S.op("dve", lambda: V.tensor_tensor(out=aT2[b][:, c0:c0 + 128], in0=aT2[b][:, c0:c0 + 128], in1=strict[:], op=ALU.mult),
                                         r=[f"b_aT{b}", "const"], w=[f"b_aT{b}"])

                            def s4(Q=Q, h=h, j=j, jj=jj, b=b, g=g, gstate=gstate):
                                po = po2[g % 2]
                                kpo = f"b_po{g % 2}"
                                yq = yq2[Q % 2]
                                kyq = f"b_yq{Q % 2}"
                                for qq in range(max(0, jj), 4):
                                    S.op("pe", lambda: T.matmul(po[:, qq * 65:(qq + 1) * 65], lhsT=aT2[b][:, qq * 128:(qq + 1) * 128],
                                                                rhs=vx[:, j, h * 65:(h + 1) * 65], start=gstate["first"], stop=(j == 0),
                                                                skip_group_check=True),
                                         r=[f"b_aT{b}", "b_vx"], w=[kpo])
                                    gstate["first"] = False
                                if j == 0:
                                    S.op("act", lambda: A.activation(out=yq[:, :, h * 64:(h + 1) * 64],
                                                                     in_=po[:, 0:260].rearrange("p (a e) -> p a e", e=65)[:, :, 0:64], func=AF.Copy),
                                         r=[kpo], w=[kyq])
                                    if h == 7:
                                        for cch in range(4):
                                            for qq in range(4):
                                                S.op("pe", lambda: T.transpose(out=ptr[:, qq, :], in_=yq[:, qq, cch * 128:(cch + 1) * 128], identity=ident[:]),
                                                     r=[kyq], w=["b_ptr"])
                                            ys = yts[cch % 2]
                                            S.op("act", lambda: A.activation(out=ys[:], in_=ptr[:].rearrange("p a b -> p (a b)"), func=AF.Copy), r=["b_ptr"], w=[f"b_yts{cch % 2}"])
                                            S.dma("act", YT[4 + cch, :, Q * 512:(Q + 1) * 512], ys[:], r=[f"b_yts{cch % 2}"], w=["YT"])
                            items.append([s0, s1, s2, s3, s4])
                run_pipeline(items, 5)
                S.barrier()
                ph.close()

            for ph in ([ExitStack()] if 'pc' in PH else []):
                kt = sb(ph, "c_kt", [128, 4, SL], BF16)
                vx = sb(ph, "c_vx", [128, NT, 8 * 65], BF16)
                ikt = sb(ph, "c_ikt", [128, SL], BF16)
                qt2 = [sb(ph, f"c_qt{i}", [128, 4, 512], BF16) for i in range(2)]
                iqt2 = [sb(ph, f"c_iqt{i}", [128, 2, 512], BF16) for i in range(2)]
                score2 = [sb(ph, f"c_score{i}", [128, SL], F32) for i in range(2)]
                maskq2 = [sb(ph, f"c_maskq{i}", [128, SL], BF16) for i in range(2)]
                bst = [sb(ph, f"c_bst{i}", [128, 8], F32) for i in range(2)]
                dft = [sb(ph, f"c_dft{i}", [128, NSTEP + 1], F32) for i in range(2)]
                maskT = sb(ph, "c_maskT", [128, NT, 512], BF16)
                rb2 = [sb(ph, f"c_r{i}", [128, 512], F32) for i in range(2)]
                pT2 = [sb(ph, f"c_pT{i}", [128, 512], BF16) for i in range(2)]
                bs_ = sb(ph, "c_bis", [128, 8], F32)
                osb2 = [sb(ph, f"c_o{i}", [128, 4, 65], F32) for i in range(2)]
                rl2 = [sb(ph, f"c_rl{i}", [128, 4], F32) for i in range(2)]
                yq2 = [sb(ph, f"c_yq{i}", [128, 4, 512], BF16) for i in range(2)]
                yts = [sb(ph, f"c_yts{i}", [128, 512], BF16) for i in range(2)]
                psi = [ps(ph, f"c_psi{i}", [128, 512], F32) for i in range(2)]
                pss = [ps(ph, f"c_pss{i}", [128, 512], F32) for i in range(2)]
                po2 = [ps(ph, f"c_po{i}", [128, 512], F32) for i in range(2)]
                ptr = ps(ph, "c_ptr", [128, 4, 128], BF16)
                S.dma("sp", kt[:], KT["c"].rearrange("j p t -> p j t"), r=["QKT"], w=["c_kt"])
                for v4 in range(4):
                    S.dma("sp", vx[:, v4 * 8:(v4 + 1) * 8, :], VX["c"][v4 * 1024:(v4 + 1) * 1024, :].rearrange("(t p) e -> p t e", p=128), r=["VX"], w=["c_vx"])
                S.dma("sp", ikt[:], IKT[:, :], r=["QKT"], w=["c_ikt"])

                def c_load_q(Q):
                    kq_ = f"c_qt{Q % 2}"
                    S.dma("sp", qt2[Q % 2][:], QT["c"][:, :, Q * 512:(Q + 1) * 512].rearrange("j p t -> p j t"), r=["QKT"], w=[kq_])
                    S.dma("sp", iqt2[Q % 2][:], IQT[:, :, Q * 512:(Q + 1) * 512].rearrange("j p t -> p j t"), r=["QKT"], w=[kq_])

                c_load_q(0)
                ir = 0
                gidx = 0
                for Q in range(NQ):
                    qt = qt2[Q % 2]
                    iqt = iqt2[Q % 2]
                    kq = f"c_qt{Q % 2}"
                    if Q + 1 < NQ:
                        c_load_q(Q + 1)
                    for pair in range(2):
                        info = []
                        for ci in range(2):
                            qq = 2 * pair + ci
                            qb = 4 * Q + qq
                            nadm = (qb + 1) * 128
                            info.append((qq, qb, nadm))
                            sc_t = score2[ci]
                            ksc = f"c_score{ci}"
                            nkt = (nadm + 511) // 512
                            for k5 in range(nkt):
                                wd = min(512, nadm - k5 * 512)
                                for h in range(4):
                                    b = ir % 2
                                    ir += 1
                                    hb_ = (h % 2) * 64
                                    S.op("pe", lambda: T.matmul(psi[b][:, 0:wd], lhsT=iqt[hb_:hb_ + 64, h // 2, qq * 128:(qq + 1) * 128],
                                                                rhs=ikt[hb_:hb_ + 64, k5 * 512:k5 * 512 + wd], start=True, stop=True),
                                         r=[kq, "c_ikt"], w=[f"c_psi{b}"])
                                    S.op("act", lambda: A.activation(out=rb2[b][:, 0:wd], in_=psi[b][:, 0:wd], func=AF.Relu),
                                         r=[f"c_psi{b}"], w=[f"c_r{b}"])
                                    sc = sc_t[:, k5 * 512:k5 * 512 + wd]
                                    if h == 0:
                                        S.op("dve", lambda: V.tensor_scalar(out=sc, in0=rb2[b][:, 0:wd], scalar1=iw_all[:, qb, 0:1], scalar2=None, op0=ALU.mult),
                                             r=[f"c_r{b}", "iw_all"], w=[ksc])
                                    else:
                                        S.op("dve", lambda: V.scalar_tensor_tensor(out=sc, in0=rb2[b][:, 0:wd], scalar=iw_all[:, qb, h:h + 1], in1=sc,
                                                                                   op0=ALU.mult, op1=ALU.add),
                                             r=[f"c_r{b}", "iw_all", ksc], w=[ksc])
                            S.op("dve", lambda: V.memset(sc_t[0:64, qb * 128 + 64:(qb + 1) * 128], NEG), r=[ksc], w=[ksc])
                        bisect = info[0][1] >= 2
                        for ci in range(2):
                            qq, qb, nadm = info[ci]
                            sc_t = score2[ci]
                            ksc = f"c_score{ci}"
                            stt = bst[ci]
                            df = dft[ci]
                            KB = [f"c_bis{ci}"]
                            if not bisect:
                                S.op("dve", lambda: V.memset(stt[:, 5:6], -1.0e29), r=KB, w=KB)
                                continue
                            nfull = qb * 128 + 64
                            S.op("dve", lambda: V.tensor_reduce(out=stt[:, 4:5], in_=sc_t[:, 0:nadm], axis=AX.X, op=ALU.max), r=[ksc], w=KB)
                            S.op("dve", lambda: V.tensor_reduce(out=stt[:, 5:6], in_=sc_t[:, 0:nfull], axis=AX.X, op=ALU.min), r=[ksc], w=KB)
                            S.op("dve", lambda: V.tensor_scalar(out=stt[:, 5:6], in0=stt[:, 5:6], scalar1=-1.0, scalar2=None, op0=ALU.add), r=KB, w=KB)
                            S.op("dve", lambda: V.tensor_tensor(out=stt[:, 1:2], in0=stt[:, 4:5], in1=stt[:, 5:6], op=ALU.subtract), r=KB, w=KB)
                            S.op("dve", lambda: V.tensor_scalar(out=df[:], in0=pw2[:], scalar1=stt[:, 1:2], scalar2=None, op0=ALU.mult), r=KB + ["const"], w=KB)
                            if ci == 0:
                                S.op("dve", lambda: V.tensor_tensor(out=stt[:, 0:1], in0=stt[:, 5:6], in1=df[:, 0:1], op=ALU.add), r=KB, w=KB)
                            else:
                                S.op("dve", lambda: V.scalar_tensor_tensor(out=stt[:, 0:1], in0=stt[:, 5:6], scalar=-1.0, in1=df[:, 0:1],
                                                                           op0=ALU.mult, op1=ALU.subtract), r=KB, w=KB)
                        if bisect:
                            for i in range(NSTEP):
                                lastst = (i == NSTEP - 1)
                                qq, qb, nadm = info[0]
                                stt, df, KB, ksc = bst[0], dft[0], ["c_bis0"], "c_score0"
                                S.op("dve", lambda: V.tensor_scalar(out=maskq2[0][:, 0:nadm], in0=score2[0][:, 0:nadm], scalar1=stt[:, 0:1], scalar2=None,
                                                                    op0=ALU.is_gt, op1=ALU.add, accum_out=stt[:, 2:3]),
                                     r=KB + [ksc, "c_maskq0"], w=KB + ["c_maskq0"])
                                S.op("dve", lambda: V.tensor_scalar(out=stt[:, 3:4], in0=stt[:, 2:3], scalar1=float(TOPK), scalar2=df[:, i:i + 1],
                                                                    op0=ALU.is_ge, op1=ALU.mult), r=KB, w=KB)
                                if not lastst:
                                    S.op("dve", lambda: V.scalar_tensor_tensor(out=stt[:, 0:1], in0=stt[:, 3:4], scalar=df[:, i + 1:i + 2], in1=stt[:, 0:1],
                                                                               op0=ALU.subtract, op1=ALU.add), r=KB, w=KB)
                                else:
                                    S.op("dve", lambda: V.scalar_tensor_tensor(out=stt[:, 5:6], in0=stt[:, 3:4], scalar=df[:, i:i + 1], in1=stt[:, 0:1],
                                                                               op0=ALU.subtract, op1=ALU.add), r=KB, w=KB)
                                qq, qb, nadm = info[1]
                                stt, df, KB, ksc = bst[1], dft[1], ["c_bis1"], "c_score1"
                                S.op("act", lambda: A.activation(out=maskq2[1][:, 0:nadm], in_=score2[1][:, 0:nadm], func=AF.Sign, bias=stt[:, 0:1],
                                                                 accum_out=stt[:, 2:3]),
                                     r=KB + [ksc, "c_maskq1"], w=KB + ["c_maskq1"])
                                S.op("pool", lambda: G.tensor_scalar(out=stt[:, 3:4], in0=stt[:, 2:3], scalar1=float(2 * TOPK - nadm), scalar2=df[:, i:i + 1],
                                                                     op0=ALU.is_ge, op1=ALU.mult), r=KB, w=KB)
                                S.op("pool", lambda: G.tensor_tensor(out=stt[:, 0:1], in0=stt[:, 0:1], in1=stt[:, 3:4], op=ALU.subtract), r=KB, w=KB)
                                if not lastst:
                                    S.op("pool", lambda: G.tensor_tensor(out=stt[:, 0:1], in0=stt[:, 0:1], in1=df[:, i + 1:i + 2], op=ALU.add), r=KB, w=KB)
                                else:
                                    S.op("pool", lambda: G.tensor_tensor(out=stt[:, 0:1], in0=stt[:, 0:1], in1=df[:, i:i + 1], op=ALU.add), r=KB, w=KB)
                                    S.op("pool", lambda: G.tensor_scalar(out=stt[:, 5:6], in0=stt[:, 0:1], scalar1=-1.0, scalar2=None, op0=ALU.mult), r=KB, w=KB)
                        for ci in range(2):
                            qq, qb, nadm = info[ci]
                            mq = maskq2[ci]
                            kmq = f"c_maskq{ci}"
                            S.op("dve", lambda: V.tensor_scalar(out=mq[:, 0:nadm], in0=score2[ci][:, 0:nadm], scalar1=bst[ci][:, 5:6], scalar2=None, op0=ALU.is_gt),
                                 r=[f"c_bis{ci}", f"c_score{ci}", kmq], w=[kmq])
                            for k1 in range(0, qb + 1, 4):
                                nk = min(4, qb + 1 - k1)
                                for u in range(nk):
                                    S.op("pe", lambda: T.transpose(out=ptr[:, u, :], in_=mq[:, (k1 + u) * 128:(k1 + u + 1) * 128], identity=ident[:]),
                                         r=[kmq], w=["c_ptr"])
                                S.op("act", lambda: A.activation(out=maskT[:, k1:k1 + nk, qq * 128:(qq + 1) * 128], in_=ptr[:, 0:nk, :], func=AF.Copy),
                                     r=["c_ptr"], w=["c_maskT"])
                    items = []
                    yq = yq2[Q % 2]
                    kyq = f"c_yq{Q % 2}"
                    for h in range(8):
                        g = gidx
                        gidx += 1
                        pr, bs = h // 2, (h % 2) * 64
                        nj = 4 * Q + 4
                        gstate = {"first": True}
                        for j in range(nj):
                            b = len(items) % 2
                            jj = j - 4 * Q
                            c0 = max(0, jj) * 128

                            def s0(j=j, c0=c0, b=b, pr=pr, bs=bs):
                                S.op("pe", lambda: T.matmul(pss[b][:, c0:512], lhsT=kt[bs:bs + 64, pr, j * 128:(j + 1) * 128],
                                                            rhs=qt[bs:bs + 64, pr, c0:512], start=True, stop=True),
                                     r=["c_kt", kq], w=[f"c_pss{b}"])

                            def s1(j=j, c0=c0, b=b):
                                S.op("act", lambda: A.activation(out=pT2[b][:, c0:512], in_=pss[b][:, c0:512], func=AF.Exp),
                                     r=[f"c_pss{b}"], w=[f"c_pT{b}"])
                                if b == 0:
                                    S.op("dve", lambda: V.tensor_tensor(out=pT2[b][:, c0:512], in0=pT2[b][:, c0:512], in1=maskT[:, j, c0:512], op=ALU.mult),
                                         r=[f"c_pT{b}", "c_maskT"], w=[f"c_pT{b}"])
                                else:
                                    S.op("pool", lambda: G.tensor_tensor(out=pT2[b][:, c0:512], in0=pT2[b][:, c0:512], in1=maskT[:, j, c0:512], op=ALU.mult),
                                         r=[f"c_pT{b}", "c_maskT"], w=[f"c_pT{b}"])

                            def s2(h=h, j=j, jj=jj, b=b, g=g, gstate=gstate, nj=nj):
                                po = po2[g % 2]
                                kpo = f"c_po{g % 2}"
                                osb = osb2[g % 2]
                                rl = rl2[g % 2]
                                for q4 in range(max(0, jj), 4):
                                    S.op("pe", lambda: T.matmul(po[:, q4 * 65:(q4 + 1) * 65], lhsT=pT2[b][:, q4 * 128:(q4 + 1) * 128],
                                                                rhs=vx[:, j, h * 65:(h + 1) * 65], start=gstate["first"], stop=(j == nj - 1),
                                                                skip_group_check=True),
                                         r=[f"c_pT{b}", "c_vx"], w=[kpo])
                                    gstate["first"] = False
                                if j == nj - 1:
                                    S.op("act", lambda: A.activation(out=osb[:], in_=po[:, 0:260].rearrange("p (a e) -> p a e", e=65), func=AF.Copy),
                                         r=[kpo], w=[f"c_o{g % 2}"])
                                    S.op("dve", lambda: V.reciprocal(out=rl[:], in_=osb[:, :, 64]), r=[f"c_o{g % 2}"], w=[f"c_rl{g % 2}"])
                                    S.op("dve", lambda: V.tensor_tensor(out=yq[:, :, h * 64:(h + 1) * 64], in0=osb[:, :, 0:64],
                                                                        in1=rl[:].unsqueeze(2).to_broadcast([128, 4, 64]), op=ALU.mult),
                                         r=[f"c_o{g % 2}", f"c_rl{g % 2}"], w=[kyq])
                            items.append([s0, s1, s2])
                    run_pipeline(items, 3)
                    for cch in range(4):
                        for q4 in range(4):
                            S.op("pe", lambda: T.transpose(out=ptr[:, q4, :], in_=yq[:, q4, cch * 128:(cch + 1) * 128], identity=ident[:]),
                                 r=[kyq], w=["c_ptr"])
                        ys = yts[cch % 2]
                        S.op("act", lambda: A.activation(out=ys[:], in_=ptr[:].rearrange("p a b -> p (a b)"), func=AF.Copy), r=["c_ptr"], w=[f"c_yts{cch % 2}"])
                        S.dma("act", YT[8 + cch, :, Q * 512:(Q + 1) * 512], ys[:], r=[f"c_yts{cch % 2}"], w=["YT"])
                S.barrier()
                ph.close()

            for ph in ([ExitStack()] if 'pm' in PH else []):
                wbr = [sb(ph, f"m_wbr{i}", [128, 4, D], BF16) for i in range(3)]
                wo = sb(ph, "m_wo", [128, 8, D], BF16)
                yt2 = [sb(ph, f"m_yt{i}", [128, 12, 512], BF16) for i in range(2)]
                gt2 = [sb(ph, f"m_gt{i}", [128, 24, 512], BF16) for i in range(2)]
                m0_2 = [sb(ph, f"m_m0{i}", [128, 512], F32) for i in range(2)]
                m1_2 = [sb(ph, f"m_m1{i}", [128, 512], F32) for i in range(2)]
                m2_2 = [sb(ph, f"m_m2{i}", [128, 512], F32) for i in range(2)]
                mT = sb(ph, "m_mT", [128, 8, 512], BF16)
                xb2 = [sb(ph, f"m_x{i}", [128, D], F32) for i in range(2)]
                ps3 = [ps(ph, f"m_ps{i}", [128, 512], F32) for i in range(6)]
                pso = [ps(ph, f"m_pso{i}", [128, 512], F32) for i in range(2)]
                for i in range(3):
                    load_w_bf16(wbr[i][:], w_br[i][l].rearrange("(c p) n -> p c n", p=128), "m_w")
                load_w_bf16(wo[:], w_out[l].rearrange("(c p) n -> p c n", p=128), "m_w")
                io = 0
                for Tq in range(NQ):
                    yt = yt2[Tq % 2]
                    gt = gt2[Tq % 2]
                    kin = f"m_in{Tq % 2}"
                    for b3 in range(3):
                        S.dma("sp", yt[:, b3 * 4:(b3 + 1) * 4, :], YT[b3 * 4:(b3 + 1) * 4, :, Tq * 512:(Tq + 1) * 512].rearrange("j p t -> p j t"), r=["YT"], w=[kin])
                        S.dma("sp", gt[:, b3 * 8:(b3 + 1) * 8, :], GT[b3 * 8:(b3 + 1) * 8, :, Tq * 512:(Tq + 1) * 512].rearrange("j p t -> p j t"), r=["GT"], w=[kin])
                    for fc in range(8):
                        fp_ = fc % 2
                        m0, m1, m2 = m0_2[fp_], m1_2[fp_], m2_2[fp_]
                        for br in range(3):
                            pb3 = fp_ * 3 + br
                            for kc in range(4):
                                S.op("pe", lambda: T.matmul(ps3[pb3][:], lhsT=wbr[br][:, kc, fc * 128:(fc + 1) * 128], rhs=yt[:, br * 4 + kc, :],
                                                            start=(kc == 0), stop=(kc == 3)), r=["m_w", kin], w=[f"m_ps{pb3}"])
                        S.op("dve", lambda: V.tensor_tensor(out=m0[:], in0=ps3[fp_ * 3][:], in1=gt[:, fc, :], op=ALU.mult), r=[f"m_ps{fp_ * 3}", kin], w=[f"m_m0{fp_}"])
                        S.op("dve", lambda: V.tensor_tensor(out=m1[:], in0=ps3[fp_ * 3 + 1][:], in1=gt[:, 8 + fc, :], op=ALU.mult), r=[f"m_ps{fp_ * 3 + 1}", kin], w=[f"m_m1{fp_}"])
                        S.op("dve", lambda: V.tensor_tensor(out=m2[:], in0=ps3[fp_ * 3 + 2][:], in1=gt[:, 16 + fc, :], op=ALU.mult), r=[f"m_ps{fp_ * 3 + 2}", kin], w=[f"m_m2{fp_}"])
                        S.op("pool", lambda: G.tensor_tensor(out=m0[:], in0=m0[:], in1=m1[:], op=ALU.add), r=[f"m_m0{fp_}", f"m_m1{fp_}"], w=[f"m_m0{fp_}"])
                        S.op("pool", lambda: G.tensor_tensor(out=mT[:, fc, :], in0=m0[:], in1=m2[:], op=ALU.add), r=[f"m_m0{fp_}", f"m_m2{fp_}"], w=["m_mT"])
                    for tt in range(4):
                        t = Tq * 4 + tt
                        xt = xb2[t % 2]
                        kx = f"m_x{t % 2}"
                        S.dma("sp", xt[:], xsrc[t * 128:(t + 1) * 128, :], r=[f"xr{t}"], w=[kx])
                        for nb in range(2):
                            pq = pso[io % 2]
                            kpq = f"m_pso{io % 2}"
                            io += 1
                            for fc in range(8):
                                S.op("pe", lambda: T.matmul(pq[:], lhsT=mT[:, fc, tt * 128:(tt + 1) * 128], rhs=wo[:, fc, nb * 512:(nb + 1) * 512],
                                                            start=(fc == 0), stop=(fc == 7)), r=["m_mT", "m_w"], w=[kpq])
                            S.op("dve", lambda: V.tensor_tensor(out=xt[:, nb * 512:(nb + 1) * 512], in0=pq[:], in1=xt[:, nb * 512:(nb + 1) * 512], op=ALU.add),
                                 r=[kpq, kx], w=[kx])
                        S.dma("sp", xres[t * 128:(t + 1) * 128, :], xt[:], r=[kx], w=[f"xr{t}"])
                S.barrier()
                ph.close()

            for ph in ([ExitStack()] if 'pf' in PH else []):
                wu = sb(ph, "f_wu", [128, 8, 4 * D], BF16)
                wd = sb(ph, "f_wd", [128, 32, D], BF16)
                xb4 = [sb(ph, f"f_x{i}", [128, D], F32) for i in range(4)]
                hb = sb(ph, "f_hb", [128, D], BF16)
                junk = sb(ph, "f_junk", [128, D], BF16)
                ssq = sb(ph, "f_ssq", [128, 8], F32)
                rstd = sb(ph, "f_rstd", [128, 8], F32)
                hmT2 = [sb(ph, f"f_hmT{i}", [128, 8, 256], BF16) for i in range(2)]
                rr2 = [sb(ph, f"f_r{i}", [128, 256], F32) for i in range(2)]
                uT = sb(ph, "f_uT", [128, 32, 256], BF16)
                pst2 = [ps(ph, f"f_pst{i}", [128, 4, 128], BF16) for i in range(2)]
                psu = [ps(ph, f"f_psu{i}", [128, 256], F32) for i in range(2)]
                psd = [ps(ph, f"f_psd{i}", [128, 512], F32) for i in range(2)]
                for c in range(8):
                    load_w_bf16(wu[:, c, :], w_up[l, c * 128:(c + 1) * 128, :], "f_wu")
                for c4 in range(4):
                    load_w_bf16(wd[:, c4 * 8:(c4 + 1) * 8, :], w_down[l, c4 * 1024:(c4 + 1) * 1024, :].rearrange("(c p) n -> p c n", p=128), "f_wd")
                iof = [0]
                items = []
                for tg in range(NT // 2):
                    def s0(tg=tg):
                        hm = hmT2[tg % 2]
                        for t2 in range(2):
                            t = tg * 2 + t2
                            xi = (tg % 2) * 2 + t2
                            xt = xb4[xi]
                            kx = f"f_x{xi}"
                            S.dma("sp", xt[:], xres[t * 128:(t + 1) * 128, :], r=[f"xr{t}"], w=[kx])
                            norm_transpose(xt[:], kx, gM, l, hb, pst2, lambda c, t2=t2, hm=hm: hm[:, c, t2 * 128:(t2 + 1) * 128], f"f_hmT{tg % 2}",
                                           ssq, rstd, junk, "f")

                    def s1(tg=tg):
                        hm = hmT2[tg % 2]
                        khm = f"f_hmT{tg % 2}"
                        for fc in range(32):
                            pu = psu[fc % 2]
                            for c in range(8):
                                S.op("pe", lambda: T.matmul(pu[:], lhsT=wu[:, c, fc * 128:(fc + 1) * 128], rhs=hm[:, c, :], start=(c == 0), stop=(c == 7)),
                                     r=["f_wu", khm], w=[f"f_psu{fc % 2}"])
                            rr = rr2[fc % 2]
                            S.op("act", lambda: A.activation(out=rr[:], in_=pu[:], func=AF.Relu), r=[f"f_psu{fc % 2}"], w=[f"f_r{fc % 2}"])
                            if fc % 2 == 0:
                                S.op("pool", lambda: G.tensor_tensor(out=uT[:, fc, :], in0=rr[:], in1=rr[:], op=ALU.mult), r=[f"f_r{fc % 2}"], w=["f_uT"])
                            else:
                                S.op("dve", lambda: V.tensor_tensor(out=uT[:, fc, :], in0=rr[:], in1=rr[:], op=ALU.mult), r=[f"f_r{fc % 2}"], w=["f_uT"])
                        for t2 in range(2):
                            t = tg * 2 + t2
                            xi = (tg % 2) * 2 + t2
                            xt = xb4[xi]
                            kx = f"f_x{xi}"
                            for nb in range(2):
                                pq = psd[iof[0] % 2]
                                kpq = f"f_psd{iof[0] % 2}"
                                iof[0] += 1
                                for fc in range(32):
                                    S.op("pe", lambda: T.matmul(pq[:], lhsT=uT[:, fc, t2 * 128:(t2 + 1) * 128], rhs=wd[:, fc, nb * 512:(nb + 1) * 512],
                                                                start=(fc == 0), stop=(fc == 31)), r=["f_uT", "f_wd"], w=[kpq])
                                S.op("dve", lambda: V.tensor_tensor(out=xt[:, nb * 512:(nb + 1) * 512], in0=pq[:], in1=xt[:, nb * 512:(nb + 1) * 512], op=ALU.add),
                                     r=[kpq, kx], w=[kx])
                            S.dma("sp", xres[t * 128:(t + 1) * 128, :], xt[:], r=[kx], w=[f"xr{t}"])
                    items.append([s0, s1])
                run_pipeline(items, 2)
                S.barrier()
                ph.close()

            for ph in ([ExitStack()] if 'pp' in PH else []):
                wg = sb(ph, "e_wg", [128, 8, D], BF16)
                wp = sb(ph, "e_wp", [128, 2, D], BF16)
                xb3 = [sb(ph, f"e_x{i}", [128, D], F32) for i in range(3)]
                pb3_ = [sb(ph, f"e_p{i}", [128, 256], F32) for i in range(3)]
                pbb = sb(ph, "e_pbb", [128, 256], BF16)
                hb = sb(ph, "e_hb", [128, D], BF16)
                junk = sb(ph, "e_junk", [128, D], BF16)
                ssq = sb(ph, "e_ssq", [128, 8], F32)
                rstd = sb(ph, "e_rstd", [128, 8], F32)
                hpT2 = [sb(ph, f"e_hpT{i}", [128, 8, 128], BF16) for i in range(2)]
                pT2 = [sb(ph, f"e_pT{i}", [128, 2, 128], BF16) for i in range(2)]
                sg2 = [sb(ph, f"e_sg{i}", [128, 512], F32) for i in range(2)]
                tm2 = [sb(ph, f"e_tm{i}", [128, 512], F32) for i in range(2)]
                pst2 = [ps(ph, f"e_pst{i}", [128, 4, 128], BF16) for i in range(2)]
                ptp = ps(ph, "e_ptp", [128, 2, 128], BF16)
                psg = [ps(ph, f"e_psg{i}", [128, 512], F32) for i in range(2)]
                psp = [ps(ph, f"e_psp{i}", [128, 512], F32) for i in range(2)]
                load_w_bf16(wg[:], w_ple_gate[l].rearrange("(c p) n -> p c n", p=128), "e_w")
                load_w_bf16(wp[:], w_ple_proj[l].rearrange("(c p) n -> p c n", p=128), "e_w")
                dst = out_d if l == depth - 1 else xres
                kdst = "out"
                ioe = [0]
                items = []
                for t in range(NT):
                    def s0(t=t):
                        xt = xb3[t % 3]
                        kx = f"e_x{t % 3}"
                        pb = pb3_[t % 3]
                        hp = hpT2[t % 2]
                        S.dma("sp", xt[:], xres[t * 128:(t + 1) * 128, :], r=[f"xr{t}"], w=[kx])
                        S.dma("sp", pb[:], p_in[l, t * 128:(t + 1) * 128, :], w=[f"e_p{t % 3}"])
                        norm_transpose(xt[:], kx, gP, l, hb, pst2, lambda c, hp=hp: hp[:, c, :], f"e_hpT{t % 2}", ssq, rstd, junk, "e")
                        S.op("pool", lambda: G.tensor_copy(out=pbb[:], in_=pb[:]), r=[f"e_p{t % 3}"], w=["e_pbb"])
                        for c in range(2):
                            S.op("pe", lambda: T.transpose(out=ptp[:, c, :], in_=pbb[:, c * 128:(c + 1) * 128], identity=ident[:]), r=["e_pbb"], w=["e_ptp"])
                        S.op("act", lambda: A.activation(out=pT2[t % 2][:], in_=ptp[:], func=AF.Copy), r=["e_ptp"], w=[f"e_pT{t % 2}"])

                    def s1(t=t):
                        xt = xb3[t % 3]
                        kx = f"e_x{t % 3}"
                        hp = hpT2[t % 2]
                        pT = pT2[t % 2]
                        for nb in range(2):
                            b = ioe[0] % 2
                            ioe[0] += 1
                            for c in range(8):
                                S.op("pe", lambda: T.matmul(psg[b][:], lhsT=hp[:, c, :], rhs=wg[:, c, nb * 512:(nb + 1) * 512], start=(c == 0), stop=(c == 7)),
                                     r=[f"e_hpT{t % 2}", "e_w"], w=[f"e_psg{b}"])
                            for c in range(2):
                                S.op("pe", lambda: T.matmul(psp[b][:], lhsT=pT[:, c, :], rhs=wp[:, c, nb * 512:(nb + 1) * 512], start=(c == 0), stop=(c == 1)),
                                     r=[f"e_pT{t % 2}", "e_w"], w=[f"e_psp{b}"])
                            S.op("act", lambda: A.activation(out=sg2[b][:], in_=psg[b][:], func=AF.Sigmoid), r=[f"e_psg{b}"], w=[f"e_sg{b}"])
                            S.op("dve", lambda: V.tensor_tensor(out=tm2[b][:], in0=psp[b][:], in1=sg2[b][:], op=ALU.mult), r=[f"e_psp{b}", f"e_sg{b}"], w=[f"e_tm{b}"])
                            S.op("pool", lambda: G.tensor_tensor(out=xt[:, nb * 512:(nb + 1) * 512], in0=xt[:, nb * 512:(nb + 1) * 512], in1=tm2[b][:], op=ALU.add),
                                 r=[f"e_tm{b}", kx], w=[kx])
                        S.dma("sp", dst[t * 128:(t + 1) * 128, :], xt[:], r=[kx], w=[kdst if l == depth - 1 else f"xr{t}"])
                    items.append([s0, s1])
                run_pipeline(items, 2)
                S.barrier()
                ph.close()
        S.barrier()
        build_nc.ninstr = dict(S.ninstr)
    return nc


def _host_layout(inputs, b):
    f32 = np.float32
    m = {}
    m["x"] = np.ascontiguousarray(inputs["x"][b], dtype=f32)
    m["p"] = np.ascontiguousarray(inputs["p"][:, b], dtype=f32)
    pos = np.asarray(inputs["positions"][b]).astype(np.int32)
    m["pos"] = np.ascontiguousarray(pos.reshape(NT, 128).T)
    m["invf"] = (10000.0 ** (-np.arange(0, 64, 2, dtype=np.float32) / 64)).astype(f32)
    for k in ("attn_norm", "mlp_norm", "ple_norm"):
        a = np.asarray(inputs[k], dtype=f32)
        m[k + "_l"] = np.ascontiguousarray(a.reshape(DEPTH, 8, 128).transpose(2, 0, 1).reshape(128, DEPTH * 8))
    for k in ("a_q_norm", "a_k_norm", "a_lambda", "a_subln", "c_q_norm", "c_k_norm", "idx_k_norm"):
        m[k] = np.ascontiguousarray(np.asarray(inputs[k], dtype=f32).reshape(-1))
    for k in ("w_in", "w_br_a", "w_br_b", "w_br_c", "w_out", "w_up", "w_down", "w_ple_gate", "w_ple_proj"):
        m[k] = np.ascontiguousarray(inputs[k], dtype=f32)
    return m


def kernel(**inputs):
    nc = build_nc()
    in_maps = [_host_layout(inputs, c % 4) for c in range(4)]
    in_maps = in_maps + in_maps
    res = run_bass_kernel_spmd(nc, in_maps, core_ids=list(range(8)))
    out = np.stack([np.asarray(res.results[b]["out"], dtype=np.float32) for b in range(4)], axis=0)
    return out
```

```python
import math
from contextlib import ExitStack

import numpy as np
import concourse.bass as bass
import concourse.mybir as mybir
from concourse.bass_utils import run_bass_kernel_spmd

F32 = mybir.dt.float32
BF16 = mybir.dt.bfloat16
I32 = mybir.dt.int32
AF = mybir.ActivationFunctionType
ALU = mybir.AluOpType
AX = mybir.AxisListType

DEPTH = 4
SL = 4096
D = 1024
NT = 32
NQ = 8
EPS = 1e-6
W_IN_COLS = 8004
TOPK = 256
NSTEP = 20
NEG = -1.0e30

SEM_EPOCH = 30000
N_DMA_SEMS = 24


class Sched:
    def __init__(self, nc, stack):
        self.nc = nc
        self.stack = stack
        self.engs = {"pe": nc.tensor, "act": nc.scalar, "dve": nc.vector,
                     "pool": nc.gpsimd, "sp": nc.sync}
        self.esem = {}
        self.ecount = {}
        self.n_sems = 0
        for e in self.engs:
            self._new_esem(e)
        self.waited = {e: {} for e in self.engs}
        self.dsems = [[self._sem(f"dma{i}"), 0] for i in range(N_DMA_SEMS)]
        self.dnext = 0
        self.lastw = {}
        self.reads = {}
        self.ninstr = {e: 0 for e in self.engs}

    def _sem(self, name):
        self.n_sems += 1
        return self.stack.enter_context(self.nc.semaphore(name))

    def _new_esem(self, e):
        self.esem[e] = self._sem(f"e_{e}_{self.n_sems}")
        self.ecount[e] = 0

    def _wait(self, e, dep):
        sem, val, tag = dep
        w = self.waited[e]
        k = id(sem)
        if w.get(k, 0) >= val:
            return
        w[k] = val
        self.engs[e].wait_ge(sem, val)
        self.ninstr[e] += 1

    def _collect(self, r, w):
        deps = []
        for k in list(r) + list(w):
            deps.extend(self.lastw.get(k, ()))
        for k in w:
            rd = self.reads.get(k)
            if rd:
                deps.extend(rd.values())
        return deps

    def _commit(self, r, w, dep, merge=()):
        for k in w:
            if k in merge:
                self.lastw[k].append(dep)
            else:
                self.lastw[k] = [dep]
                self.reads[k] = {}
        for k in r:
            if k in w:
                continue
            self.reads.setdefault(k, {})[dep[2]] = dep

    def op(self, e, fn, r=(), w=()):
        if self.ecount[e] >= SEM_EPOCH:
            self._new_esem(e)
        for d in self._collect(r, w):
            if e == "pe" and d[2] == "pe":
                continue
            self._wait(e, d)
        ins = fn()
        self.ecount[e] += 1
        ins.then_inc(self.esem[e], 1)
        dep = (self.esem[e], self.ecount[e], e)
        self.ninstr[e] += 1
        self._commit(r, w, dep)
        return ins

    def dma(self, q, out, in_, r=(), w=()):
        slot = self.dsems[self.dnext]
        self.dnext = (self.dnext + 1) % len(self.dsems)
        sem = slot[0]
        if slot[1] > 0:
            self._wait(q, (sem, slot[1], "dma"))
        deps = []
        merge = set()
        for k in r:
            deps.extend(self.lastw.get(k, ()))
        for k in w:
            lw = self.lastw.get(k, ())
            rd = self.reads.get(k) or {}
            if lw and not rd and all(d[2].startswith("dma") for d in lw):
                merge.add(k)
            else:
                deps.extend(lw)
                deps.extend(rd.values())
        for d in deps:
            self._wait(q, d)
        ins = self.engs[q].dma_start(out=out, in_=in_)
        slot[1] += 16
        ins.then_inc(sem, 16)
        dep = (sem, slot[1], f"dma{id(sem)}")
        self.ninstr[q] += 1
        self._commit(r, w, dep, merge)
        return ins

    def barrier(self):
        for e in self.engs:
            for o in self.engs:
                if o != e and self.ecount[o] > 0:
                    self._wait(e, (self.esem[o], self.ecount[o], o))
            for sem, val in self.dsems:
                if val:
                    self._wait(e, (sem, val, "dma"))


def run_pipeline(items, nstage):
    n = len(items)
    for step in range(n + nstage - 1):
        for k in range(nstage - 1, -1, -1):
            i = step - k
            if 0 <= i < n and items[i][k] is not None:
                items[i][k]()


def build_nc(depth=DEPTH, debug=False, phases=None):
    PH = set(phases) if phases is not None else {'p1', 'pa', 'pb', 'pc', 'pm', 'pf', 'pp'}
    nc = bass.Bass("TRN2", target_bir_lowering=False)

    def din(name, shape, dt=F32):
        return nc.dram_tensor(name, list(shape), dt, kind="ExternalInput").ap()

    def dscr(name, shape, dt):
        kind = "ExternalOutput" if debug else "Internal"
        return nc.dram_tensor(name, list(shape), dt, kind=kind).ap()

    x_in = din("x", [SL, D])
    p_in = din("p", [DEPTH, SL, 256])
    pos_in = din("pos", [128, NT], I32)
    invf_in = din("invf", [32])
    attn_norm = din("attn_norm_l", [128, DEPTH * 8])
    mlp_norm = din("mlp_norm_l", [128, DEPTH * 8])
    ple_norm = din("ple_norm_l", [128, DEPTH * 8])
    w_in = din("w_in", [DEPTH, D, W_IN_COLS])
    a_q_norm = din("a_q_norm", [DEPTH * 64])
    a_k_norm = din("a_k_norm", [DEPTH * 64])
    a_lambda = din("a_lambda", [DEPTH * 256])
    a_subln = din("a_subln", [DEPTH * 128])
    c_q_norm = din("c_q_norm", [DEPTH * 64])
    c_k_norm = din("c_k_norm", [DEPTH * 64])
    idx_k_norm = din("idx_k_norm", [DEPTH * 64])
    w_br = [din("w_br_a", [DEPTH, 512, D]), din("w_br_b", [DEPTH, 512, D]), din("w_br_c", [DEPTH, 512, D])]
    w_out = din("w_out", [DEPTH, D, D])
    w_up = din("w_up", [DEPTH, D, 4 * D])
    w_down = din("w_down", [DEPTH, 4 * D, D])
    w_ple_gate = din("w_ple_gate", [DEPTH, D, D])
    w_ple_proj = din("w_ple_proj", [DEPTH, 256, D])
    out_d = nc.dram_tensor("out", [SL, D], F32, kind="ExternalOutput").ap()

    xres = dscr("xres", [SL, D], F32)
    QT = {k: dscr("QT_" + k, [4, 128, SL], BF16) for k in "abc"}
    KT = {k: dscr("KT_" + k, [4, 128, SL], BF16) for k in "abc"}
    VX = {"a": dscr("V_a", [SL, 4 * 129], BF16), "b": dscr("V_b", [SL, 8 * 65], BF16),
          "c": dscr("V_c", [SL, 8 * 65], BF16)}
    IQT = dscr("IQT", [2, 128, SL], BF16)
    IKT = dscr("IKT", [128, SL], BF16)
    GT = dscr("GT", [24, 128, SL], BF16)
    YT = dscr("YT", [12, 128, SL], BF16)

    with ExitStack() as st:
        S = Sched(nc, st)
        E = st.enter_context
        V = nc.vector
        A = nc.scalar
        G = nc.gpsimd
        T = nc.tensor

        uid = [0]

        def sb(stack, name, shape, dt):
            uid[0] += 1
            return stack.enter_context(nc.sbuf_tensor(f"s{uid[0]}_{name}", list(shape), dt))

        def ps(stack, name, shape, dt):
            uid[0] += 1
            return stack.enter_context(nc.psum_tensor(f"q{uid[0]}_{name}", list(shape), dt))

        ident = sb(st, "ident", [128, 128], BF16)
        ntri = sb(st, "ntri", [128, 128], BF16)
        nones = sb(st, "nones", [128, 128], BF16)
        strict = sb(st, "strict", [128, 128], BF16)
        cos_t = sb(st, "cos_t", [128, NT, 32], F32)
        sin_t = sb(st, "sin_t", [128, NT, 32], F32)
        gA = sb(st, "gA", [128, DEPTH * 8], F32)
        gM = sb(st, "gM", [128, DEPTH * 8], F32)
        gP = sb(st, "gP", [128, DEPTH * 8], F32)
        aqn = sb(st, "aqn", [128, DEPTH * 64], F32)
        akn = sb(st, "akn", [128, DEPTH * 64], F32)
        cqn = sb(st, "cqn", [128, DEPTH * 64], F32)
        ckn = sb(st, "ckn", [128, DEPTH * 64], F32)
        ikn = sb(st, "ikn", [128, DEPTH * 64], F32)
        subg = sb(st, "subg", [128, DEPTH * 128], F32)
        lam_t = sb(st, "lam_t", [128, DEPTH], F32)
        nlam_t = sb(st, "nlam_t", [128, DEPTH], F32)
        iw_all = sb(st, "iw_all", [128, NT, 4], F32)
        pw2 = sb(st, "pw2", [128, NSTEP + 1], F32)

        CONST = ["const"]

        with ExitStack() as ph:
            posi = sb(ph, "posi", [128, NT], I32)
            posf = sb(ph, "posf", [128, NT], F32)
            invf = sb(ph, "invf", [128, 32], F32)
            ang = sb(ph, "ang", [128, NT, 32], F32)
            t0 = sb(ph, "t0", [128, NT, 32], F32)
            t1 = sb(ph, "t1", [128, NT, 32], F32)
            lpb = sb(ph, "lpb", [128, DEPTH * 256], F32)
            tmp64 = sb(ph, "tmp64", [128, 64], F32)
            s12 = sb(ph, "s12", [128, 2], F32)
            K0 = ["p0"]
            S.op("pool", lambda: G.memset(ident[:], 1.0), w=CONST)
            S.op("pool", lambda: G.affine_select(out=ident[:], in_=ident[:], pattern=[[-1, 128]],
                                                 compare_op=ALU.is_equal, fill=0.0, base=0, channel_multiplier=1),
                 r=CONST, w=CONST)
            S.op("pool", lambda: G.memset(nones[:], -1.0), w=CONST)
            for i_ in range(NSTEP + 1):
                S.op("pool", lambda: G.memset(pw2[:, i_:i_ + 1], float(2.0 ** -(i_ + 1))), w=CONST)
            S.op("pool", lambda: G.memset(ntri[:], -1.0), w=CONST)
            S.op("pool", lambda: G.affine_select(out=ntri[:], in_=ntri[:], pattern=[[-1, 128]],
                                                 compare_op=ALU.is_ge, fill=0.0, base=0, channel_multiplier=1),
                 r=CONST, w=CONST)
            S.op("pool", lambda: G.memset(strict[:], 1.0), w=CONST)
            S.op("pool", lambda: G.affine_select(out=strict[:], in_=strict[:], pattern=[[1, 128]],
                                                 compare_op=ALU.is_gt, fill=0.0, base=0, channel_multiplier=-1),
                 r=CONST, w=CONST)
            for dst, src in [(gA, attn_norm), (gM, mlp_norm), (gP, ple_norm)]:
                S.dma("sp", dst[:], src[:, :], w=CONST)
            for dst, src in [(aqn, a_q_norm), (akn, a_k_norm), (cqn, c_q_norm), (ckn, c_k_norm),
                             (ikn, idx_k_norm), (subg, a_subln), (lpb, a_lambda), (invf, invf_in)]:
                S.dma("sp", dst[:], src.partition_broadcast(128), w=CONST + K0)
            S.dma("sp", posi[:], pos_in[:, :], w=K0)
            S.op("dve", lambda: V.tensor_scalar(out=aqn[:], in0=aqn[:], scalar1=0.125, scalar2=None, op0=ALU.mult), r=CONST, w=CONST)
            S.op("dve", lambda: V.tensor_scalar(out=cqn[:], in0=cqn[:], scalar1=0.125, scalar2=None, op0=ALU.mult), r=CONST, w=CONST)
            for l in range(DEPTH):
                li = 0.8 - 0.6 * math.exp(-0.3 * l)
                S.op("dve", lambda: V.tensor_scalar(out=subg[:, l * 128:(l + 1) * 128], in0=subg[:, l * 128:(l + 1) * 128],
                                                    scalar1=float(1.0 - li), scalar2=None, op0=ALU.mult), r=CONST, w=CONST)
                for j in range(2):
                    S.op("dve", lambda: V.tensor_tensor(out=tmp64[:], in0=lpb[:, l * 256 + j * 128: l * 256 + j * 128 + 64],
                                                        in1=lpb[:, l * 256 + j * 128 + 64: l * 256 + j * 128 + 128], op=ALU.mult),
                         r=K0, w=["tmp64"])
                    S.op("dve", lambda: V.tensor_reduce(out=s12[:, j:j + 1], in_=tmp64[:], axis=AX.X, op=ALU.add),
                         r=["tmp64"], w=["s12"])
                S.op("act", lambda: A.activation(out=s12[:], in_=s12[:], func=AF.Exp), r=["s12"], w=["s12"])
                S.op("dve", lambda: V.tensor_tensor(out=lam_t[:, l:l + 1], in0=s12[:, 0:1], in1=s12[:, 1:2], op=ALU.subtract),
                     r=["s12"], w=CONST)
                S.op("dve", lambda: V.tensor_scalar(out=lam_t[:, l:l + 1], in0=lam_t[:, l:l + 1], scalar1=float(li), scalar2=None, op0=ALU.add),
                     r=CONST, w=CONST)
                S.op("dve", lambda: V.tensor_scalar(out=nlam_t[:, l:l + 1], in0=lam_t[:, l:l + 1], scalar1=-1.0, scalar2=None, op0=ALU.mult),
                     r=CONST, w=CONST)
            S.op("dve", lambda: V.tensor_copy(out=posf[:], in_=posi[:]), r=K0, w=["posf"])
            S.op("dve", lambda: V.tensor_tensor(out=ang[:], in0=posf[:].unsqueeze(2).to_broadcast([128, NT, 32]),
                                                in1=invf[:].unsqueeze(1).to_broadcast([128, NT, 32]), op=ALU.mult),
                 r=["posf"] + K0, w=["ang"])
            MAGIC = 12582912.0
            C1 = 6.28125
            C2 = 2.0 * math.pi - C1
            PI_LO = 3.1415925
            S.op("dve", lambda: V.tensor_scalar(out=t0[:], in0=ang[:], scalar1=float(1.0 / (2.0 * math.pi)), scalar2=None, op0=ALU.mult), r=["ang"], w=["t0"])
            S.op("dve", lambda: V.tensor_scalar(out=t0[:], in0=t0[:], scalar1=MAGIC, scalar2=None, op0=ALU.add), r=["t0"], w=["t0"])
            S.op("dve", lambda: V.tensor_scalar(out=t0[:], in0=t0[:], scalar1=-MAGIC, scalar2=None, op0=ALU.add), r=["t0"], w=["t0"])
            S.op("dve", lambda: V.scalar_tensor_tensor(out=t1[:], in0=t0[:], scalar=-C1, in1=ang[:], op0=ALU.mult, op1=ALU.add), r=["t0", "ang"], w=["t1"])
            S.op("dve", lambda: V.scalar_tensor_tensor(out=t1[:], in0=t0[:], scalar=-C2, in1=t1[:], op0=ALU.mult, op1=ALU.add), r=["t0", "t1"], w=["t1"])
            S.op("dve", lambda: V.tensor_scalar(out=t0[:], in0=t1[:], scalar1=PI_LO, scalar2=-PI_LO, op0=ALU.min, op1=ALU.max), r=["t1"], w=["t0"])
            S.op("act", lambda: A.activation(out=sin_t[:], in_=t0[:], func=AF.Sin), r=["t0"], w=CONST)
            S.op("dve", lambda: V.tensor_scalar(out=t1[:], in0=t1[:], scalar1=float(math.pi / 2), scalar2=None, op0=ALU.add), r=["t1"], w=["t1"])
            S.op("dve", lambda: V.tensor_scalar(out=t0[:], in0=t1[:], scalar1=float(math.pi), scalar2=float(-2.0 * math.pi), op0=ALU.is_gt, op1=ALU.mult), r=["t1", "const"], w=["t0"])
            S.op("dve", lambda: V.tensor_tensor(out=t1[:], in0=t1[:], in1=t0[:], op=ALU.add), r=["t0", "t1"], w=["t1"])
            S.op("dve", lambda: V.tensor_scalar(out=t1[:], in0=t1[:], scalar1=PI_LO, scalar2=-PI_LO, op0=ALU.min, op1=ALU.max), r=["t1"], w=["t1"])
            S.op("act", lambda: A.activation(out=cos_t[:], in_=t1[:], func=AF.Sin), r=["t1"], w=CONST)
            S.barrier()

        def rms_rows(xt, kx, ssq, rstd, junk, kpre):
            S.op("act", lambda: A.activation(out=junk[:], in_=xt, func=AF.Square, accum_out=ssq[:, 0:1]),
                 r=[kx], w=[kpre + "junk", kpre + "ssq"])
            S.op("act", lambda: A.activation(out=rstd[:, 0:1], in_=ssq[:, 0:1], func=AF.Sqrt, bias=EPS, scale=1.0 / D),
                 r=[kpre + "ssq"], w=[kpre + "rstd"])
            S.op("dve", lambda: V.reciprocal(out=rstd[:, 0:1], in_=rstd[:, 0:1]), r=[kpre + "rstd"], w=[kpre + "rstd"])

        def norm_transpose(xt, kx, gain, l, hb, pst2, dst_fn, kdst, ssq, rstd, junk, kpre):
            rms_rows(xt, kx, ssq, rstd, junk, kpre)
            S.op("dve", lambda: V.tensor_scalar(out=hb[:], in0=xt, scalar1=rstd[:, 0:1], scalar2=None, op0=ALU.mult),
                 r=[kx, kpre + "rstd"], w=[kpre + "hb"])
            for half in range(2):
                pt = pst2[half]
                for c4 in range(4):
                    c = half * 4 + c4
                    S.op("pe", lambda: T.transpose(out=pt[:, c4, :], in_=hb[:, c * 128:(c + 1) * 128], identity=ident[:]),
                         r=[kpre + "hb"], w=[kpre + f"pst{half}"])
                for c4 in range(4):
                    c = half * 4 + c4
                    if c4 % 2 == 0:
                        S.op("dve", lambda: V.tensor_scalar(out=dst_fn(c), in0=pt[:, c4, :], scalar1=gain[:, l * 8 + c:l * 8 + c + 1],
                                                            scalar2=None, op0=ALU.mult),
                             r=[kpre + f"pst{half}"], w=[kdst])
                    else:
                        S.op("act", lambda: A.activation(out=dst_fn(c), in_=pt[:, c4, :], func=AF.Copy,
                                                         scale=gain[:, l * 8 + c:l * 8 + c + 1]),
                             r=[kpre + f"pst{half}"], w=[kdst])

        def load_w_bf16(dst, src, key):
            S.dma("pool", dst, src, w=[key])

        for l in range(depth):
            xsrc = x_in if l == 0 else xres
            lam_init = 0.8 - 0.6 * math.exp(-0.3 * l)

            for ph in ([ExitStack()] if 'p1' in PH else []):
                hT = sb(ph, "hT", [128, 8, SL], BF16)
                xb2 = [sb(ph, f"p1x{i}", [128, D], F32) for i in range(2)]
                hb = sb(ph, "p1hb", [128, D], BF16)
                junk = sb(ph, "p1junk", [128, D], BF16)
                ssq = sb(ph, "p1ssq", [128, 8], F32)
                rstd = sb(ph, "p1rstd", [128, 8], F32)
                pst2 = [ps(ph, f"p1pst{i}", [128, 4, 128], BF16) for i in range(2)]
                psm = [ps(ph, f"p1ps{i}", [128, 512], F32) for i in range(2)]
                wblk = [sb(ph, f"p1w{i}", [128, 8, 512], BF16) for i in range(2)]
                sq_2 = [sb(ph, f"p1sq{i}", [128, 512], F32) for i in range(2)]
                xn_2 = [sb(ph, f"p1xn{i}", [128, 512], F32) for i in range(2)]
                xg_2 = [sb(ph, f"p1xg{i}", [128, 512], F32) for i in range(2)]
                ra_2 = [sb(ph, f"p1ra{i}", [128, 256], F32) for i in range(2)]
                rb_2 = [sb(ph, f"p1rb{i}", [128, 256], F32) for i in range(2)]
                ob_2 = [sb(ph, f"p1ob{i}", [128, 512], BF16) for i in range(2)]
                ssq_2 = [sb(ph, f"p1ssq{i}", [128, 8], F32) for i in range(2)]
                rstd_2 = [sb(ph, f"p1rstd{i}", [128, 8], F32) for i in range(2)]
                stg = [sb(ph, f"p1stg{i}", [128, 4, 512], BF16) for i in range(2)]
                vst = [sb(ph, f"p1vst{i}", [128, 520], BF16) for i in range(4)]
                gst = [sb(ph, f"p1gst{i}", [128, 512], BF16) for i in range(2)]

                for t in range(NT):
                    xt = xb2[t % 2]
                    kx = f"p1x{t % 2}"
                    S.dma("sp", xt[:], xsrc[t * 128:(t + 1) * 128, :], r=[f"xr{t}"], w=[kx])
                    norm_transpose(xt[:], kx, gA, l, hb, pst2,
                                   lambda c, t=t: hT[:, c, t * 128:(t + 1) * 128], f"hT{t}", ssq, rstd, junk, "p1")
                for i in range(4):
                    S.op("pool", lambda: G.memset(vst[i][:], 1.0), w=[f"vst{i}"])

                blocks = []
                names = ["aq", "ak", "av", "bq", "bk", "bv", "cq", "ck", "cv"]
                for i, nm in enumerate(names):
                    blocks.append((nm, i * 512, 512))
                blocks.append(("idx", 4608, 324))
                for i in range(6):
                    blocks.append((f"gl{i}", 4932 + i * 512, 512))
                w_l = w_in[l].rearrange("(c p) n -> p c n", p=128)

                def qk_post(pv, nh, gain, do_norm, do_rope, t, outv, scale, kout):
                    n = nh * 64
                    pz = t % 2
                    sq, xn, xg, ra, rb = sq_2[pz], xn_2[pz], xg_2[pz], ra_2[pz], rb_2[pz]
                    ssq, rstd = ssq_2[pz], rstd_2[pz]
                    ksq, kxn, kxg, kra, krb, kss, krs = (f"sq{pz}", f"xn{pz}", f"xg{pz}", f"ra{pz}", f"rb{pz}", f"qssq{pz}", f"qrstd{pz}")
                    if do_norm:
                        sqv = sq[:, 0:n].rearrange("p (h d) -> p h d", d=64)
                        S.op("act", lambda: A.activation(out=sqv, in_=pv, func=AF.Square), r=["psm"], w=[ksq])
                        S.op("dve", lambda: V.tensor_reduce(out=ssq[:, 0:nh], in_=sqv, axis=AX.X, op=ALU.add), r=[ksq], w=[kss])
                        S.op("act", lambda: A.activation(out=rstd[:, 0:nh], in_=ssq[:, 0:nh], func=AF.Sqrt, bias=EPS, scale=1.0 / 64),
                             r=[kss], w=[krs])
                        S.op("dve", lambda: V.reciprocal(out=rstd[:, 0:nh], in_=rstd[:, 0:nh]), r=[krs], w=[krs])
                        xnv = xn[:, 0:n].rearrange("p (h d) -> p h d", d=64)
                        S.op("dve", lambda: V.tensor_tensor(out=xnv, in0=pv, in1=rstd[:, 0:nh].unsqueeze(2).to_broadcast([128, nh, 64]), op=ALU.mult),
                             r=["psm", krs], w=[kxn])
                        xgv = xg[:, 0:n].rearrange("p (h d) -> p h d", d=64)
                        S.op("pool", lambda: G.tensor_tensor(out=xgv, in0=xnv, in1=gain.unsqueeze(1).to_broadcast([128, nh, 64]), op=ALU.mult),
                             r=[kxn, "const"], w=[kxg])
                    else:
                        xgv = xg[:, 0:n].rearrange("p (h d) -> p h d", d=64)
                        S.op("act", lambda: A.activation(out=xgv, in_=pv, func=AF.Copy, scale=float(scale)), r=["psm"], w=[kxg])
                    if do_rope:
                        x1 = xgv[:, :, 0:32]
                        x2 = xgv[:, :, 32:64]
                        cb = cos_t[:, t, :].unsqueeze(1).to_broadcast([128, nh, 32])
                        sbb = sin_t[:, t, :].unsqueeze(1).to_broadcast([128, nh, 32])
                        rav = ra[:, 0:nh * 32].rearrange("p (h d) -> p h d", d=32)
                        rbv = rb[:, 0:nh * 32].rearrange("p (h d) -> p h d", d=32)
                        S.op("pool", lambda: G.tensor_tensor(out=rav, in0=x1, in1=cb, op=ALU.mult), r=[kxg], w=[kra])
                        S.op("dve", lambda: V.tensor_tensor(out=rbv, in0=x2, in1=sbb, op=ALU.mult), r=[kxg], w=[krb])
                        S.op("dve", lambda: V.tensor_tensor(out=outv[:, :, 0:32], in0=rav, in1=rbv, op=ALU.subtract), r=[kra, krb], w=[kout])
                        S.op("pool", lambda: G.tensor_tensor(out=rav, in0=x2, in1=cb, op=ALU.mult), r=[kxg, kra], w=[kra])
                        S.op("dve", lambda: V.tensor_tensor(out=rbv, in0=x1, in1=sbb, op=ALU.mult), r=[kxg, krb], w=[krb])
                        S.op("dve", lambda: V.tensor_tensor(out=outv[:, :, 32:64], in0=rav, in1=rbv, op=ALU.add), r=[kra, krb], w=[kout])
                    else:
                        S.op("dve", lambda: V.tensor_copy(out=outv, in_=xgv), r=[kxg], w=[kout])

                nblk = len(blocks)
                S.dma("pool", wblk[0][:, :, 0:blocks[0][2]], w_l[:, :, blocks[0][1]:blocks[0][1] + blocks[0][2]], w=["wblk0"])
                items = []

                def prefetch_w(bi):
                    if bi + 1 < nblk:
                        nn = blocks[bi + 1]
                        S.dma("pool", wblk[(bi + 1) % 2][:, :, 0:nn[2]], w_l[:, :, nn[1]:nn[1] + nn[2]], w=[f"wblk{(bi + 1) % 2}"])

                for bi, (nm, c0, n) in enumerate(blocks):
                    wb = wblk[bi % 2]
                    kw = f"wblk{bi % 2}"
                    if nm.startswith("gl"):
                        gi = int(nm[2:])
                        for Tq in range(NQ):
                            for j in range(4):
                                pb_ = len(items) % 2

                                def g0(bi=bi, wb=wb, kw=kw, Tq=Tq, j=j, pb_=pb_, first=(Tq == 0 and j == 0)):
                                    if first:
                                        prefetch_w(bi)
                                    pm = psm[pb_]
                                    for c in range(8):
                                        S.op("pe", lambda: T.matmul(pm[:], lhsT=wb[:, c, j * 128:(j + 1) * 128], rhs=hT[:, c, Tq * 512:(Tq + 1) * 512],
                                                                    start=(c == 0), stop=(c == 7)), r=[kw] + [f"hT{Tq * 4 + i_}" for i_ in range(4)], w=[f"psm{pb_}"])

                                def g1(gi=gi, Tq=Tq, j=j, pb_=pb_):
                                    pm = psm[pb_]
                                    gs = gst[pb_]
                                    S.op("act", lambda: A.activation(out=gs[:], in_=pm[:], func=AF.Sigmoid), r=[f"psm{pb_}"], w=[f"gst{pb_}"])
                                    S.dma("act", GT[gi * 4 + j, :, Tq * 512:(Tq + 1) * 512], gs[:], r=[f"gst{pb_}"], w=["GT"])
                                items.append([g0, g1, None])
                        continue
                    for t in range(NT):
                        pb_ = len(items) % 2

                        def s0(bi=bi, wb=wb, kw=kw, t=t, n=n, pb_=pb_):
                            if t == 0:
                                prefetch_w(bi)
                            pm = psm[pb_]
                            for c in range(8):
                                S.op("pe", lambda: T.matmul(pm[:, 0:n], lhsT=hT[:, c, t * 128:(t + 1) * 128], rhs=wb[:, c, 0:n],
                                                            start=(c == 0), stop=(c == 7)), r=[kw, f"hT{t}"], w=[f"psm{pb_}"])

                        def s1(nm=nm, t=t, pb_=pb_):
                            pm = psm[pb_]
                            kp = f"psm{pb_}"
                            S.lastw["psm"] = S.lastw[kp]
                            S.reads["psm"] = S.reads[kp]
                            ob = ob_2[t % 2]
                            kob = f"ob{t % 2}"
                            if nm in ("aq", "ak", "cq", "ck", "bq", "bk"):
                                pv = pm[:, 0:512].rearrange("p (h d) -> p h d", d=64)
                                obv = ob[:].rearrange("p (h d) -> p h d", d=64)
                                if nm[0] == "b":
                                    qk_post(pv, 8, None, False, False, t, obv, 0.125 if nm == "bq" else 1.0, kob)
                                else:
                                    gain = {"aq": aqn, "ak": akn, "cq": cqn, "ck": ckn}[nm][:, l * 64:(l + 1) * 64]
                                    qk_post(pv, 8, gain, True, True, t, obv, 1.0, kob)
                            elif nm in ("av", "bv", "cv"):
                                nh, dv = (4, 128) if nm == "av" else (8, 64)
                                vi = (t % 2) + (0 if nm == "av" else 2)
                                vs = vst[vi]
                                vv = vs[:, 0:nh * (dv + 1)].rearrange("p (h e) -> p h e", e=dv + 1)[:, :, 0:dv]
                                S.op("act", lambda: A.activation(out=vv, in_=pm[:, 0:512].rearrange("p (h e) -> p h e", e=dv), func=AF.Copy),
                                     r=[kp], w=[f"vst{vi}"])
                                S.dma("act", VX[nm[0]][t * 128:(t + 1) * 128, :], vs[:, 0:nh * (dv + 1)], r=[f"vst{vi}"], w=["VX"])
                            else:
                                obv = ob[:, 0:256].rearrange("p (h d) -> p h d", d=64)
                                qk_post(pm[:, 0:256].rearrange("p (h d) -> p h d", d=64), 4, None, False, True, t, obv, 1.0, kob)
                                obk = ob[:, 256:320].rearrange("p (h d) -> p h d", d=64)
                                qk_post(pm[:, 256:320].rearrange("p (h d) -> p h d", d=64), 1, ikn[:, l * 64:(l + 1) * 64], True, True, t, obk, 1.0, kob)
                                S.op("dve", lambda: V.tensor_copy(out=ob[:, 320:384], in_=ob[:, 256:320]), r=[kob], w=[kob])
                                S.op("act", lambda: A.activation(out=iw_all[:, t, :], in_=pm[:, 320:324], func=AF.Copy), r=[kp], w=["iw_all"])

                        def s2(nm=nm, t=t):
                            if nm in ("av", "bv", "cv"):
                                return
                            ob = ob_2[t % 2]
                            kob = f"ob{t % 2}"
                            sg = stg[(t // 4) % 2]
                            ksg = f"stg{(t // 4) % 2}"
                            tt = t % 4
                            pt = pst2[t % 2]
                            nj_ = 3 if nm == "idx" else 4
                            for j in range(nj_):
                                S.op("pe", lambda: T.transpose(out=pt[:, j, :], in_=ob[:, j * 128:(j + 1) * 128], identity=ident[:]),
                                     r=[kob], w=[f"p1pst{t % 2}"])
                            S.op("act", lambda: A.activation(out=sg[:, 0:nj_, tt * 128:(tt + 1) * 128], in_=pt[:, 0:nj_, :], func=AF.Copy),
                                 r=[f"p1pst{t % 2}"], w=[ksg])
                            if tt == 3:
                                t0_ = (t - 3) * 128
                                if nm == "idx":
                                    S.dma("act", IQT[:, :, t0_:t0_ + 512].rearrange("j p t -> p j t"), sg[:, 0:2, :], r=[ksg], w=["QKT"])
                                    S.dma("act", IKT[:, t0_:t0_ + 512], sg[:, 2, :], r=[ksg], w=["QKT"])
                                else:
                                    dst = (QT if nm[1] == "q" else KT)[nm[0]]
                                    S.dma("act", dst[:, :, t0_:t0_ + 512].rearrange("j p t -> p j t"), sg[:], r=[ksg], w=["QKT"])
                        items.append([s0, s1, s2])
                run_pipeline(items, 3)
                S.barrier()
                ph.close()

            for ph in ([ExitStack()] if 'pa' in PH else []):
                kt = sb(ph, "a_kt", [128, 4, SL], BF16)
                vx = sb(ph, "a_vx", [128, NT, 4 * 129], BF16)
                qt2 = [sb(ph, f"a_qt{i}", [128, 4, 512], BF16) for i in range(2)]
                pT2 = [sb(ph, f"a_pT{i}", [128, 512], BF16) for i in range(2)]
                a_qm = [sb(ph, f"a_qm{i}", [128, 512], BF16) for i in range(2)]
                o2 = [sb(ph, f"a_o{i}", [128, 4, 129], F32) for i in range(2)]
                rl = sb(ph, "a_rl", [128, 8], F32)
                y0 = sb(ph, "a_y0", [128, 4, 128], F32)
                y1 = sb(ph, "a_y1", [128, 4, 128], F32)
                ss = sb(ph, "a_ss", [128, 8], F32)
                yb2 = [sb(ph, f"a_yb{i}", [128, 4, 128], BF16) for i in range(2)]
                yts = [sb(ph, f"a_yts{i}", [128, 512], BF16) for i in range(2)]
                pss = [ps(ph, f"a_pss{i}", [128, 512], F32) for i in range(2)]
                po2 = [ps(ph, f"a_po{i}", [128, 2, 512], F32) for i in range(2)]
                ptr = ps(ph, "a_ptr", [128, 4, 128], BF16)
                S.dma("sp", kt[:], KT["a"].rearrange("j p t -> p j t"), r=["QKT"], w=["a_kt"])
                for v4 in range(4):
                    S.dma("sp", vx[:, v4 * 8:(v4 + 1) * 8, :], VX["a"][v4 * 1024:(v4 + 1) * 1024, :].rearrange("(t p) e -> p t e", p=128), r=["VX"], w=["a_vx"])

                def a_load_q(Q):
                    S.dma("sp", qt2[Q % 2][:], QT["a"][:, :, Q * 512:(Q + 1) * 512].rearrange("j p t -> p j t"), r=["QKT"], w=[f"a_qt{Q % 2}"])

                a_load_q(0)
                items = []
                deferred = []

                def a_epilogue(Q, h, c, po, kpo):
                    for bk in range(2):
                        S.op("act", lambda: A.activation(out=o2[c][:, 2 * bk:2 * bk + 2, :], in_=po[:, bk, 0:258].rearrange("p (r e) -> p r e", e=129), func=AF.Copy),
                             r=[kpo], w=[f"a_o{c}"])
                    if c == 0:
                        return
                    yb = yb2[h % 2]
                    kyb = f"a_yb{h % 2}"
                    S.op("dve", lambda: V.reciprocal(out=rl[:, 0:4], in_=o2[0][:, :, 128]), r=["a_o0"], w=["a_rl"])
                    S.op("dve", lambda: V.reciprocal(out=rl[:, 4:8], in_=o2[1][:, :, 128]), r=["a_o1"], w=["a_rl"])
                    S.op("dve", lambda: V.tensor_scalar(out=rl[:, 4:8], in0=rl[:, 4:8], scalar1=nlam_t[:, l:l + 1], scalar2=None, op0=ALU.mult),
                         r=["a_rl", "const"], w=["a_rl"])
                    S.op("dve", lambda: V.tensor_tensor(out=y0[:], in0=o2[0][:, :, 0:128], in1=rl[:, 0:4].unsqueeze(2).to_broadcast([128, 4, 128]), op=ALU.mult),
                         r=["a_o0", "a_rl"], w=["a_y0"])
                    S.op("pool", lambda: G.tensor_tensor(out=y1[:], in0=o2[1][:, :, 0:128], in1=rl[:, 4:8].unsqueeze(2).to_broadcast([128, 4, 128]), op=ALU.mult),
                         r=["a_o1", "a_rl"], w=["a_y1"])
                    S.op("dve", lambda: V.tensor_tensor(out=y0[:], in0=y0[:], in1=y1[:], op=ALU.add), r=["a_y0", "a_y1"], w=["a_y0"])
                    S.op("pool", lambda: G.tensor_tensor(out=y1[:], in0=y0[:], in1=y0[:], op=ALU.mult), r=["a_y0"], w=["a_y1"])
                    S.op("dve", lambda: V.tensor_reduce(out=ss[:, 0:4], in_=y1[:], axis=AX.X, op=ALU.add), r=["a_y1"], w=["a_ss"])
                    S.op("act", lambda: A.activation(out=ss[:, 0:4], in_=ss[:, 0:4], func=AF.Sqrt, bias=EPS, scale=1.0 / 128), r=["a_ss"], w=["a_ss"])
                    S.op("dve", lambda: V.reciprocal(out=ss[:, 0:4], in_=ss[:, 0:4]), r=["a_ss"], w=["a_ss"])
                    S.op("dve", lambda: V.tensor_tensor(out=y0[:], in0=y0[:], in1=ss[:, 0:4].unsqueeze(2).to_broadcast([128, 4, 128]), op=ALU.mult),
                         r=["a_y0", "a_ss"], w=["a_y0"])
                    S.op("dve", lambda: V.tensor_tensor(out=yb[:], in0=y0[:], in1=subg[:, l * 128:(l + 1) * 128].unsqueeze(1).to_broadcast([128, 4, 128]), op=ALU.mult),
                         r=["a_y0", "const"], w=[kyb])

                    def fin():
                        for qq in range(4):
                            S.op("pe", lambda: T.transpose(out=ptr[:, qq, :], in_=yb[:, qq, :], identity=ident[:]), r=[kyb], w=["a_ptr"])
                        ys = yts[h % 2]
                        S.op("act", lambda: A.activation(out=ys[:], in_=ptr[:].rearrange("p a b -> p (a b)"), func=AF.Copy), r=["a_ptr"], w=[f"a_yts{h % 2}"])
                        S.dma("act", YT[h, :, Q * 512:(Q + 1) * 512], ys[:], r=[f"a_yts{h % 2}"], w=["YT"])
                    deferred.append([4, fin])

                def run_deferred(force=False):
                    keep = []
                    for dly in deferred:
                        dly[0] -= 1
                        if dly[0] <= 0 or force:
                            dly[1]()
                        else:
                            keep.append(dly)
                    deferred[:] = keep

                gidx = 0
                for Q in range(NQ):
                    for h in range(4):
                        for c in range(2):
                            g = gidx
                            gidx += 1
                            nj = 4 * Q + 4
                            for j in range(nj):
                                b = len(items) % 2
                                jj = j - 4 * Q
                                c0 = max(0, jj) * 128

                                def s0(Q=Q, h=h, c=c, j=j, c0=c0, b=b, g=g, pre=(h == 0 and c == 0 and j == 0)):
                                    if pre and Q + 1 < NQ:
                                        a_load_q(Q + 1)
                                    qt = qt2[Q % 2]
                                    qm = a_qm[g % 2]
                                    kqm = f"a_qm{g % 2}"
                                    if j == 0:
                                        S.op("pool", lambda: G.memset(qm[:], 0.0), w=[kqm])
                                        S.op("pool", lambda: G.tensor_copy(out=qm[c * 64:(c + 1) * 64, :], in_=qt[c * 64:(c + 1) * 64, h, :]),
                                             r=[f"a_qt{Q % 2}", kqm], w=[kqm])
                                    S.op("pe", lambda: T.matmul(pss[b][:, c0:512], lhsT=kt[:, h, j * 128:(j + 1) * 128],
                                                                rhs=qm[:, c0:512], start=True, stop=True),
                                         r=["a_kt", kqm], w=[f"a_pss{b}"])

                                def s1(c0=c0, b=b, jj=jj):
                                    S.op("act", lambda: A.activation(out=pT2[b][:, c0:512], in_=pss[b][:, c0:512], func=AF.Exp),
                                         r=[f"a_pss{b}"], w=[f"a_pT{b}"])
                                    if jj >= 0:
                                        S.op("pool", lambda: G.memset(pT2[b][64:128, c0:c0 + 64], 0.0), w=[f"a_pT{b}"])

                                def s2(Q=Q, h=h, c=c, j=j, jj=jj, b=b, g=g, nj=nj):
                                    po = po2[g % 2]
                                    kpo = f"a_po{g % 2}"
                                    for qq in range(max(0, jj), 4):
                                        off = (qq % 2) * 129
                                        S.op("pe", lambda: T.matmul(po[:, qq // 2, off:off + 129], lhsT=pT2[b][:, qq * 128:(qq + 1) * 128],
                                                                    rhs=vx[:, j, h * 129:(h + 1) * 129], start=(j == 0 and qq % 2 == 0), stop=(j == 4 * Q + qq),
                                                                    skip_group_check=True),
                                             r=[f"a_pT{b}", "a_vx"], w=[kpo])
                                    if j == nj - 1:
                                        a_epilogue(Q, h, c, po, kpo)
                                    run_deferred()
                                items.append([s0, s1, s2])
                run_pipeline(items, 3)
                run_deferred(force=True)
                S.barrier()
                ph.close()

            for ph in ([ExitStack()] if 'pb' in PH else []):
                kt = sb(ph, "b_kt", [128, 4, SL], BF16)
                vx = sb(ph, "b_vx", [128, NT, 8 * 65], BF16)
                qt2 = [sb(ph, f"b_qt{i}", [128, 4, 512], BF16) for i in range(2)]
                e2 = [sb(ph, f"b_e{i}", [128, 512], F32) for i in range(2)]
                nl2 = [sb(ph, f"b_nl{i}", [128, 512], BF16) for i in range(2)]
                aT2 = [sb(ph, f"b_aT{i}", [128, 512], BF16) for i in range(2)]
                R = sb(ph, "b_R", [128, 512], BF16)
                qm2 = [sb(ph, f"b_qm{i}", [128, 512], BF16) for i in range(2)]
                yq2 = [sb(ph, f"b_yq{i}", [128, 4, 512], BF16) for i in range(2)]
                yts = [sb(ph, f"b_yts{i}", [128, 512], BF16) for i in range(2)]
                psz = [ps(ph, f"b_psz{i}", [128, 512], F32) for i in range(2)]
                psc = [ps(ph, f"b_psc{i}", [128, 512], F32) for i in range(2)]
                po2 = [ps(ph, f"b_po{i}", [128, 512], F32) for i in range(2)]
                ptr = ps(ph, "b_ptr", [128, 4, 128], BF16)
                S.dma("sp", kt[:], KT["b"].rearrange("j p t -> p j t"), r=["QKT"], w=["b_kt"])
                for v4 in range(4):
                    S.dma("sp", vx[:, v4 * 8:(v4 + 1) * 8, :], VX["b"][v4 * 1024:(v4 + 1) * 1024, :].rearrange("(t p) e -> p t e", p=128), r=["VX"], w=["b_vx"])

                def b_load_q(Q):
                    S.dma("sp", qt2[Q % 2][:], QT["b"][:, :, Q * 512:(Q + 1) * 512].rearrange("j p t -> p j t"), r=["QKT"], w=[f"b_qt{Q % 2}"])

                b_load_q(0)
                items = []
                gidx = 0
                for Q in range(NQ):
                    for h in range(8):
                        g = gidx
                        gidx += 1
                        pr, bs = h // 2, (h % 2) * 64
                        jtop = 4 * Q + 3
                        gstate = {"first": True}
                        for j in range(jtop, -1, -1):
                            b = len(items) % 2
                            jj = j - 4 * Q
                            c0 = max(0, jj) * 128

                            def s0(Q=Q, h=h, j=j, c0=c0, b=b, pr=pr, bs=bs, pre=(h == 0 and j == jtop)):
                                if pre and Q + 1 < NQ:
                                    b_load_q(Q + 1)
                                qt = qt2[Q % 2]
                                S.op("pe", lambda: T.matmul(psz[b][:, c0:512], lhsT=kt[bs:bs + 64, pr, j * 128:(j + 1) * 128], rhs=qt[bs:bs + 64, pr, c0:512],
                                                            start=True, stop=True), r=["b_kt", f"b_qt{Q % 2}"], w=[f"b_psz{b}"])

                            def s1(c0=c0, b=b, jj=jj):
                                S.op("act", lambda: A.activation(out=e2[b][:, c0:512], in_=psz[b][:, c0:512], func=AF.Exp),
                                     r=[f"b_psz{b}"], w=[f"b_e{b}"])
                                S.op("act", lambda: A.activation(out=nl2[b][:, c0:512], in_=e2[b][:, c0:512], func=AF.Ln, bias=1.0),
                                     r=[f"b_e{b}"], w=[f"b_nl{b}"])
                                if jj >= 0:
                                    S.op("dve", lambda: V.tensor_tensor(out=nl2[b][:, c0:c0 + 128], in0=nl2[b][:, c0:c0 + 128], in1=strict[:], op=ALU.mult),
                                         r=[f"b_nl{b}", "const"], w=[f"b_nl{b}"])

                            def s2(Q=Q, j=j, c0=c0, b=b, pr=pr, bs=bs, g=g, jtop=jtop):
                                qm = qm2[g % 2]
                                kqm = f"b_qm{g % 2}"
                                qt = qt2[Q % 2]
                                if j == jtop:
                                    S.op("pool", lambda: G.memset(R[:], 0.0), w=["b_R"])
                                    S.op("pool", lambda: G.memset(qm[:], 0.0), w=[kqm])
                                    S.op("dve", lambda: V.tensor_copy(out=qm[bs:bs + 64, :], in_=qt[bs:bs + 64, pr, :]), r=[f"b_qt{Q % 2}", kqm], w=[kqm])
                                S.op("pe", lambda: T.matmul(psc[b][:, c0:512], lhsT=kt[:, pr, j * 128:(j + 1) * 128], rhs=qm[:, c0:512], start=True, stop=False),
                                     r=["b_kt", kqm], w=[f"b_psc{b}"])
                                last = (j == jtop)
                                S.op("pe", lambda: T.matmul(psc[b][:, c0:512], lhsT=ntri[:], rhs=nl2[b][:, c0:512], start=False, stop=last),
                                     r=[f"b_nl{b}", "const"], w=[f"b_psc{b}"])
                                if not last:
                                    S.op("pe", lambda: T.matmul(psc[b][:, c0:512], lhsT=nones[:], rhs=R[:, c0:512], start=False, stop=True),
                                         r=["b_R", "const"], w=[f"b_psc{b}"])
                                if j > 0:
                                    S.op("dve", lambda: V.tensor_tensor(out=R[:, c0:512], in0=R[:, c0:512], in1=nl2[b][:, c0:512], op=ALU.add),
                                         r=["b_R", f"b_nl{b}"], w=["b_R"])

                            def s3(c0=c0, b=b, jj=jj):
                                S.op("act", lambda: A.activation(out=aT2[b][:, c0:512], in_=psc[b][:, c0:512], func=AF.Exp),
                                     r=[f"b_psc{b}"], w=[f"b_aT{b}"])
                                if jj >= 0:
                                    S.op("dve", lambda: V.tensor_tensor(out=aT2[b][:, c0:c0 + 128], in0=aT2[b][:, c0:c0 + 128], in1=strict[:], op=ALU.mult),
                                         r=[f"b_aT{b}", "const"], w=[f"b_aT{b}"])

                            def s4(Q=Q, h=h, j=j, jj=jj, b=b, g=g, gstate=gstate):
                                po = po2[g % 2]
                                kpo = f"b_po{g % 2}"
                                yq = yq2[Q % 2]
                                kyq = f"b_yq{Q % 2}"
                                for qq in range(max(0, jj), 4):
                                    S.op("pe", lambda: T.matmul(po[:, qq * 65:(qq + 1) * 65], lhsT=aT2[b][:, qq * 128:(qq + 1) * 128],
                                                                rhs=vx[:, j, h * 65:(h + 1) * 65], start=gstate["first"], stop=(j == 0),
                                                                skip_group_check=True),
                                         r=[f"b_aT{b}", "b_vx"], w=[kpo])
                                    gstate["first"] = False
                                if j == 0:
                                    S.op("act", lambda: A.activation(out=yq[:, :, h * 64:(h + 1) * 64],
                                                                     in_=po[:, 0:260].rearrange("p (a e) -> p a e", e=65)[:, :, 0:64], func=AF.Copy),
                                         r=[kpo], w=[kyq])
                                    if h == 7:
                                        for cch in range(4):
                                            for qq in range(4):
                                                S.op("pe", lambda: T.transpose(out=ptr[:, qq, :], in_=yq[:, qq, cch * 128:(cch + 1) * 128], identity=ident[:]),
                                                     r=[kyq], w=["b_ptr"])
                                            ys = yts[cch % 2]
                                            S.op("act", lambda: A.activation(out=ys[:], in_=ptr[:].rearrange("p a b -> p (a b)"), func=AF.Copy), r=["b_ptr"], w=[f"b_yts{cch % 2}"])
                                            S.dma("act", YT[4 + cch, :, Q * 512:(Q + 1) * 512], ys[:], r=[f"b_yts{cch % 2}"], w=["YT"])
                            items.append([s0, s1, s2, s3, s4])
                run_pipeline(items, 5)
                S.barrier()
                ph.close()

            for ph in ([ExitStack()] if 'pc' in PH else []):
                kt = sb(ph, "c_kt", [128, 4, SL], BF16)
                vx = sb(ph, "c_vx", [128, NT, 8 * 65], BF16)
                ikt = sb(ph, "c_ikt", [128, SL], BF16)
                qt2 = [sb(ph, f"c_qt{i}", [128, 4, 512], BF16) for i in range(2)]
                iqt2 = [sb(ph, f"c_iqt{i}", [128, 2, 512], BF16) for i in range(2)]
                score2 = [sb(ph, f"c_score{i}", [128, SL], F32) for i in range(2)]
                maskq2 = [sb(ph, f"c_maskq{i}", [128, SL], BF16) for i in range(2)]
                bst = [sb(ph, f"c_bst{i}", [128, 8], F32) for i in range(2)]
                dft = [sb(ph, f"c_dft{i}", [128, NSTEP + 1], F32) for i in range(2)]
                maskT = sb(ph, "c_maskT", [128, NT, 512], BF16)
                rb2 = [sb(ph, f"c_r{i}", [128, 512], F32) for i in range(2)]
                pT2 = [sb(ph, f"c_pT{i}", [128, 512], BF16) for i in range(2)]
                bs_ = sb(ph, "c_bis", [128, 8], F32)
                osb2 = [sb(ph, f"c_o{i}", [128, 4, 65], F32) for i in range(2)]
                rl2 = [sb(ph, f"c_rl{i}", [128, 4], F32) for i in range(2)]
                yq2 = [sb(ph, f"c_yq{i}", [128, 4, 512], BF16) for i in range(2)]
                yts = [sb(ph, f"c_yts{i}", [128, 512], BF16) for i in range(2)]
                psi = [ps(ph, f"c_psi{i}", [128, 512], F32) for i in range(2)]
                pss = [ps(ph, f"c_pss{i}", [128, 512], F32) for i in range(2)]
                po2 = [ps(ph, f"c_po{i}", [128, 512], F32) for i in range(2)]
                ptr = ps(ph, "c_ptr", [128, 4, 128], BF16)
                S.dma("sp", kt[:], KT["c"].rearrange("j p t -> p j t"), r=["QKT"], w=["c_kt"])
                for v4 in range(4):
                    S.dma("sp", vx[:, v4 * 8:(v4 + 1) * 8, :], VX["c"][v4 * 1024:(v4 + 1) * 1024, :].rearrange("(t p) e -> p t e", p=128), r=["VX"], w=["c_vx"])
                S.dma("sp", ikt[:], IKT[:, :], r=["QKT"], w=["c_ikt"])

                def c_load_q(Q):
                    kq_ = f"c_qt{Q % 2}"
                    S.dma("sp", qt2[Q % 2][:], QT["c"][:, :, Q * 512:(Q + 1) * 512].rearrange("j p t -> p j t"), r=["QKT"], w=[kq_])
                    S.dma("sp", iqt2[Q % 2][:], IQT[:, :, Q * 512:(Q + 1) * 512].rearrange("j p t -> p j t"), r=["QKT"], w=[kq_])

                c_load_q(0)
                ir = 0
                gidx = 0
                for Q in range(NQ):
                    qt = qt2[Q % 2]
                    iqt = iqt2[Q % 2]
                    kq = f"c_qt{Q % 2}"
                    if Q + 1 < NQ:
                        c_load_q(Q + 1)
                    for pair in range(2):
                        info = []
                        for ci in range(2):
                            qq = 2 * pair + ci
                            qb = 4 * Q + qq
                            nadm = (qb + 1) * 128
                            info.append((qq, qb, nadm))
                            sc_t = score2[ci]
                            ksc = f"c_score{ci}"
                            nkt = (nadm + 511) // 512
                            for k5 in range(nkt):
                                wd = min(512, nadm - k5 * 512)
                                for h in range(4):
                                    b = ir % 2
                                    ir += 1
                                    hb_ = (h % 2) * 64
                                    S.op("pe", lambda: T.matmul(psi[b][:, 0:wd], lhsT=iqt[hb_:hb_ + 64, h // 2, qq * 128:(qq + 1) * 128],
                                                                rhs=ikt[hb_:hb_ + 64, k5 * 512:k5 * 512 + wd], start=True, stop=True),
                                         r=[kq, "c_ikt"], w=[f"c_psi{b}"])
                                    S.op("act", lambda: A.activation(out=rb2[b][:, 0:wd], in_=psi[b][:, 0:wd], func=AF.Relu),
                                         r=[f"c_psi{b}"], w=[f"c_r{b}"])
                                    sc = sc_t[:, k5 * 512:k5 * 512 + wd]
                                    if h == 0:
                                        S.op("dve", lambda: V.tensor_scalar(out=sc, in0=rb2[b][:, 0:wd], scalar1=iw_all[:, qb, 0:1], scalar2=None, op0=ALU.mult),
                                             r=[f"c_r{b}", "iw_all"], w=[ksc])
                                    else:
                                        S.op("dve", lambda: V.scalar_tensor_tensor(out=sc, in0=rb2[b][:, 0:wd], scalar=iw_all[:, qb, h:h + 1], in1=sc,
                                                                                   op0=ALU.mult, op1=ALU.add),
                                             r=[f"c_r{b}", "iw_all", ksc], w=[ksc])
                            S.op("dve", lambda: V.memset(sc_t[0:64, qb * 128 + 64:(qb + 1) * 128], NEG), r=[ksc], w=[ksc])
                        bisect = info[0][1] >= 2
                        for ci in range(2):
                            qq, qb, nadm = info[ci]
                            sc_t = score2[ci]
                            ksc = f"c_score{ci}"
                            stt = bst[ci]
                            df = dft[ci]
                            KB = [f"c_bis{ci}"]
                            if not bisect:
                                S.op("dve", lambda: V.memset(stt[:, 5:6], -1.0e29), r=KB, w=KB)
                                continue
                            nfull = qb * 128 + 64
                            S.op("dve", lambda: V.tensor_reduce(out=stt[:, 4:5], in_=sc_t[:, 0:nadm], axis=AX.X, op=ALU.max), r=[ksc], w=KB)
                            S.op("dve", lambda: V.tensor_reduce(out=stt[:, 5:6], in_=sc_t[:, 0:nfull], axis=AX.X, op=ALU.min), r=[ksc], w=KB)
                            S.op("dve", lambda: V.tensor_scalar(out=stt[:, 5:6], in0=stt[:, 5:6], scalar1=-1.0, scalar2=None, op0=ALU.add), r=KB, w=KB)
                            S.op("dve", lambda: V.tensor_tensor(out=stt[:, 1:2], in0=stt[:, 4:5], in1=stt[:, 5:6], op=ALU.subtract), r=KB, w=KB)
                            S.op("dve", lambda: V.tensor_scalar(out=df[:], in0=pw2[:], scalar1=stt[:, 1:2], scalar2=None, op0=ALU.mult), r=KB + ["const"], w=KB)
                            if ci == 0:
                                S.op("dve", lambda: V.tensor_tensor(out=stt[:, 0:1], in0=stt[:, 5:6], in1=df[:, 0:1], op=ALU.add), r=KB, w=KB)
                            else:
                                S.op("dve", lambda: V.scalar_tensor_tensor(out=stt[:, 0:1], in0=stt[:, 5:6], scalar=-1.0, in1=df[:, 0:1],
                                                                           op0=ALU.mult, op1=ALU.subtract), r=KB, w=KB)
                        if bisect:
                            for i in range(NSTEP):
                                lastst = (i == NSTEP - 1)
                                qq, qb, nadm = info[0]
                                stt, df, KB, ksc = bst[0], dft[0], ["c_bis0"], "c_score0"
                                S.op("dve", lambda: V.tensor_scalar(out=maskq2[0][:, 0:nadm], in0=score2[0][:, 0:nadm], scalar1=stt[:, 0:1], scalar2=None,
                                                                    op0=ALU.is_gt, op1=ALU.add, accum_out=stt[:, 2:3]),
                                     r=KB + [ksc, "c_maskq0"], w=KB + ["c_maskq0"])
                                S.op("dve", lambda: V.tensor_scalar(out=stt[:, 3:4], in0=stt[:, 2:3], scalar1=float(TOPK), scalar2=df[:, i:i + 1],
                                                                    op0=ALU.is_ge, op1=ALU.mult), r=KB, w=KB)
                                if not lastst:
                                    S.op("dve", lambda: V.scalar_tensor_tensor(out=stt[:, 0:1], in0=stt[:, 3:4], scalar=df[:, i + 1:i + 2], in1=stt[:, 0:1],
                                                                               op0=ALU.subtract, op1=ALU.add), r=KB, w=KB)
                                else:
                                    S.op("dve", lambda: V.scalar_tensor_tensor(out=stt[:, 5:6], in0=stt[:, 3:4], scalar=df[:, i:i + 1], in1=stt[:, 0:1],
                                                                               op0=ALU.subtract, op1=ALU.add), r=KB, w=KB)
                                qq, qb, nadm = info[1]
                                stt, df, KB, ksc = bst[1], dft[1], ["c_bis1"], "c_score1"
                                S.op("act", lambda: A.activation(out=maskq2[1][:, 0:nadm], in_=score2[1][:, 0:nadm], func=AF.Sign, bias=stt[:, 0:1],
                                                                 accum_out=stt[:, 2:3]),
                                     r=KB + [ksc, "c_maskq1"], w=KB + ["c_maskq1"])
                                S.op("pool", lambda: G.tensor_scalar(out=stt[:, 3:4], in0=stt[:, 2:3], scalar1=float(2 * TOPK - nadm), scalar2=df[:, i:i + 1],
                                                                     op0=ALU.is_ge, op1=ALU.mult), r=KB, w=KB)
                                S.op("pool", lambda: G.tensor_tensor(out=stt[:, 0:1], in0=stt[:, 0:1], in1=stt[:, 3:4], op=ALU.subtract), r=KB, w=KB)
                                if not lastst:
                                    S.op("pool", lambda: G.tensor_tensor(out=stt[:, 0:1], in0=stt[:, 0:1], in1=df[:, i + 1:i + 2], op=ALU.add), r=KB, w=KB)
                                else:
                                    S.op("pool", lambda: G.tensor_tensor(out=stt[:, 0:1], in0=stt[:, 0:1], in1=df[:, i:i + 1], op=ALU.add), r=KB, w=KB)
                                    S.op("pool", lambda: G.tensor_scalar(out=stt[:, 5:6], in0=stt[:, 0:1], scalar1=-1.0, scalar2=None, op0=ALU.mult), r=KB, w=KB)
                        for ci in range(2):
                            qq, qb, nadm = info[ci]
                            mq = maskq2[ci]
                            kmq = f"c_maskq{ci}"
                            S.op("dve", lambda: V.tensor_scalar(out=mq[:, 0:nadm], in0=score2[ci][:, 0:nadm], scalar1=bst[ci][:, 5:6], scalar2=None, op0=ALU.is_gt),
                                 r=[f"c_bis{ci}", f"c_score{ci}", kmq], w=[kmq])
                            for k1 in range(0, qb + 1, 4):
                                nk = min(4, qb + 1 - k1)
                                for u in range(nk):
                                    S.op("pe", lambda: T.transpose(out=ptr[:, u, :], in_=mq[:, (k1 + u) * 128:(k1 + u + 1) * 128], identity=ident[:]),
                                         r=[kmq], w=["c_ptr"])
                                S.op("act", lambda: A.activation(out=maskT[:, k1:k1 + nk, qq * 128:(qq + 1) * 128], in_=ptr[:, 0:nk, :], func=AF.Copy),
                                     r=["c_ptr"], w=["c_maskT"])
                    items = []
                    yq = yq2[Q % 2]
                    kyq = f"c_yq{Q % 2}"
                    for h in range(8):
                        g = gidx
                        gidx += 1
                        pr, bs = h // 2, (h % 2) * 64
                        nj = 4 * Q + 4
                        gstate = {"first": True}
                        for j in range(nj):
                            b = len(items) % 2
                            jj = j - 4 * Q
                            c0 = max(0, jj) * 128

                            def s0(j=j, c0=c0, b=b, pr=pr, bs=bs):
                                S.op("pe", lambda: T.matmul(pss[b][:, c0:512], lhsT=kt[bs:bs + 64, pr, j * 128:(j + 1) * 128],
                                                            rhs=qt[bs:bs + 64, pr, c0:512], start=True, stop=True),
                                     r=["c_kt", kq], w=[f"c_pss{b}"])

                            def s1(j=j, c0=c0, b=b):
                                S.op("act", lambda: A.activation(out=pT2[b][:, c0:512], in_=pss[b][:, c0:512], func=AF.Exp),
                                     r=[f"c_pss{b}"], w=[f"c_pT{b}"])
                                if b == 0:
                                    S.op("dve", lambda: V.tensor_tensor(out=pT2[b][:, c0:512], in0=pT2[b][:, c0:512], in1=maskT[:, j, c0:512], op=ALU.mult),
                                         r=[f"c_pT{b}", "c_maskT"], w=[f"c_pT{b}"])
                                else:
                                    S.op("pool", lambda: G.tensor_tensor(out=pT2[b][:, c0:512], in0=pT2[b][:, c0:512], in1=maskT[:, j, c0:512], op=ALU.mult),
                                         r=[f"c_pT{b}", "c_maskT"], w=[f"c_pT{b}"])

                            def s2(h=h, j=j, jj=jj, b=b, g=g, gstate=gstate, nj=nj):
                                po = po2[g % 2]
                                kpo = f"c_po{g % 2}"
                                osb = osb2[g % 2]
                                rl = rl2[g % 2]
                                for q4 in range(max(0, jj), 4):
                                    S.op("pe", lambda: T.matmul(po[:, q4 * 65:(q4 + 1) * 65], lhsT=pT2[b][:, q4 * 128:(q4 + 1) * 128],
                                                                rhs=vx[:, j, h * 65:(h + 1) * 65], start=gstate["first"], stop=(j == nj - 1),
                                                                skip_group_check=True),
                                         r=[f"c_pT{b}", "c_vx"], w=[kpo])
                                    gstate["first"] = False
                                if j == nj - 1:
                                    S.op("act", lambda: A.activation(out=osb[:], in_=po[:, 0:260].rearrange("p (a e) -> p a e", e=65), func=AF.Copy),
                                         r=[kpo], w=[f"c_o{g % 2}"])
                                    S.op("dve", lambda: V.reciprocal(out=rl[:], in_=osb[:, :, 64]), r=[f"c_o{g % 2}"], w=[f"c_rl{g % 2}"])
                                    S.op("dve", lambda: V.tensor_tensor(out=yq[:, :, h * 64:(h + 1) * 64], in0=osb[:, :, 0:64],
                                                                        in1=rl[:].unsqueeze(2).to_broadcast([128, 4, 64]), op=ALU.mult),
                                         r=[f"c_o{g % 2}", f"c_rl{g % 2}"], w=[kyq])
                            items.append([s0, s1, s2])
                    run_pipeline(items, 3)
                    for cch in range(4):
                        for q4 in range(4):
                            S.op("pe", lambda: T.transpose(out=ptr[:, q4, :], in_=yq[:, q4, cch * 128:(cch + 1) * 128], identity=ident[:]),
                                 r=[kyq], w=["c_ptr"])
                        ys = yts[cch % 2]
                        S.op("act", lambda: A.activation(out=ys[:], in_=ptr[:].rearrange("p a b -> p (a b)"), func=AF.Copy), r=["c_ptr"], w=[f"c_yts{cch % 2}"])
                        S.dma("act", YT[8 + cch, :, Q * 512:(Q + 1) * 512], ys[:], r=[f"c_yts{cch % 2}"], w=["YT"])
                S.barrier()
                ph.close()

            for ph in ([ExitStack()] if 'pm' in PH else []):
                wbr = [sb(ph, f"m_wbr{i}", [128, 4, D], BF16) for i in range(3)]
                wo = sb(ph, "m_wo", [128, 8, D], BF16)
                yt2 = [sb(ph, f"m_yt{i}", [128, 12, 512], BF16) for i in range(2)]
                gt2 = [sb(ph, f"m_gt{i}", [128, 24, 512], BF16) for i in range(2)]
                m0_2 = [sb(ph, f"m_m0{i}", [128, 512], F32) for i in range(2)]
                m1_2 = [sb(ph, f"m_m1{i}", [128, 512], F32) for i in range(2)]
                m2_2 = [sb(ph, f"m_m2{i}", [128, 512], F32) for i in range(2)]
                mT = sb(ph, "m_mT", [128, 8, 512], BF16)
                xb2 = [sb(ph, f"m_x{i}", [128, D], F32) for i in range(2)]
                ps3 = [ps(ph, f"m_ps{i}", [128, 512], F32) for i in range(6)]
                pso = [ps(ph, f"m_pso{i}", [128, 512], F32) for i in range(2)]
                for i in range(3):
                    load_w_bf16(wbr[i][:], w_br[i][l].rearrange("(c p) n -> p c n", p=128), "m_w")
                load_w_bf16(wo[:], w_out[l].rearrange("(c p) n -> p c n", p=128), "m_w")
                io = 0
                for Tq in range(NQ):
                    yt = yt2[Tq % 2]
                    gt = gt2[Tq % 2]
                    kin = f"m_in{Tq % 2}"
                    for b3 in range(3):
                        S.dma("sp", yt[:, b3 * 4:(b3 + 1) * 4, :], YT[b3 * 4:(b3 + 1) * 4, :, Tq * 512:(Tq + 1) * 512].rearrange("j p t -> p j t"), r=["YT"], w=[kin])
                        S.dma("sp", gt[:, b3 * 8:(b3 + 1) * 8, :], GT[b3 * 8:(b3 + 1) * 8, :, Tq * 512:(Tq + 1) * 512].rearrange("j p t -> p j t"), r=["GT"], w=[kin])
                    for fc in range(8):
                        fp_ = fc % 2
                        m0, m1, m2 = m0_2[fp_], m1_2[fp_], m2_2[fp_]
                        for br in range(3):
                            pb3 = fp_ * 3 + br
                            for kc in range(4):
                                S.op("pe", lambda: T.matmul(ps3[pb3][:], lhsT=wbr[br][:, kc, fc * 128:(fc + 1) * 128], rhs=yt[:, br * 4 + kc, :],
                                                            start=(kc == 0), stop=(kc == 3)), r=["m_w", kin], w=[f"m_ps{pb3}"])
                        S.op("dve", lambda: V.tensor_tensor(out=m0[:], in0=ps3[fp_ * 3][:], in1=gt[:, fc, :], op=ALU.mult), r=[f"m_ps{fp_ * 3}", kin], w=[f"m_m0{fp_}"])
                        S.op("dve", lambda: V.tensor_tensor(out=m1[:], in0=ps3[fp_ * 3 + 1][:], in1=gt[:, 8 + fc, :], op=ALU.mult), r=[f"m_ps{fp_ * 3 + 1}", kin], w=[f"m_m1{fp_}"])
                        S.op("dve", lambda: V.tensor_tensor(out=m2[:], in0=ps3[fp_ * 3 + 2][:], in1=gt[:, 16 + fc, :], op=ALU.mult), r=[f"m_ps{fp_ * 3 + 2}", kin], w=[f"m_m2{fp_}"])
                        S.op("pool", lambda: G.tensor_tensor(out=m0[:], in0=m0[:], in1=m1[:], op=ALU.add), r=[f"m_m0{fp_}", f"m_m1{fp_}"], w=[f"m_m0{fp_}"])
                        S.op("pool", lambda: G.tensor_tensor(out=mT[:, fc, :], in0=m0[:], in1=m2[:], op=ALU.add), r=[f"m_m0{fp_}", f"m_m2{fp_}"], w=["m_mT"])
                    for tt in range(4):
                        t = Tq * 4 + tt
                        xt = xb2[t % 2]
                        kx = f"m_x{t % 2}"
                        S.dma("sp", xt[:], xsrc[t * 128:(t + 1) * 128, :], r=[f"xr{t}"], w=[kx])
                        for nb in range(2):
                            pq = pso[io % 2]
                            kpq = f"m_pso{io % 2}"
                            io += 1
                            for fc in range(8):
                                S.op("pe", lambda: T.matmul(pq[:], lhsT=mT[:, fc, tt * 128:(tt + 1) * 128], rhs=wo[:, fc, nb * 512:(nb + 1) * 512],
                                                            start=(fc == 0), stop=(fc == 7)), r=["m_mT", "m_w"], w=[kpq])
                            S.op("dve", lambda: V.tensor_tensor(out=xt[:, nb * 512:(nb + 1) * 512], in0=pq[:], in1=xt[:, nb * 512:(nb + 1) * 512], op=ALU.add),
                                 r=[kpq, kx], w=[kx])
                        S.dma("sp", xres[t * 128:(t + 1) * 128, :], xt[:], r=[kx], w=[f"xr{t}"])
                S.barrier()
                ph.close()

            for ph in ([ExitStack()] if 'pf' in PH else []):
                wu = sb(ph, "f_wu", [128, 8, 4 * D], BF16)
                wd = sb(ph, "f_wd", [128, 32, D], BF16)
                xb4 = [sb(ph, f"f_x{i}", [128, D], F32) for i in range(4)]
                hb = sb(ph, "f_hb", [128, D], BF16)
                junk = sb(ph, "f_junk", [128, D], BF16)
                ssq = sb(ph, "f_ssq", [128, 8], F32)
                rstd = sb(ph, "f_rstd", [128, 8], F32)
                hmT2 = [sb(ph, f"f_hmT{i}", [128, 8, 256], BF16) for i in range(2)]
                rr2 = [sb(ph, f"f_r{i}", [128, 256], F32) for i in range(2)]
                uT = sb(ph, "f_uT", [128, 32, 256], BF16)
                pst2 = [ps(ph, f"f_pst{i}", [128, 4, 128], BF16) for i in range(2)]
                psu = [ps(ph, f"f_psu{i}", [128, 256], F32) for i in range(2)]
                psd = [ps(ph, f"f_psd{i}", [128, 512], F32) for i in range(2)]
                for c in range(8):
                    load_w_bf16(wu[:, c, :], w_up[l, c * 128:(c + 1) * 128, :], "f_wu")
                for c4 in range(4):
                    load_w_bf16(wd[:, c4 * 8:(c4 + 1) * 8, :], w_down[l, c4 * 1024:(c4 + 1) * 1024, :].rearrange("(c p) n -> p c n", p=128), "f_wd")
                iof = [0]
                items = []
                for tg in range(NT // 2):
                    def s0(tg=tg):
                        hm = hmT2[tg % 2]
                        for t2 in range(2):
                            t = tg * 2 + t2
                            xi = (tg % 2) * 2 + t2
                            xt = xb4[xi]
                            kx = f"f_x{xi}"
                            S.dma("sp", xt[:], xres[t * 128:(t + 1) * 128, :], r=[f"xr{t}"], w=[kx])
                            norm_transpose(xt[:], kx, gM, l, hb, pst2, lambda c, t2=t2, hm=hm: hm[:, c, t2 * 128:(t2 + 1) * 128], f"f_hmT{tg % 2}",
                                           ssq, rstd, junk, "f")

                    def s1(tg=tg):
                        hm = hmT2[tg % 2]
                        khm = f"f_hmT{tg % 2}"
                        for fc in range(32):
                            pu = psu[fc % 2]
                            for c in range(8):
                                S.op("pe", lambda: T.matmul(pu[:], lhsT=wu[:, c, fc * 128:(fc + 1) * 128], rhs=hm[:, c, :], start=(c == 0), stop=(c == 7)),
                                     r=["f_wu", khm], w=[f"f_psu{fc % 2}"])
                            rr = rr2[fc % 2]
                            S.op("act", lambda: A.activation(out=rr[:], in_=pu[:], func=AF.Relu), r=[f"f_psu{fc % 2}"], w=[f"f_r{fc % 2}"])
                            if fc % 2 == 0:
                                S.op("pool", lambda: G.tensor_tensor(out=uT[:, fc, :], in0=rr[:], in1=rr[:], op=ALU.mult), r=[f"f_r{fc % 2}"], w=["f_uT"])
                            else:
                                S.op("dve", lambda: V.tensor_tensor(out=uT[:, fc, :], in0=rr[:], in1=rr[:], op=ALU.mult), r=[f"f_r{fc % 2}"], w=["f_uT"])
                        for t2 in range(2):
                            t = tg * 2 + t2
                            xi = (tg % 2) * 2 + t2
                            xt = xb4[xi]
                            kx = f"f_x{xi}"
                            for nb in range(2):
                                pq = psd[iof[0] % 2]
                                kpq = f"f_psd{iof[0] % 2}"
                                iof[0] += 1
                                for fc in range(32):
                                    S.op("pe", lambda: T.matmul(pq[:], lhsT=uT[:, fc, t2 * 128:(t2 + 1) * 128], rhs=wd[:, fc, nb * 512:(nb + 1) * 512],
                                                                start=(fc == 0), stop=(fc == 31)), r=["f_uT", "f_wd"], w=[kpq])
                                S.op("dve", lambda: V.tensor_tensor(out=xt[:, nb * 512:(nb + 1) * 512], in0=pq[:], in1=xt[:, nb * 512:(nb + 1) * 512], op=ALU.add),
                                     r=[kpq, kx], w=[kx])
                            S.dma("sp", xres[t * 128:(t + 1) * 128, :], xt[:], r=[kx], w=[f"xr{t}"])
                    items.append([s0, s1])
                run_pipeline(items, 2)
                S.barrier()
                ph.close()

            for ph in ([ExitStack()] if 'pp' in PH else []):
                wg = sb(ph, "e_wg", [128, 8, D], BF16)
                wp = sb(ph, "e_wp", [128, 2, D], BF16)
                xb3 = [sb(ph, f"e_x{i}", [128, D], F32) for i in range(3)]
                pb3_ = [sb(ph, f"e_p{i}", [128, 256], F32) for i in range(3)]
                pbb = sb(ph, "e_pbb", [128, 256], BF16)
                hb = sb(ph, "e_hb", [128, D], BF16)
                junk = sb(ph, "e_junk", [128, D], BF16)
                ssq = sb(ph, "e_ssq", [128, 8], F32)
                rstd = sb(ph, "e_rstd", [128, 8], F32)
                hpT2 = [sb(ph, f"e_hpT{i}", [128, 8, 128], BF16) for i in range(2)]
                pT2 = [sb(ph, f"e_pT{i}", [128, 2, 128], BF16) for i in range(2)]
                sg2 = [sb(ph, f"e_sg{i}", [128, 512], F32) for i in range(2)]
                tm2 = [sb(ph, f"e_tm{i}", [128, 512], F32) for i in range(2)]
                pst2 = [ps(ph, f"e_pst{i}", [128, 4, 128], BF16) for i in range(2)]
                ptp = ps(ph, "e_ptp", [128, 2, 128], BF16)
                psg = [ps(ph, f"e_psg{i}", [128, 512], F32) for i in range(2)]
                psp = [ps(ph, f"e_psp{i}", [128, 512], F32) for i in range(2)]
                load_w_bf16(wg[:], w_ple_gate[l].rearrange("(c p) n -> p c n", p=128), "e_w")
                load_w_bf16(wp[:], w_ple_proj[l].rearrange("(c p) n -> p c n", p=128), "e_w")
                dst = out_d if l == depth - 1 else xres
                kdst = "out"
                ioe = [0]
                items = []
                for t in range(NT):
                    def s0(t=t):
                        xt = xb3[t % 3]
                        kx = f"e_x{t % 3}"
                        pb = pb3_[t % 3]
                        hp = hpT2[t % 2]
                        S.dma("sp", xt[:], xres[t * 128:(t + 1) * 128, :], r=[f"xr{t}"], w=[kx])
                        S.dma("sp", pb[:], p_in[l, t * 128:(t + 1) * 128, :], w=[f"e_p{t % 3}"])
                        norm_transpose(xt[:], kx, gP, l, hb, pst2, lambda c, hp=hp: hp[:, c, :], f"e_hpT{t % 2}", ssq, rstd, junk, "e")
                        S.op("pool", lambda: G.tensor_copy(out=pbb[:], in_=pb[:]), r=[f"e_p{t % 3}"], w=["e_pbb"])
                        for c in range(2):
                            S.op("pe", lambda: T.transpose(out=ptp[:, c, :], in_=pbb[:, c * 128:(c + 1) * 128], identity=ident[:]), r=["e_pbb"], w=["e_ptp"])
                        S.op("act", lambda: A.activation(out=pT2[t % 2][:], in_=ptp[:], func=AF.Copy), r=["e_ptp"], w=[f"e_pT{t % 2}"])

                    def s1(t=t):
                        xt = xb3[t % 3]
                        kx = f"e_x{t % 3}"
                        hp = hpT2[t % 2]
                        pT = pT2[t % 2]
                        for nb in range(2):
                            b = ioe[0] % 2
                            ioe[0] += 1
                            for c in range(8):
                                S.op("pe", lambda: T.matmul(psg[b][:], lhsT=hp[:, c, :], rhs=wg[:, c, nb * 512:(nb + 1) * 512], start=(c == 0), stop=(c == 7)),
                                     r=[f"e_hpT{t % 2}", "e_w"], w=[f"e_psg{b}"])
                            for c in range(2):
                                S.op("pe", lambda: T.matmul(psp[b][:], lhsT=pT[:, c, :], rhs=wp[:, c, nb * 512:(nb + 1) * 512], start=(c == 0), stop=(c == 1)),
                                     r=[f"e_pT{t % 2}", "e_w"], w=[f"e_psp{b}"])
                            S.op("act", lambda: A.activation(out=sg2[b][:], in_=psg[b][:], func=AF.Sigmoid), r=[f"e_psg{b}"], w=[f"e_sg{b}"])
                            S.op("dve", lambda: V.tensor_tensor(out=tm2[b][:], in0=psp[b][:], in1=sg2[b][:], op=ALU.mult), r=[f"e_psp{b}", f"e_sg{b}"], w=[f"e_tm{b}"])
                            S.op("pool", lambda: G.tensor_tensor(out=xt[:, nb * 512:(nb + 1) * 512], in0=xt[:, nb * 512:(nb + 1) * 512], in1=tm2[b][:], op=ALU.add),
                                 r=[f"e_tm{b}", kx], w=[kx])
                        S.dma("sp", dst[t * 128:(t + 1) * 128, :], xt[:], r=[kx], w=[kdst if l == depth - 1 else f"xr{t}"])
                    items.append([s0, s1])
                run_pipeline(items, 2)
                S.barrier()
                ph.close()
        S.barrier()
        build_nc.ninstr = dict(S.ninstr)
    return nc


def _host_layout(inputs, b):
    f32 = np.float32
    m = {}
    m["x"] = np.ascontiguousarray(inputs["x"][b], dtype=f32)
    m["p"] = np.ascontiguousarray(inputs["p"][:, b], dtype=f32)
    pos = np.asarray(inputs["positions"][b]).astype(np.int32)
    m["pos"] = np.ascontiguousarray(pos.reshape(NT, 128).T)
    m["invf"] = (10000.0 ** (-np.arange(0, 64, 2, dtype=np.float32) / 64)).astype(f32)
    for k in ("attn_norm", "mlp_norm", "ple_norm"):
        a = np.asarray(inputs[k], dtype=f32)
        m[k + "_l"] = np.ascontiguousarray(a.reshape(DEPTH, 8, 128).transpose(2, 0, 1).reshape(128, DEPTH * 8))
    for k in ("a_q_norm", "a_k_norm", "a_lambda", "a_subln", "c_q_norm", "c_k_norm", "idx_k_norm"):
        m[k] = np.ascontiguousarray(np.asarray(inputs[k], dtype=f32).reshape(-1))
    for k in ("w_in", "w_br_a", "w_br_b", "w_br_c", "w_out", "w_up", "w_down", "w_ple_gate", "w_ple_proj"):
        m[k] = np.ascontiguousarray(inputs[k], dtype=f32)
    return m


def kernel(**inputs):
    nc = build_nc()
    in_maps = [_host_layout(inputs, c % 4) for c in range(4)]
    in_maps = in_maps + in_maps
    res = run_bass_kernel_spmd(nc, in_maps, core_ids=list(range(8)))
    out = np.stack([np.asarray(res.results[b]["out"], dtype=np.float32) for b in range(4)], axis=0)
    return out
```

```python
import math
from contextlib import ExitStack

import numpy as np
import concourse.bass as bass
import concourse.mybir as mybir
from concourse.bass_utils import run_bass_kernel_spmd

F32 = mybir.dt.float32
BF16 = mybir.dt.bfloat16
I32 = mybir.dt.int32
AF = mybir.ActivationFunctionType
ALU = mybir.AluOpType
AX = mybir.AxisListType

DEPTH = 4
SL = 4096
D = 1024
NT = 32
NQ = 8
EPS = 1e-6
W_IN_COLS = 8004
TOPK = 256
NSTEP = 20
NEG = -1.0e30

SEM_EPOCH = 30000
N_DMA_SEMS = 24


class Sched:
    def __init__(self, nc, stack):
        self.nc = nc
        self.stack = stack
        self.engs = {"pe": nc.tensor, "act": nc.scalar, "dve": nc.vector,
                     "pool": nc.gpsimd, "sp": nc.sync}
        self.esem = {}
        self.ecount = {}
        self.n_sems = 0
        for e in self.engs:
            self._new_esem(e)
        self.waited = {e: {} for e in self.engs}
        self.dsems = [[self._sem(f"dma{i}"), 0] for i in range(N_DMA_SEMS)]
        self.dnext = 0
        self.lastw = {}
        self.reads = {}
        self.ninstr = {e: 0 for e in self.engs}

    def _sem(self, name):
        self.n_sems += 1
        return self.stack.enter_context(self.nc.semaphore(name))

    def _new_esem(self, e):
        self.esem[e] = self._sem(f"e_{e}_{self.n_sems}")
        self.ecount[e] = 0

    def _wait(self, e, dep):
        sem, val, tag = dep
        w = self.waited[e]
        k = id(sem)
        if w.get(k, 0) >= val:
            return
        w[k] = val
        self.engs[e].wait_ge(sem, val)
        self.ninstr[e] += 1

    def _collect(self, r, w):
        deps = []
        for k in list(r) + list(w):
            deps.extend(self.lastw.get(k, ()))
        for k in w:
            rd = self.reads.get(k)
            if rd:
                deps.extend(rd.values())
        return deps

    def _commit(self, r, w, dep, merge=()):
        for k in w:
            if k in merge:
                self.lastw[k].append(dep)
            else:
                self.lastw[k] = [dep]
                self.reads[k] = {}
        for k in r:
            if k in w:
                continue
            self.reads.setdefault(k, {})[dep[2]] = dep

    def op(self, e, fn, r=(), w=()):
        if self.ecount[e] >= SEM_EPOCH:
            self._new_esem(e)
        for d in self._collect(r, w):
            if e == "pe" and d[2] == "pe":
                continue
            self._wait(e, d)
        ins = fn()
        self.ecount[e] += 1
        ins.then_inc(self.esem[e], 1)
        dep = (self.esem[e], self.ecount[e], e)
        self.ninstr[e] += 1
        self._commit(r, w, dep)
        return ins

    def dma(self, q, out, in_, r=(), w=()):
        slot = self.dsems[self.dnext]
        self.dnext = (self.dnext + 1) % len(self.dsems)
        sem = slot[0]
        if slot[1] > 0:
            self._wait(q, (sem, slot[1], "dma"))
        deps = []
        merge = set()
        for k in r:
            deps.extend(self.lastw.get(k, ()))
        for k in w:
            lw = self.lastw.get(k, ())
            rd = self.reads.get(k) or {}
            if lw and not rd and all(d[2].startswith("dma") for d in lw):
                merge.add(k)
            else:
                deps.extend(lw)
                deps.extend(rd.values())
        for d in deps:
            self._wait(q, d)
        ins = self.engs[q].dma_start(out=out, in_=in_)
        slot[1] += 16
        ins.then_inc(sem, 16)
        dep = (sem, slot[1], f"dma{id(sem)}")
        self.ninstr[q] += 1
        self._commit(r, w, dep, merge)
        return ins

    def barrier(self):
        for e in self.engs:
            for o in self.engs:
                if o != e and self.ecount[o] > 0:
                    self._wait(e, (self.esem[o], self.ecount[o], o))
            for sem, val in self.dsems:
                if val:
                    self._wait(e, (sem, val, "dma"))


def run_pipeline(items, nstage):
    n = len(items)
    for step in range(n + nstage - 1):
        for k in range(nstage - 1, -1, -1):
            i = step - k
            if 0 <= i < n and items[i][k] is not None:
                items[i][k]()


def build_nc(depth=DEPTH, debug=False, phases=None):
    PH = set(phases) if phases is not None else {'p1', 'pa', 'pb', 'pc', 'pm', 'pf', 'pp'}
    nc = bass.Bass("TRN2", target_bir_lowering=False)

    def din(name, shape, dt=F32):
        return nc.dram_tensor(name, list(shape), dt, kind="ExternalInput").ap()

    def dscr(name, shape, dt):
        kind = "ExternalOutput" if debug else "Internal"
        return nc.dram_tensor(name, list(shape), dt, kind=kind).ap()

    x_in = din("x", [SL, D])
    p_in = din("p", [DEPTH, SL, 256])
    pos_in = din("pos", [128, NT], I32)
    invf_in = din("invf", [32])
    attn_norm = din("attn_norm_l", [128, DEPTH * 8])
    mlp_norm = din("mlp_norm_l", [128, DEPTH * 8])
    ple_norm = din("ple_norm_l", [128, DEPTH * 8])
    w_in = din("w_in", [DEPTH, D, W_IN_COLS])
    a_q_norm = din("a_q_norm", [DEPTH * 64])
    a_k_norm = din("a_k_norm", [DEPTH * 64])
    a_lambda = din("a_lambda", [DEPTH * 256])
    a_subln = din("a_subln", [DEPTH * 128])
    c_q_norm = din("c_q_norm", [DEPTH * 64])
    c_k_norm = din("c_k_norm", [DEPTH * 64])
    idx_k_norm = din("idx_k_norm", [DEPTH * 64])
    w_br = [din("w_br_a", [DEPTH, 512, D]), din("w_br_b", [DEPTH, 512, D]), din("w_br_c", [DEPTH, 512, D])]
    w_out = din("w_out", [DEPTH, D, D])
    w_up = din("w_up", [DEPTH, D, 4 * D])
    w_down = din("w_down", [DEPTH, 4 * D, D])
    w_ple_gate = din("w_ple_gate", [DEPTH, D, D])
    w_ple_proj = din("w_ple_proj", [DEPTH, 256, D])
    out_d = nc.dram_tensor("out", [SL, D], F32, kind="ExternalOutput").ap()

    xres = dscr("xres", [SL, D], F32)
    QT = {k: dscr("QT_" + k, [4, 128, SL], BF16) for k in "abc"}
    KT = {k: dscr("KT_" + k, [4, 128, SL], BF16) for k in "abc"}
    VX = {"a": dscr("V_a", [SL, 4 * 129], BF16), "b": dscr("V_b", [SL, 8 * 65], BF16),
          "c": dscr("V_c", [SL, 8 * 65], BF16)}
    IQT = dscr("IQT", [2, 128, SL], BF16)
    IKT = dscr("IKT", [128, SL], BF16)
    GT = dscr("GT", [24, 128, SL], BF16)
    YT = dscr("YT", [12, 128, SL], BF16)

    with ExitStack() as st:
        S = Sched(nc, st)
        E = st.enter_context
        V = nc.vector
        A = nc.scalar
        G = nc.gpsimd
        T = nc.tensor

        uid = [0]

        def sb(stack, name, shape, dt):
            uid[0] += 1
            return stack.enter_context(nc.sbuf_tensor(f"s{uid[0]}_{name}", list(shape), dt))

        def ps(stack, name, shape, dt):
            uid[0] += 1
            return stack.enter_context(nc.psum_tensor(f"q{uid[0]}_{name}", list(shape), dt))

        ident = sb(st, "ident", [128, 128], BF16)
        ntri = sb(st, "ntri", [128, 128], BF16)
        nones = sb(st, "nones", [128, 128], BF16)
        strict = sb(st, "strict", [128, 128], BF16)
        cos_t = sb(st, "cos_t", [128, NT, 32], F32)
        sin_t = sb(st, "sin_t", [128, NT, 32], F32)
        gA = sb(st, "gA", [128, DEPTH * 8], F32)
        gM = sb(st, "gM", [128, DEPTH * 8], F32)
        gP = sb(st, "gP", [128, DEPTH * 8], F32)
        aqn = sb(st, "aqn", [128, DEPTH * 64], F32)
        akn = sb(st, "akn", [128, DEPTH * 64], F32)
        cqn = sb(st, "cqn", [128, DEPTH * 64], F32)
        ckn = sb(st, "ckn", [128, DEPTH * 64], F32)
        ikn = sb(st, "ikn", [128, DEPTH * 64], F32)
        subg = sb(st, "subg", [128, DEPTH * 128], F32)
        lam_t = sb(st, "lam_t", [128, DEPTH], F32)
        nlam_t = sb(st, "nlam_t", [128, DEPTH], F32)
        iw_all = sb(st, "iw_all", [128, NT, 4], F32)
        pw2 = sb(st, "pw2", [128, NSTEP + 1], F32)

        CONST = ["const"]

        with ExitStack() as ph:
            posi = sb(ph, "posi", [128, NT], I32)
            posf = sb(ph, "posf", [128, NT], F32)
            invf = sb(ph, "invf", [128, 32], F32)
            ang = sb(ph, "ang", [128, NT, 32], F32)
            t0 = sb(ph, "t0", [128, NT, 32], F32)
            t1 = sb(ph, "t1", [128, NT, 32], F32)
            lpb = sb(ph, "lpb", [128, DEPTH * 256], F32)
            tmp64 = sb(ph, "tmp64", [128, 64], F32)
            s12 = sb(ph, "s12", [128, 2], F32)
            K0 = ["p0"]
            S.op("pool", lambda: G.memset(ident[:], 1.0), w=CONST)
            S.op("pool", lambda: G.affine_select(out=ident[:], in_=ident[:], pattern=[[-1, 128]],
                                                 compare_op=ALU.is_equal, fill=0.0, base=0, channel_multiplier=1),
                 r=CONST, w=CONST)
            S.op("pool", lambda: G.memset(nones[:], -1.0), w=CONST)
            for i_ in range(NSTEP + 1):
                S.op("pool", lambda: G.memset(pw2[:, i_:i_ + 1], float(2.0 ** -(i_ + 1))), w=CONST)
            S.op("pool", lambda: G.memset(ntri[:], -1.0), w=CONST)
            S.op("pool", lambda: G.affine_select(out=ntri[:], in_=ntri[:], pattern=[[-1, 128]],
                                                 compare_op=ALU.is_ge, fill=0.0, base=0, channel_multiplier=1),
                 r=CONST, w=CONST)
            S.op("pool", lambda: G.memset(strict[:], 1.0), w=CONST)
            S.op("pool", lambda: G.affine_select(out=strict[:], in_=strict[:], pattern=[[1, 128]],
                                                 compare_op=ALU.is_gt, fill=0.0, base=0, channel_multiplier=-1),
                 r=CONST, w=CONST)
            for dst, src in [(gA, attn_norm), (gM, mlp_norm), (gP, ple_norm)]:
                S.dma("sp", dst[:], src[:, :], w=CONST)
            for dst, src in [(aqn, a_q_norm), (akn, a_k_norm), (cqn, c_q_norm), (ckn, c_k_norm),
                             (ikn, idx_k_norm), (subg, a_subln), (lpb, a_lambda), (invf, invf_in)]:
                S.dma("sp", dst[:], src.partition_broadcast(128), w=CONST + K0)
            S.dma("sp", posi[:], pos_in[:, :], w=K0)
            S.op("dve", lambda: V.tensor_scalar(out=aqn[:], in0=aqn[:], scalar1=0.125, scalar2=None, op0=ALU.mult), r=CONST, w=CONST)
            S.op("dve", lambda: V.tensor_scalar(out=cqn[:], in0=cqn[:], scalar1=0.125, scalar2=None, op0=ALU.mult), r=CONST, w=CONST)
            for l in range(DEPTH):
                li = 0.8 - 0.6 * math.exp(-0.3 * l)
                S.op("dve", lambda: V.tensor_scalar(out=subg[:, l * 128:(l + 1) * 128], in0=subg[:, l * 128:(l + 1) * 128],
                                                    scalar1=float(1.0 - li), scalar2=None, op0=ALU.mult), r=CONST, w=CONST)
                for j in range(2):
                    S.op("dve", lambda: V.tensor_tensor(out=tmp64[:], in0=lpb[:, l * 256 + j * 128: l * 256 + j * 128 + 64],
                                                        in1=lpb[:, l * 256 + j * 128 + 64: l * 256 + j * 128 + 128], op=ALU.mult),
                         r=K0, w=["tmp64"])
                    S.op("dve", lambda: V.tensor_reduce(out=s12[:, j:j + 1], in_=tmp64[:], axis=AX.X, op=ALU.add),
                         r=["tmp64"], w=["s12"])
                S.op("act", lambda: A.activation(out=s12[:], in_=s12[:], func=AF.Exp), r=["s12"], w=["s12"])
                S.op("dve", lambda: V.tensor_tensor(out=lam_t[:, l:l + 1], in0=s12[:, 0:1], in1=s12[:, 1:2], op=ALU.subtract),
                     r=["s12"], w=CONST)
                S.op("dve", lambda: V.tensor_scalar(out=lam_t[:, l:l + 1], in0=lam_t[:, l:l + 1], scalar1=float(li), scalar2=None, op0=ALU.add),
                     r=CONST, w=CONST)
                S.op("dve", lambda: V.tensor_scalar(out=nlam_t[:, l:l + 1], in0=lam_t[:, l:l + 1], scalar1=-1.0, scalar2=None, op0=ALU.mult),
                     r=CONST, w=CONST)
            S.op("dve", lambda: V.tensor_copy(out=posf[:], in_=posi[:]), r=K0, w=["posf"])
            S.op("dve", lambda: V.tensor_tensor(out=ang[:], in0=posf[:].unsqueeze(2).to_broadcast([128, NT, 32]),
                                                in1=invf[:].unsqueeze(1).to_broadcast([128, NT, 32]), op=ALU.mult),
                 r=["posf"] + K0, w=["ang"])
            MAGIC = 12582912.0
            C1 = 6.28125
            C2 = 2.0 * math.pi - C1
            PI_LO = 3.1415925
            S.op("dve", lambda: V.tensor_scalar(out=t0[:], in0=ang[:], scalar1=float(1.0 / (2.0 * math.pi)), scalar2=None, op0=ALU.mult), r=["ang"], w=["t0"])
            S.op("dve", lambda: V.tensor_scalar(out=t0[:], in0=t0[:], scalar1=MAGIC, scalar2=None, op0=ALU.add), r=["t0"], w=["t0"])
            S.op("dve", lambda: V.tensor_scalar(out=t0[:], in0=t0[:], scalar1=-MAGIC, scalar2=None, op0=ALU.add), r=["t0"], w=["t0"])
            S.op("dve", lambda: V.scalar_tensor_tensor(out=t1[:], in0=t0[:], scalar=-C1, in1=ang[:], op0=ALU.mult, op1=ALU.add), r=["t0", "ang"], w=["t1"])
            S.op("dve", lambda: V.scalar_tensor_tensor(out=t1[:], in0=t0[:], scalar=-C2, in1=t1[:], op0=ALU.mult, op1=ALU.add), r=["t0", "t1"], w=["t1"])
            S.op("dve", lambda: V.tensor_scalar(out=t0[:], in0=t1[:], scalar1=PI_LO, scalar2=-PI_LO, op0=ALU.min, op1=ALU.max), r=["t1"], w=["t0"])
            S.op("act", lambda: A.activation(out=sin_t[:], in_=t0[:], func=AF.Sin), r=["t0"], w=CONST)
            S.op("dve", lambda: V.tensor_scalar(out=t1[:], in0=t1[:], scalar1=float(math.pi / 2), scalar2=None, op0=ALU.add), r=["t1"], w=["t1"])
            S.op("dve", lambda: V.tensor_scalar(out=t0[:], in0=t1[:], scalar1=float(math.pi), scalar2=float(-2.0 * math.pi), op0=ALU.is_gt, op1=ALU.mult), r=["t1", "const"], w=["t0"])
            S.op("dve", lambda: V.tensor_tensor(out=t1[:], in0=t1[:], in1=t0[:], op=ALU.add), r=["t0", "t1"], w=["t1"])
            S.op("dve", lambda: V.tensor_scalar(out=t1[:], in0=t1[:], scalar1=PI_LO, scalar2=-PI_LO, op0=ALU.min, op1=ALU.max), r=["t1"], w=["t1"])
            S.op("act", lambda: A.activation(out=cos_t[:], in_=t1[:], func=AF.Sin), r=["t1"], w=CONST)
            S.barrier()

        def rms_rows(xt, kx, ssq, rstd, junk, kpre):
            S.op("act", lambda: A.activation(out=junk[:], in_=xt, func=AF.Square, accum_out=ssq[:, 0:1]),
                 r=[kx], w=[kpre + "junk", kpre + "ssq"])
            S.op("act", lambda: A.activation(out=rstd[:, 0:1], in_=ssq[:, 0:1], func=AF.Sqrt, bias=EPS, scale=1.0 / D),
                 r=[kpre + "ssq"], w=[kpre + "rstd"])
            S.op("dve", lambda: V.reciprocal(out=rstd[:, 0:1], in_=rstd[:, 0:1]), r=[kpre + "rstd"], w=[kpre + "rstd"])

        def norm_transpose(xt, kx, gain, l, hb, pst2, dst_fn, kdst, ssq, rstd, junk, kpre):
            rms_rows(xt, kx, ssq, rstd, junk, kpre)
            S.op("dve", lambda: V.tensor_scalar(out=hb[:], in0=xt, scalar1=rstd[:, 0:1], scalar2=None, op0=ALU.mult),
                 r=[kx, kpre + "rstd"], w=[kpre + "hb"])
            for half in range(2):
                pt = pst2[half]
                for c4 in range(4):
                    c = half * 4 + c4
                    S.op("pe", lambda: T.transpose(out=pt[:, c4, :], in_=hb[:, c * 128:(c + 1) * 128], identity=ident[:]),
                         r=[kpre + "hb"], w=[kpre + f"pst{half}"])
                for c4 in range(4):
                    c = half * 4 + c4
                    if c4 % 2 == 0:
                        S.op("dve", lambda: V.tensor_scalar(out=dst_fn(c), in0=pt[:, c4, :], scalar1=gain[:, l * 8 + c:l * 8 + c + 1],
                                                            scalar2=None, op0=ALU.mult),
                             r=[kpre + f"pst{half}"], w=[kdst])
                    else:
                        S.op("act", lambda: A.activation(out=dst_fn(c), in_=pt[:, c4, :], func=AF.Copy,
                                                         scale=gain[:, l * 8 + c:l * 8 + c + 1]),
                             r=[kpre + f"pst{half}"], w=[kdst])

        def load_w_bf16(dst, src, key):
            S.dma("pool", dst, src, w=[key])

        for l in range(depth):
            xsrc = x_in if l == 0 else xres
            lam_init = 0.8 - 0.6 * math.exp(-0.3 * l)

            for ph in ([ExitStack()] if 'p1' in PH else []):
                hT = sb(ph, "hT", [128, 8, SL], BF16)
                xb2 = [sb(ph, f"p1x{i}", [128, D], F32) for i in range(2)]
                hb = sb(ph, "p1hb", [128, D], BF16)
                junk = sb(ph, "p1junk", [128, D], BF16)
                ssq = sb(ph, "p1ssq", [128, 8], F32)
                rstd = sb(ph, "p1rstd", [128, 8], F32)
                pst2 = [ps(ph, f"p1pst{i}", [128, 4, 128], BF16) for i in range(2)]
                psm = [ps(ph, f"p1ps{i}", [128, 512], F32) for i in range(2)]
                wblk = [sb(ph, f"p1w{i}", [128, 8, 512], BF16) for i in range(2)]
                sq_2 = [sb(ph, f"p1sq{i}", [128, 512], F32) for i in range(2)]
                xn_2 = [sb(ph, f"p1xn{i}", [128, 512], F32) for i in range(2)]
                xg_2 = [sb(ph, f"p1xg{i}", [128, 512], F32) for i in range(2)]
                ra_2 = [sb(ph, f"p1ra{i}", [128, 256], F32) for i in range(2)]
                rb_2 = [sb(ph, f"p1rb{i}", [128, 256], F32) for i in range(2)]
                ob_2 = [sb(ph, f"p1ob{i}", [128, 512], BF16) for i in range(2)]
                ssq_2 = [sb(ph, f"p1ssq{i}", [128, 8], F32) for i in range(2)]
                rstd_2 = [sb(ph, f"p1rstd{i}", [128, 8], F32) for i in range(2)]
                stg = [sb(ph, f"p1stg{i}", [128, 4, 512], BF16) for i in range(2)]
                vst = [sb(ph, f"p1vst{i}", [128, 520], BF16) for i in range(4)]
                gst = [sb(ph, f"p1gst{i}", [128, 512], BF16) for i in range(2)]

                for t in range(NT):
                    xt = xb2[t % 2]
                    kx = f"p1x{t % 2}"
                    S.dma("sp", xt[:], xsrc[t * 128:(t + 1) * 128, :], r=[f"xr{t}"], w=[kx])
                    norm_transpose(xt[:], kx, gA, l, hb, pst2,
                                   lambda c, t=t: hT[:, c, t * 128:(t + 1) * 128], f"hT{t}", ssq, rstd, junk, "p1")
                for i in range(4):
                    S.op("pool", lambda: G.memset(vst[i][:], 1.0), w=[f"vst{i}"])

                blocks = []
                names = ["aq", "ak", "av", "bq", "bk", "bv", "cq", "ck", "cv"]
                for i, nm in enumerate(names):
                    blocks.append((nm, i * 512, 512))
                blocks.append(("idx", 4608, 324))
                for i in range(6):
                    blocks.append((f"gl{i}", 4932 + i * 512, 512))
                w_l = w_in[l].rearrange("(c p) n -> p c n", p=128)

                def qk_post(pv, nh, gain, do_norm, do_rope, t, outv, scale, kout):
                    n = nh * 64
                    pz = t % 2
                    sq, xn, xg, ra, rb = sq_2[pz], xn_2[pz], xg_2[pz], ra_2[pz], rb_2[pz]
                    ssq, rstd = ssq_2[pz], rstd_2[pz]
                    ksq, kxn, kxg, kra, krb, kss, krs = (f"sq{pz}", f"xn{pz}", f"xg{pz}", f"ra{pz}", f"rb{pz}", f"qssq{pz}", f"qrstd{pz}")
                    if do_norm:
                        sqv = sq[:, 0:n].rearrange("p (h d) -> p h d", d=64)
                        S.op("act", lambda: A.activation(out=sqv, in_=pv, func=AF.Square), r=["psm"], w=[ksq])
                        S.op("dve", lambda: V.tensor_reduce(out=ssq[:, 0:nh], in_=sqv, axis=AX.X, op=ALU.add), r=[ksq], w=[kss])
                        S.op("act", lambda: A.activation(out=rstd[:, 0:nh], in_=ssq[:, 0:nh], func=AF.Sqrt, bias=EPS, scale=1.0 / 64),
                             r=[kss], w=[krs])
                        S.op("dve", lambda: V.reciprocal(out=rstd[:, 0:nh], in_=rstd[:, 0:nh]), r=[krs], w=[krs])
                        xnv = xn[:, 0:n].rearrange("p (h d) -> p h d", d=64)
                        S.op("dve", lambda: V.tensor_tensor(out=xnv, in0=pv, in1=rstd[:, 0:nh].unsqueeze(2).to_broadcast([128, nh, 64]), op=ALU.mult),
                             r=["psm", krs], w=[kxn])
                        xgv = xg[:, 0:n].rearrange("p (h d) -> p h d", d=64)
                        S.op("pool", lambda: G.tensor_tensor(out=xgv, in0=xnv, in1=gain.unsqueeze(1).to_broadcast([128, nh, 64]), op=ALU.mult),
                             r=[kxn, "const"], w=[kxg])
                    else:
                        xgv = xg[:, 0:n].rearrange("p (h d) -> p h d", d=64)
                        S.op("act", lambda: A.activation(out=xgv, in_=pv, func=AF.Copy, scale=float(scale)), r=["psm"], w=[kxg])
                    if do_rope:
                        x1 = xgv[:, :, 0:32]
                        x2 = xgv[:, :, 32:64]
                        cb = cos_t[:, t, :].unsqueeze(1).to_broadcast([128, nh, 32])
                        sbb = sin_t[:, t, :].unsqueeze(1).to_broadcast([128, nh, 32])
                        rav = ra[:, 0:nh * 32].rearrange("p (h d) -> p h d", d=32)
                        rbv = rb[:, 0:nh * 32].rearrange("p (h d) -> p h d", d=32)
                        S.op("pool", lambda: G.tensor_tensor(out=rav, in0=x1, in1=cb, op=ALU.mult), r=[kxg], w=[kra])
                        S.op("dve", lambda: V.tensor_tensor(out=rbv, in0=x2, in1=sbb, op=ALU.mult), r=[kxg], w=[krb])
                        S.op("dve", lambda: V.tensor_tensor(out=outv[:, :, 0:32], in0=rav, in1=rbv, op=ALU.subtract), r=[kra, krb], w=[kout])
                        S.op("pool", lambda: G.tensor_tensor(out=rav, in0=x2, in1=cb, op=ALU.mult), r=[kxg, kra], w=[kra])
                        S.op("dve", lambda: V.tensor_tensor(out=rbv, in0=x1, in1=sbb, op=ALU.mult), r=[kxg, krb], w=[krb])
                        S.op("dve", lambda: V.tensor_tensor(out=outv[:, :, 32:64], in0=rav, in1=rbv, op=ALU.add), r=[kra, krb], w=[kout])
                    else:
                        S.op("dve", lambda: V.tensor_copy(out=outv, in_=xgv), r=[kxg], w=[kout])

                nblk = len(blocks)
                S.dma("pool", wblk[0][:, :, 0:blocks[0][2]], w_l[:, :, blocks[0][1]:blocks[0][1] + blocks[0][2]], w=["wblk0"])
                items = []

                def prefetch_w(bi):
                    if bi + 1 < nblk:
                        nn = blocks[bi + 1]
                        S.dma("pool", wblk[(bi + 1) % 2][:, :, 0:nn[2]], w_l[:, :, nn[1]:nn[1] + nn[2]], w=[f"wblk{(bi + 1) % 2}"])

                for bi, (nm, c0, n) in enumerate(blocks):
                    wb = wblk[bi % 2]
                    kw = f"wblk{bi % 2}"
                    if nm.startswith("gl"):
                        gi = int(nm[2:])
                        for Tq in range(NQ):
                            for j in range(4):
                                pb_ = len(items) % 2

                                def g0(bi=bi, wb=wb, kw=kw, Tq=Tq, j=j, pb_=pb_, first=(Tq == 0 and j == 0)):
                                    if first:
                                        prefetch_w(bi)
                                    pm = psm[pb_]
                                    for c in range(8):
                                        S.op("pe", lambda: T.matmul(pm[:], lhsT=wb[:, c, j * 128:(j + 1) * 128], rhs=hT[:, c, Tq * 512:(Tq + 1) * 512],
                                                                    start=(c == 0), stop=(c == 7)), r=[kw] + [f"hT{Tq * 4 + i_}" for i_ in range(4)], w=[f"psm{pb_}"])

                                def g1(gi=gi, Tq=Tq, j=j, pb_=pb_):
                                    pm = psm[pb_]
                                    gs = gst[pb_]
                                    S.op("act", lambda: A.activation(out=gs[:], in_=pm[:], func=AF.Sigmoid), r=[f"psm{pb_}"], w=[f"gst{pb_}"])
                                    S.dma("act", GT[gi * 4 + j, :, Tq * 512:(Tq + 1) * 512], gs[:], r=[f"gst{pb_}"], w=["GT"])
                                items.append([g0, g1, None])
                        continue
                    for t in range(NT):
                        pb_ = len(items) % 2

                        def s0(bi=bi, wb=wb, kw=kw, t=t, n=n, pb_=pb_):
                            if t == 0:
                                prefetch_w(bi)
                            pm = psm[pb_]
                            for c in range(8):
                                S.op("pe", lambda: T.matmul(pm[:, 0:n], lhsT=hT[:, c, t * 128:(t + 1) * 128], rhs=wb[:, c, 0:n],
                                                            start=(c == 0), stop=(c == 7)), r=[kw, f"hT{t}"], w=[f"psm{pb_}"])

                        def s1(nm=nm, t=t, pb_=pb_):
                            pm = psm[pb_]
                            kp = f"psm{pb_}"
                            S.lastw["psm"] = S.lastw[kp]
                            S.reads["psm"] = S.reads[kp]
                            ob = ob_2[t % 2]
                            kob = f"ob{t % 2}"
                            if nm in ("aq", "ak", "cq", "ck", "bq", "bk"):
                                pv = pm[:, 0:512].rearrange("p (h d) -> p h d", d=64)
                                obv = ob[:].rearrange("p (h d) -> p h d", d=64)
                                if nm[0] == "b":
                                    qk_post(pv, 8, None, False, False, t, obv, 0.125 if nm == "bq" else 1.0, kob)
                                else:
                                    gain = {"aq": aqn, "ak": akn, "cq": cqn, "ck": ckn}[nm][:, l * 64:(l + 1) * 64]
                                    qk_post(pv, 8, gain, True, True, t, obv, 1.0, kob)
                            elif nm in ("av", "bv", "cv"):
                                nh, dv = (4, 128) if nm == "av" else (8, 64)
                                vi = (t % 2) + (0 if nm == "av" else 2)
                                vs = vst[vi]
                                vv = vs[:, 0:nh * (dv + 1)].rearrange("p (h e) -> p h e", e=dv + 1)[:, :, 0:dv]
                                S.op("act", lambda: A.activation(out=vv, in_=pm[:, 0:512].rearrange("p (h e) -> p h e", e=dv), func=AF.Copy),
                                     r=[kp], w=[f"vst{vi}"])
                                S.dma("act", VX[nm[0]][t * 128:(t + 1) * 128, :], vs[:, 0:nh * (dv + 1)], r=[f"vst{vi}"], w=["VX"])
                            else:
                                obv = ob[:, 0:256].rearrange("p (h d) -> p h d", d=64)
                                qk_post(pm[:, 0:256].rearrange("p (h d) -> p h d", d=64), 4, None, False, True, t, obv, 1.0, kob)
                                obk = ob[:, 256:320].rearrange("p (h d) -> p h d", d=64)
                                qk_post(pm[:, 256:320].rearrange("p (h d) -> p h d", d=64), 1, ikn[:, l * 64:(l + 1) * 64], True, True, t, obk, 1.0, kob)
                                S.op("dve", lambda: V.tensor_copy(out=ob[:, 320:384], in_=ob[:, 256:320]), r=[kob], w=[kob])
                                S.op("act", lambda: A.activation(out=iw_all[:, t, :], in_=pm[:, 320:324], func=AF.Copy), r=[kp], w=["iw_all"])

                        def s2(nm=nm, t=t):
                            if nm in ("av", "bv", "cv"):
                                return
                            ob = ob_2[t % 2]
                            kob = f"ob{t % 2}"
                            sg = stg[(t // 4) % 2]
                            ksg = f"stg{(t // 4) % 2}"
                            tt = t % 4
                            pt = pst2[t % 2]
                            nj_ = 3 if nm == "idx" else 4
                            for j in range(nj_):
                                S.op("pe", lambda: T.transpose(out=pt[:, j, :], in_=ob[:, j * 128:(j + 1) * 128], identity=ident[:]),
                                     r=[kob], w=[f"p1pst{t % 2}"])
                            S.op("act", lambda: A.activation(out=sg[:, 0:nj_, tt * 128:(tt + 1) * 128], in_=pt[:, 0:nj_, :], func=AF.Copy),
                                 r=[f"p1pst{t % 2}"], w=[ksg])
                            if tt == 3:
                                t0_ = (t - 3) * 128
                                if nm == "idx":
                                    S.dma("act", IQT[:, :, t0_:t0_ + 512].rearrange("j p t -> p j t"), sg[:, 0:2, :], r=[ksg], w=["QKT"])
                                    S.dma("act", IKT[:, t0_:t0_ + 512], sg[:, 2, :], r=[ksg], w=["QKT"])
                                else:
                                    dst = (QT if nm[1] == "q" else KT)[nm[0]]
                                    S.dma("act", dst[:, :, t0_:t0_ + 512].rearrange("j p t -> p j t"), sg[:], r=[ksg], w=["QKT"])
                        items.append([s0, s1, s2])
                run_pipeline(items, 3)
                S.barrier()
                ph.close()

            for ph in ([ExitStack()] if 'pa' in PH else []):
                kt = sb(ph, "a_kt", [128, 4, SL], BF16)
                vx = sb(ph, "a_vx", [128, NT, 4 * 129], BF16)
                qt2 = [sb(ph, f"a_qt{i}", [128, 4, 512], BF16) for i in range(2)]
                pT2 = [sb(ph, f"a_pT{i}", [128, 512], BF16) for i in range(3)]
                a_qm = [sb(ph, f"a_qm{i}", [128, 512], BF16) for i in range(2)]
                o2 = [sb(ph, f"a_o{i}", [128, 4, 129], F32) for i in range(2)]
                rl = sb(ph, "a_rl", [128, 8], F32)
                y0 = sb(ph, "a_y0", [128, 4, 128], F32)
                y1 = sb(ph, "a_y1", [128, 4, 128], F32)
                ss = sb(ph, "a_ss", [128, 8], F32)
                yb2 = [sb(ph, f"a_yb{i}", [128, 4, 128], BF16) for i in range(2)]
                yts = [sb(ph, f"a_yts{i}", [128, 512], BF16) for i in range(2)]
                pss = [ps(ph, f"a_pss{i}", [128, 512], F32) for i in range(3)]
                po2 = [ps(ph, f"a_po{i}", [128, 2, 512], F32) for i in range(2)]
                ptr = ps(ph, "a_ptr", [128, 4, 128], BF16)
                S.dma("sp", kt[:], KT["a"].rearrange("j p t -> p j t"), r=["QKT"], w=["a_kt"])
                for v4 in range(4):
                    S.dma("sp", vx[:, v4 * 8:(v4 + 1) * 8, :], VX["a"][v4 * 1024:(v4 + 1) * 1024, :].rearrange("(t p) e -> p t e", p=128), r=["VX"], w=["a_vx"])

                def a_load_q(Q):
                    S.dma("sp", qt2[Q % 2][:], QT["a"][:, :, Q * 512:(Q + 1) * 512].rearrange("j p t -> p j t"), r=["QKT"], w=[f"a_qt{Q % 2}"])

                a_load_q(0)
                items = []
                deferred = []

                def a_epilogue(Q, h, c, po, kpo):
                    for bk in range(2):
                        S.op("act", lambda: A.activation(out=o2[c][:, 2 * bk:2 * bk + 2, :], in_=po[:, bk, 0:258].rearrange("p (r e) -> p r e", e=129), func=AF.Copy),
                             r=[kpo], w=[f"a_o{c}"])
                    if c == 0:
                        return
                    yb = yb2[h % 2]
                    kyb = f"a_yb{h % 2}"
                    S.op("dve", lambda: V.reciprocal(out=rl[:, 0:4], in_=o2[0][:, :, 128]), r=["a_o0"], w=["a_rl"])
                    S.op("dve", lambda: V.reciprocal(out=rl[:, 4:8], in_=o2[1][:, :, 128]), r=["a_o1"], w=["a_rl"])
                    S.op("dve", lambda: V.tensor_scalar(out=rl[:, 4:8], in0=rl[:, 4:8], scalar1=nlam_t[:, l:l + 1], scalar2=None, op0=ALU.mult),
                         r=["a_rl", "const"], w=["a_rl"])
                    S.op("dve", lambda: V.tensor_tensor(out=y0[:], in0=o2[0][:, :, 0:128], in1=rl[:, 0:4].unsqueeze(2).to_broadcast([128, 4, 128]), op=ALU.mult),
                         r=["a_o0", "a_rl"], w=["a_y0"])
                    S.op("pool", lambda: G.tensor_tensor(out=y1[:], in0=o2[1][:, :, 0:128], in1=rl[:, 4:8].unsqueeze(2).to_broadcast([128, 4, 128]), op=ALU.mult),
                         r=["a_o1", "a_rl"], w=["a_y1"])
                    S.op("dve", lambda: V.tensor_tensor(out=y0[:], in0=y0[:], in1=y1[:], op=ALU.add), r=["a_y0", "a_y1"], w=["a_y0"])
                    S.op("pool", lambda: G.tensor_tensor(out=y1[:], in0=y0[:], in1=y0[:], op=ALU.mult), r=["a_y0"], w=["a_y1"])
                    S.op("dve", lambda: V.tensor_reduce(out=ss[:, 0:4], in_=y1[:], axis=AX.X, op=ALU.add), r=["a_y1"], w=["a_ss"])
                    S.op("act", lambda: A.activation(out=ss[:, 0:4], in_=ss[:, 0:4], func=AF.Sqrt, bias=EPS, scale=1.0 / 128), r=["a_ss"], w=["a_ss"])
                    S.op("dve", lambda: V.reciprocal(out=ss[:, 0:4], in_=ss[:, 0:4]), r=["a_ss"], w=["a_ss"])
                    S.op("dve", lambda: V.tensor_tensor(out=y0[:], in0=y0[:], in1=ss[:, 0:4].unsqueeze(2).to_broadcast([128, 4, 128]), op=ALU.mult),
                         r=["a_y0", "a_ss"], w=["a_y0"])
                    S.op("dve", lambda: V.tensor_tensor(out=yb[:], in0=y0[:], in1=subg[:, l * 128:(l + 1) * 128].unsqueeze(1).to_broadcast([128, 4, 128]), op=ALU.mult),
                         r=["a_y0", "const"], w=[kyb])

                    def fin():
                        for qq in range(4):
                            S.op("pe", lambda: T.transpose(out=ptr[:, qq, :], in_=yb[:, qq, :], identity=ident[:]), r=[kyb], w=["a_ptr"])
                        ys = yts[h % 2]
                        S.op("act", lambda: A.activation(out=ys[:], in_=ptr[:].rearrange("p a b -> p (a b)"), func=AF.Copy), r=["a_ptr"], w=[f"a_yts{h % 2}"])
                        S.dma("act", YT[h, :, Q * 512:(Q + 1) * 512], ys[:], r=[f"a_yts{h % 2}"], w=["YT"])
                    deferred.append([4, fin])

                def run_deferred(force=False):
                    keep = []
                    for dly in deferred:
                        dly[0] -= 1
                        if dly[0] <= 0 or force:
                            dly[1]()
                        else:
                            keep.append(dly)
                    deferred[:] = keep

                gidx = 0
                for Q in range(NQ):
                    for h in range(4):
                        for c in range(2):
                            g = gidx
                            gidx += 1
                            nj = 4 * Q + 4
                            for j in range(nj):
                                b = len(items) % 3
                                jj = j - 4 * Q
                                c0 = max(0, jj) * 128

                                def s0(Q=Q, h=h, c=c, j=j, c0=c0, b=b, g=g, pre=(h == 0 and c == 0 and j == 0)):
                                    if pre and Q + 1 < NQ:
                                        a_load_q(Q + 1)
                                    qt = qt2[Q % 2]
                                    qm = a_qm[g % 2]
                                    kqm = f"a_qm{g % 2}"
                                    if j == 0:
                                        S.op("pool", lambda: G.memset(qm[:], 0.0), w=[kqm])
                                        S.op("pool", lambda: G.tensor_copy(out=qm[c * 64:(c + 1) * 64, :], in_=qt[c * 64:(c + 1) * 64, h, :]),
                                             r=[f"a_qt{Q % 2}", kqm], w=[kqm])
                                    S.op("pe", lambda: T.matmul(pss[b][:, c0:512], lhsT=kt[:, h, j * 128:(j + 1) * 128],
                                                                rhs=qm[:, c0:512], start=True, stop=True),
                                         r=["a_kt", kqm], w=[f"a_pss{b}"])

                                def s1(c0=c0, b=b, jj=jj):
                                    S.op("act", lambda: A.activation(out=pT2[b][:, c0:512], in_=pss[b][:, c0:512], func=AF.Exp),
                                         r=[f"a_pss{b}"], w=[f"a_pT{b}"])
                                    if jj >= 0:
                                        S.op("pool", lambda: G.memset(pT2[b][64:128, c0:c0 + 64], 0.0), w=[f"a_pT{b}"])

                                def s2(Q=Q, h=h, c=c, j=j, jj=jj, b=b, g=g, nj=nj):
                                    po = po2[g % 2]
                                    kpo = f"a_po{g % 2}"
                                    for qq in range(max(0, jj), 4):
                                        off = (qq % 2) * 129
                                        S.op("pe", lambda: T.matmul(po[:, qq // 2, off:off + 129], lhsT=pT2[b][:, qq * 128:(qq + 1) * 128],
                                                                    rhs=vx[:, j, h * 129:(h + 1) * 129], start=(j == 0 and qq % 2 == 0), stop=(j == 4 * Q + qq),
                                                                    skip_group_check=True),
                                             r=[f"a_pT{b}", "a_vx"], w=[kpo])
                                    if j == nj - 1:
                                        a_epilogue(Q, h, c, po, kpo)
                                    run_deferred()
                                items.append([s0, None, s1, None, s2])
                run_pipeline(items, 5)
                run_deferred(force=True)
                S.barrier()
                ph.close()

            for ph in ([ExitStack()] if 'pb' in PH else []):
                kt = sb(ph, "b_kt", [128, 4, SL], BF16)
                vx = sb(ph, "b_vx", [128, NT, 8 * 65], BF16)
                qt2 = [sb(ph, f"b_qt{i}", [128, 4, 512], BF16) for i in range(2)]
                e2 = [sb(ph, f"b_e{i}", [128, 512], F32) for i in range(3)]
                nl2 = [sb(ph, f"b_nl{i}", [128, 512], BF16) for i in range(3)]
                aT2 = [sb(ph, f"b_aT{i}", [128, 512], BF16) for i in range(3)]
                R = sb(ph, "b_R", [128, 512], BF16)
                qm2 = [sb(ph, f"b_qm{i}", [128, 512], BF16) for i in range(2)]
                yq2 = [sb(ph, f"b_yq{i}", [128, 4, 512], BF16) for i in range(2)]
                yts = [sb(ph, f"b_yts{i}", [128, 512], BF16) for i in range(2)]
                psz = [ps(ph, f"b_psz{i}", [128, 512], F32) for i in range(3)]
                psc = [ps(ph, f"b_psc{i}", [128, 512], F32) for i in range(2)]
                po2 = [ps(ph, f"b_po{i}", [128, 512], F32) for i in range(2)]
                ptr = ps(ph, "b_ptr", [128, 4, 128], BF16)
                S.dma("sp", kt[:], KT["b"].rearrange("j p t -> p j t"), r=["QKT"], w=["b_kt"])
                for v4 in range(4):
                    S.dma("sp", vx[:, v4 * 8:(v4 + 1) * 8, :], VX["b"][v4 * 1024:(v4 + 1) * 1024, :].rearrange("(t p) e -> p t e", p=128), r=["VX"], w=["b_vx"])

                def b_load_q(Q):
                    S.dma("sp", qt2[Q % 2][:], QT["b"][:, :, Q * 512:(Q + 1) * 512].rearrange("j p t -> p j t"), r=["QKT"], w=[f"b_qt{Q % 2}"])

                b_load_q(0)
                items = []
                gidx = 0
                for Q in range(NQ):
                    for h in range(8):
                        g = gidx
                        gidx += 1
                        pr, bs = h // 2, (h % 2) * 64
                        jtop = 4 * Q + 3
                        gstate = {"first": True}
                        for j in range(jtop, -1, -1):
                            b = len(items) % 3
                            bc = len(items) % 2
                            jj = j - 4 * Q
                            c0 = max(0, jj) * 128

                            def s0(Q=Q, h=h, j=j, c0=c0, b=b, pr=pr, bs=bs, pre=(h == 0 and j == jtop)):
                                if pre and Q + 1 < NQ:
                                    b_load_q(Q + 1)
                                qt = qt2[Q % 2]
                                S.op("pe", lambda: T.matmul(psz[b][:, c0:512], lhsT=kt[bs:bs + 64, pr, j * 128:(j + 1) * 128], rhs=qt[bs:bs + 64, pr, c0:512],
                                                            start=True, stop=True), r=["b_kt", f"b_qt{Q % 2}"], w=[f"b_psz{b}"])

                            def s1(c0=c0, b=b, jj=jj):
                                S.op("act", lambda: A.activation(out=e2[b][:, c0:512], in_=psz[b][:, c0:512], func=AF.Exp),
                                     r=[f"b_psz{b}"], w=[f"b_e{b}"])
                                S.op("act", lambda: A.activation(out=nl2[b][:, c0:512], in_=e2[b][:, c0:512], func=AF.Ln, bias=1.0),
                                     r=[f"b_e{b}"], w=[f"b_nl{b}"])
                                if jj >= 0:
                                    S.op("dve", lambda: V.tensor_tensor(out=nl2[b][:, c0:c0 + 128], in0=nl2[b][:, c0:c0 + 128], in1=strict[:], op=ALU.mult),
                                         r=[f"b_nl{b}", "const"], w=[f"b_nl{b}"])

                            def s2(Q=Q, j=j, c0=c0, b=b, bc=bc, pr=pr, bs=bs, g=g, jtop=jtop):
                                qm = qm2[g % 2]
                                kqm = f"b_qm{g % 2}"
                                qt = qt2[Q % 2]
                                if j == jtop:
                                    S.op("pool", lambda: G.memset(R[:], 0.0), w=["b_R"])
                                    S.op("pool", lambda: G.memset(qm[:], 0.0), w=[kqm])
                                    S.op("dve", lambda: V.tensor_copy(out=qm[bs:bs + 64, :], in_=qt[bs:bs + 64, pr, :]), r=[f"b_qt{Q % 2}", kqm], w=[kqm])
                                S.op("pe", lambda: T.matmul(psc[bc][:, c0:512], lhsT=kt[:, pr, j * 128:(j + 1) * 128], rhs=qm[:, c0:512], start=True, stop=False),
                                     r=["b_kt", kqm], w=[f"b_psc{bc}"])
                                last = (j == jtop)
                                S.op("pe", lambda: T.matmul(psc[bc][:, c0:512], lhsT=ntri[:], rhs=nl2[b][:, c0:512], start=False, stop=last),
                                     r=[f"b_nl{b}", "const"], w=[f"b_psc{bc}"])
                                if not last:
                                    S.op("pe", lambda: T.matmul(psc[bc][:, c0:512], lhsT=nones[:], rhs=R[:, c0:512], start=False, stop=True),
                                         r=["b_R", "const"], w=[f"b_psc{bc}"])
                                if j > 0:
                                    S.op("dve", lambda: V.tensor_tensor(out=R[:, c0:512], in0=R[:, c0:512], in1=nl2[b][:, c0:512], op=ALU.add),
                                         r=["b_R", f"b_nl{b}"], w=["b_R"])

                            def s3(c0=c0, b=b, bc=bc, jj=jj):
                                S.op("act", lambda: A.activation(out=aT2[b][:, c0:512], in_=psc[bc][:, c0:512], func=AF.Exp),
                                     r=[f"b_psc{bc}"], w=[f"b_aT{b}"])
                                if jj >= 0:
                                    S.op("dve", lambda: V.tensor_tensor(out=aT2[b][:, c0:c0 + 128], in0=aT2[b][:, c0:c0 + 128], in1=strict[:], op=ALU.mult),
                                         r=[f"b_aT{b}", "const"], w=[f"b_aT{b}"])

                            def s4(Q=Q, h=h, j=j, jj=jj, b=b, g=g, gstate=gstate):
                                po = po2[g % 2]
                                kpo = f"b_po{g % 2}"
                                yq = yq2[Q % 2]
                                kyq = f"b_yq{Q % 2}"
                                for qq in range(max(0, jj), 4):
                                    S.op("pe", lambda: T.matmul(po[:, qq * 65:(qq + 1) * 65], lhsT=aT2[b][:, qq * 128:(qq + 1) * 128],
                                                                rhs=vx[:, j, h * 65:(h + 1) * 65], start=gstate["first"], stop=(j == 0),
                                                                skip_group_check=True),
                                         r=[f"b_aT{b}", "b_vx"], w=[kpo])
                                    gstate["first"] = False
                                if j == 0:
                                    S.op("act", lambda: A.activation(out=yq[:, :, h * 64:(h + 1) * 64],
                                                                     in_=po[:, 0:260].rearrange("p (a e) -> p a e", e=65)[:, :, 0:64], func=AF.Copy),
                                         r=[kpo], w=[kyq])
                                    if h == 7:
                                        for cch in range(4):
                                            for qq in range(4):
                                                S.op("pe", lambda: T.transpose(out=ptr[:, qq, :], in_=yq[:, qq, cch * 128:(cch + 1) * 128], identity=ident[:]),
                                                     r=[kyq], w=["b_ptr"])
                                            ys = yts[cch % 2]
                                            S.op("act", lambda: A.activation(out=ys[:], in_=ptr[:].rearrange("p a b -> p (a b)"), func=AF.Copy), r=["b_ptr"], w=[f"b_yts{cch % 2}"])
                                            S.dma("act", YT[4 + cch, :, Q * 512:(Q + 1) * 512], ys[:], r=[f"b_yts{cch % 2}"], w=["YT"])
                            items.append([s0, None, s1, None, s2, s3, None, s4])
                run_pipeline(items, 8)
                S.barrier()
                ph.close()

            for ph in ([ExitStack()] if 'pc' in PH else []):
                kt = sb(ph, "c_kt", [128, 4, SL], BF16)
                vx = sb(ph, "c_vx", [128, NT, 8 * 65], BF16)
                ikt = sb(ph, "c_ikt", [128, SL], BF16)
                qt2 = [sb(ph, f"c_qt{i}", [128, 4, 512], BF16) for i in range(2)]
                iqt2 = [sb(ph, f"c_iqt{i}", [128, 2, 512], BF16) for i in range(2)]
                score2 = [sb(ph, f"c_score{i}", [128, SL], F32) for i in range(2)]
                maskq2 = [sb(ph, f"c_maskq{i}", [128, SL], BF16) for i in range(2)]
                bst = [sb(ph, f"c_bst{i}", [128, 8], F32) for i in range(2)]
                dft = [sb(ph, f"c_dft{i}", [128, NSTEP + 1], F32) for i in range(2)]
                maskT = sb(ph, "c_maskT", [128, NT, 512], BF16)
                rb2 = [sb(ph, f"c_r{i}", [128, 512], F32) for i in range(2)]
                pT2 = [sb(ph, f"c_pT{i}", [128, 512], BF16) for i in range(3)]
                bs_ = sb(ph, "c_bis", [128, 8], F32)
                osb2 = [sb(ph, f"c_o{i}", [128, 4, 65], F32) for i in range(2)]
                rl2 = [sb(ph, f"c_rl{i}", [128, 4], F32) for i in range(2)]
                yq2 = [sb(ph, f"c_yq{i}", [128, 4, 512], BF16) for i in range(2)]
                yts = [sb(ph, f"c_yts{i}", [128, 512], BF16) for i in range(2)]
                psi = [ps(ph, f"c_psi{i}", [128, 512], F32) for i in range(2)]
                pss = [ps(ph, f"c_pss{i}", [128, 512], F32) for i in range(3)]
                po2 = [ps(ph, f"c_po{i}", [128, 512], F32) for i in range(2)]
                ptr = ps(ph, "c_ptr", [128, 4, 128], BF16)
                S.dma("sp", kt[:], KT["c"].rearrange("j p t -> p j t"), r=["QKT"], w=["c_kt"])
                for v4 in range(4):
                    S.dma("sp", vx[:, v4 * 8:(v4 + 1) * 8, :], VX["c"][v4 * 1024:(v4 + 1) * 1024, :].rearrange("(t p) e -> p t e", p=128), r=["VX"], w=["c_vx"])
                S.dma("sp", ikt[:], IKT[:, :], r=["QKT"], w=["c_ikt"])

                def c_load_q(Q):
                    kq_ = f"c_qt{Q % 2}"
                    S.dma("sp", qt2[Q % 2][:], QT["c"][:, :, Q * 512:(Q + 1) * 512].rearrange("j p t -> p j t"), r=["QKT"], w=[kq_])
                    S.dma("sp", iqt2[Q % 2][:], IQT[:, :, Q * 512:(Q + 1) * 512].rearrange("j p t -> p j t"), r=["QKT"], w=[kq_])

                c_load_q(0)
                ir = 0
                gidx = 0
                for Q in range(NQ):
                    qt = qt2[Q % 2]
                    iqt = iqt2[Q % 2]
                    kq = f"c_qt{Q % 2}"
                    if Q + 1 < NQ:
                        c_load_q(Q + 1)
                    for pair in range(2):
                        info = []
                        for ci in range(2):
                            qq = 2 * pair + ci
                            qb = 4 * Q + qq
                            nadm = (qb + 1) * 128
                            info.append((qq, qb, nadm))
                            sc_t = score2[ci]
                            ksc = f"c_score{ci}"
                            nkt = (nadm + 511) // 512
                            for k5 in range(nkt):
                                wd = min(512, nadm - k5 * 512)
                                for h in range(4):
                                    b = ir % 2
                                    ir += 1
                                    hb_ = (h % 2) * 64
                                    S.op("pe", lambda: T.matmul(psi[b][:, 0:wd], lhsT=iqt[hb_:hb_ + 64, h // 2, qq * 128:(qq + 1) * 128],
                                                                rhs=ikt[hb_:hb_ + 64, k5 * 512:k5 * 512 + wd], start=True, stop=True),
                                         r=[kq, "c_ikt"], w=[f"c_psi{b}"])
                                    S.op("act", lambda: A.activation(out=rb2[b][:, 0:wd], in_=psi[b][:, 0:wd], func=AF.Relu),
                                         r=[f"c_psi{b}"], w=[f"c_r{b}"])
                                    sc = sc_t[:, k5 * 512:k5 * 512 + wd]
                                    if h == 0:
                                        S.op("dve", lambda: V.tensor_scalar(out=sc, in0=rb2[b][:, 0:wd], scalar1=iw_all[:, qb, 0:1], scalar2=None, op0=ALU.mult),
                                             r=[f"c_r{b}", "iw_all"], w=[ksc])
                                    else:
                                        S.op("dve", lambda: V.scalar_tensor_tensor(out=sc, in0=rb2[b][:, 0:wd], scalar=iw_all[:, qb, h:h + 1], in1=sc,
                                                                                   op0=ALU.mult, op1=ALU.add),
                                             r=[f"c_r{b}", "iw_all", ksc], w=[ksc])
                            S.op("dve", lambda: V.memset(sc_t[0:64, qb * 128 + 64:(qb + 1) * 128], NEG), r=[ksc], w=[ksc])
                        bisect = info[0][1] >= 2
                        for ci in range(2):
                            qq, qb, nadm = info[ci]
                            sc_t = score2[ci]
                            ksc = f"c_score{ci}"
                            stt = bst[ci]
                            df = dft[ci]
                            KB = [f"c_bis{ci}"]
                            if not bisect:
                                S.op("dve", lambda: V.memset(stt[:, 5:6], -1.0e29), r=KB, w=KB)
                                continue
                            nfull = qb * 128 + 64
                            S.op("dve", lambda: V.tensor_reduce(out=stt[:, 4:5], in_=sc_t[:, 0:nadm], axis=AX.X, op=ALU.max), r=[ksc], w=KB)
                            S.op("dve", lambda: V.tensor_reduce(out=stt[:, 5:6], in_=sc_t[:, 0:nfull], axis=AX.X, op=ALU.min), r=[ksc], w=KB)
                            S.op("dve", lambda: V.tensor_scalar(out=stt[:, 5:6], in0=stt[:, 5:6], scalar1=-1.0, scalar2=None, op0=ALU.add), r=KB, w=KB)
                            S.op("dve", lambda: V.tensor_tensor(out=stt[:, 1:2], in0=stt[:, 4:5], in1=stt[:, 5:6], op=ALU.subtract), r=KB, w=KB)
                            S.op("dve", lambda: V.tensor_scalar(out=df[:], in0=pw2[:], scalar1=stt[:, 1:2], scalar2=None, op0=ALU.mult), r=KB + ["const"], w=KB)
                            if ci == 0:
                                S.op("dve", lambda: V.tensor_tensor(out=stt[:, 0:1], in0=stt[:, 5:6], in1=df[:, 0:1], op=ALU.add), r=KB, w=KB)
                            else:
                                S.op("dve", lambda: V.scalar_tensor_tensor(out=stt[:, 0:1], in0=stt[:, 5:6], scalar=-1.0, in1=df[:, 0:1],
                                                                           op0=ALU.mult, op1=ALU.subtract), r=KB, w=KB)
                        if bisect:
                            for i in range(NSTEP):
                                lastst = (i == NSTEP - 1)
                                qq, qb, nadm = info[0]
                                stt, df, KB, ksc = bst[0], dft[0], ["c_bis0"], "c_score0"
                                S.op("dve", lambda: V.tensor_scalar(out=maskq2[0][:, 0:nadm], in0=score2[0][:, 0:nadm], scalar1=stt[:, 0:1], scalar2=None,
                                                                    op0=ALU.is_gt, op1=ALU.add, accum_out=stt[:, 2:3]),
                                     r=KB + [ksc, "c_maskq0"], w=KB + ["c_maskq0"])
                                S.op("dve", lambda: V.tensor_scalar(out=stt[:, 3:4], in0=stt[:, 2:3], scalar1=float(TOPK), scalar2=df[:, i:i + 1],
                                                                    op0=ALU.is_ge, op1=ALU.mult), r=KB, w=KB)
                                if not lastst:
                                    S.op("dve", lambda: V.scalar_tensor_tensor(out=stt[:, 0:1], in0=stt[:, 3:4], scalar=df[:, i + 1:i + 2], in1=stt[:, 0:1],
                                                                               op0=ALU.subtract, op1=ALU.add), r=KB, w=KB)
                                else:
                                    S.op("dve", lambda: V.scalar_tensor_tensor(out=stt[:, 5:6], in0=stt[:, 3:4], scalar=df[:, i:i + 1], in1=stt[:, 0:1],
                                                                               op0=ALU.subtract, op1=ALU.add), r=KB, w=KB)
                                qq, qb, nadm = info[1]
                                stt, df, KB, ksc = bst[1], dft[1], ["c_bis1"], "c_score1"
                                S.op("act", lambda: A.activation(out=maskq2[1][:, 0:nadm], in_=score2[1][:, 0:nadm], func=AF.Sign, bias=stt[:, 0:1],
                                                                 accum_out=stt[:, 2:3]),
                                     r=KB + [ksc, "c_maskq1"], w=KB + ["c_maskq1"])
                                S.op("pool", lambda: G.tensor_scalar(out=stt[:, 3:4], in0=stt[:, 2:3], scalar1=float(2 * TOPK - nadm), scalar2=df[:, i:i + 1],
                                                                     op0=ALU.is_ge, op1=ALU.mult), r=KB, w=KB)
                                S.op("pool", lambda: G.tensor_tensor(out=stt[:, 0:1], in0=stt[:, 0:1], in1=stt[:, 3:4], op=ALU.subtract), r=KB, w=KB)
                                if not lastst:
                                    S.op("pool", lambda: G.tensor_tensor(out=stt[:, 0:1], in0=stt[:, 0:1], in1=df[:, i + 1:i + 2], op=ALU.add), r=KB, w=KB)
                                else:
                                    S.op("pool", lambda: G.tensor_tensor(out=stt[:, 0:1], in0=stt[:, 0:1], in1=df[:, i:i + 1], op=ALU.add), r=KB, w=KB)
                                    S.op("pool", lambda: G.tensor_scalar(out=stt[:, 5:6], in0=stt[:, 0:1], scalar1=-1.0, scalar2=None, op0=ALU.mult), r=KB, w=KB)
                        for ci in range(2):
                            qq, qb, nadm = info[ci]
                            mq = maskq2[ci]
                            kmq = f"c_maskq{ci}"
                            S.op("dve", lambda: V.tensor_scalar(out=mq[:, 0:nadm], in0=score2[ci][:, 0:nadm], scalar1=bst[ci][:, 5:6], scalar2=None, op0=ALU.is_gt),
                                 r=[f"c_bis{ci}", f"c_score{ci}", kmq], w=[kmq])
                            for k1 in range(0, qb + 1, 4):
                                nk = min(4, qb + 1 - k1)
                                for u in range(nk):
                                    S.op("pe", lambda: T.transpose(out=ptr[:, u, :], in_=mq[:, (k1 + u) * 128:(k1 + u + 1) * 128], identity=ident[:]),
                                         r=[kmq], w=["c_ptr"])
                                S.op("act", lambda: A.activation(out=maskT[:, k1:k1 + nk, qq * 128:(qq + 1) * 128], in_=ptr[:, 0:nk, :], func=AF.Copy),
                                     r=["c_ptr"], w=["c_maskT"])
                    items = []
                    yq = yq2[Q % 2]
                    kyq = f"c_yq{Q % 2}"
                    for h in range(8):
                        g = gidx
                        gidx += 1
                        pr, bs = h // 2, (h % 2) * 64
                        nj = 4 * Q + 4
                        gstate = {"first": True}
                        for j in range(nj):
                            b = len(items) % 3
                            jj = j - 4 * Q
                            c0 = max(0, jj) * 128

                            def s0(j=j, c0=c0, b=b, pr=pr, bs=bs):
                                S.op("pe", lambda: T.matmul(pss[b][:, c0:512], lhsT=kt[bs:bs + 64, pr, j * 128:(j + 1) * 128],
                                                            rhs=qt[bs:bs + 64, pr, c0:512], start=True, stop=True),
                                     r=["c_kt", kq], w=[f"c_pss{b}"])

                            def s1(j=j, c0=c0, b=b):
                                S.op("act", lambda: A.activation(out=pT2[b][:, c0:512], in_=pss[b][:, c0:512], func=AF.Exp),
                                     r=[f"c_pss{b}"], w=[f"c_pT{b}"])
                                if b == 0:
                                    S.op("dve", lambda: V.tensor_tensor(out=pT2[b][:, c0:512], in0=pT2[b][:, c0:512], in1=maskT[:, j, c0:512], op=ALU.mult),
                                         r=[f"c_pT{b}", "c_maskT"], w=[f"c_pT{b}"])
                                else:
                                    S.op("pool", lambda: G.tensor_tensor(out=pT2[b][:, c0:512], in0=pT2[b][:, c0:512], in1=maskT[:, j, c0:512], op=ALU.mult),
                                         r=[f"c_pT{b}", "c_maskT"], w=[f"c_pT{b}"])

                            def s2(h=h, j=j, jj=jj, b=b, g=g, gstate=gstate, nj=nj):
                                po = po2[g % 2]
                                kpo = f"c_po{g % 2}"
                                osb = osb2[g % 2]
                                rl = rl2[g % 2]
                                for q4 in range(max(0, jj), 4):
                                    S.op("pe", lambda: T.matmul(po[:, q4 * 65:(q4 + 1) * 65], lhsT=pT2[b][:, q4 * 128:(q4 + 1) * 128],
                                                                rhs=vx[:, j, h * 65:(h + 1) * 65], start=gstate["first"], stop=(j == nj - 1),
                                                                skip_group_check=True),
                                         r=[f"c_pT{b}", "c_vx"], w=[kpo])
                                    gstate["first"] = False
                                if j == nj - 1:
                                    S.op("act", lambda: A.activation(out=osb[:], in_=po[:, 0:260].rearrange("p (a e) -> p a e", e=65), func=AF.Copy),
                                         r=[kpo], w=[f"c_o{g % 2}"])
                                    S.op("dve", lambda: V.reciprocal(out=rl[:], in_=osb[:, :, 64]), r=[f"c_o{g % 2}"], w=[f"c_rl{g % 2}"])
                                    S.op("dve", lambda: V.tensor_tensor(out=yq[:, :, h * 64:(h + 1) * 64], in0=osb[:, :, 0:64],
                                                                        in1=rl[:].unsqueeze(2).to_broadcast([128, 4, 64]), op=ALU.mult),
                                         r=[f"c_o{g % 2}", f"c_rl{g % 2}"], w=[kyq])
                            items.append([s0, None, s1, None, s2])
                    run_pipeline(items, 5)
                    for cch in range(4):
                        for q4 in range(4):
                            S.op("pe", lambda: T.transpose(out=ptr[:, q4, :], in_=yq[:, q4, cch * 128:(cch + 1) * 128], identity=ident[:]),
                                 r=[kyq], w=["c_ptr"])
                        ys = yts[cch % 2]
                        S.op("act", lambda: A.activation(out=ys[:], in_=ptr[:].rearrange("p a b -> p (a b)"), func=AF.Copy), r=["c_ptr"], w=[f"c_yts{cch % 2}"])
                        S.dma("act", YT[8 + cch, :, Q * 512:(Q + 1) * 512], ys[:], r=[f"c_yts{cch % 2}"], w=["YT"])
                S.barrier()
                ph.close()

            for ph in ([ExitStack()] if 'pm' in PH else []):
                wbr = [sb(ph, f"m_wbr{i}", [128, 4, D], BF16) for i in range(3)]
                wo = sb(ph, "m_wo", [128, 8, D], BF16)
                yt2 = [sb(ph, f"m_yt{i}", [128, 12, 512], BF16) for i in range(2)]
                gt2 = [sb(ph, f"m_gt{i}", [128, 24, 512], BF16) for i in range(2)]
                m0_2 = [sb(ph, f"m_m0{i}", [128, 512], F32) for i in range(2)]
                m1_2 = [sb(ph, f"m_m1{i}", [128, 512], F32) for i in range(2)]
                m2_2 = [sb(ph, f"m_m2{i}", [128, 512], F32) for i in range(2)]
                mT = sb(ph, "m_mT", [128, 8, 512], BF16)
                xb2 = [sb(ph, f"m_x{i}", [128, D], F32) for i in range(2)]
                ps3 = [ps(ph, f"m_ps{i}", [128, 512], F32) for i in range(6)]
                pso = [ps(ph, f"m_pso{i}", [128, 512], F32) for i in range(2)]
                for i in range(3):
                    load_w_bf16(wbr[i][:], w_br[i][l].rearrange("(c p) n -> p c n", p=128), "m_w")
                load_w_bf16(wo[:], w_out[l].rearrange("(c p) n -> p c n", p=128), "m_w")
                io = 0
                for Tq in range(NQ):
                    yt = yt2[Tq % 2]
                    gt = gt2[Tq % 2]
                    kin = f"m_in{Tq % 2}"
                    for b3 in range(3):
                        S.dma("sp", yt[:, b3 * 4:(b3 + 1) * 4, :], YT[b3 * 4:(b3 + 1) * 4, :, Tq * 512:(Tq + 1) * 512].rearrange("j p t -> p j t"), r=["YT"], w=[kin])
                        S.dma("sp", gt[:, b3 * 8:(b3 + 1) * 8, :], GT[b3 * 8:(b3 + 1) * 8, :, Tq * 512:(Tq + 1) * 512].rearrange("j p t -> p j t"), r=["GT"], w=[kin])
                    for fc in range(8):
                        fp_ = fc % 2
                        m0, m1, m2 = m0_2[fp_], m1_2[fp_], m2_2[fp_]
                        for br in range(3):
                            pb3 = fp_ * 3 + br
                            for kc in range(4):
                                S.op("pe", lambda: T.matmul(ps3[pb3][:], lhsT=wbr[br][:, kc, fc * 128:(fc + 1) * 128], rhs=yt[:, br * 4 + kc, :],
                                                            start=(kc == 0), stop=(kc == 3)), r=["m_w", kin], w=[f"m_ps{pb3}"])
                        S.op("dve", lambda: V.tensor_tensor(out=m0[:], in0=ps3[fp_ * 3][:], in1=gt[:, fc, :], op=ALU.mult), r=[f"m_ps{fp_ * 3}", kin], w=[f"m_m0{fp_}"])
                        S.op("dve", lambda: V.tensor_tensor(out=m1[:], in0=ps3[fp_ * 3 + 1][:], in1=gt[:, 8 + fc, :], op=ALU.mult), r=[f"m_ps{fp_ * 3 + 1}", kin], w=[f"m_m1{fp_}"])
                        S.op("dve", lambda: V.tensor_tensor(out=m2[:], in0=ps3[fp_ * 3 + 2][:], in1=gt[:, 16 + fc, :], op=ALU.mult), r=[f"m_ps{fp_ * 3 + 2}", kin], w=[f"m_m2{fp_}"])
                        S.op("pool", lambda: G.tensor_tensor(out=m0[:], in0=m0[:], in1=m1[:], op=ALU.add), r=[f"m_m0{fp_}", f"m_m1{fp_}"], w=[f"m_m0{fp_}"])
                        S.op("pool", lambda: G.tensor_tensor(out=mT[:, fc, :], in0=m0[:], in1=m2[:], op=ALU.add), r=[f"m_m0{fp_}", f"m_m2{fp_}"], w=["m_mT"])
                    for tt in range(4):
                        t = Tq * 4 + tt
                        xt = xb2[t % 2]
                        kx = f"m_x{t % 2}"
                        S.dma("sp", xt[:], xsrc[t * 128:(t + 1) * 128, :], r=[f"xr{t}"], w=[kx])
                        for nb in range(2):
                            pq = pso[io % 2]
                            kpq = f"m_pso{io % 2}"
                            io += 1
                            for fc in range(8):
                                S.op("pe", lambda: T.matmul(pq[:], lhsT=mT[:, fc, tt * 128:(tt + 1) * 128], rhs=wo[:, fc, nb * 512:(nb + 1) * 512],
                                                            start=(fc == 0), stop=(fc == 7)), r=["m_mT", "m_w"], w=[kpq])
                            S.op("dve", lambda: V.tensor_tensor(out=xt[:, nb * 512:(nb + 1) * 512], in0=pq[:], in1=xt[:, nb * 512:(nb + 1) * 512], op=ALU.add),
                                 r=[kpq, kx], w=[kx])
                        S.dma("sp", xres[t * 128:(t + 1) * 128, :], xt[:], r=[kx], w=[f"xr{t}"])
                S.barrier()
                ph.close()

            for ph in ([ExitStack()] if 'pf' in PH else []):
                wu = sb(ph, "f_wu", [128, 8, 4 * D], BF16)
                wd = sb(ph, "f_wd", [128, 32, D], BF16)
                xb4 = [sb(ph, f"f_x{i}", [128, D], F32) for i in range(4)]
                hb = sb(ph, "f_hb", [128, D], BF16)
                junk = sb(ph, "f_junk", [128, D], BF16)
                ssq = sb(ph, "f_ssq", [128, 8], F32)
                rstd = sb(ph, "f_rstd", [128, 8], F32)
                hmT2 = [sb(ph, f"f_hmT{i}", [128, 8, 256], BF16) for i in range(2)]
                rr2 = [sb(ph, f"f_r{i}", [128, 256], F32) for i in range(2)]
                uT = sb(ph, "f_uT", [128, 32, 256], BF16)
                pst2 = [ps(ph, f"f_pst{i}", [128, 4, 128], BF16) for i in range(2)]
                psu = [ps(ph, f"f_psu{i}", [128, 256], F32) for i in range(2)]
                psd = [ps(ph, f"f_psd{i}", [128, 512], F32) for i in range(2)]
                for c in range(8):
                    load_w_bf16(wu[:, c, :], w_up[l, c * 128:(c + 1) * 128, :], "f_wu")
                for c4 in range(4):
                    load_w_bf16(wd[:, c4 * 8:(c4 + 1) * 8, :], w_down[l, c4 * 1024:(c4 + 1) * 1024, :].rearrange("(c p) n -> p c n", p=128), "f_wd")
                iof = [0]
                items = []
                for tg in range(NT // 2):
                    def s0(tg=tg):
                        hm = hmT2[tg % 2]
                        for t2 in range(2):
                            t = tg * 2 + t2
                            xi = (tg % 2) * 2 + t2
                            xt = xb4[xi]
                            kx = f"f_x{xi}"
                            S.dma("sp", xt[:], xres[t * 128:(t + 1) * 128, :], r=[f"xr{t}"], w=[kx])
                            norm_transpose(xt[:], kx, gM, l, hb, pst2, lambda c, t2=t2, hm=hm: hm[:, c, t2 * 128:(t2 + 1) * 128], f"f_hmT{tg % 2}",
                                           ssq, rstd, junk, "f")

                    def s1(tg=tg):
                        hm = hmT2[tg % 2]
                        khm = f"f_hmT{tg % 2}"
                        for fc in range(32):
                            pu = psu[fc % 2]
                            for c in range(8):
                                S.op("pe", lambda: T.matmul(pu[:], lhsT=wu[:, c, fc * 128:(fc + 1) * 128], rhs=hm[:, c, :], start=(c == 0), stop=(c == 7)),
                                     r=["f_wu", khm], w=[f"f_psu{fc % 2}"])
                            rr = rr2[fc % 2]
                            S.op("act", lambda: A.activation(out=rr[:], in_=pu[:], func=AF.Relu), r=[f"f_psu{fc % 2}"], w=[f"f_r{fc % 2}"])
                            if fc % 2 == 0:
                                S.op("pool", lambda: G.tensor_tensor(out=uT[:, fc, :], in0=rr[:], in1=rr[:], op=ALU.mult), r=[f"f_r{fc % 2}"], w=["f_uT"])
                            else:
                                S.op("dve", lambda: V.tensor_tensor(out=uT[:, fc, :], in0=rr[:], in1=rr[:], op=ALU.mult), r=[f"f_r{fc % 2}"], w=["f_uT"])
                        for t2 in range(2):
                            t = tg * 2 + t2
                            xi = (tg % 2) * 2 + t2
                            xt = xb4[xi]
                            kx = f"f_x{xi}"
                            for nb in range(2):
                                pq = psd[iof[0] % 2]
                                kpq = f"f_psd{iof[0] % 2}"
                                iof[0] += 1
                                for fc in range(32):
                                    S.op("pe", lambda: T.matmul(pq[:], lhsT=uT[:, fc, t2 * 128:(t2 + 1) * 128], rhs=wd[:, fc, nb * 512:(nb + 1) * 512],
                                                                start=(fc == 0), stop=(fc == 31)), r=["f_uT", "f_wd"], w=[kpq])
                                S.op("dve", lambda: V.tensor_tensor(out=xt[:, nb * 512:(nb + 1) * 512], in0=pq[:], in1=xt[:, nb * 512:(nb + 1) * 512], op=ALU.add),
                                     r=[kpq, kx], w=[kx])
                            S.dma("sp", xres[t * 128:(t + 1) * 128, :], xt[:], r=[kx], w=[f"xr{t}"])
                    items.append([s0, s1])
                run_pipeline(items, 2)
                S.barrier()
                ph.close()

            for ph in ([ExitStack()] if 'pp' in PH else []):
                wg = sb(ph, "e_wg", [128, 8, D], BF16)
                wp = sb(ph, "e_wp", [128, 2, D], BF16)
                xb3 = [sb(ph, f"e_x{i}", [128, D], F32) for i in range(3)]
                pb3_ = [sb(ph, f"e_p{i}", [128, 256], F32) for i in range(3)]
                pbb = sb(ph, "e_pbb", [128, 256], BF16)
                hb = sb(ph, "e_hb", [128, D], BF16)
                junk = sb(ph, "e_junk", [128, D], BF16)
                ssq = sb(ph, "e_ssq", [128, 8], F32)
                rstd = sb(ph, "e_rstd", [128, 8], F32)
                hpT2 = [sb(ph, f"e_hpT{i}", [128, 8, 128], BF16) for i in range(2)]
                pT2 = [sb(ph, f"e_pT{i}", [128, 2, 128], BF16) for i in range(2)]
                sg2 = [sb(ph, f"e_sg{i}", [128, 512], F32) for i in range(2)]
                tm2 = [sb(ph, f"e_tm{i}", [128, 512], F32) for i in range(2)]
                pst2 = [ps(ph, f"e_pst{i}", [128, 4, 128], BF16) for i in range(2)]
                ptp = ps(ph, "e_ptp", [128, 2, 128], BF16)
                psg = [ps(ph, f"e_psg{i}", [128, 512], F32) for i in range(2)]
                psp = [ps(ph, f"e_psp{i}", [128, 512], F32) for i in range(2)]
                load_w_bf16(wg[:], w_ple_gate[l].rearrange("(c p) n -> p c n", p=128), "e_w")
                load_w_bf16(wp[:], w_ple_proj[l].rearrange("(c p) n -> p c n", p=128), "e_w")
                dst = out_d if l == depth - 1 else xres
                kdst = "out"
                ioe = [0]
                items = []
                for t in range(NT):
                    def s0(t=t):
                        xt = xb3[t % 3]
                        kx = f"e_x{t % 3}"
                        pb = pb3_[t % 3]
                        hp = hpT2[t % 2]
                        S.dma("sp", xt[:], xres[t * 128:(t + 1) * 128, :], r=[f"xr{t}"], w=[kx])
                        S.dma("sp", pb[:], p_in[l, t * 128:(t + 1) * 128, :], w=[f"e_p{t % 3}"])
                        norm_transpose(xt[:], kx, gP, l, hb, pst2, lambda c, hp=hp: hp[:, c, :], f"e_hpT{t % 2}", ssq, rstd, junk, "e")
                        S.op("pool", lambda: G.tensor_copy(out=pbb[:], in_=pb[:]), r=[f"e_p{t % 3}"], w=["e_pbb"])
                        for c in range(2):
                            S.op("pe", lambda: T.transpose(out=ptp[:, c, :], in_=pbb[:, c * 128:(c + 1) * 128], identity=ident[:]), r=["e_pbb"], w=["e_ptp"])
                        S.op("act", lambda: A.activation(out=pT2[t % 2][:], in_=ptp[:], func=AF.Copy), r=["e_ptp"], w=[f"e_pT{t % 2}"])

                    def s1(t=t):
                        xt = xb3[t % 3]
                        kx = f"e_x{t % 3}"
                        hp = hpT2[t % 2]
                        pT = pT2[t % 2]
                        for nb in range(2):
                            b = ioe[0] % 2
                            ioe[0] += 1
                            for c in range(8):
                                S.op("pe", lambda: T.matmul(psg[b][:], lhsT=hp[:, c, :], rhs=wg[:, c, nb * 512:(nb + 1) * 512], start=(c == 0), stop=(c == 7)),
                                     r=[f"e_hpT{t % 2}", "e_w"], w=[f"e_psg{b}"])
                            for c in range(2):
                                S.op("pe", lambda: T.matmul(psp[b][:], lhsT=pT[:, c, :], rhs=wp[:, c, nb * 512:(nb + 1) * 512], start=(c == 0), stop=(c == 1)),
                                     r=[f"e_pT{t % 2}", "e_w"], w=[f"e_psp{b}"])
                            S.op("act", lambda: A.activation(out=sg2[b][:], in_=psg[b][:], func=AF.Sigmoid), r=[f"e_psg{b}"], w=[f"e_sg{b}"])
                            S.op("dve", lambda: V.tensor_tensor(out=tm2[b][:], in0=psp[b][:], in1=sg2[b][:], op=ALU.mult), r=[f"e_psp{b}", f"e_sg{b}"], w=[f"e_tm{b}"])
                            S.op("pool", lambda: G.tensor_tensor(out=xt[:, nb * 512:(nb + 1) * 512], in0=xt[:, nb * 512:(nb + 1) * 512], in1=tm2[b][:], op=ALU.add),
                                 r=[f"e_tm{b}", kx], w=[kx])
                        S.dma("sp", dst[t * 128:(t + 1) * 128, :], xt[:], r=[kx], w=[kdst if l == depth - 1 else f"xr{t}"])
                    items.append([s0, s1])
                run_pipeline(items, 2)
                S.barrier()
                ph.close()
        S.barrier()
        build_nc.ninstr = dict(S.ninstr)
    return nc


def _host_layout(inputs, b):
    f32 = np.float32
    m = {}
    m["x"] = np.ascontiguousarray(inputs["x"][b], dtype=f32)
    m["p"] = np.ascontiguousarray(inputs["p"][:, b], dtype=f32)
    pos = np.asarray(inputs["positions"][b]).astype(np.int32)
    m["pos"] = np.ascontiguousarray(pos.reshape(NT, 128).T)
    m["invf"] = (10000.0 ** (-np.arange(0, 64, 2, dtype=np.float32) / 64)).astype(f32)
    for k in ("attn_norm", "mlp_norm", "ple_norm"):
        a = np.asarray(inputs[k], dtype=f32)
        m[k + "_l"] = np.ascontiguousarray(a.reshape(DEPTH, 8, 128).transpose(2, 0, 1).reshape(128, DEPTH * 8))
    for k in ("a_q_norm", "a_k_norm", "a_lambda", "a_subln", "c_q_norm", "c_k_norm", "idx_k_norm"):
        m[k] = np.ascontiguousarray(np.asarray(inputs[k], dtype=f32).reshape(-1))
    for k in ("w_in", "w_br_a", "w_br_b", "w_br_c", "w_out", "w_up", "w_down", "w_ple_gate", "w_ple_proj"):
        m[k] = np.ascontiguousarray(inputs[k], dtype=f32)
    return m


def kernel(**inputs):
    nc = build_nc()
    in_maps = [_host_layout(inputs, c % 4) for c in range(4)]
    in_maps = in_maps + in_maps
    res = run_bass_kernel_spmd(nc, in_maps, core_ids=list(range(8)))
    out = np.stack([np.asarray(res.results[b]["out"], dtype=np.float32) for b in range(4)], axis=0)
    return out
```

```python
import math
from contextlib import ExitStack

import numpy as np
import concourse.bass as bass
import concourse.mybir as mybir
from concourse.bass_utils import run_bass_kernel_spmd

F32 = mybir.dt.float32
BF16 = mybir.dt.bfloat16
I32 = mybir.dt.int32
AF = mybir.ActivationFunctionType
ALU = mybir.AluOpType
AX = mybir.AxisListType

DEPTH = 4
SL = 4096
D = 1024
NT = 32
NQ = 8
EPS = 1e-6
W_IN_COLS = 8004
TOPK = 256
NSTEP = 20
NEG = -1.0e30

SEM_EPOCH = 30000
N_DMA_SEMS = 24


class Sched:
    def __init__(self, nc, stack):
        self.nc = nc
        self.stack = stack
        self.engs = {"pe": nc.tensor, "act": nc.scalar, "dve": nc.vector,
                     "pool": nc.gpsimd, "sp": nc.sync}
        self.esem = {}
        self.ecount = {}
        self.n_sems = 0
        for e in self.engs:
            self._new_esem(e)
        self.waited = {e: {} for e in self.engs}
        self.dsems = [[self._sem(f"dma{i}"), 0] for i in range(N_DMA_SEMS)]
        self.dnext = 0
        self.lastw = {}
        self.reads = {}
        self.ninstr = {e: 0 for e in self.engs}

    def _sem(self, name):
        self.n_sems += 1
        return self.stack.enter_context(self.nc.semaphore(name))

    def _new_esem(self, e):
        self.esem[e] = self._sem(f"e_{e}_{self.n_sems}")
        self.ecount[e] = 0

    def _wait(self, e, dep):
        sem, val, tag = dep
        w = self.waited[e]
        k = id(sem)
        if w.get(k, 0) >= val:
            return
        w[k] = val
        self.engs[e].wait_ge(sem, val)
        self.ninstr[e] += 1

    def _collect(self, r, w):
        deps = []
        for k in list(r) + list(w):
            deps.extend(self.lastw.get(k, ()))
        for k in w:
            rd = self.reads.get(k)
            if rd:
                deps.extend(rd.values())
        return deps

    def _commit(self, r, w, dep, merge=()):
        for k in w:
            if k in merge:
                self.lastw[k].append(dep)
            else:
                self.lastw[k] = [dep]
                self.reads[k] = {}
        for k in r:
            if k in w:
                continue
            self.reads.setdefault(k, {})[dep[2]] = dep

    def op(self, e, fn, r=(), w=()):
        if self.ecount[e] >= SEM_EPOCH:
            self._new_esem(e)
        for d in self._collect(r, w):
            if e == "pe" and d[2] == "pe":
                continue
            self._wait(e, d)
        ins = fn()
        self.ecount[e] += 1
        ins.then_inc(self.esem[e], 1)
        dep = (self.esem[e], self.ecount[e], e)
        self.ninstr[e] += 1
        self._commit(r, w, dep)
        return ins

    def dma(self, q, out, in_, r=(), w=()):
        slot = self.dsems[self.dnext]
        self.dnext = (self.dnext + 1) % len(self.dsems)
        sem = slot[0]
        if slot[1] > 0:
            self._wait(q, (sem, slot[1], "dma"))
        deps = []
        merge = set()
        for k in r:
            deps.extend(self.lastw.get(k, ()))
        for k in w:
            lw = self.lastw.get(k, ())
            rd = self.reads.get(k) or {}
            if lw and not rd and all(d[2].startswith("dma") for d in lw):
                merge.add(k)
            else:
                deps.extend(lw)
                deps.extend(rd.values())
        for d in deps:
            self._wait(q, d)
        ins = self.engs[q].dma_start(out=out, in_=in_)
        slot[1] += 16
        ins.then_inc(sem, 16)
        dep = (sem, slot[1], f"dma{id(sem)}")
        self.ninstr[q] += 1
        self._commit(r, w, dep, merge)
        return ins

    def barrier(self):
        for e in self.engs:
            for o in self.engs:
                if o != e and self.ecount[o] > 0:
                    self._wait(e, (self.esem[o], self.ecount[o], o))
            for sem, val in self.dsems:
                if val:
                    self._wait(e, (sem, val, "dma"))


def run_pipeline(items, nstage):
    n = len(items)
    for step in range(n + nstage - 1):
        for k in range(nstage - 1, -1, -1):
            i = step - k
            if 0 <= i < n and items[i][k] is not None:
                items[i][k]()


def build_nc(depth=DEPTH, debug=False, phases=None):
    PH = set(phases) if phases is not None else {'p1', 'pa', 'pb', 'pc', 'pm', 'pf', 'pp'}
    nc = bass.Bass("TRN2", target_bir_lowering=False)

    def din(name, shape, dt=F32):
        return nc.dram_tensor(name, list(shape), dt, kind="ExternalInput").ap()

    def dscr(name, shape, dt):
        kind = "ExternalOutput" if debug else "Internal"
        return nc.dram_tensor(name, list(shape), dt, kind=kind).ap()

    x_in = din("x", [SL, D])
    p_in = din("p", [DEPTH, SL, 256])
    pos_in = din("pos", [128, NT], I32)
    invf_in = din("invf", [32])
    attn_norm = din("attn_norm_l", [128, DEPTH * 8])
    mlp_norm = din("mlp_norm_l", [128, DEPTH * 8])
    ple_norm = din("ple_norm_l", [128, DEPTH * 8])
    w_in = din("w_in", [DEPTH, D, W_IN_COLS])
    a_q_norm = din("a_q_norm", [DEPTH * 64])
    a_k_norm = din("a_k_norm", [DEPTH * 64])
    a_lambda = din("a_lambda", [DEPTH * 256])
    a_subln = din("a_subln", [DEPTH * 128])
    c_q_norm = din("c_q_norm", [DEPTH * 64])
    c_k_norm = din("c_k_norm", [DEPTH * 64])
    idx_k_norm = din("idx_k_norm", [DEPTH * 64])
    w_br = [din("w_br_a", [DEPTH, 512, D]), din("w_br_b", [DEPTH, 512, D]), din("w_br_c", [DEPTH, 512, D])]
    w_out = din("w_out", [DEPTH, D, D])
    w_up = din("w_up", [DEPTH, D, 4 * D])
    w_down = din("w_down", [DEPTH, 4 * D, D])
    w_ple_gate = din("w_ple_gate", [DEPTH, D, D])
    w_ple_proj = din("w_ple_proj", [DEPTH, 256, D])
    out_d = nc.dram_tensor("out", [SL, D], F32, kind="ExternalOutput").ap()

    xres = dscr("xres", [SL, D], F32)
    QT = {k: dscr("QT_" + k, [4, 128, SL], BF16) for k in "abc"}
    KT = {k: dscr("KT_" + k, [4, 128, SL], BF16) for k in "abc"}
    VX = {"a": dscr("V_a", [SL, 4 * 129], BF16), "b": dscr("V_b", [SL, 8 * 65], BF16),
          "c": dscr("V_c", [SL, 8 * 65], BF16)}
    IQT = dscr("IQT", [2, 128, SL], BF16)
    IKT = dscr("IKT", [128, SL], BF16)
    GT = dscr("GT", [24, 128, SL], BF16)
    YT = dscr("YT", [12, 128, SL], BF16)

    with ExitStack() as st:
        S = Sched(nc, st)
        E = st.enter_context
        V = nc.vector
        A = nc.scalar
        G = nc.gpsimd
        T = nc.tensor

        uid = [0]

        def sb(stack, name, shape, dt):
            uid[0] += 1
            return stack.enter_context(nc.sbuf_tensor(f"s{uid[0]}_{name}", list(shape), dt))

        def ps(stack, name, shape, dt):
            uid[0] += 1
            return stack.enter_context(nc.psum_tensor(f"q{uid[0]}_{name}", list(shape), dt))

        ident = sb(st, "ident", [128, 128], BF16)
        ntri = sb(st, "ntri", [128, 128], BF16)
        nones = sb(st, "nones", [128, 128], BF16)
        strict = sb(st, "strict", [128, 128], BF16)
        cos_t = sb(st, "cos_t", [128, NT, 32], F32)
        sin_t = sb(st, "sin_t", [128, NT, 32], F32)
        gA = sb(st, "gA", [128, DEPTH * 8], F32)
        gM = sb(st, "gM", [128, DEPTH * 8], F32)
        gP = sb(st, "gP", [128, DEPTH * 8], F32)
        aqn = sb(st, "aqn", [128, DEPTH * 64], F32)
        akn = sb(st, "akn", [128, DEPTH * 64], F32)
        cqn = sb(st, "cqn", [128, DEPTH * 64], F32)
        ckn = sb(st, "ckn", [128, DEPTH * 64], F32)
        ikn = sb(st, "ikn", [128, DEPTH * 64], F32)
        subg = sb(st, "subg", [128, DEPTH * 128], F32)
        lam_t = sb(st, "lam_t", [128, DEPTH], F32)
        nlam_t = sb(st, "nlam_t", [128, DEPTH], F32)
        iw_all = sb(st, "iw_all", [128, NT, 4], F32)
        pw2 = sb(st, "pw2", [128, NSTEP + 1], F32)

        CONST = ["const"]

        with ExitStack() as ph:
            posi = sb(ph, "posi", [128, NT], I32)
            posf = sb(ph, "posf", [128, NT], F32)
            invf = sb(ph, "invf", [128, 32], F32)
            ang = sb(ph, "ang", [128, NT, 32], F32)
            t0 = sb(ph, "t0", [128, NT, 32], F32)
            t1 = sb(ph, "t1", [128, NT, 32], F32)
            lpb = sb(ph, "lpb", [128, DEPTH * 256], F32)
            tmp64 = sb(ph, "tmp64", [128, 64], F32)
            s12 = sb(ph, "s12", [128, 2], F32)
            K0 = ["p0"]
            S.op("pool", lambda: G.memset(ident[:], 1.0), w=CONST)
            S.op("pool", lambda: G.affine_select(out=ident[:], in_=ident[:], pattern=[[-1, 128]],
                                                 compare_op=ALU.is_equal, fill=0.0, base=0, channel_multiplier=1),
                 r=CONST, w=CONST)
            S.op("pool", lambda: G.memset(nones[:], -1.0), w=CONST)
            for i_ in range(NSTEP + 1):
                S.op("pool", lambda: G.memset(pw2[:, i_:i_ + 1], float(2.0 ** -(i_ + 1))), w=CONST)
            S.op("pool", lambda: G.memset(ntri[:], -1.0), w=CONST)
            S.op("pool", lambda: G.affine_select(out=ntri[:], in_=ntri[:], pattern=[[-1, 128]],
                                                 compare_op=ALU.is_ge, fill=0.0, base=0, channel_multiplier=1),
                 r=CONST, w=CONST)
            S.op("pool", lambda: G.memset(strict[:], 1.0), w=CONST)
            S.op("pool", lambda: G.affine_select(out=strict[:], in_=strict[:], pattern=[[1, 128]],
                                                 compare_op=ALU.is_gt, fill=0.0, base=0, channel_multiplier=-1),
                 r=CONST, w=CONST)
            for dst, src in [(gA, attn_norm), (gM, mlp_norm), (gP, ple_norm)]:
                S.dma("sp", dst[:], src[:, :], w=CONST)
            for dst, src in [(aqn, a_q_norm), (akn, a_k_norm), (cqn, c_q_norm), (ckn, c_k_norm),
                             (ikn, idx_k_norm), (subg, a_subln), (lpb, a_lambda), (invf, invf_in)]:
                S.dma("sp", dst[:], src.partition_broadcast(128), w=CONST + K0)
            S.dma("sp", posi[:], pos_in[:, :], w=K0)
            S.op("dve", lambda: V.tensor_scalar(out=aqn[:], in0=aqn[:], scalar1=0.125, scalar2=None, op0=ALU.mult), r=CONST, w=CONST)
            S.op("dve", lambda: V.tensor_scalar(out=cqn[:], in0=cqn[:], scalar1=0.125, scalar2=None, op0=ALU.mult), r=CONST, w=CONST)
            for l in range(DEPTH):
                li = 0.8 - 0.6 * math.exp(-0.3 * l)
                S.op("dve", lambda: V.tensor_scalar(out=subg[:, l * 128:(l + 1) * 128], in0=subg[:, l * 128:(l + 1) * 128],
                                                    scalar1=float(1.0 - li), scalar2=None, op0=ALU.mult), r=CONST, w=CONST)
                for j in range(2):
                    S.op("dve", lambda: V.tensor_tensor(out=tmp64[:], in0=lpb[:, l * 256 + j * 128: l * 256 + j * 128 + 64],
                                                        in1=lpb[:, l * 256 + j * 128 + 64: l * 256 + j * 128 + 128], op=ALU.mult),
                         r=K0, w=["tmp64"])
                    S.op("dve", lambda: V.tensor_reduce(out=s12[:, j:j + 1], in_=tmp64[:], axis=AX.X, op=ALU.add),
                         r=["tmp64"], w=["s12"])
                S.op("act", lambda: A.activation(out=s12[:], in_=s12[:], func=AF.Exp), r=["s12"], w=["s12"])
                S.op("dve", lambda: V.tensor_tensor(out=lam_t[:, l:l + 1], in0=s12[:, 0:1], in1=s12[:, 1:2], op=ALU.subtract),
                     r=["s12"], w=CONST)
                S.op("dve", lambda: V.tensor_scalar(out=lam_t[:, l:l + 1], in0=lam_t[:, l:l + 1], scalar1=float(li), scalar2=None, op0=ALU.add),
                     r=CONST, w=CONST)
                S.op("dve", lambda: V.tensor_scalar(out=nlam_t[:, l:l + 1], in0=lam_t[:, l:l + 1], scalar1=-1.0, scalar2=None, op0=ALU.mult),
                     r=CONST, w=CONST)
            S.op("dve", lambda: V.tensor_copy(out=posf[:], in_=posi[:]), r=K0, w=["posf"])
            S.op("dve", lambda: V.tensor_tensor(out=ang[:], in0=posf[:].unsqueeze(2).to_broadcast([128, NT, 32]),
                                                in1=invf[:].unsqueeze(1).to_broadcast([128, NT, 32]), op=ALU.mult),
                 r=["posf"] + K0, w=["ang"])
            MAGIC = 12582912.0
            C1 = 6.28125
            C2 = 2.0 * math.pi - C1
            PI_LO = 3.1415925
            S.op("dve", lambda: V.tensor_scalar(out=t0[:], in0=ang[:], scalar1=float(1.0 / (2.0 * math.pi)), scalar2=None, op0=ALU.mult), r=["ang"], w=["t0"])
            S.op("dve", lambda: V.tensor_scalar(out=t0[:], in0=t0[:], scalar1=MAGIC, scalar2=None, op0=ALU.add), r=["t0"], w=["t0"])
            S.op("dve", lambda: V.tensor_scalar(out=t0[:], in0=t0[:], scalar1=-MAGIC, scalar2=None, op0=ALU.add), r=["t0"], w=["t0"])
            S.op("dve", lambda: V.scalar_tensor_tensor(out=t1[:], in0=t0[:], scalar=-C1, in1=ang[:], op0=ALU.mult, op1=ALU.add), r=["t0", "ang"], w=["t1"])
            S.op("dve", lambda: V.scalar_tensor_tensor(out=t1[:], in0=t0[:], scalar=-C2, in1=t1[:], op0=ALU.mult, op1=ALU.add), r=["t0", "t1"], w=["t1"])
            S.op("dve", lambda: V.tensor_scalar(out=t0[:], in0=t1[:], scalar1=PI_LO, scalar2=-PI_LO, op0=ALU.min, op1=ALU.max), r=["t1"], w=["t0"])
            S.op("act", lambda: A.activation(out=sin_t[:], in_=t0[:], func=AF.Sin), r=["t0"], w=CONST)
            S.op("dve", lambda: V.tensor_scalar(out=t1[:], in0=t1[:], scalar1=float(math.pi / 2), scalar2=None, op0=ALU.add), r=["t1"], w=["t1"])
            S.op("dve", lambda: V.tensor_scalar(out=t0[:], in0=t1[:], scalar1=float(math.pi), scalar2=float(-2.0 * math.pi), op0=ALU.is_gt, op1=ALU.mult), r=["t1", "const"], w=["t0"])
            S.op("dve", lambda: V.tensor_tensor(out=t1[:], in0=t1[:], in1=t0[:], op=ALU.add), r=["t0", "t1"], w=["t1"])
            S.op("dve", lambda: V.tensor_scalar(out=t1[:], in0=t1[:], scalar1=PI_LO, scalar2=-PI_LO, op0=ALU.min, op1=ALU.max), r=["t1"], w=["t1"])
            S.op("act", lambda: A.activation(out=cos_t[:], in_=t1[:], func=AF.Sin), r=["t1"], w=CONST)
            S.barrier()

        def rms_rows(xt, kx, ssq, rstd, junk, kpre):
            S.op("act", lambda: A.activation(out=junk[:], in_=xt, func=AF.Square, accum_out=ssq[:, 0:1]),
                 r=[kx], w=[kpre + "junk", kpre + "ssq"])
            S.op("act", lambda: A.activation(out=rstd[:, 0:1], in_=ssq[:, 0:1], func=AF.Sqrt, bias=EPS, scale=1.0 / D),
                 r=[kpre + "ssq"], w=[kpre + "rstd"])
            S.op("dve", lambda: V.reciprocal(out=rstd[:, 0:1], in_=rstd[:, 0:1]), r=[kpre + "rstd"], w=[kpre + "rstd"])

        def norm_transpose(xt, kx, gain, l, hb, pst2, dst_fn, kdst, ssq, rstd, junk, kpre):
            rms_rows(xt, kx, ssq, rstd, junk, kpre)
            S.op("dve", lambda: V.tensor_scalar(out=hb[:], in0=xt, scalar1=rstd[:, 0:1], scalar2=None, op0=ALU.mult),
                 r=[kx, kpre + "rstd"], w=[kpre + "hb"])
            for half in range(2):
                pt = pst2[half]
                for c4 in range(4):
                    c = half * 4 + c4
                    S.op("pe", lambda: T.transpose(out=pt[:, c4, :], in_=hb[:, c * 128:(c + 1) * 128], identity=ident[:]),
                         r=[kpre + "hb"], w=[kpre + f"pst{half}"])
                for c4 in range(4):
                    c = half * 4 + c4
                    if c4 % 2 == 0:
                        S.op("dve", lambda: V.tensor_scalar(out=dst_fn(c), in0=pt[:, c4, :], scalar1=gain[:, l * 8 + c:l * 8 + c + 1],
                                                            scalar2=None, op0=ALU.mult),
                             r=[kpre + f"pst{half}"], w=[kdst])
                    else:
                        S.op("act", lambda: A.activation(out=dst_fn(c), in_=pt[:, c4, :], func=AF.Copy,
                                                         scale=gain[:, l * 8 + c:l * 8 + c + 1]),
                             r=[kpre + f"pst{half}"], w=[kdst])

        def load_w_bf16(dst, src, key):
            S.dma("pool", dst, src, w=[key])

        for l in range(depth):
            xsrc = x_in if l == 0 else xres
            lam_init = 0.8 - 0.6 * math.exp(-0.3 * l)

            for ph in ([ExitStack()] if 'p1' in PH else []):
                hT = sb(ph, "hT", [128, 8, SL], BF16)
                xb2 = [sb(ph, f"p1x{i}", [128, D], F32) for i in range(2)]
                hb = sb(ph, "p1hb", [128, D], BF16)
                junk = sb(ph, "p1junk", [128, D], BF16)
                ssq = sb(ph, "p1ssq", [128, 8], F32)
                rstd = sb(ph, "p1rstd", [128, 8], F32)
                pst2 = [ps(ph, f"p1pst{i}", [128, 4, 128], BF16) for i in range(2)]
                psm = [ps(ph, f"p1ps{i}", [128, 512], F32) for i in range(3)]
                wblk = [sb(ph, f"p1w{i}", [128, 8, 512], BF16) for i in range(2)]
                sq_2 = [sb(ph, f"p1sq{i}", [128, 512], F32) for i in range(2)]
                xn_2 = [sb(ph, f"p1xn{i}", [128, 512], F32) for i in range(2)]
                xg_2 = [sb(ph, f"p1xg{i}", [128, 512], F32) for i in range(2)]
                ra_2 = [sb(ph, f"p1ra{i}", [128, 256], F32) for i in range(2)]
                rb_2 = [sb(ph, f"p1rb{i}", [128, 256], F32) for i in range(2)]
                ob_2 = [sb(ph, f"p1ob{i}", [128, 512], BF16) for i in range(2)]
                ssq_2 = [sb(ph, f"p1ssq{i}", [128, 8], F32) for i in range(2)]
                rstd_2 = [sb(ph, f"p1rstd{i}", [128, 8], F32) for i in range(2)]
                stg = [sb(ph, f"p1stg{i}", [128, 4, 512], BF16) for i in range(2)]
                vst = [sb(ph, f"p1vst{i}", [128, 520], BF16) for i in range(4)]
                gst = [sb(ph, f"p1gst{i}", [128, 512], BF16) for i in range(3)]

                for t in range(NT):
                    xt = xb2[t % 2]
                    kx = f"p1x{t % 2}"
                    S.dma("sp", xt[:], xsrc[t * 128:(t + 1) * 128, :], r=[f"xr{t}"], w=[kx])
                    norm_transpose(xt[:], kx, gA, l, hb, pst2,
                                   lambda c, t=t: hT[:, c, t * 128:(t + 1) * 128], f"hT{t}", ssq, rstd, junk, "p1")
                for i in range(4):
                    S.op("pool", lambda: G.memset(vst[i][:], 1.0), w=[f"vst{i}"])

                blocks = []
                names = ["aq", "ak", "av", "bq", "bk", "bv", "cq", "ck", "cv"]
                for i, nm in enumerate(names):
                    blocks.append((nm, i * 512, 512))
                blocks.append(("idx", 4608, 324))
                for i in range(6):
                    blocks.append((f"gl{i}", 4932 + i * 512, 512))
                w_l = w_in[l].rearrange("(c p) n -> p c n", p=128)

                def qk_post(pv, nh, gain, do_norm, do_rope, t, outv, scale, kout):
                    n = nh * 64
                    pz = t % 2
                    sq, xn, xg, ra, rb = sq_2[pz], xn_2[pz], xg_2[pz], ra_2[pz], rb_2[pz]
                    ssq, rstd = ssq_2[pz], rstd_2[pz]
                    ksq, kxn, kxg, kra, krb, kss, krs = (f"sq{pz}", f"xn{pz}", f"xg{pz}", f"ra{pz}", f"rb{pz}", f"qssq{pz}", f"qrstd{pz}")
                    if do_norm:
                        sqv = sq[:, 0:n].rearrange("p (h d) -> p h d", d=64)
                        S.op("act", lambda: A.activation(out=sqv, in_=pv, func=AF.Square), r=["psm"], w=[ksq])
                        S.op("dve", lambda: V.tensor_reduce(out=ssq[:, 0:nh], in_=sqv, axis=AX.X, op=ALU.add), r=[ksq], w=[kss])
                        S.op("act", lambda: A.activation(out=rstd[:, 0:nh], in_=ssq[:, 0:nh], func=AF.Sqrt, bias=EPS, scale=1.0 / 64),
                             r=[kss], w=[krs])
                        S.op("dve", lambda: V.reciprocal(out=rstd[:, 0:nh], in_=rstd[:, 0:nh]), r=[krs], w=[krs])
                        xnv = xn[:, 0:n].rearrange("p (h d) -> p h d", d=64)
                        S.op("dve", lambda: V.tensor_tensor(out=xnv, in0=pv, in1=rstd[:, 0:nh].unsqueeze(2).to_broadcast([128, nh, 64]), op=ALU.mult),
                             r=["psm", krs], w=[kxn])
                        xgv = xg[:, 0:n].rearrange("p (h d) -> p h d", d=64)
                        S.op("pool", lambda: G.tensor_tensor(out=xgv, in0=xnv, in1=gain.unsqueeze(1).to_broadcast([128, nh, 64]), op=ALU.mult),
                             r=[kxn, "const"], w=[kxg])
                    else:
                        xgv = xg[:, 0:n].rearrange("p (h d) -> p h d", d=64)
                        S.op("act", lambda: A.activation(out=xgv, in_=pv, func=AF.Copy, scale=float(scale)), r=["psm"], w=[kxg])
                    if do_rope:
                        x1 = xgv[:, :, 0:32]
                        x2 = xgv[:, :, 32:64]
                        cb = cos_t[:, t, :].unsqueeze(1).to_broadcast([128, nh, 32])
                        sbb = sin_t[:, t, :].unsqueeze(1).to_broadcast([128, nh, 32])
                        rav = ra[:, 0:nh * 32].rearrange("p (h d) -> p h d", d=32)
                        rbv = rb[:, 0:nh * 32].rearrange("p (h d) -> p h d", d=32)
                        S.op("pool", lambda: G.tensor_tensor(out=rav, in0=x1, in1=cb, op=ALU.mult), r=[kxg], w=[kra])
                        S.op("dve", lambda: V.tensor_tensor(out=rbv, in0=x2, in1=sbb, op=ALU.mult), r=[kxg], w=[krb])
                        S.op("dve", lambda: V.tensor_tensor(out=outv[:, :, 0:32], in0=rav, in1=rbv, op=ALU.subtract), r=[kra, krb], w=[kout])
                        S.op("pool", lambda: G.tensor_tensor(out=rav, in0=x2, in1=cb, op=ALU.mult), r=[kxg, kra], w=[kra])
                        S.op("dve", lambda: V.tensor_tensor(out=rbv, in0=x1, in1=sbb, op=ALU.mult), r=[kxg, krb], w=[krb])
                        S.op("dve", lambda: V.tensor_tensor(out=outv[:, :, 32:64], in0=rav, in1=rbv, op=ALU.add), r=[kra, krb], w=[kout])
                    else:
                        S.op("dve", lambda: V.tensor_copy(out=outv, in_=xgv), r=[kxg], w=[kout])

                nblk = len(blocks)
                S.dma("pool", wblk[0][:, :, 0:blocks[0][2]], w_l[:, :, blocks[0][1]:blocks[0][1] + blocks[0][2]], w=["wblk0"])
                items = []

                def prefetch_w(bi):
                    if bi + 1 < nblk:
                        nn = blocks[bi + 1]
                        S.dma("pool", wblk[(bi + 1) % 2][:, :, 0:nn[2]], w_l[:, :, nn[1]:nn[1] + nn[2]], w=[f"wblk{(bi + 1) % 2}"])

                for bi, (nm, c0, n) in enumerate(blocks):
                    wb = wblk[bi % 2]
                    kw = f"wblk{bi % 2}"
                    if nm.startswith("gl"):
                        gi = int(nm[2:])
                        for Tq in range(NQ):
                            for j in range(4):
                                pb_ = len(items) % 3

                                def g0(bi=bi, wb=wb, kw=kw, Tq=Tq, j=j, pb_=pb_, first=(Tq == 0 and j == 0)):
                                    if first:
                                        prefetch_w(bi)
                                    pm = psm[pb_]
                                    for c in range(8):
                                        S.op("pe", lambda: T.matmul(pm[:], lhsT=wb[:, c, j * 128:(j + 1) * 128], rhs=hT[:, c, Tq * 512:(Tq + 1) * 512],
                                                                    start=(c == 0), stop=(c == 7)), r=[kw] + [f"hT{Tq * 4 + i_}" for i_ in range(4)], w=[f"psm{pb_}"])

                                def g1(gi=gi, Tq=Tq, j=j, pb_=pb_):
                                    pm = psm[pb_]
                                    gs = gst[pb_]
                                    S.op("act", lambda: A.activation(out=gs[:], in_=pm[:], func=AF.Sigmoid), r=[f"psm{pb_}"], w=[f"gst{pb_}"])
                                    S.dma("act", GT[gi * 4 + j, :, Tq * 512:(Tq + 1) * 512], gs[:], r=[f"gst{pb_}"], w=["GT"])
                                items.append([g0, None, g1, None, None])
                        continue
                    for t in range(NT):
                        pb_ = len(items) % 3

                        def s0(bi=bi, wb=wb, kw=kw, t=t, n=n, pb_=pb_):
                            if t == 0:
                                prefetch_w(bi)
                            pm = psm[pb_]
                            for c in range(8):
                                S.op("pe", lambda: T.matmul(pm[:, 0:n], lhsT=hT[:, c, t * 128:(t + 1) * 128], rhs=wb[:, c, 0:n],
                                                            start=(c == 0), stop=(c == 7)), r=[kw, f"hT{t}"], w=[f"psm{pb_}"])

                        def s1(nm=nm, t=t, pb_=pb_):
                            pm = psm[pb_]
                            kp = f"psm{pb_}"
                            S.lastw["psm"] = S.lastw[kp]
                            S.reads["psm"] = S.reads[kp]
                            ob = ob_2[t % 2]
                            kob = f"ob{t % 2}"
                            if nm in ("aq", "ak", "cq", "ck", "bq", "bk"):
                                pv = pm[:, 0:512].rearrange("p (h d) -> p h d", d=64)
                                obv = ob[:].rearrange("p (h d) -> p h d", d=64)
                                if nm[0] == "b":
                                    qk_post(pv, 8, None, False, False, t, obv, 0.125 if nm == "bq" else 1.0, kob)
                                else:
                                    gain = {"aq": aqn, "ak": akn, "cq": cqn, "ck": ckn}[nm][:, l * 64:(l + 1) * 64]
                                    qk_post(pv, 8, gain, True, True, t, obv, 1.0, kob)
                            elif nm in ("av", "bv", "cv"):
                                nh, dv = (4, 128) if nm == "av" else (8, 64)
                                vi = (t % 2) + (0 if nm == "av" else 2)
                                vs = vst[vi]
                                vv = vs[:, 0:nh * (dv + 1)].rearrange("p (h e) -> p h e", e=dv + 1)[:, :, 0:dv]
                                S.op("act", lambda: A.activation(out=vv, in_=pm[:, 0:512].rearrange("p (h e) -> p h e", e=dv), func=AF.Copy),
                                     r=[kp], w=[f"vst{vi}"])
                                S.dma("act", VX[nm[0]][t * 128:(t + 1) * 128, :], vs[:, 0:nh * (dv + 1)], r=[f"vst{vi}"], w=["VX"])
                            else:
                                obv = ob[:, 0:256].rearrange("p (h d) -> p h d", d=64)
                                qk_post(pm[:, 0:256].rearrange("p (h d) -> p h d", d=64), 4, None, False, True, t, obv, 1.0, kob)
                                obk = ob[:, 256:320].rearrange("p (h d) -> p h d", d=64)
                                qk_post(pm[:, 256:320].rearrange("p (h d) -> p h d", d=64), 1, ikn[:, l * 64:(l + 1) * 64], True, True, t, obk, 1.0, kob)
                                S.op("dve", lambda: V.tensor_copy(out=ob[:, 320:384], in_=ob[:, 256:320]), r=[kob], w=[kob])
                                S.op("act", lambda: A.activation(out=iw_all[:, t, :], in_=pm[:, 320:324], func=AF.Copy), r=[kp], w=["iw_all"])

                        def s2(nm=nm, t=t):
                            if nm in ("av", "bv", "cv"):
                                return
                            ob = ob_2[t % 2]
                            kob = f"ob{t % 2}"
                            sg = stg[(t // 4) % 2]
                            ksg = f"stg{(t // 4) % 2}"
                            tt = t % 4
                            pt = pst2[t % 2]
                            nj_ = 3 if nm == "idx" else 4
                            for j in range(nj_):
                                S.op("pe", lambda: T.transpose(out=pt[:, j, :], in_=ob[:, j * 128:(j + 1) * 128], identity=ident[:]),
                                     r=[kob], w=[f"p1pst{t % 2}"])
                            S.op("act", lambda: A.activation(out=sg[:, 0:nj_, tt * 128:(tt + 1) * 128], in_=pt[:, 0:nj_, :], func=AF.Copy),
                                 r=[f"p1pst{t % 2}"], w=[ksg])
                            if tt == 3:
                                t0_ = (t - 3) * 128
                                if nm == "idx":
                                    S.dma("act", IQT[:, :, t0_:t0_ + 512].rearrange("j p t -> p j t"), sg[:, 0:2, :], r=[ksg], w=["QKT"])
                                    S.dma("act", IKT[:, t0_:t0_ + 512], sg[:, 2, :], r=[ksg], w=["QKT"])
                                else:
                                    dst = (QT if nm[1] == "q" else KT)[nm[0]]
                                    S.dma("act", dst[:, :, t0_:t0_ + 512].rearrange("j p t -> p j t"), sg[:], r=[ksg], w=["QKT"])
                        items.append([s0, None, s1, None, s2])
                run_pipeline(items, 5)
                S.barrier()
                ph.close()

            for ph in ([ExitStack()] if 'pa' in PH else []):
                kt = sb(ph, "a_kt", [128, 4, SL], BF16)
                vx = sb(ph, "a_vx", [128, NT, 4 * 129], BF16)
                qt2 = [sb(ph, f"a_qt{i}", [128, 4, 512], BF16) for i in range(2)]
                pT2 = [sb(ph, f"a_pT{i}", [128, 512], BF16) for i in range(4)]
                a_qm = [sb(ph, f"a_qm{i}", [128, 512], BF16) for i in range(2)]
                o2 = [sb(ph, f"a_o{i}", [128, 4, 129], F32) for i in range(2)]
                rl = sb(ph, "a_rl", [128, 8], F32)
                y0 = sb(ph, "a_y0", [128, 4, 128], F32)
                y1 = sb(ph, "a_y1", [128, 4, 128], F32)
                ss = sb(ph, "a_ss", [128, 8], F32)
                yb2 = [sb(ph, f"a_yb{i}", [128, 4, 128], BF16) for i in range(2)]
                yts = [sb(ph, f"a_yts{i}", [128, 512], BF16) for i in range(2)]
                pss = [ps(ph, f"a_pss{i}", [128, 512], F32) for i in range(3)]
                po2 = [ps(ph, f"a_po{i}", [128, 2, 512], F32) for i in range(2)]
                ptr = ps(ph, "a_ptr", [128, 4, 128], BF16)
                S.dma("sp", kt[:], KT["a"].rearrange("j p t -> p j t"), r=["QKT"], w=["a_kt"])
                for v4 in range(4):
                    S.dma("sp", vx[:, v4 * 8:(v4 + 1) * 8, :], VX["a"][v4 * 1024:(v4 + 1) * 1024, :].rearrange("(t p) e -> p t e", p=128), r=["VX"], w=["a_vx"])

                def a_load_q(Q):
                    S.dma("sp", qt2[Q % 2][:], QT["a"][:, :, Q * 512:(Q + 1) * 512].rearrange("j p t -> p j t"), r=["QKT"], w=[f"a_qt{Q % 2}"])

                a_load_q(0)
                items = []
                deferred = []

                def a_epilogue(Q, h, c, po, kpo):
                    for bk in range(2):
                        S.op("act", lambda: A.activation(out=o2[c][:, 2 * bk:2 * bk + 2, :], in_=po[:, bk, 0:258].rearrange("p (r e) -> p r e", e=129), func=AF.Copy),
                             r=[kpo], w=[f"a_o{c}"])
                    if c == 0:
                        return
                    yb = yb2[h % 2]
                    kyb = f"a_yb{h % 2}"
                    S.op("dve", lambda: V.reciprocal(out=rl[:, 0:4], in_=o2[0][:, :, 128]), r=["a_o0"], w=["a_rl"])
                    S.op("dve", lambda: V.reciprocal(out=rl[:, 4:8], in_=o2[1][:, :, 128]), r=["a_o1"], w=["a_rl"])
                    S.op("dve", lambda: V.tensor_scalar(out=rl[:, 4:8], in0=rl[:, 4:8], scalar1=nlam_t[:, l:l + 1], scalar2=None, op0=ALU.mult),
                         r=["a_rl", "const"], w=["a_rl"])
                    S.op("dve", lambda: V.tensor_tensor(out=y0[:], in0=o2[0][:, :, 0:128], in1=rl[:, 0:4].unsqueeze(2).to_broadcast([128, 4, 128]), op=ALU.mult),
                         r=["a_o0", "a_rl"], w=["a_y0"])
                    S.op("pool", lambda: G.tensor_tensor(out=y1[:], in0=o2[1][:, :, 0:128], in1=rl[:, 4:8].unsqueeze(2).to_broadcast([128, 4, 128]), op=ALU.mult),
                         r=["a_o1", "a_rl"], w=["a_y1"])
                    S.op("dve", lambda: V.tensor_tensor(out=y0[:], in0=y0[:], in1=y1[:], op=ALU.add), r=["a_y0", "a_y1"], w=["a_y0"])
                    S.op("pool", lambda: G.tensor_tensor(out=y1[:], in0=y0[:], in1=y0[:], op=ALU.mult), r=["a_y0"], w=["a_y1"])
                    S.op("dve", lambda: V.tensor_reduce(out=ss[:, 0:4], in_=y1[:], axis=AX.X, op=ALU.add), r=["a_y1"], w=["a_ss"])
                    S.op("act", lambda: A.activation(out=ss[:, 0:4], in_=ss[:, 0:4], func=AF.Sqrt, bias=EPS, scale=1.0 / 128), r=["a_ss"], w=["a_ss"])
                    S.op("dve", lambda: V.reciprocal(out=ss[:, 0:4], in_=ss[:, 0:4]), r=["a_ss"], w=["a_ss"])
                    S.op("dve", lambda: V.tensor_tensor(out=y0[:], in0=y0[:], in1=ss[:, 0:4].unsqueeze(2).to_broadcast([128, 4, 128]), op=ALU.mult),
                         r=["a_y0", "a_ss"], w=["a_y0"])
                    S.op("dve", lambda: V.tensor_tensor(out=yb[:], in0=y0[:], in1=subg[:, l * 128:(l + 1) * 128].unsqueeze(1).to_broadcast([128, 4, 128]), op=ALU.mult),
                         r=["a_y0", "const"], w=[kyb])

                    def fin():
                        for qq in range(4):
                            S.op("pe", lambda: T.transpose(out=ptr[:, qq, :], in_=yb[:, qq, :], identity=ident[:]), r=[kyb], w=["a_ptr"])
                        ys = yts[h % 2]
                        S.op("act", lambda: A.activation(out=ys[:], in_=ptr[:].rearrange("p a b -> p (a b)"), func=AF.Copy), r=["a_ptr"], w=[f"a_yts{h % 2}"])
                        S.dma("act", YT[h, :, Q * 512:(Q + 1) * 512], ys[:], r=[f"a_yts{h % 2}"], w=["YT"])
                    deferred.append([4, fin])

                def run_deferred(force=False):
                    keep = []
                    for dly in deferred:
                        dly[0] -= 1
                        if dly[0] <= 0 or force:
                            dly[1]()
                        else:
                            keep.append(dly)
                    deferred[:] = keep

                gidx = 0
                for Q in range(NQ):
                    for h in range(4):
                        for c in range(2):
                            g = gidx
                            gidx += 1
                            nj = 4 * Q + 4
                            for j in range(nj):
                                b = len(items) % 3
                                bp = len(items) % 4
                                jj = j - 4 * Q
                                c0 = max(0, jj) * 128

                                def s0(Q=Q, h=h, c=c, j=j, c0=c0, b=b, g=g, pre=(h == 0 and c == 0 and j == 0)):
                                    if pre and Q + 1 < NQ:
                                        a_load_q(Q + 1)
                                    qt = qt2[Q % 2]
                                    qm = a_qm[g % 2]
                                    kqm = f"a_qm{g % 2}"
                                    if j == 0:
                                        S.op("pool", lambda: G.memset(qm[:], 0.0), w=[kqm])
                                        S.op("pool", lambda: G.tensor_copy(out=qm[c * 64:(c + 1) * 64, :], in_=qt[c * 64:(c + 1) * 64, h, :]),
                                             r=[f"a_qt{Q % 2}", kqm], w=[kqm])
                                    S.op("pe", lambda: T.matmul(pss[b][:, c0:512], lhsT=kt[:, h, j * 128:(j + 1) * 128],
                                                                rhs=qm[:, c0:512], start=True, stop=True),
                                         r=["a_kt", kqm], w=[f"a_pss{b}"])

                                def s1(c0=c0, b=b, bp=bp, jj=jj):
                                    S.op("act", lambda: A.activation(out=pT2[bp][:, c0:512], in_=pss[b][:, c0:512], func=AF.Exp),
                                         r=[f"a_pss{b}"], w=[f"a_pT{bp}"])
                                    if jj >= 0:
                                        S.op("pool", lambda: G.memset(pT2[bp][64:128, c0:c0 + 64], 0.0), w=[f"a_pT{bp}"])

                                def s2(Q=Q, h=h, c=c, j=j, jj=jj, b=bp, g=g, nj=nj):
                                    po = po2[g % 2]
                                    kpo = f"a_po{g % 2}"
                                    for qq in range(max(0, jj), 4):
                                        off = (qq % 2) * 129
                                        S.op("pe", lambda: T.matmul(po[:, qq // 2, off:off + 129], lhsT=pT2[b][:, qq * 128:(qq + 1) * 128],
                                                                    rhs=vx[:, j, h * 129:(h + 1) * 129], start=(j == 0 and qq % 2 == 0), stop=(j == 4 * Q + qq),
                                                                    skip_group_check=True),
                                             r=[f"a_pT{b}", "a_vx"], w=[kpo])
                                    if j == nj - 1:
                                        a_epilogue(Q, h, c, po, kpo)
                                    run_deferred()
                                items.append([s0, None, s1, None, None, s2])
                run_pipeline(items, 6)
                run_deferred(force=True)
                S.barrier()
                ph.close()

            for ph in ([ExitStack()] if 'pb' in PH else []):
                kt = sb(ph, "b_kt", [128, 4, SL], BF16)
                vx = sb(ph, "b_vx", [128, NT, 8 * 65], BF16)
                qt2 = [sb(ph, f"b_qt{i}", [128, 4, 512], BF16) for i in range(2)]
                e2 = [sb(ph, f"b_e{i}", [128, 512], F32) for i in range(3)]
                nl2 = [sb(ph, f"b_nl{i}", [128, 512], BF16) for i in range(3)]
                aT2 = [sb(ph, f"b_aT{i}", [128, 512], BF16) for i in range(3)]
                R = sb(ph, "b_R", [128, 512], BF16)
                qm2 = [sb(ph, f"b_qm{i}", [128, 512], BF16) for i in range(2)]
                yq2 = [sb(ph, f"b_yq{i}", [128, 4, 512], BF16) for i in range(2)]
                yts = [sb(ph, f"b_yts{i}", [128, 512], BF16) for i in range(2)]
                psz = [ps(ph, f"b_psz{i}", [128, 512], F32) for i in range(3)]
                psc = [ps(ph, f"b_psc{i}", [128, 512], F32) for i in range(2)]
                po2 = [ps(ph, f"b_po{i}", [128, 512], F32) for i in range(2)]
                ptr = ps(ph, "b_ptr", [128, 4, 128], BF16)
                S.dma("sp", kt[:], KT["b"].rearrange("j p t -> p j t"), r=["QKT"], w=["b_kt"])
                for v4 in range(4):
                    S.dma("sp", vx[:, v4 * 8:(v4 + 1) * 8, :], VX["b"][v4 * 1024:(v4 + 1) * 1024, :].rearrange("(t p) e -> p t e", p=128), r=["VX"], w=["b_vx"])

                def b_load_q(Q):
                    S.dma("sp", qt2[Q % 2][:], QT["b"][:, :, Q * 512:(Q + 1) * 512].rearrange("j p t -> p j t"), r=["QKT"], w=[f"b_qt{Q % 2}"])

                b_load_q(0)
                items = []
                gidx = 0
                for Q in range(NQ):
                    for h in range(8):
                        g = gidx
                        gidx += 1
                        pr, bs = h // 2, (h % 2) * 64
                        jtop = 4 * Q + 3
                        gstate = {"first": True}
                        for j in range(jtop, -1, -1):
                            b = len(items) % 3
                            bc = len(items) % 2
                            jj = j - 4 * Q
                            c0 = max(0, jj) * 128

                            def s0(Q=Q, h=h, j=j, c0=c0, b=b, pr=pr, bs=bs, pre=(h == 0 and j == jtop)):
                                if pre and Q + 1 < NQ:
                                    b_load_q(Q + 1)
                                qt = qt2[Q % 2]
                                S.op("pe", lambda: T.matmul(psz[b][:, c0:512], lhsT=kt[bs:bs + 64, pr, j * 128:(j + 1) * 128], rhs=qt[bs:bs + 64, pr, c0:512],
                                                            start=True, stop=True), r=["b_kt", f"b_qt{Q % 2}"], w=[f"b_psz{b}"])

                            def s1(c0=c0, b=b, jj=jj):
                                S.op("act", lambda: A.activation(out=e2[b][:, c0:512], in_=psz[b][:, c0:512], func=AF.Exp),
                                     r=[f"b_psz{b}"], w=[f"b_e{b}"])
                                S.op("act", lambda: A.activation(out=nl2[b][:, c0:512], in_=e2[b][:, c0:512], func=AF.Ln, bias=1.0),
                                     r=[f"b_e{b}"], w=[f"b_nl{b}"])
                                if jj >= 0:
                                    S.op("dve", lambda: V.tensor_tensor(out=nl2[b][:, c0:c0 + 128], in0=nl2[b][:, c0:c0 + 128], in1=strict[:], op=ALU.mult),
                                         r=[f"b_nl{b}", "const"], w=[f"b_nl{b}"])

                            def s2(Q=Q, j=j, c0=c0, b=b, bc=bc, pr=pr, bs=bs, g=g, jtop=jtop):
                                qm = qm2[g % 2]
                                kqm = f"b_qm{g % 2}"
                                qt = qt2[Q % 2]
                                if j == jtop:
                                    S.op("pool", lambda: G.memset(R[:], 0.0), w=["b_R"])
                                    S.op("pool", lambda: G.memset(qm[:], 0.0), w=[kqm])
                                    S.op("dve", lambda: V.tensor_copy(out=qm[bs:bs + 64, :], in_=qt[bs:bs + 64, pr, :]), r=[f"b_qt{Q % 2}", kqm], w=[kqm])
                                S.op("pe", lambda: T.matmul(psc[bc][:, c0:512], lhsT=kt[:, pr, j * 128:(j + 1) * 128], rhs=qm[:, c0:512], start=True, stop=False),
                                     r=["b_kt", kqm], w=[f"b_psc{bc}"])
                                last = (j == jtop)
                                S.op("pe", lambda: T.matmul(psc[bc][:, c0:512], lhsT=ntri[:], rhs=nl2[b][:, c0:512], start=False, stop=last),
                                     r=[f"b_nl{b}", "const"], w=[f"b_psc{bc}"])
                                if not last:
                                    S.op("pe", lambda: T.matmul(psc[bc][:, c0:512], lhsT=nones[:], rhs=R[:, c0:512], start=False, stop=True),
                                         r=["b_R", "const"], w=[f"b_psc{bc}"])
                                if j > 0:
                                    S.op("dve", lambda: V.tensor_tensor(out=R[:, c0:512], in0=R[:, c0:512], in1=nl2[b][:, c0:512], op=ALU.add),
                                         r=["b_R", f"b_nl{b}"], w=["b_R"])

                            def s3(c0=c0, b=b, bc=bc, jj=jj):
                                S.op("act", lambda: A.activation(out=aT2[b][:, c0:512], in_=psc[bc][:, c0:512], func=AF.Exp),
                                     r=[f"b_psc{bc}"], w=[f"b_aT{b}"])
                                if jj >= 0:
                                    S.op("dve", lambda: V.tensor_tensor(out=aT2[b][:, c0:c0 + 128], in0=aT2[b][:, c0:c0 + 128], in1=strict[:], op=ALU.mult),
                                         r=[f"b_aT{b}", "const"], w=[f"b_aT{b}"])

                            def s4(Q=Q, h=h, j=j, jj=jj, b=b, g=g, gstate=gstate):
                                po = po2[g % 2]
                                kpo = f"b_po{g % 2}"
                                yq = yq2[Q % 2]
                                kyq = f"b_yq{Q % 2}"
                                for qq in range(max(0, jj), 4):
                                    S.op("pe", lambda: T.matmul(po[:, qq * 65:(qq + 1) * 65], lhsT=aT2[b][:, qq * 128:(qq + 1) * 128],
                                                                rhs=vx[:, j, h * 65:(h + 1) * 65], start=gstate["first"], stop=(j == 0),
                                                                skip_group_check=True),
                                         r=[f"b_aT{b}", "b_vx"], w=[kpo])
                                    gstate["first"] = False
                                if j == 0:
                                    S.op("act", lambda: A.activation(out=yq[:, :, h * 64:(h + 1) * 64],
                                                                     in_=po[:, 0:260].rearrange("p (a e) -> p a e", e=65)[:, :, 0:64], func=AF.Copy),
                                         r=[kpo], w=[kyq])
                                    if h == 7:
                                        for cch in range(4):
                                            for qq in range(4):
                                                S.op("pe", lambda: T.transpose(out=ptr[:, qq, :], in_=yq[:, qq, cch * 128:(cch + 1) * 128], identity=ident[:]),
                                                     r=[kyq], w=["b_ptr"])
                                            ys = yts[cch % 2]
                                            S.op("act", lambda: A.activation(out=ys[:], in_=ptr[:].rearrange("p a b -> p (a b)"), func=AF.Copy), r=["b_ptr"], w=[f"b_yts{cch % 2}"])
                                            S.dma("act", YT[4 + cch, :, Q * 512:(Q + 1) * 512], ys[:], r=[f"b_yts{cch % 2}"], w=["YT"])
                            items.append([s0, None, s1, None, s2, s3, None, s4])
                run_pipeline(items, 8)
                S.barrier()
                ph.close()

            for ph in ([ExitStack()] if 'pc' in PH else []):
                kt = sb(ph, "c_kt", [128, 4, SL], BF16)
                vx = sb(ph, "c_vx", [128, NT, 8 * 65], BF16)
                ikt = sb(ph, "c_ikt", [128, SL], BF16)
                qt2 = [sb(ph, f"c_qt{i}", [128, 4, 512], BF16) for i in range(2)]
                iqt2 = [sb(ph, f"c_iqt{i}", [128, 2, 512], BF16) for i in range(2)]
                score2 = [sb(ph, f"c_score{i}", [128, SL], F32) for i in range(2)]
                maskq2 = [sb(ph, f"c_maskq{i}", [128, SL], BF16) for i in range(2)]
                bst = [sb(ph, f"c_bst{i}", [128, 8], F32) for i in range(2)]
                dft = [sb(ph, f"c_dft{i}", [128, NSTEP + 1], F32) for i in range(2)]
                maskT = sb(ph, "c_maskT", [128, NT, 512], BF16)
                rb2 = [sb(ph, f"c_r{i}", [128, 512], F32) for i in range(2)]
                pT2 = [sb(ph, f"c_pT{i}", [128, 512], BF16) for i in range(3)]
                bs_ = sb(ph, "c_bis", [128, 8], F32)
                osb2 = [sb(ph, f"c_o{i}", [128, 4, 65], F32) for i in range(2)]
                rl2 = [sb(ph, f"c_rl{i}", [128, 4], F32) for i in range(2)]
                yq2 = [sb(ph, f"c_yq{i}", [128, 4, 512], BF16) for i in range(2)]
                yts = [sb(ph, f"c_yts{i}", [128, 512], BF16) for i in range(2)]
                psi = [ps(ph, f"c_psi{i}", [128, 512], F32) for i in range(2)]
                pss = [ps(ph, f"c_pss{i}", [128, 512], F32) for i in range(3)]
                po2 = [ps(ph, f"c_po{i}", [128, 512], F32) for i in range(2)]
                ptr = ps(ph, "c_ptr", [128, 4, 128], BF16)
                S.dma("sp", kt[:], KT["c"].rearrange("j p t -> p j t"), r=["QKT"], w=["c_kt"])
                for v4 in range(4):
                    S.dma("sp", vx[:, v4 * 8:(v4 + 1) * 8, :], VX["c"][v4 * 1024:(v4 + 1) * 1024, :].rearrange("(t p) e -> p t e", p=128), r=["VX"], w=["c_vx"])
                S.dma("sp", ikt[:], IKT[:, :], r=["QKT"], w=["c_ikt"])

                def c_load_q(Q):
                    kq_ = f"c_qt{Q % 2}"
                    S.dma("sp", qt2[Q % 2][:], QT["c"][:, :, Q * 512:(Q + 1) * 512].rearrange("j p t -> p j t"), r=["QKT"], w=[kq_])
                    S.dma("sp", iqt2[Q % 2][:], IQT[:, :, Q * 512:(Q + 1) * 512].rearrange("j p t -> p j t"), r=["QKT"], w=[kq_])

                c_load_q(0)
                ir = 0
                gidx = 0
                for Q in range(NQ):
                    qt = qt2[Q % 2]
                    iqt = iqt2[Q % 2]
                    kq = f"c_qt{Q % 2}"
                    if Q + 1 < NQ:
                        c_load_q(Q + 1)
                    for pair in range(2):
                        info = []
                        for ci in range(2):
                            qq = 2 * pair + ci
                            qb = 4 * Q + qq
                            nadm = (qb + 1) * 128
                            info.append((qq, qb, nadm))
                            sc_t = score2[ci]
                            ksc = f"c_score{ci}"
                            nkt = (nadm + 511) // 512
                            for k5 in range(nkt):
                                wd = min(512, nadm - k5 * 512)
                                for h in range(4):
                                    b = ir % 2
                                    ir += 1
                                    hb_ = (h % 2) * 64
                                    S.op("pe", lambda: T.matmul(psi[b][:, 0:wd], lhsT=iqt[hb_:hb_ + 64, h // 2, qq * 128:(qq + 1) * 128],
                                                                rhs=ikt[hb_:hb_ + 64, k5 * 512:k5 * 512 + wd], start=True, stop=True),
                                         r=[kq, "c_ikt"], w=[f"c_psi{b}"])
                                    S.op("act", lambda: A.activation(out=rb2[b][:, 0:wd], in_=psi[b][:, 0:wd], func=AF.Relu),
                                         r=[f"c_psi{b}"], w=[f"c_r{b}"])
                                    sc = sc_t[:, k5 * 512:k5 * 512 + wd]
                                    if h == 0:
                                        S.op("dve", lambda: V.tensor_scalar(out=sc, in0=rb2[b][:, 0:wd], scalar1=iw_all[:, qb, 0:1], scalar2=None, op0=ALU.mult),
                                             r=[f"c_r{b}", "iw_all"], w=[ksc])
                                    else:
                                        S.op("dve", lambda: V.scalar_tensor_tensor(out=sc, in0=rb2[b][:, 0:wd], scalar=iw_all[:, qb, h:h + 1], in1=sc,
                                                                                   op0=ALU.mult, op1=ALU.add),
                                             r=[f"c_r{b}", "iw_all", ksc], w=[ksc])
                            S.op("dve", lambda: V.memset(sc_t[0:64, qb * 128 + 64:(qb + 1) * 128], NEG), r=[ksc], w=[ksc])
                        bisect = info[0][1] >= 2
                        for ci in range(2):
                            qq, qb, nadm = info[ci]
                            sc_t = score2[ci]
                            ksc = f"c_score{ci}"
                            stt = bst[ci]
                            df = dft[ci]
                            KB = [f"c_bis{ci}"]
                            if not bisect:
                                S.op("dve", lambda: V.memset(stt[:, 5:6], -1.0e29), r=KB, w=KB)
                                continue
                            nfull = qb * 128 + 64
                            S.op("dve", lambda: V.tensor_reduce(out=stt[:, 4:5], in_=sc_t[:, 0:nadm], axis=AX.X, op=ALU.max), r=[ksc], w=KB)
                            S.op("dve", lambda: V.tensor_reduce(out=stt[:, 5:6], in_=sc_t[:, 0:nfull], axis=AX.X, op=ALU.min), r=[ksc], w=KB)
                            S.op("dve", lambda: V.tensor_scalar(out=stt[:, 5:6], in0=stt[:, 5:6], scalar1=-1.0, scalar2=None, op0=ALU.add), r=KB, w=KB)
                            S.op("dve", lambda: V.tensor_tensor(out=stt[:, 1:2], in0=stt[:, 4:5], in1=stt[:, 5:6], op=ALU.subtract), r=KB, w=KB)
                            S.op("dve", lambda: V.tensor_scalar(out=df[:], in0=pw2[:], scalar1=stt[:, 1:2], scalar2=None, op0=ALU.mult), r=KB + ["const"], w=KB)
                            if ci == 0:
                                S.op("dve", lambda: V.tensor_tensor(out=stt[:, 0:1], in0=stt[:, 5:6], in1=df[:, 0:1], op=ALU.add), r=KB, w=KB)
                            else:
                                S.op("dve", lambda: V.scalar_tensor_tensor(out=stt[:, 0:1], in0=stt[:, 5:6], scalar=-1.0, in1=df[:, 0:1],
                                                                           op0=ALU.mult, op1=ALU.subtract), r=KB, w=KB)
                        if bisect:
                            for i in range(NSTEP):
                                lastst = (i == NSTEP - 1)
                                qq, qb, nadm = info[0]
                                stt, df, KB, ksc = bst[0], dft[0], ["c_bis0"], "c_score0"
                                S.op("dve", lambda: V.tensor_scalar(out=maskq2[0][:, 0:nadm], in0=score2[0][:, 0:nadm], scalar1=stt[:, 0:1], scalar2=None,
                                                                    op0=ALU.is_gt, op1=ALU.add, accum_out=stt[:, 2:3]),
                                     r=KB + [ksc, "c_maskq0"], w=KB + ["c_maskq0"])
                                S.op("dve", lambda: V.tensor_scalar(out=stt[:, 3:4], in0=stt[:, 2:3], scalar1=float(TOPK), scalar2=df[:, i:i + 1],
                                                                    op0=ALU.is_ge, op1=ALU.mult), r=KB, w=KB)
                                if not lastst:
                                    S.op("dve", lambda: V.scalar_tensor_tensor(out=stt[:, 0:1], in0=stt[:, 3:4], scalar=df[:, i + 1:i + 2], in1=stt[:, 0:1],
                                                                               op0=ALU.subtract, op1=ALU.add), r=KB, w=KB)
                                else:
                                    S.op("dve", lambda: V.scalar_tensor_tensor(out=stt[:, 5:6], in0=stt[:, 3:4], scalar=df[:, i:i + 1], in1=stt[:, 0:1],
                                                                               op0=ALU.subtract, op1=ALU.add), r=KB, w=KB)
                                qq, qb, nadm = info[1]
                                stt, df, KB, ksc = bst[1], dft[1], ["c_bis1"], "c_score1"
                                S.op("act", lambda: A.activation(out=maskq2[1][:, 0:nadm], in_=score2[1][:, 0:nadm], func=AF.Sign, bias=stt[:, 0:1],
                                                                 accum_out=stt[:, 2:3]),
                                     r=KB + [ksc, "c_maskq1"], w=KB + ["c_maskq1"])
                                S.op("pool", lambda: G.tensor_scalar(out=stt[:, 3:4], in0=stt[:, 2:3], scalar1=float(2 * TOPK - nadm), scalar2=df[:, i:i + 1],
                                                                     op0=ALU.is_ge, op1=ALU.mult), r=KB, w=KB)
                                S.op("pool", lambda: G.tensor_tensor(out=stt[:, 0:1], in0=stt[:, 0:1], in1=stt[:, 3:4], op=ALU.subtract), r=KB, w=KB)
                                if not lastst:
                                    S.op("pool", lambda: G.tensor_tensor(out=stt[:, 0:1], in0=stt[:, 0:1], in1=df[:, i + 1:i + 2], op=ALU.add), r=KB, w=KB)
                                else:
                                    S.op("pool", lambda: G.tensor_tensor(out=stt[:, 0:1], in0=stt[:, 0:1], in1=df[:, i:i + 1], op=ALU.add), r=KB, w=KB)
                                    S.op("pool", lambda: G.tensor_scalar(out=stt[:, 5:6], in0=stt[:, 0:1], scalar1=-1.0, scalar2=None, op0=ALU.mult), r=KB, w=KB)
                        for ci in range(2):
                            qq, qb, nadm = info[ci]
                            mq = maskq2[ci]
                            kmq = f"c_maskq{ci}"
                            S.op("dve", lambda: V.tensor_scalar(out=mq[:, 0:nadm], in0=score2[ci][:, 0:nadm], scalar1=bst[ci][:, 5:6], scalar2=None, op0=ALU.is_gt),
                                 r=[f"c_bis{ci}", f"c_score{ci}", kmq], w=[kmq])
                            for k1 in range(0, qb + 1, 4):
                                nk = min(4, qb + 1 - k1)
                                for u in range(nk):
                                    S.op("pe", lambda: T.transpose(out=ptr[:, u, :], in_=mq[:, (k1 + u) * 128:(k1 + u + 1) * 128], identity=ident[:]),
                                         r=[kmq], w=["c_ptr"])
                                S.op("act", lambda: A.activation(out=maskT[:, k1:k1 + nk, qq * 128:(qq + 1) * 128], in_=ptr[:, 0:nk, :], func=AF.Copy),
                                     r=["c_ptr"], w=["c_maskT"])
                    items = []
                    yq = yq2[Q % 2]
                    kyq = f"c_yq{Q % 2}"
                    for h in range(8):
                        g = gidx
                        gidx += 1
                        pr, bs = h // 2, (h % 2) * 64
                        nj = 4 * Q + 4
                        gstate = {"first": True}
                        for j in range(nj):
                            b = len(items) % 3
                            jj = j - 4 * Q
                            c0 = max(0, jj) * 128

                            def s0(j=j, c0=c0, b=b, pr=pr, bs=bs):
                                S.op("pe", lambda: T.matmul(pss[b][:, c0:512], lhsT=kt[bs:bs + 64, pr, j * 128:(j + 1) * 128],
                                                            rhs=qt[bs:bs + 64, pr, c0:512], start=True, stop=True),
                                     r=["c_kt", kq], w=[f"c_pss{b}"])

                            def s1(j=j, c0=c0, b=b):
                                S.op("act", lambda: A.activation(out=pT2[b][:, c0:512], in_=pss[b][:, c0:512], func=AF.Exp),
                                     r=[f"c_pss{b}"], w=[f"c_pT{b}"])
                                if b == 0:
                                    S.op("dve", lambda: V.tensor_tensor(out=pT2[b][:, c0:512], in0=pT2[b][:, c0:512], in1=maskT[:, j, c0:512], op=ALU.mult),
                                         r=[f"c_pT{b}", "c_maskT"], w=[f"c_pT{b}"])
                                else:
                                    S.op("pool", lambda: G.tensor_tensor(out=pT2[b][:, c0:512], in0=pT2[b][:, c0:512], in1=maskT[:, j, c0:512], op=ALU.mult),
                                         r=[f"c_pT{b}", "c_maskT"], w=[f"c_pT{b}"])

                            def s2(h=h, j=j, jj=jj, b=b, g=g, gstate=gstate, nj=nj):
                                po = po2[g % 2]
                                kpo = f"c_po{g % 2}"
                                osb = osb2[g % 2]
                                rl = rl2[g % 2]
                                for q4 in range(max(0, jj), 4):
                                    S.op("pe", lambda: T.matmul(po[:, q4 * 65:(q4 + 1) * 65], lhsT=pT2[b][:, q4 * 128:(q4 + 1) * 128],
                                                                rhs=vx[:, j, h * 65:(h + 1) * 65], start=gstate["first"], stop=(j == nj - 1),
                                                                skip_group_check=True),
                                         r=[f"c_pT{b}", "c_vx"], w=[kpo])
                                    gstate["first"] = False
                                if j == nj - 1:
                                    S.op("act", lambda: A.activation(out=osb[:], in_=po[:, 0:260].rearrange("p (a e) -> p a e", e=65), func=AF.Copy),
                                         r=[kpo], w=[f"c_o{g % 2}"])
                                    S.op("dve", lambda: V.reciprocal(out=rl[:], in_=osb[:, :, 64]), r=[f"c_o{g % 2}"], w=[f"c_rl{g % 2}"])
                                    S.op("dve", lambda: V.tensor_tensor(out=yq[:, :, h * 64:(h + 1) * 64], in0=osb[:, :, 0:64],
                                                                        in1=rl[:].unsqueeze(2).to_broadcast([128, 4, 64]), op=ALU.mult),
                                         r=[f"c_o{g % 2}", f"c_rl{g % 2}"], w=[kyq])
                            items.append([s0, None, s1, None, s2])
                    run_pipeline(items, 5)
                    for cch in range(4):
                        for q4 in range(4):
                            S.op("pe", lambda: T.transpose(out=ptr[:, q4, :], in_=yq[:, q4, cch * 128:(cch + 1) * 128], identity=ident[:]),
                                 r=[kyq], w=["c_ptr"])
                        ys = yts[cch % 2]
                        S.op("act", lambda: A.activation(out=ys[:], in_=ptr[:].rearrange("p a b -> p (a b)"), func=AF.Copy), r=["c_ptr"], w=[f"c_yts{cch % 2}"])
                        S.dma("act", YT[8 + cch, :, Q * 512:(Q + 1) * 512], ys[:], r=[f"c_yts{cch % 2}"], w=["YT"])
                S.barrier()
                ph.close()

            for ph in ([ExitStack()] if 'pm' in PH else []):
                wbr = [sb(ph, f"m_wbr{i}", [128, 4, D], BF16) for i in range(3)]
                wo = sb(ph, "m_wo", [128, 8, D], BF16)
                yt2 = [sb(ph, f"m_yt{i}", [128, 12, 512], BF16) for i in range(2)]
                gt2 = [sb(ph, f"m_gt{i}", [128, 24, 512], BF16) for i in range(2)]
                m0_2 = [sb(ph, f"m_m0{i}", [128, 512], F32) for i in range(2)]
                m1_2 = [sb(ph, f"m_m1{i}", [128, 512], F32) for i in range(2)]
                m2_2 = [sb(ph, f"m_m2{i}", [128, 512], F32) for i in range(2)]
                mT = sb(ph, "m_mT", [128, 8, 512], BF16)
                xb2 = [sb(ph, f"m_x{i}", [128, D], F32) for i in range(2)]
                ps3 = [ps(ph, f"m_ps{i}", [128, 512], F32) for i in range(6)]
                pso = [ps(ph, f"m_pso{i}", [128, 512], F32) for i in range(2)]
                for i in range(3):
                    load_w_bf16(wbr[i][:], w_br[i][l].rearrange("(c p) n -> p c n", p=128), "m_w")
                load_w_bf16(wo[:], w_out[l].rearrange("(c p) n -> p c n", p=128), "m_w")
                io = 0
                for Tq in range(NQ):
                    yt = yt2[Tq % 2]
                    gt = gt2[Tq % 2]
                    kin = f"m_in{Tq % 2}"
                    for b3 in range(3):
                        S.dma("sp", yt[:, b3 * 4:(b3 + 1) * 4, :], YT[b3 * 4:(b3 + 1) * 4, :, Tq * 512:(Tq + 1) * 512].rearrange("j p t -> p j t"), r=["YT"], w=[kin])
                        S.dma("sp", gt[:, b3 * 8:(b3 + 1) * 8, :], GT[b3 * 8:(b3 + 1) * 8, :, Tq * 512:(Tq + 1) * 512].rearrange("j p t -> p j t"), r=["GT"], w=[kin])
                    for fc in range(8):
                        fp_ = fc % 2
                        m0, m1, m2 = m0_2[fp_], m1_2[fp_], m2_2[fp_]
                        for br in range(3):
                            pb3 = fp_ * 3 + br
                            for kc in range(4):
                                S.op("pe", lambda: T.matmul(ps3[pb3][:], lhsT=wbr[br][:, kc, fc * 128:(fc + 1) * 128], rhs=yt[:, br * 4 + kc, :],
                                                            start=(kc == 0), stop=(kc == 3)), r=["m_w", kin], w=[f"m_ps{pb3}"])
                        S.op("dve", lambda: V.tensor_tensor(out=m0[:], in0=ps3[fp_ * 3][:], in1=gt[:, fc, :], op=ALU.mult), r=[f"m_ps{fp_ * 3}", kin], w=[f"m_m0{fp_}"])
                        S.op("dve", lambda: V.tensor_tensor(out=m1[:], in0=ps3[fp_ * 3 + 1][:], in1=gt[:, 8 + fc, :], op=ALU.mult), r=[f"m_ps{fp_ * 3 + 1}", kin], w=[f"m_m1{fp_}"])
                        S.op("dve", lambda: V.tensor_tensor(out=m2[:], in0=ps3[fp_ * 3 + 2][:], in1=gt[:, 16 + fc, :], op=ALU.mult), r=[f"m_ps{fp_ * 3 + 2}", kin], w=[f"m_m2{fp_}"])
                        S.op("pool", lambda: G.tensor_tensor(out=m0[:], in0=m0[:], in1=m1[:], op=ALU.add), r=[f"m_m0{fp_}", f"m_m1{fp_}"], w=[f"m_m0{fp_}"])
                        S.op("pool", lambda: G.tensor_tensor(out=mT[:, fc, :], in0=m0[:], in1=m2[:], op=ALU.add), r=[f"m_m0{fp_}", f"m_m2{fp_}"], w=["m_mT"])
                    for tt in range(4):
                        t = Tq * 4 + tt
                        xt = xb2[t % 2]
                        kx = f"m_x{t % 2}"
                        S.dma("sp", xt[:], xsrc[t * 128:(t + 1) * 128, :], r=[f"xr{t}"], w=[kx])
                        for nb in range(2):
                            pq = pso[io % 2]
                            kpq = f"m_pso{io % 2}"
                            io += 1
                            for fc in range(8):
                                S.op("pe", lambda: T.matmul(pq[:], lhsT=mT[:, fc, tt * 128:(tt + 1) * 128], rhs=wo[:, fc, nb * 512:(nb + 1) * 512],
                                                            start=(fc == 0), stop=(fc == 7)), r=["m_mT", "m_w"], w=[kpq])
                            S.op("dve", lambda: V.tensor_tensor(out=xt[:, nb * 512:(nb + 1) * 512], in0=pq[:], in1=xt[:, nb * 512:(nb + 1) * 512], op=ALU.add),
                                 r=[kpq, kx], w=[kx])
                        S.dma("sp", xres[t * 128:(t + 1) * 128, :], xt[:], r=[kx], w=[f"xr{t}"])
                S.barrier()
                ph.close()

            for ph in ([ExitStack()] if 'pf' in PH else []):
                wu = sb(ph, "f_wu", [128, 8, 4 * D], BF16)
                wd = sb(ph, "f_wd", [128, 32, D], BF16)
                xb4 = [sb(ph, f"f_x{i}", [128, D], F32) for i in range(4)]
                hb = sb(ph, "f_hb", [128, D], BF16)
                junk = sb(ph, "f_junk", [128, D], BF16)
                ssq = sb(ph, "f_ssq", [128, 8], F32)
                rstd = sb(ph, "f_rstd", [128, 8], F32)
                hmT2 = [sb(ph, f"f_hmT{i}", [128, 8, 256], BF16) for i in range(2)]
                rr2 = [sb(ph, f"f_r{i}", [128, 256], F32) for i in range(2)]
                uT = sb(ph, "f_uT", [128, 32, 256], BF16)
                pst2 = [ps(ph, f"f_pst{i}", [128, 4, 128], BF16) for i in range(2)]
                psu = [ps(ph, f"f_psu{i}", [128, 256], F32) for i in range(2)]
                psd = [ps(ph, f"f_psd{i}", [128, 512], F32) for i in range(2)]
                for c in range(8):
                    load_w_bf16(wu[:, c, :], w_up[l, c * 128:(c + 1) * 128, :], "f_wu")
                for c4 in range(4):
                    load_w_bf16(wd[:, c4 * 8:(c4 + 1) * 8, :], w_down[l, c4 * 1024:(c4 + 1) * 1024, :].rearrange("(c p) n -> p c n", p=128), "f_wd")
                iof = [0]
                items = []
                for tg in range(NT // 2):
                    def s0(tg=tg):
                        hm = hmT2[tg % 2]
                        for t2 in range(2):
                            t = tg * 2 + t2
                            xi = (tg % 2) * 2 + t2
                            xt = xb4[xi]
                            kx = f"f_x{xi}"
                            S.dma("sp", xt[:], xres[t * 128:(t + 1) * 128, :], r=[f"xr{t}"], w=[kx])
                            norm_transpose(xt[:], kx, gM, l, hb, pst2, lambda c, t2=t2, hm=hm: hm[:, c, t2 * 128:(t2 + 1) * 128], f"f_hmT{tg % 2}",
                                           ssq, rstd, junk, "f")

                    def s1(tg=tg):
                        hm = hmT2[tg % 2]
                        khm = f"f_hmT{tg % 2}"
                        for fc in range(32):
                            pu = psu[fc % 2]
                            for c in range(8):
                                S.op("pe", lambda: T.matmul(pu[:], lhsT=wu[:, c, fc * 128:(fc + 1) * 128], rhs=hm[:, c, :], start=(c == 0), stop=(c == 7)),
                                     r=["f_wu", khm], w=[f"f_psu{fc % 2}"])
                            rr = rr2[fc % 2]
                            S.op("act", lambda: A.activation(out=rr[:], in_=pu[:], func=AF.Relu), r=[f"f_psu{fc % 2}"], w=[f"f_r{fc % 2}"])
                            if fc % 2 == 0:
                                S.op("pool", lambda: G.tensor_tensor(out=uT[:, fc, :], in0=rr[:], in1=rr[:], op=ALU.mult), r=[f"f_r{fc % 2}"], w=["f_uT"])
                            else:
                                S.op("dve", lambda: V.tensor_tensor(out=uT[:, fc, :], in0=rr[:], in1=rr[:], op=ALU.mult), r=[f"f_r{fc % 2}"], w=["f_uT"])
                        for t2 in range(2):
                            t = tg * 2 + t2
                            xi = (tg % 2) * 2 + t2
                            xt = xb4[xi]
                            kx = f"f_x{xi}"
                            for nb in range(2):
                                pq = psd[iof[0] % 2]
                                kpq = f"f_psd{iof[0] % 2}"
                                iof[0] += 1
                                for fc in range(32):
                                    S.op("pe", lambda: T.matmul(pq[:], lhsT=uT[:, fc, t2 * 128:(t2 + 1) * 128], rhs=wd[:, fc, nb * 512:(nb + 1) * 512],
                                                                start=(fc == 0), stop=(fc == 31)), r=["f_uT", "f_wd"], w=[kpq])
                                S.op("dve", lambda: V.tensor_tensor(out=xt[:, nb * 512:(nb + 1) * 512], in0=pq[:], in1=xt[:, nb * 512:(nb + 1) * 512], op=ALU.add),
                                     r=[kpq, kx], w=[kx])
                            S.dma("sp", xres[t * 128:(t + 1) * 128, :], xt[:], r=[kx], w=[f"xr{t}"])
                    items.append([s0, s1])
                run_pipeline(items, 2)
                S.barrier()
                ph.close()

            for ph in ([ExitStack()] if 'pp' in PH else []):
                wg = sb(ph, "e_wg", [128, 8, D], BF16)
                wp = sb(ph, "e_wp", [128, 2, D], BF16)
                xb3 = [sb(ph, f"e_x{i}", [128, D], F32) for i in range(3)]
                pb3_ = [sb(ph, f"e_p{i}", [128, 256], F32) for i in range(3)]
                pbb = sb(ph, "e_pbb", [128, 256], BF16)
                hb = sb(ph, "e_hb", [128, D], BF16)
                junk = sb(ph, "e_junk", [128, D], BF16)
                ssq = sb(ph, "e_ssq", [128, 8], F32)
                rstd = sb(ph, "e_rstd", [128, 8], F32)
                hpT2 = [sb(ph, f"e_hpT{i}", [128, 8, 128], BF16) for i in range(2)]
                pT2 = [sb(ph, f"e_pT{i}", [128, 2, 128], BF16) for i in range(2)]
                sg2 = [sb(ph, f"e_sg{i}", [128, 512], F32) for i in range(2)]
                tm2 = [sb(ph, f"e_tm{i}", [128, 512], F32) for i in range(2)]
                pst2 = [ps(ph, f"e_pst{i}", [128, 4, 128], BF16) for i in range(2)]
                ptp = ps(ph, "e_ptp", [128, 2, 128], BF16)
                psg = [ps(ph, f"e_psg{i}", [128, 512], F32) for i in range(2)]
                psp = [ps(ph, f"e_psp{i}", [128, 512], F32) for i in range(2)]
                load_w_bf16(wg[:], w_ple_gate[l].rearrange("(c p) n -> p c n", p=128), "e_w")
                load_w_bf16(wp[:], w_ple_proj[l].rearrange("(c p) n -> p c n", p=128), "e_w")
                dst = out_d if l == depth - 1 else xres
                kdst = "out"
                ioe = [0]
                items = []
                for t in range(NT):
                    def s0(t=t):
                        xt = xb3[t % 3]
                        kx = f"e_x{t % 3}"
                        pb = pb3_[t % 3]
                        hp = hpT2[t % 2]
                        S.dma("sp", xt[:], xres[t * 128:(t + 1) * 128, :], r=[f"xr{t}"], w=[kx])
                        S.dma("sp", pb[:], p_in[l, t * 128:(t + 1) * 128, :], w=[f"e_p{t % 3}"])
                        norm_transpose(xt[:], kx, gP, l, hb, pst2, lambda c, hp=hp: hp[:, c, :], f"e_hpT{t % 2}", ssq, rstd, junk, "e")
                        S.op("pool", lambda: G.tensor_copy(out=pbb[:], in_=pb[:]), r=[f"e_p{t % 3}"], w=["e_pbb"])
                        for c in range(2):
                            S.op("pe", lambda: T.transpose(out=ptp[:, c, :], in_=pbb[:, c * 128:(c + 1) * 128], identity=ident[:]), r=["e_pbb"], w=["e_ptp"])
                        S.op("act", lambda: A.activation(out=pT2[t % 2][:], in_=ptp[:], func=AF.Copy), r=["e_ptp"], w=[f"e_pT{t % 2}"])

                    def s1(t=t):
                        xt = xb3[t % 3]
                        kx = f"e_x{t % 3}"
                        hp = hpT2[t % 2]
                        pT = pT2[t % 2]
                        for nb in range(2):
                            b = ioe[0] % 2
                            ioe[0] += 1
                            for c in range(8):
                                S.op("pe", lambda: T.matmul(psg[b][:], lhsT=hp[:, c, :], rhs=wg[:, c, nb * 512:(nb + 1) * 512], start=(c == 0), stop=(c == 7)),
                                     r=[f"e_hpT{t % 2}", "e_w"], w=[f"e_psg{b}"])
                            for c in range(2):
                                S.op("pe", lambda: T.matmul(psp[b][:], lhsT=pT[:, c, :], rhs=wp[:, c, nb * 512:(nb + 1) * 512], start=(c == 0), stop=(c == 1)),
                                     r=[f"e_pT{t % 2}", "e_w"], w=[f"e_psp{b}"])
                            S.op("act", lambda: A.activation(out=sg2[b][:], in_=psg[b][:], func=AF.Sigmoid), r=[f"e_psg{b}"], w=[f"e_sg{b}"])
                            S.op("dve", lambda: V.tensor_tensor(out=tm2[b][:], in0=psp[b][:], in1=sg2[b][:], op=ALU.mult), r=[f"e_psp{b}", f"e_sg{b}"], w=[f"e_tm{b}"])
                            S.op("pool", lambda: G.tensor_tensor(out=xt[:, nb * 512:(nb + 1) * 512], in0=xt[:, nb * 512:(nb + 1) * 512], in1=tm2[b][:], op=ALU.add),
                                 r=[f"e_tm{b}", kx], w=[kx])
                        S.dma("sp", dst[t * 128:(t + 1) * 128, :], xt[:], r=[kx], w=[kdst if l == depth - 1 else f"xr{t}"])
                    items.append([s0, s1])
                run_pipeline(items, 2)
                S.barrier()
                ph.close()
        S.barrier()
        build_nc.ninstr = dict(S.ninstr)
    return nc


def _host_layout(inputs, b):
    f32 = np.float32
    m = {}
    m["x"] = np.ascontiguousarray(inputs["x"][b], dtype=f32)
    m["p"] = np.ascontiguousarray(inputs["p"][:, b], dtype=f32)
    pos = np.asarray(inputs["positions"][b]).astype(np.int32)
    m["pos"] = np.ascontiguousarray(pos.reshape(NT, 128).T)
    m["invf"] = (10000.0 ** (-np.arange(0, 64, 2, dtype=np.float32) / 64)).astype(f32)
    for k in ("attn_norm", "mlp_norm", "ple_norm"):
        a = np.asarray(inputs[k], dtype=f32)
        m[k + "_l"] = np.ascontiguousarray(a.reshape(DEPTH, 8, 128).transpose(2, 0, 1).reshape(128, DEPTH * 8))
    for k in ("a_q_norm", "a_k_norm", "a_lambda", "a_subln", "c_q_norm", "c_k_norm", "idx_k_norm"):
        m[k] = np.ascontiguousarray(np.asarray(inputs[k], dtype=f32).reshape(-1))
    for k in ("w_in", "w_br_a", "w_br_b", "w_br_c", "w_out", "w_up", "w_down", "w_ple_gate", "w_ple_proj"):
        m[k] = np.ascontiguousarray(inputs[k], dtype=f32)
    return m


def kernel(**inputs):
    nc = build_nc()
    in_maps = [_host_layout(inputs, c % 4) for c in range(4)]
    in_maps = in_maps + in_maps
    res = run_bass_kernel_spmd(nc, in_maps, core_ids=list(range(8)))
    out = np.stack([np.asarray(res.results[b]["out"], dtype=np.float32) for b in range(4)], axis=0)
    return out
```

```python
import math
from contextlib import ExitStack

import numpy as np
import concourse.bass as bass
import concourse.mybir as mybir
from concourse.bass_utils import run_bass_kernel_spmd

F32 = mybir.dt.float32
BF16 = mybir.dt.bfloat16
I32 = mybir.dt.int32
AF = mybir.ActivationFunctionType
ALU = mybir.AluOpType
AX = mybir.AxisListType

DEPTH = 4
SL = 4096
D = 1024
NT = 32
NQ = 8
EPS = 1e-6
W_IN_COLS = 8004
TOPK = 256
NSTEP = 20
NEG = -1.0e30

SEM_EPOCH = 30000
N_DMA_SEMS = 24


class Sched:
    def __init__(self, nc, stack):
        self.nc = nc
        self.stack = stack
        self.engs = {"pe": nc.tensor, "act": nc.scalar, "dve": nc.vector,
                     "pool": nc.gpsimd, "sp": nc.sync}
        self.esem = {}
        self.ecount = {}
        self.n_sems = 0
        for e in self.engs:
            self._new_esem(e)
        self.waited = {e: {} for e in self.engs}
        self.dsems = [[self._sem(f"dma{i}"), 0] for i in range(N_DMA_SEMS)]
        self.dnext = 0
        self.lastw = {}
        self.reads = {}
        self.ninstr = {e: 0 for e in self.engs}

    def _sem(self, name):
        self.n_sems += 1
        return self.stack.enter_context(self.nc.semaphore(name))

    def _new_esem(self, e):
        self.esem[e] = self._sem(f"e_{e}_{self.n_sems}")
        self.ecount[e] = 0

    def _wait(self, e, dep):
        sem, val, tag = dep
        w = self.waited[e]
        k = id(sem)
        if w.get(k, 0) >= val:
            return
        w[k] = val
        self.engs[e].wait_ge(sem, val)
        self.ninstr[e] += 1

    def _collect(self, r, w):
        deps = []
        for k in list(r) + list(w):
            deps.extend(self.lastw.get(k, ()))
        for k in w:
            rd = self.reads.get(k)
            if rd:
                deps.extend(rd.values())
        return deps

    def _commit(self, r, w, dep, merge=()):
        for k in w:
            if k in merge:
                self.lastw[k].append(dep)
            else:
                self.lastw[k] = [dep]
                self.reads[k] = {}
        for k in r:
            if k in w:
                continue
            self.reads.setdefault(k, {})[dep[2]] = dep

    def op(self, e, fn, r=(), w=()):
        if self.ecount[e] >= SEM_EPOCH:
            self._new_esem(e)
        for d in self._collect(r, w):
            if e == "pe" and d[2] == "pe":
                continue
            self._wait(e, d)
        ins = fn()
        self.ecount[e] += 1
        ins.then_inc(self.esem[e], 1)
        dep = (self.esem[e], self.ecount[e], e)
        self.ninstr[e] += 1
        self._commit(r, w, dep)
        return ins

    def dma(self, q, out, in_, r=(), w=()):
        slot = self.dsems[self.dnext]
        self.dnext = (self.dnext + 1) % len(self.dsems)
        sem = slot[0]
        if slot[1] > 0:
            self._wait(q, (sem, slot[1], "dma"))
        deps = []
        merge = set()
        for k in r:
            deps.extend(self.lastw.get(k, ()))
        for k in w:
            lw = self.lastw.get(k, ())
            rd = self.reads.get(k) or {}
            if lw and not rd and all(d[2].startswith("dma") for d in lw):
                merge.add(k)
            else:
                deps.extend(lw)
                deps.extend(rd.values())
        for d in deps:
            self._wait(q, d)
        ins = self.engs[q].dma_start(out=out, in_=in_)
        slot[1] += 16
        ins.then_inc(sem, 16)
        dep = (sem, slot[1], f"dma{id(sem)}")
        self.ninstr[q] += 1
        self._commit(r, w, dep, merge)
        return ins

    def barrier(self):
        for e in self.engs:
            for o in self.engs:
                if o != e and self.ecount[o] > 0:
                    self._wait(e, (self.esem[o], self.ecount[o], o))
            for sem, val in self.dsems:
                if val:
                    self._wait(e, (sem, val, "dma"))


def run_pipeline(items, nstage):
    n = len(items)
    for step in range(n + nstage - 1):
        for k in range(nstage - 1, -1, -1):
            i = step - k
            if 0 <= i < n and items[i][k] is not None:
                items[i][k]()


def build_nc(depth=DEPTH, debug=False, phases=None):
    PH = set(phases) if phases is not None else {'p1', 'pa', 'pb', 'pc', 'pm', 'pf', 'pp'}
    nc = bass.Bass("TRN2", target_bir_lowering=False)

    def din(name, shape, dt=F32):
        return nc.dram_tensor(name, list(shape), dt, kind="ExternalInput").ap()

    def dscr(name, shape, dt):
        kind = "ExternalOutput" if debug else "Internal"
        return nc.dram_tensor(name, list(shape), dt, kind=kind).ap()

    x_in = din("x", [SL, D])
    p_in = din("p", [DEPTH, SL, 256])
    pos_in = din("pos", [128, NT], I32)
    invf_in = din("invf", [32])
    attn_norm = din("attn_norm_l", [128, DEPTH * 8])
    mlp_norm = din("mlp_norm_l", [128, DEPTH * 8])
    ple_norm = din("ple_norm_l", [128, DEPTH * 8])
    w_in = din("w_in", [DEPTH, D, W_IN_COLS])
    a_q_norm = din("a_q_norm", [DEPTH * 64])
    a_k_norm = din("a_k_norm", [DEPTH * 64])
    a_lambda = din("a_lambda", [DEPTH * 256])
    a_subln = din("a_subln", [DEPTH * 128])
    c_q_norm = din("c_q_norm", [DEPTH * 64])
    c_k_norm = din("c_k_norm", [DEPTH * 64])
    idx_k_norm = din("idx_k_norm", [DEPTH * 64])
    w_br = [din("w_br_a", [DEPTH, 512, D]), din("w_br_b", [DEPTH, 512, D]), din("w_br_c", [DEPTH, 512, D])]
    w_out = din("w_out", [DEPTH, D, D])
    w_up = din("w_up", [DEPTH, D, 4 * D])
    w_down = din("w_down", [DEPTH, 4 * D, D])
    w_ple_gate = din("w_ple_gate", [DEPTH, D, D])
    w_ple_proj = din("w_ple_proj", [DEPTH, 256, D])
    out_d = nc.dram_tensor("out", [SL, D], F32, kind="ExternalOutput").ap()

    xres = dscr("xres", [SL, D], F32)
    QT = {k: dscr("QT_" + k, [4, 128, SL], BF16) for k in "abc"}
    KT = {k: dscr("KT_" + k, [4, 128, SL], BF16) for k in "abc"}
    VX = {"a": dscr("V_a", [SL, 4 * 129], BF16), "b": dscr("V_b", [SL, 8 * 65], BF16),
          "c": dscr("V_c", [SL, 8 * 65], BF16)}
    IQT = dscr("IQT", [2, 128, SL], BF16)
    IKT = dscr("IKT", [128, SL], BF16)
    GT = dscr("GT", [24, 128, SL], BF16)
    YT = dscr("YT", [12, 128, SL], BF16)

    with ExitStack() as st:
        S = Sched(nc, st)
        E = st.enter_context
        V = nc.vector
        A = nc.scalar
        G = nc.gpsimd
        T = nc.tensor

        uid = [0]

        def sb(stack, name, shape, dt):
            uid[0] += 1
            return stack.enter_context(nc.sbuf_tensor(f"s{uid[0]}_{name}", list(shape), dt))

        def ps(stack, name, shape, dt):
            uid[0] += 1
            return stack.enter_context(nc.psum_tensor(f"q{uid[0]}_{name}", list(shape), dt))

        ident = sb(st, "ident", [128, 128], BF16)
        ntri = sb(st, "ntri", [128, 128], BF16)
        nones = sb(st, "nones", [128, 128], BF16)
        strict = sb(st, "strict", [128, 128], BF16)
        cos_t = sb(st, "cos_t", [128, NT, 32], F32)
        sin_t = sb(st, "sin_t", [128, NT, 32], F32)
        gA = sb(st, "gA", [128, DEPTH * 8], F32)
        gM = sb(st, "gM", [128, DEPTH * 8], F32)
        gP = sb(st, "gP", [128, DEPTH * 8], F32)
        aqn = sb(st, "aqn", [128, DEPTH * 64], F32)
        akn = sb(st, "akn", [128, DEPTH * 64], F32)
        cqn = sb(st, "cqn", [128, DEPTH * 64], F32)
        ckn = sb(st, "ckn", [128, DEPTH * 64], F32)
        ikn = sb(st, "ikn", [128, DEPTH * 64], F32)
        subg = sb(st, "subg", [128, DEPTH * 128], F32)
        lam_t = sb(st, "lam_t", [128, DEPTH], F32)
        nlam_t = sb(st, "nlam_t", [128, DEPTH], F32)
        iw_all = sb(st, "iw_all", [128, NT, 4], F32)
        pw2 = sb(st, "pw2", [128, NSTEP + 1], F32)

        CONST = ["const"]

        with ExitStack() as ph:
            posi = sb(ph, "posi", [128, NT], I32)
            posf = sb(ph, "posf", [128, NT], F32)
            invf = sb(ph, "invf", [128, 32], F32)
            ang = sb(ph, "ang", [128, NT, 32], F32)
            t0 = sb(ph, "t0", [128, NT, 32], F32)
            t1 = sb(ph, "t1", [128, NT, 32], F32)
            lpb = sb(ph, "lpb", [128, DEPTH * 256], F32)
            tmp64 = sb(ph, "tmp64", [128, 64], F32)
            s12 = sb(ph, "s12", [128, 2], F32)
            K0 = ["p0"]
            S.op("pool", lambda: G.memset(ident[:], 1.0), w=CONST)
            S.op("pool", lambda: G.affine_select(out=ident[:], in_=ident[:], pattern=[[-1, 128]],
                                                 compare_op=ALU.is_equal, fill=0.0, base=0, channel_multiplier=1),
                 r=CONST, w=CONST)
            S.op("pool", lambda: G.memset(nones[:], -1.0), w=CONST)
            for i_ in range(NSTEP + 1):
                S.op("pool", lambda: G.memset(pw2[:, i_:i_ + 1], float(2.0 ** -(i_ + 1))), w=CONST)
            S.op("pool", lambda: G.memset(ntri[:], -1.0), w=CONST)
            S.op("pool", lambda: G.affine_select(out=ntri[:], in_=ntri[:], pattern=[[-1, 128]],
                                                 compare_op=ALU.is_ge, fill=0.0, base=0, channel_multiplier=1),
                 r=CONST, w=CONST)
            S.op("pool", lambda: G.memset(strict[:], 1.0), w=CONST)
            S.op("pool", lambda: G.affine_select(out=strict[:], in_=strict[:], pattern=[[1, 128]],
                                                 compare_op=ALU.is_gt, fill=0.0, base=0, channel_multiplier=-1),
                 r=CONST, w=CONST)
            for dst, src in [(gA, attn_norm), (gM, mlp_norm), (gP, ple_norm)]:
                S.dma("sp", dst[:], src[:, :], w=CONST)
            for dst, src in [(aqn, a_q_norm), (akn, a_k_norm), (cqn, c_q_norm), (ckn, c_k_norm),
                             (ikn, idx_k_norm), (subg, a_subln), (lpb, a_lambda), (invf, invf_in)]:
                S.dma("sp", dst[:], src.partition_broadcast(128), w=CONST + K0)
            S.dma("sp", posi[:], pos_in[:, :], w=K0)
            S.op("dve", lambda: V.tensor_scalar(out=aqn[:], in0=aqn[:], scalar1=0.125, scalar2=None, op0=ALU.mult), r=CONST, w=CONST)
            S.op("dve", lambda: V.tensor_scalar(out=cqn[:], in0=cqn[:], scalar1=0.125, scalar2=None, op0=ALU.mult), r=CONST, w=CONST)
            for l in range(DEPTH):
                li = 0.8 - 0.6 * math.exp(-0.3 * l)
                S.op("dve", lambda: V.tensor_scalar(out=subg[:, l * 128:(l + 1) * 128], in0=subg[:, l * 128:(l + 1) * 128],
                                                    scalar1=float(1.0 - li), scalar2=None, op0=ALU.mult), r=CONST, w=CONST)
                for j in range(2):
                    S.op("dve", lambda: V.tensor_tensor(out=tmp64[:], in0=lpb[:, l * 256 + j * 128: l * 256 + j * 128 + 64],
                                                        in1=lpb[:, l * 256 + j * 128 + 64: l * 256 + j * 128 + 128], op=ALU.mult),
                         r=K0, w=["tmp64"])
                    S.op("dve", lambda: V.tensor_reduce(out=s12[:, j:j + 1], in_=tmp64[:], axis=AX.X, op=ALU.add),
                         r=["tmp64"], w=["s12"])
                S.op("act", lambda: A.activation(out=s12[:], in_=s12[:], func=AF.Exp), r=["s12"], w=["s12"])
                S.op("dve", lambda: V.tensor_tensor(out=lam_t[:, l:l + 1], in0=s12[:, 0:1], in1=s12[:, 1:2], op=ALU.subtract),
                     r=["s12"], w=CONST)
                S.op("dve", lambda: V.tensor_scalar(out=lam_t[:, l:l + 1], in0=lam_t[:, l:l + 1], scalar1=float(li), scalar2=None, op0=ALU.add),
                     r=CONST, w=CONST)
                S.op("dve", lambda: V.tensor_scalar(out=nlam_t[:, l:l + 1], in0=lam_t[:, l:l + 1], scalar1=-1.0, scalar2=None, op0=ALU.mult),
                     r=CONST, w=CONST)
            S.op("dve", lambda: V.tensor_copy(out=posf[:], in_=posi[:]), r=K0, w=["posf"])
            S.op("dve", lambda: V.tensor_tensor(out=ang[:], in0=posf[:].unsqueeze(2).to_broadcast([128, NT, 32]),
                                                in1=invf[:].unsqueeze(1).to_broadcast([128, NT, 32]), op=ALU.mult),
                 r=["posf"] + K0, w=["ang"])
            MAGIC = 12582912.0
            C1 = 6.28125
            C2 = 2.0 * math.pi - C1
            PI_LO = 3.1415925
            S.op("dve", lambda: V.tensor_scalar(out=t0[:], in0=ang[:], scalar1=float(1.0 / (2.0 * math.pi)), scalar2=None, op0=ALU.mult), r=["ang"], w=["t0"])
            S.op("dve", lambda: V.tensor_scalar(out=t0[:], in0=t0[:], scalar1=MAGIC, scalar2=None, op0=ALU.add), r=["t0"], w=["t0"])
            S.op("dve", lambda: V.tensor_scalar(out=t0[:], in0=t0[:], scalar1=-MAGIC, scalar2=None, op0=ALU.add), r=["t0"], w=["t0"])
            S.op("dve", lambda: V.scalar_tensor_tensor(out=t1[:], in0=t0[:], scalar=-C1, in1=ang[:], op0=ALU.mult, op1=ALU.add), r=["t0", "ang"], w=["t1"])
            S.op("dve", lambda: V.scalar_tensor_tensor(out=t1[:], in0=t0[:], scalar=-C2, in1=t1[:], op0=ALU.mult, op1=ALU.add), r=["t0", "t1"], w=["t1"])
            S.op("dve", lambda: V.tensor_scalar(out=t0[:], in0=t1[:], scalar1=PI_LO, scalar2=-PI_LO, op0=ALU.min, op1=ALU.max), r=["t1"], w=["t0"])
            S.op("act", lambda: A.activation(out=sin_t[:], in_=t0[:], func=AF.Sin), r=["t0"], w=CONST)
            S.op("dve", lambda: V.tensor_scalar(out=t1[:], in0=t1[:], scalar1=float(math.pi / 2), scalar2=None, op0=ALU.add), r=["t1"], w=["t1"])
            S.op("dve", lambda: V.tensor_scalar(out=t0[:], in0=t1[:], scalar1=float(math.pi), scalar2=float(-2.0 * math.pi), op0=ALU.is_gt, op1=ALU.mult), r=["t1", "const"], w=["t0"])
            S.op("dve", lambda: V.tensor_tensor(out=t1[:], in0=t1[:], in1=t0[:], op=ALU.add), r=["t0", "t1"], w=["t1"])
            S.op("dve", lambda: V.tensor_scalar(out=t1[:], in0=t1[:], scalar1=PI_LO, scalar2=-PI_LO, op0=ALU.min, op1=ALU.max), r=["t1"], w=["t1"])
            S.op("act", lambda: A.activation(out=cos_t[:], in_=t1[:], func=AF.Sin), r=["t1"], w=CONST)
            S.barrier()

        def rms_rows(xt, kx, ssq, rstd, junk, kpre):
            S.op("act", lambda: A.activation(out=junk[:], in_=xt, func=AF.Square, accum_out=ssq[:, 0:1]),
                 r=[kx], w=[kpre + "junk", kpre + "ssq"])
            S.op("act", lambda: A.activation(out=rstd[:, 0:1], in_=ssq[:, 0:1], func=AF.Sqrt, bias=EPS, scale=1.0 / D),
                 r=[kpre + "ssq"], w=[kpre + "rstd"])
            S.op("dve", lambda: V.reciprocal(out=rstd[:, 0:1], in_=rstd[:, 0:1]), r=[kpre + "rstd"], w=[kpre + "rstd"])

        def norm_transpose(xt, kx, gain, l, hb, pst2, dst_fn, kdst, ssq, rstd, junk, kpre):
            rms_rows(xt, kx, ssq, rstd, junk, kpre)
            S.op("dve", lambda: V.tensor_scalar(out=hb[:], in0=xt, scalar1=rstd[:, 0:1], scalar2=None, op0=ALU.mult),
                 r=[kx, kpre + "rstd"], w=[kpre + "hb"])
            for half in range(2):
                pt = pst2[half]
                for c4 in range(4):
                    c = half * 4 + c4
                    S.op("pe", lambda: T.transpose(out=pt[:, c4, :], in_=hb[:, c * 128:(c + 1) * 128], identity=ident[:]),
                         r=[kpre + "hb"], w=[kpre + f"pst{half}"])
                for c4 in range(4):
                    c = half * 4 + c4
                    if c4 % 2 == 0:
                        S.op("dve", lambda: V.tensor_scalar(out=dst_fn(c), in0=pt[:, c4, :], scalar1=gain[:, l * 8 + c:l * 8 + c + 1],
                                                            scalar2=None, op0=ALU.mult),
                             r=[kpre + f"pst{half}"], w=[kdst])
                    else:
                        S.op("act", lambda: A.activation(out=dst_fn(c), in_=pt[:, c4, :], func=AF.Copy,
                                                         scale=gain[:, l * 8 + c:l * 8 + c + 1]),
                             r=[kpre + f"pst{half}"], w=[kdst])

        def load_w_bf16(dst, src, key):
            S.dma("pool", dst, src, w=[key])

        for l in range(depth):
            xsrc = x_in if l == 0 else xres
            lam_init = 0.8 - 0.6 * math.exp(-0.3 * l)

            for ph in ([ExitStack()] if 'p1' in PH else []):
                hT = sb(ph, "hT", [128, 8, SL], BF16)
                xb2 = [sb(ph, f"p1x{i}", [128, D], F32) for i in range(2)]
                hb = sb(ph, "p1hb", [128, D], BF16)
                junk = sb(ph, "p1junk", [128, D], BF16)
                ssq = sb(ph, "p1ssq", [128, 8], F32)
                rstd = sb(ph, "p1rstd", [128, 8], F32)
                pst2 = [ps(ph, f"p1pst{i}", [128, 4, 128], BF16) for i in range(2)]
                psm = [ps(ph, f"p1ps{i}", [128, 512], F32) for i in range(3)]
                wblk = [sb(ph, f"p1w{i}", [128, 8, 512], BF16) for i in range(2)]
                sq_2 = [sb(ph, f"p1sq{i}", [128, 512], F32) for i in range(2)]
                xn_2 = [sb(ph, f"p1xn{i}", [128, 512], F32) for i in range(2)]
                xg_2 = [sb(ph, f"p1xg{i}", [128, 512], F32) for i in range(2)]
                ra_2 = [sb(ph, f"p1ra{i}", [128, 256], F32) for i in range(2)]
                rb_2 = [sb(ph, f"p1rb{i}", [128, 256], F32) for i in range(2)]
                ob_2 = [sb(ph, f"p1ob{i}", [128, 512], BF16) for i in range(2)]
                ssq_2 = [sb(ph, f"p1ssq{i}", [128, 8], F32) for i in range(2)]
                rstd_2 = [sb(ph, f"p1rstd{i}", [128, 8], F32) for i in range(2)]
                stg = [sb(ph, f"p1stg{i}", [128, 4, 512], BF16) for i in range(2)]
                vst = [sb(ph, f"p1vst{i}", [128, 520], BF16) for i in range(4)]
                gst = [sb(ph, f"p1gst{i}", [128, 512], BF16) for i in range(3)]

                for t in range(NT):
                    xt = xb2[t % 2]
                    kx = f"p1x{t % 2}"
                    S.dma("sp", xt[:], xsrc[t * 128:(t + 1) * 128, :], r=[f"xr{t}"], w=[kx])
                    norm_transpose(xt[:], kx, gA, l, hb, pst2,
                                   lambda c, t=t: hT[:, c, t * 128:(t + 1) * 128], f"hT{t}", ssq, rstd, junk, "p1")
                for i in range(4):
                    S.op("pool", lambda: G.memset(vst[i][:], 1.0), w=[f"vst{i}"])

                blocks = []
                names = ["aq", "ak", "av", "bq", "bk", "bv", "cq", "ck", "cv"]
                for i, nm in enumerate(names):
                    blocks.append((nm, i * 512, 512))
                blocks.append(("idx", 4608, 324))
                for i in range(6):
                    blocks.append((f"gl{i}", 4932 + i * 512, 512))
                w_l = w_in[l].rearrange("(c p) n -> p c n", p=128)

                def qk_post(pv, nh, gain, do_norm, do_rope, t, outv, scale, kout):
                    n = nh * 64
                    pz = t % 2
                    sq, xn, xg, ra, rb = sq_2[pz], xn_2[pz], xg_2[pz], ra_2[pz], rb_2[pz]
                    ssq, rstd = ssq_2[pz], rstd_2[pz]
                    ksq, kxn, kxg, kra, krb, kss, krs = (f"sq{pz}", f"xn{pz}", f"xg{pz}", f"ra{pz}", f"rb{pz}", f"qssq{pz}", f"qrstd{pz}")
                    if do_norm:
                        sqv = sq[:, 0:n].rearrange("p (h d) -> p h d", d=64)
                        S.op("act", lambda: A.activation(out=sqv, in_=pv, func=AF.Square), r=["psm"], w=[ksq])
                        S.op("dve", lambda: V.tensor_reduce(out=ssq[:, 0:nh], in_=sqv, axis=AX.X, op=ALU.add), r=[ksq], w=[kss])
                        S.op("act", lambda: A.activation(out=rstd[:, 0:nh], in_=ssq[:, 0:nh], func=AF.Sqrt, bias=EPS, scale=1.0 / 64),
                             r=[kss], w=[krs])
                        S.op("dve", lambda: V.reciprocal(out=rstd[:, 0:nh], in_=rstd[:, 0:nh]), r=[krs], w=[krs])
                        xnv = xn[:, 0:n].rearrange("p (h d) -> p h d", d=64)
                        S.op("dve", lambda: V.tensor_tensor(out=xnv, in0=pv, in1=rstd[:, 0:nh].unsqueeze(2).to_broadcast([128, nh, 64]), op=ALU.mult),
                             r=["psm", krs], w=[kxn])
                        xgv = xg[:, 0:n].rearrange("p (h d) -> p h d", d=64)
                        S.op("pool", lambda: G.tensor_tensor(out=xgv, in0=xnv, in1=gain.unsqueeze(1).to_broadcast([128, nh, 64]), op=ALU.mult),
                             r=[kxn, "const"], w=[kxg])
                    else:
                        xgv = xg[:, 0:n].rearrange("p (h d) -> p h d", d=64)
                        S.op("act", lambda: A.activation(out=xgv, in_=pv, func=AF.Copy, scale=float(scale)), r=["psm"], w=[kxg])
                    if do_rope:
                        x1 = xgv[:, :, 0:32]
                        x2 = xgv[:, :, 32:64]
                        cb = cos_t[:, t, :].unsqueeze(1).to_broadcast([128, nh, 32])
                        sbb = sin_t[:, t, :].unsqueeze(1).to_broadcast([128, nh, 32])
                        rav = ra[:, 0:nh * 32].rearrange("p (h d) -> p h d", d=32)
                        rbv = rb[:, 0:nh * 32].rearrange("p (h d) -> p h d", d=32)
                        S.op("pool", lambda: G.tensor_tensor(out=rav, in0=x1, in1=cb, op=ALU.mult), r=[kxg], w=[kra])
                        S.op("dve", lambda: V.tensor_tensor(out=rbv, in0=x2, in1=sbb, op=ALU.mult), r=[kxg], w=[krb])
                        S.op("dve", lambda: V.tensor_tensor(out=outv[:, :, 0:32], in0=rav, in1=rbv, op=ALU.subtract), r=[kra, krb], w=[kout])
                        S.op("pool", lambda: G.tensor_tensor(out=rav, in0=x2, in1=cb, op=ALU.mult), r=[kxg, kra], w=[kra])
                        S.op("dve", lambda: V.tensor_tensor(out=rbv, in0=x1, in1=sbb, op=ALU.mult), r=[kxg, krb], w=[krb])
                        S.op("dve", lambda: V.tensor_tensor(out=outv[:, :, 32:64], in0=rav, in1=rbv, op=ALU.add), r=[kra, krb], w=[kout])
                    else:
                        S.op("dve", lambda: V.tensor_copy(out=outv, in_=xgv), r=[kxg], w=[kout])

                nblk = len(blocks)
                S.dma("pool", wblk[0][:, :, 0:blocks[0][2]], w_l[:, :, blocks[0][1]:blocks[0][1] + blocks[0][2]], w=["wblk0"])
                items = []

                def prefetch_w(bi):
                    if bi + 1 < nblk:
                        nn = blocks[bi + 1]
                        S.dma("pool", wblk[(bi + 1) % 2][:, :, 0:nn[2]], w_l[:, :, nn[1]:nn[1] + nn[2]], w=[f"wblk{(bi + 1) % 2}"])

                for bi, (nm, c0, n) in enumerate(blocks):
                    wb = wblk[bi % 2]
                    kw = f"wblk{bi % 2}"
                    if nm.startswith("gl"):
                        gi = int(nm[2:])
                        for Tq in range(NQ):
                            for j in range(4):
                                pb_ = len(items) % 3

                                def g0(bi=bi, wb=wb, kw=kw, Tq=Tq, j=j, pb_=pb_, first=(Tq == 0 and j == 0)):
                                    if first:
                                        prefetch_w(bi)
                                    pm = psm[pb_]
                                    for c in range(8):
                                        S.op("pe", lambda: T.matmul(pm[:], lhsT=wb[:, c, j * 128:(j + 1) * 128], rhs=hT[:, c, Tq * 512:(Tq + 1) * 512],
                                                                    start=(c == 0), stop=(c == 7)), r=[kw] + [f"hT{Tq * 4 + i_}" for i_ in range(4)], w=[f"psm{pb_}"])

                                def g1(gi=gi, Tq=Tq, j=j, pb_=pb_):
                                    pm = psm[pb_]
                                    gs = gst[pb_]
                                    S.op("act", lambda: A.activation(out=gs[:], in_=pm[:], func=AF.Sigmoid), r=[f"psm{pb_}"], w=[f"gst{pb_}"])
                                    S.dma("act", GT[gi * 4 + j, :, Tq * 512:(Tq + 1) * 512], gs[:], r=[f"gst{pb_}"], w=["GT"])
                                items.append([g0, None, g1, None, None])
                        continue
                    for t in range(NT):
                        pb_ = len(items) % 3

                        def s0(bi=bi, wb=wb, kw=kw, t=t, n=n, pb_=pb_):
                            if t == 0:
                                prefetch_w(bi)
                            pm = psm[pb_]
                            for c in range(8):
                                S.op("pe", lambda: T.matmul(pm[:, 0:n], lhsT=hT[:, c, t * 128:(t + 1) * 128], rhs=wb[:, c, 0:n],
                                                            start=(c == 0), stop=(c == 7)), r=[kw, f"hT{t}"], w=[f"psm{pb_}"])

                        def s1(nm=nm, t=t, pb_=pb_):
                            pm = psm[pb_]
                            kp = f"psm{pb_}"
                            S.lastw["psm"] = S.lastw[kp]
                            S.reads["psm"] = S.reads[kp]
                            ob = ob_2[t % 2]
                            kob = f"ob{t % 2}"
                            if nm in ("aq", "ak", "cq", "ck", "bq", "bk"):
                                pv = pm[:, 0:512].rearrange("p (h d) -> p h d", d=64)
                                obv = ob[:].rearrange("p (h d) -> p h d", d=64)
                                if nm[0] == "b":
                                    qk_post(pv, 8, None, False, False, t, obv, 0.125 if nm == "bq" else 1.0, kob)
                                else:
                                    gain = {"aq": aqn, "ak": akn, "cq": cqn, "ck": ckn}[nm][:, l * 64:(l + 1) * 64]
                                    qk_post(pv, 8, gain, True, True, t, obv, 1.0, kob)
                            elif nm in ("av", "bv", "cv"):
                                nh, dv = (4, 128) if nm == "av" else (8, 64)
                                vi = (t % 2) + (0 if nm == "av" else 2)
                                vs = vst[vi]
                                vv = vs[:, 0:nh * (dv + 1)].rearrange("p (h e) -> p h e", e=dv + 1)[:, :, 0:dv]
                                S.op("act", lambda: A.activation(out=vv, in_=pm[:, 0:512].rearrange("p (h e) -> p h e", e=dv), func=AF.Copy),
                                     r=[kp], w=[f"vst{vi}"])
                                S.dma("act", VX[nm[0]][t * 128:(t + 1) * 128, :], vs[:, 0:nh * (dv + 1)], r=[f"vst{vi}"], w=["VX"])
                            else:
                                obv = ob[:, 0:256].rearrange("p (h d) -> p h d", d=64)
                                qk_post(pm[:, 0:256].rearrange("p (h d) -> p h d", d=64), 4, None, False, True, t, obv, 1.0, kob)
                                obk = ob[:, 256:320].rearrange("p (h d) -> p h d", d=64)
                                qk_post(pm[:, 256:320].rearrange("p (h d) -> p h d", d=64), 1, ikn[:, l * 64:(l + 1) * 64], True, True, t, obk, 1.0, kob)
                                S.op("dve", lambda: V.tensor_copy(out=ob[:, 320:384], in_=ob[:, 256:320]), r=[kob], w=[kob])
                                S.op("act", lambda: A.activation(out=iw_all[:, t, :], in_=pm[:, 320:324], func=AF.Copy), r=[kp], w=["iw_all"])

                        def s2(nm=nm, t=t):
                            if nm in ("av", "bv", "cv"):
                                return
                            ob = ob_2[t % 2]
                            kob = f"ob{t % 2}"
                            sg = stg[(t // 4) % 2]
                            ksg = f"stg{(t // 4) % 2}"
                            tt = t % 4
                            pt = pst2[t % 2]
                            nj_ = 3 if nm == "idx" else 4
                            for j in range(nj_):
                                S.op("pe", lambda: T.transpose(out=pt[:, j, :], in_=ob[:, j * 128:(j + 1) * 128], identity=ident[:]),
                                     r=[kob], w=[f"p1pst{t % 2}"])
                            S.op("act", lambda: A.activation(out=sg[:, 0:nj_, tt * 128:(tt + 1) * 128], in_=pt[:, 0:nj_, :], func=AF.Copy),
                                 r=[f"p1pst{t % 2}"], w=[ksg])
                            if tt == 3:
                                t0_ = (t - 3) * 128
                                if nm == "idx":
                                    S.dma("act", IQT[:, :, t0_:t0_ + 512].rearrange("j p t -> p j t"), sg[:, 0:2, :], r=[ksg], w=["QKT"])
                                    S.dma("act", IKT[:, t0_:t0_ + 512], sg[:, 2, :], r=[ksg], w=["QKT"])
                                else:
                                    dst = (QT if nm[1] == "q" else KT)[nm[0]]
                                    S.dma("act", dst[:, :, t0_:t0_ + 512].rearrange("j p t -> p j t"), sg[:], r=[ksg], w=["QKT"])
                        items.append([s0, None, s1, None, s2])
                run_pipeline(items, 5)
                S.barrier()
                ph.close()

            for ph in ([ExitStack()] if 'pa' in PH else []):
                kt = sb(ph, "a_kt", [128, 4, SL], BF16)
                vx = sb(ph, "a_vx", [128, NT, 4 * 129], BF16)
                qt2 = [sb(ph, f"a_qt{i}", [128, 4, 512], BF16) for i in range(2)]
                pT2 = [sb(ph, f"a_pT{i}", [128, 512], BF16) for i in range(4)]
                a_qm = [sb(ph, f"a_qm{i}", [128, 512], BF16) for i in range(2)]
                o2 = [sb(ph, f"a_o{i}", [128, 4, 129], F32) for i in range(2)]
                rl = sb(ph, "a_rl", [128, 8], F32)
                y0 = sb(ph, "a_y0", [128, 4, 128], F32)
                y1 = sb(ph, "a_y1", [128, 4, 128], F32)
                ss = sb(ph, "a_ss", [128, 8], F32)
                yb2 = [sb(ph, f"a_yb{i}", [128, 4, 128], BF16) for i in range(2)]
                yts = [sb(ph, f"a_yts{i}", [128, 512], BF16) for i in range(2)]
                pss = [ps(ph, f"a_pss{i}", [128, 512], F32) for i in range(3)]
                po2 = [ps(ph, f"a_po{i}", [128, 2, 512], F32) for i in range(2)]
                ptr = ps(ph, "a_ptr", [128, 4, 128], BF16)
                S.dma("sp", kt[:], KT["a"].rearrange("j p t -> p j t"), r=["QKT"], w=["a_kt"])
                for v4 in range(4):
                    S.dma("sp", vx[:, v4 * 8:(v4 + 1) * 8, :], VX["a"][v4 * 1024:(v4 + 1) * 1024, :].rearrange("(t p) e -> p t e", p=128), r=["VX"], w=["a_vx"])

                def a_load_q(Q):
                    S.dma("sp", qt2[Q % 2][:], QT["a"][:, :, Q * 512:(Q + 1) * 512].rearrange("j p t -> p j t"), r=["QKT"], w=[f"a_qt{Q % 2}"])

                a_load_q(0)
                items = []
                deferred = []

                def a_epilogue(Q, h, c, po, kpo):
                    for bk in range(2):
                        S.op("act", lambda: A.activation(out=o2[c][:, 2 * bk:2 * bk + 2, :], in_=po[:, bk, 0:258].rearrange("p (r e) -> p r e", e=129), func=AF.Copy),
                             r=[kpo], w=[f"a_o{c}"])
                    if c == 0:
                        return
                    yb = yb2[h % 2]
                    kyb = f"a_yb{h % 2}"
                    S.op("dve", lambda: V.reciprocal(out=rl[:, 0:4], in_=o2[0][:, :, 128]), r=["a_o0"], w=["a_rl"])
                    S.op("dve", lambda: V.reciprocal(out=rl[:, 4:8], in_=o2[1][:, :, 128]), r=["a_o1"], w=["a_rl"])
                    S.op("dve", lambda: V.tensor_scalar(out=rl[:, 4:8], in0=rl[:, 4:8], scalar1=nlam_t[:, l:l + 1], scalar2=None, op0=ALU.mult),
                         r=["a_rl", "const"], w=["a_rl"])
                    S.op("dve", lambda: V.tensor_tensor(out=y0[:], in0=o2[0][:, :, 0:128], in1=rl[:, 0:4].unsqueeze(2).to_broadcast([128, 4, 128]), op=ALU.mult),
                         r=["a_o0", "a_rl"], w=["a_y0"])
                    S.op("pool", lambda: G.tensor_tensor(out=y1[:], in0=o2[1][:, :, 0:128], in1=rl[:, 4:8].unsqueeze(2).to_broadcast([128, 4, 128]), op=ALU.mult),
                         r=["a_o1", "a_rl"], w=["a_y1"])
                    S.op("dve", lambda: V.tensor_tensor(out=y0[:], in0=y0[:], in1=y1[:], op=ALU.add), r=["a_y0", "a_y1"], w=["a_y0"])
                    S.op("pool", lambda: G.tensor_tensor(out=y1[:], in0=y0[:], in1=y0[:], op=ALU.mult), r=["a_y0"], w=["a_y1"])
                    S.op("dve", lambda: V.tensor_reduce(out=ss[:, 0:4], in_=y1[:], axis=AX.X, op=ALU.add), r=["a_y1"], w=["a_ss"])
                    S.op("act", lambda: A.activation(out=ss[:, 0:4], in_=ss[:, 0:4], func=AF.Sqrt, bias=EPS, scale=1.0 / 128), r=["a_ss"], w=["a_ss"])
                    S.op("dve", lambda: V.reciprocal(out=ss[:, 0:4], in_=ss[:, 0:4]), r=["a_ss"], w=["a_ss"])
                    S.op("dve", lambda: V.tensor_tensor(out=y0[:], in0=y0[:], in1=ss[:, 0:4].unsqueeze(2).to_broadcast([128, 4, 128]), op=ALU.mult),
                         r=["a_y0", "a_ss"], w=["a_y0"])
                    S.op("dve", lambda: V.tensor_tensor(out=yb[:], in0=y0[:], in1=subg[:, l * 128:(l + 1) * 128].unsqueeze(1).to_broadcast([128, 4, 128]), op=ALU.mult),
                         r=["a_y0", "const"], w=[kyb])

                    def fin():
                        for qq in range(4):
                            S.op("pe", lambda: T.transpose(out=ptr[:, qq, :], in_=yb[:, qq, :], identity=ident[:]), r=[kyb], w=["a_ptr"])
                        ys = yts[h % 2]
                        S.op("act", lambda: A.activation(out=ys[:], in_=ptr[:].rearrange("p a b -> p (a b)"), func=AF.Copy), r=["a_ptr"], w=[f"a_yts{h % 2}"])
                        S.dma("act", YT[h, :, Q * 512:(Q + 1) * 512], ys[:], r=[f"a_yts{h % 2}"], w=["YT"])
                    deferred.append([4, fin])

                def run_deferred(force=False):
                    keep = []
                    for dly in deferred:
                        dly[0] -= 1
                        if dly[0] <= 0 or force:
                            dly[1]()
                        else:
                            keep.append(dly)
                    deferred[:] = keep

                gidx = 0
                for Q in range(NQ):
                    for h in range(4):
                        for c in range(2):
                            g = gidx
                            gidx += 1
                            nj = 4 * Q + 4
                            for j in range(nj):
                                b = len(items) % 3
                                bp = len(items) % 4
                                jj = j - 4 * Q
                                c0 = max(0, jj) * 128

                                def s0(Q=Q, h=h, c=c, j=j, c0=c0, b=b, g=g, pre=(h == 0 and c == 0 and j == 0)):
                                    if pre and Q + 1 < NQ:
                                        a_load_q(Q + 1)
                                    qt = qt2[Q % 2]
                                    qm = a_qm[g % 2]
                                    kqm = f"a_qm{g % 2}"
                                    if j == 0:
                                        S.op("pool", lambda: G.memset(qm[:], 0.0), w=[kqm])
                                        S.op("pool", lambda: G.tensor_copy(out=qm[c * 64:(c + 1) * 64, :], in_=qt[c * 64:(c + 1) * 64, h, :]),
                                             r=[f"a_qt{Q % 2}", kqm], w=[kqm])
                                    S.op("pe", lambda: T.matmul(pss[b][:, c0:512], lhsT=kt[:, h, j * 128:(j + 1) * 128],
                                                                rhs=qm[:, c0:512], start=True, stop=True),
                                         r=["a_kt", kqm], w=[f"a_pss{b}"])

                                def s1(c0=c0, b=b, bp=bp, jj=jj):
                                    S.op("act", lambda: A.activation(out=pT2[bp][:, c0:512], in_=pss[b][:, c0:512], func=AF.Exp),
                                         r=[f"a_pss{b}"], w=[f"a_pT{bp}"])
                                    if jj >= 0:
                                        S.op("pool", lambda: G.memset(pT2[bp][64:128, c0:c0 + 64], 0.0), w=[f"a_pT{bp}"])

                                def s2(Q=Q, h=h, c=c, j=j, jj=jj, b=bp, g=g, nj=nj):
                                    po = po2[g % 2]
                                    kpo = f"a_po{g % 2}"
                                    for qq in range(max(0, jj), 4):
                                        off = (qq % 2) * 129
                                        S.op("pe", lambda: T.matmul(po[:, qq // 2, off:off + 129], lhsT=pT2[b][:, qq * 128:(qq + 1) * 128],
                                                                    rhs=vx[:, j, h * 129:(h + 1) * 129], start=(j == 0 and qq % 2 == 0), stop=(j == 4 * Q + qq),
                                                                    skip_group_check=True),
                                             r=[f"a_pT{b}", "a_vx"], w=[kpo])
                                    if j == nj - 1:
                                        a_epilogue(Q, h, c, po, kpo)
                                    run_deferred()
                                items.append([s0, None, s1, None, None, s2])
                run_pipeline(items, 6)
                run_deferred(force=True)
                S.barrier()
                ph.close()

            for ph in ([ExitStack()] if 'pb' in PH else []):
                kt = sb(ph, "b_kt", [128, 4, SL], BF16)
                vx = sb(ph, "b_vx", [128, NT, 8 * 65], BF16)
                qt2 = [sb(ph, f"b_qt{i}", [128, 4, 512], BF16) for i in range(2)]
                e2 = [sb(ph, f"b_e{i}", [128, 512], F32) for i in range(3)]
                nl2 = [sb(ph, f"b_nl{i}", [128, 512], BF16) for i in range(3)]
                aT2 = [sb(ph, f"b_aT{i}", [128, 512], BF16) for i in range(3)]
                R = sb(ph, "b_R", [128, 512], BF16)
                qm2 = [sb(ph, f"b_qm{i}", [128, 512], BF16) for i in range(2)]
                yq2 = [sb(ph, f"b_yq{i}", [128, 4, 512], BF16) for i in range(2)]
                yts = [sb(ph, f"b_yts{i}", [128, 512], BF16) for i in range(2)]
                psz = [ps(ph, f"b_psz{i}", [128, 512], F32) for i in range(3)]
                psc = [ps(ph, f"b_psc{i}", [128, 512], F32) for i in range(2)]
                po2 = [ps(ph, f"b_po{i}", [128, 512], F32) for i in range(2)]
                ptr = ps(ph, "b_ptr", [128, 4, 128], BF16)
                S.dma("sp", kt[:], KT["b"].rearrange("j p t -> p j t"), r=["QKT"], w=["b_kt"])
                for v4 in range(4):
                    S.dma("sp", vx[:, v4 * 8:(v4 + 1) * 8, :], VX["b"][v4 * 1024:(v4 + 1) * 1024, :].rearrange("(t p) e -> p t e", p=128), r=["VX"], w=["b_vx"])

                def b_load_q(Q):
                    S.dma("sp", qt2[Q % 2][:], QT["b"][:, :, Q * 512:(Q + 1) * 512].rearrange("j p t -> p j t"), r=["QKT"], w=[f"b_qt{Q % 2}"])

                b_load_q(0)
                items = []
                gidx = 0
                for Q in range(NQ):
                    for h in range(8):
                        g = gidx
                        gidx += 1
                        pr, bs = h // 2, (h % 2) * 64
                        jtop = 4 * Q + 3
                        gstate = {"first": True}
                        for j in range(jtop, -1, -1):
                            b = len(items) % 3
                            bc = len(items) % 2
                            jj = j - 4 * Q
                            c0 = max(0, jj) * 128

                            def s0(Q=Q, h=h, j=j, c0=c0, b=b, pr=pr, bs=bs, pre=(h == 0 and j == jtop)):
                                if pre and Q + 1 < NQ:
                                    b_load_q(Q + 1)
                                qt = qt2[Q % 2]
                                S.op("pe", lambda: T.matmul(psz[b][:, c0:512], lhsT=kt[bs:bs + 64, pr, j * 128:(j + 1) * 128], rhs=qt[bs:bs + 64, pr, c0:512],
                                                            start=True, stop=True), r=["b_kt", f"b_qt{Q % 2}"], w=[f"b_psz{b}"])

                            def s1(c0=c0, b=b, jj=jj):
                                S.op("act", lambda: A.activation(out=e2[b][:, c0:512], in_=psz[b][:, c0:512], func=AF.Exp),
                                     r=[f"b_psz{b}"], w=[f"b_e{b}"])
                                S.op("act", lambda: A.activation(out=nl2[b][:, c0:512], in_=e2[b][:, c0:512], func=AF.Ln, bias=1.0),
                                     r=[f"b_e{b}"], w=[f"b_nl{b}"])
                                if jj >= 0:
                                    S.op("dve", lambda: V.tensor_tensor(out=nl2[b][:, c0:c0 + 128], in0=nl2[b][:, c0:c0 + 128], in1=strict[:], op=ALU.mult),
                                         r=[f"b_nl{b}", "const"], w=[f"b_nl{b}"])

                            def s2(Q=Q, j=j, c0=c0, b=b, bc=bc, pr=pr, bs=bs, g=g, jtop=jtop):
                                qm = qm2[g % 2]
                                kqm = f"b_qm{g % 2}"
                                qt = qt2[Q % 2]
                                if j == jtop:
                                    S.op("pool", lambda: G.memset(R[:], 0.0), w=["b_R"])
                                    S.op("pool", lambda: G.memset(qm[:], 0.0), w=[kqm])
                                    S.op("dve", lambda: V.tensor_copy(out=qm[bs:bs + 64, :], in_=qt[bs:bs + 64, pr, :]), r=[f"b_qt{Q % 2}", kqm], w=[kqm])
                                S.op("pe", lambda: T.matmul(psc[bc][:, c0:512], lhsT=kt[:, pr, j * 128:(j + 1) * 128], rhs=qm[:, c0:512], start=True, stop=False),
                                     r=["b_kt", kqm], w=[f"b_psc{bc}"])
                                last = (j == jtop)
                                S.op("pe", lambda: T.matmul(psc[bc][:, c0:512], lhsT=ntri[:], rhs=nl2[b][:, c0:512], start=False, stop=last),
                                     r=[f"b_nl{b}", "const"], w=[f"b_psc{bc}"])
                                if not last:
                                    S.op("pe", lambda: T.matmul(psc[bc][:, c0:512], lhsT=nones[:], rhs=R[:, c0:512], start=False, stop=True),
                                         r=["b_R", "const"], w=[f"b_psc{bc}"])
                                if j > 0:
                                    S.op("dve", lambda: V.tensor_tensor(out=R[:, c0:512], in0=R[:, c0:512], in1=nl2[b][:, c0:512], op=ALU.add),
                                         r=["b_R", f"b_nl{b}"], w=["b_R"])

                            def s3(c0=c0, b=b, bc=bc, jj=jj):
                                S.op("act", lambda: A.activation(out=aT2[b][:, c0:512], in_=psc[bc][:, c0:512], func=AF.Exp),
                                     r=[f"b_psc{bc}"], w=[f"b_aT{b}"])
                                if jj >= 0:
                                    S.op("dve", lambda: V.tensor_tensor(out=aT2[b][:, c0:c0 + 128], in0=aT2[b][:, c0:c0 + 128], in1=strict[:], op=ALU.mult),
                                         r=[f"b_aT{b}", "const"], w=[f"b_aT{b}"])

                            def s4(Q=Q, h=h, j=j, jj=jj, b=b, g=g, gstate=gstate):
                                po = po2[g % 2]
                                kpo = f"b_po{g % 2}"
                                yq = yq2[Q % 2]
                                kyq = f"b_yq{Q % 2}"
                                for qq in range(max(0, jj), 4):
                                    S.op("pe", lambda: T.matmul(po[:, qq * 65:(qq + 1) * 65], lhsT=aT2[b][:, qq * 128:(qq + 1) * 128],
                                                                rhs=vx[:, j, h * 65:(h + 1) * 65], start=gstate["first"], stop=(j == 0),
                                                                skip_group_check=True),
                                         r=[f"b_aT{b}", "b_vx"], w=[kpo])
                                    gstate["first"] = False
                                if j == 0:
                                    S.op("act", lambda: A.activation(out=yq[:, :, h * 64:(h + 1) * 64],
                                                                     in_=po[:, 0:260].rearrange("p (a e) -> p a e", e=65)[:, :, 0:64], func=AF.Copy),
                                         r=[kpo], w=[kyq])
                                    if h == 7:
                                        for cch in range(4):
                                            for qq in range(4):
                                                S.op("pe", lambda: T.transpose(out=ptr[:, qq, :], in_=yq[:, qq, cch * 128:(cch + 1) * 128], identity=ident[:]),
                                                     r=[kyq], w=["b_ptr"])
                                            ys = yts[cch % 2]
                                            S.op("act", lambda: A.activation(out=ys[:], in_=ptr[:].rearrange("p a b -> p (a b)"), func=AF.Copy), r=["b_ptr"], w=[f"b_yts{cch % 2}"])
                                            S.dma("act", YT[4 + cch, :, Q * 512:(Q + 1) * 512], ys[:], r=[f"b_yts{cch % 2}"], w=["YT"])
                            items.append([s0, None, s1, None, s2, s3, None, s4])
                run_pipeline(items, 8)
                S.barrier()
                ph.close()

            for ph in ([ExitStack()] if 'pc' in PH else []):
                kt = sb(ph, "c_kt", [128, 4, SL], BF16)
                vx = sb(ph, "c_vx", [128, NT, 8 * 65], BF16)
                ikt = sb(ph, "c_ikt", [128, SL], BF16)
                qt2 = [sb(ph, f"c_qt{i}", [128, 4, 512], BF16) for i in range(2)]
                iqt2 = [sb(ph, f"c_iqt{i}", [128, 2, 512], BF16) for i in range(2)]
                score2 = [sb(ph, f"c_score{i}", [128, SL], F32) for i in range(2)]
                maskq2 = [sb(ph, f"c_maskq{i}", [128, SL], BF16) for i in range(2)]
                bst = [sb(ph, f"c_bst{i}", [128, 8], F32) for i in range(2)]
                dft = [sb(ph, f"c_dft{i}", [128, NSTEP + 1], F32) for i in range(2)]
                maskT = sb(ph, "c_maskT", [128, NT, 512], BF16)
                rb2 = [sb(ph, f"c_r{i}", [128, 512], F32) for i in range(2)]
                pT2 = [sb(ph, f"c_pT{i}", [128, 512], BF16) for i in range(3)]
                bs_ = sb(ph, "c_bis", [128, 8], F32)
                osb2 = [sb(ph, f"c_o{i}", [128, 4, 65], F32) for i in range(2)]
                rl2 = [sb(ph, f"c_rl{i}", [128, 4], F32) for i in range(2)]
                yq2 = [sb(ph, f"c_yq{i}", [128, 4, 512], BF16) for i in range(2)]
                yts = [sb(ph, f"c_yts{i}", [128, 512], BF16) for i in range(2)]
                psi = [ps(ph, f"c_psi{i}", [128, 512], F32) for i in range(2)]
                pss = [ps(ph, f"c_pss{i}", [128, 512], F32) for i in range(3)]
                po2 = [ps(ph, f"c_po{i}", [128, 512], F32) for i in range(2)]
                ptr = ps(ph, "c_ptr", [128, 4, 128], BF16)
                S.dma("sp", kt[:], KT["c"].rearrange("j p t -> p j t"), r=["QKT"], w=["c_kt"])
                for v4 in range(4):
                    S.dma("sp", vx[:, v4 * 8:(v4 + 1) * 8, :], VX["c"][v4 * 1024:(v4 + 1) * 1024, :].rearrange("(t p) e -> p t e", p=128), r=["VX"], w=["c_vx"])
                S.dma("sp", ikt[:], IKT[:, :], r=["QKT"], w=["c_ikt"])

                def c_load_q(Q):
                    kq_ = f"c_qt{Q % 2}"
                    S.dma("sp", qt2[Q % 2][:], QT["c"][:, :, Q * 512:(Q + 1) * 512].rearrange("j p t -> p j t"), r=["QKT"], w=[kq_])
                    S.dma("sp", iqt2[Q % 2][:], IQT[:, :, Q * 512:(Q + 1) * 512].rearrange("j p t -> p j t"), r=["QKT"], w=[kq_])

                c_load_q(0)
                ir = 0
                gidx = 0
                for Q in range(NQ):
                    qt = qt2[Q % 2]
                    iqt = iqt2[Q % 2]
                    kq = f"c_qt{Q % 2}"
                    if Q + 1 < NQ:
                        c_load_q(Q + 1)
                    for pair in range(2):
                        info = []
                        for ci in range(2):
                            qq = 2 * pair + ci
                            qb = 4 * Q + qq
                            nadm = (qb + 1) * 128
                            info.append((qq, qb, nadm))
                            sc_t = score2[ci]
                            ksc = f"c_score{ci}"
                            nkt = (nadm + 511) // 512
                            for k5 in range(nkt):
                                wd = min(512, nadm - k5 * 512)
                                for h in range(4):
                                    b = ir % 2
                                    ir += 1
                                    hb_ = (h % 2) * 64
                                    S.op("pe", lambda: T.matmul(psi[b][:, 0:wd], lhsT=iqt[hb_:hb_ + 64, h // 2, qq * 128:(qq + 1) * 128],
                                                                rhs=ikt[hb_:hb_ + 64, k5 * 512:k5 * 512 + wd], start=True, stop=True),
                                         r=[kq, "c_ikt"], w=[f"c_psi{b}"])
                                    S.op("act", lambda: A.activation(out=rb2[b][:, 0:wd], in_=psi[b][:, 0:wd], func=AF.Relu),
                                         r=[f"c_psi{b}"], w=[f"c_r{b}"])
                                    sc = sc_t[:, k5 * 512:k5 * 512 + wd]
                                    if h == 0:
                                        S.op("dve", lambda: V.tensor_scalar(out=sc, in0=rb2[b][:, 0:wd], scalar1=iw_all[:, qb, 0:1], scalar2=None, op0=ALU.mult),
                                             r=[f"c_r{b}", "iw_all"], w=[ksc])
                                    else:
                                        S.op("dve", lambda: V.scalar_tensor_tensor(out=sc, in0=rb2[b][:, 0:wd], scalar=iw_all[:, qb, h:h + 1], in1=sc,
                                                                                   op0=ALU.mult, op1=ALU.add),
                                             r=[f"c_r{b}", "iw_all", ksc], w=[ksc])
                            S.op("dve", lambda: V.memset(sc_t[0:64, qb * 128 + 64:(qb + 1) * 128], NEG), r=[ksc], w=[ksc])
                        bisect = info[0][1] >= 2
                        for ci in range(2):
                            qq, qb, nadm = info[ci]
                            sc_t = score2[ci]
                            ksc = f"c_score{ci}"
                            stt = bst[ci]
                            df = dft[ci]
                            KB = [f"c_bis{ci}"]
                            if not bisect:
                                S.op("dve", lambda: V.memset(stt[:, 5:6], -1.0e29), r=KB, w=KB)
                                continue
                            nfull = qb * 128 + 64
                            S.op("dve", lambda: V.tensor_reduce(out=stt[:, 4:5], in_=sc_t[:, 0:nadm], axis=AX.X, op=ALU.max), r=[ksc], w=KB)
                            S.op("dve", lambda: V.tensor_reduce(out=stt[:, 5:6], in_=sc_t[:, 0:nfull], axis=AX.X, op=ALU.min), r=[ksc], w=KB)
                            S.op("dve", lambda: V.tensor_scalar(out=stt[:, 5:6], in0=stt[:, 5:6], scalar1=-1.0, scalar2=None, op0=ALU.add), r=KB, w=KB)
                            S.op("dve", lambda: V.tensor_tensor(out=stt[:, 1:2], in0=stt[:, 4:5], in1=stt[:, 5:6], op=ALU.subtract), r=KB, w=KB)
                            S.op("dve", lambda: V.tensor_scalar(out=df[:], in0=pw2[:], scalar1=stt[:, 1:2], scalar2=None, op0=ALU.mult), r=KB + ["const"], w=KB)
                            if ci == 0:
                                S.op("dve", lambda: V.tensor_tensor(out=stt[:, 0:1], in0=stt[:, 5:6], in1=df[:, 0:1], op=ALU.add), r=KB, w=KB)
                            else:
                                S.op("dve", lambda: V.scalar_tensor_tensor(out=stt[:, 0:1], in0=stt[:, 5:6], scalar=-1.0, in1=df[:, 0:1],
                                                                           op0=ALU.mult, op1=ALU.subtract), r=KB, w=KB)
                        if bisect:
                            for i in range(NSTEP):
                                lastst = (i == NSTEP - 1)
                                qq, qb, nadm = info[0]
                                stt, df, KB, ksc = bst[0], dft[0], ["c_bis0"], "c_score0"
                                S.op("dve", lambda: V.tensor_scalar(out=maskq2[0][:, 0:nadm], in0=score2[0][:, 0:nadm], scalar1=stt[:, 0:1], scalar2=None,
                                                                    op0=ALU.is_gt, op1=ALU.add, accum_out=stt[:, 2:3]),
                                     r=KB + [ksc, "c_maskq0"], w=KB + ["c_maskq0"])
                                S.op("dve", lambda: V.tensor_scalar(out=stt[:, 3:4], in0=stt[:, 2:3], scalar1=float(TOPK), scalar2=df[:, i:i + 1],
                                                                    op0=ALU.is_ge, op1=ALU.mult), r=KB, w=KB)
                                if not lastst:
                                    S.op("dve", lambda: V.scalar_tensor_tensor(out=stt[:, 0:1], in0=stt[:, 3:4], scalar=df[:, i + 1:i + 2], in1=stt[:, 0:1],
                                                                               op0=ALU.subtract, op1=ALU.add), r=KB, w=KB)
                                else:
                                    S.op("dve", lambda: V.scalar_tensor_tensor(out=stt[:, 5:6], in0=stt[:, 3:4], scalar=df[:, i:i + 1], in1=stt[:, 0:1],
                                                                               op0=ALU.subtract, op1=ALU.add), r=KB, w=KB)
                                qq, qb, nadm = info[1]
                                stt, df, KB, ksc = bst[1], dft[1], ["c_bis1"], "c_score1"
                                S.op("act", lambda: A.activation(out=maskq2[1][:, 0:nadm], in_=score2[1][:, 0:nadm], func=AF.Sign, bias=stt[:, 0:1],
                                                                 accum_out=stt[:, 2:3]),
                                     r=KB + [ksc, "c_maskq1"], w=KB + ["c_maskq1"])
                                S.op("pool", lambda: G.tensor_scalar(out=stt[:, 3:4], in0=stt[:, 2:3], scalar1=float(2 * TOPK - nadm), scalar2=df[:, i:i + 1],
                                                                     op0=ALU.is_ge, op1=ALU.mult), r=KB, w=KB)
                                S.op("pool", lambda: G.tensor_tensor(out=stt[:, 0:1], in0=stt[:, 0:1], in1=stt[:, 3:4], op=ALU.subtract), r=KB, w=KB)
                                if not lastst:
                                    S.op("pool", lambda: G.tensor_tensor(out=stt[:, 0:1], in0=stt[:, 0:1], in1=df[:, i + 1:i + 2], op=ALU.add), r=KB, w=KB)
                                else:
                                    S.op("pool", lambda: G.tensor_tensor(out=stt[:, 0:1], in0=stt[:, 0:1], in1=df[:, i:i + 1], op=ALU.add), r=KB, w=KB)
                                    S.op("pool", lambda: G.tensor_scalar(out=stt[:, 5:6], in0=stt[:, 0:1], scalar1=-1.0, scalar2=None, op0=ALU.mult), r=KB, w=KB)
                        for ci in range(2):
                            qq, qb, nadm = info[ci]
                            mq = maskq2[ci]
                            kmq = f"c_maskq{ci}"
                            S.op("dve", lambda: V.tensor_scalar(out=mq[:, 0:nadm], in0=score2[ci][:, 0:nadm], scalar1=bst[ci][:, 5:6], scalar2=None, op0=ALU.is_gt),
                                 r=[f"c_bis{ci}", f"c_score{ci}", kmq], w=[kmq])
                            for k1 in range(0, qb + 1, 4):
                                nk = min(4, qb + 1 - k1)
                                for u in range(nk):
                                    S.op("pe", lambda: T.transpose(out=ptr[:, u, :], in_=mq[:, (k1 + u) * 128:(k1 + u + 1) * 128], identity=ident[:]),
                                         r=[kmq], w=["c_ptr"])
                                S.op("act", lambda: A.activation(out=maskT[:, k1:k1 + nk, qq * 128:(qq + 1) * 128], in_=ptr[:, 0:nk, :], func=AF.Copy),
                                     r=["c_ptr"], w=["c_maskT"])
                    items = []
                    yq = yq2[Q % 2]
                    kyq = f"c_yq{Q % 2}"
                    for h in range(8):
                        g = gidx
                        gidx += 1
                        pr, bs = h // 2, (h % 2) * 64
                        nj = 4 * Q + 4
                        gstate = {"first": True}
                        for j in range(nj):
                            b = len(items) % 3
                            jj = j - 4 * Q
                            c0 = max(0, jj) * 128

                            def s0(j=j, c0=c0, b=b, pr=pr, bs=bs):
                                S.op("pe", lambda: T.matmul(pss[b][:, c0:512], lhsT=kt[bs:bs + 64, pr, j * 128:(j + 1) * 128],
                                                            rhs=qt[bs:bs + 64, pr, c0:512], start=True, stop=True),
                                     r=["c_kt", kq], w=[f"c_pss{b}"])

                            def s1(j=j, c0=c0, b=b):
                                S.op("act", lambda: A.activation(out=pT2[b][:, c0:512], in_=pss[b][:, c0:512], func=AF.Exp),
                                     r=[f"c_pss{b}"], w=[f"c_pT{b}"])
                                if b == 0:
                                    S.op("dve", lambda: V.tensor_tensor(out=pT2[b][:, c0:512], in0=pT2[b][:, c0:512], in1=maskT[:, j, c0:512], op=ALU.mult),
                                         r=[f"c_pT{b}", "c_maskT"], w=[f"c_pT{b}"])
                                else:
                                    S.op("pool", lambda: G.tensor_tensor(out=pT2[b][:, c0:512], in0=pT2[b][:, c0:512], in1=maskT[:, j, c0:512], op=ALU.mult),
                                         r=[f"c_pT{b}", "c_maskT"], w=[f"c_pT{b}"])

                            def s2(h=h, j=j, jj=jj, b=b, g=g, gstate=gstate, nj=nj):
                                po = po2[g % 2]
                                kpo = f"c_po{g % 2}"
                                osb = osb2[g % 2]
                                rl = rl2[g % 2]
                                for q4 in range(max(0, jj), 4):
                                    S.op("pe", lambda: T.matmul(po[:, q4 * 65:(q4 + 1) * 65], lhsT=pT2[b][:, q4 * 128:(q4 + 1) * 128],
                                                                rhs=vx[:, j, h * 65:(h + 1) * 65], start=gstate["first"], stop=(j == nj - 1),
                                                                skip_group_check=True),
                                         r=[f"c_pT{b}", "c_vx"], w=[kpo])
                                    gstate["first"] = False
                                if j == nj - 1:
                                    S.op("act", lambda: A.activation(out=osb[:], in_=po[:, 0:260].rearrange("p (a e) -> p a e", e=65), func=AF.Copy),
                                         r=[kpo], w=[f"c_o{g % 2}"])
                                    S.op("dve", lambda: V.reciprocal(out=rl[:], in_=osb[:, :, 64]), r=[f"c_o{g % 2}"], w=[f"c_rl{g % 2}"])
                                    S.op("dve", lambda: V.tensor_tensor(out=yq[:, :, h * 64:(h + 1) * 64], in0=osb[:, :, 0:64],
                                                                        in1=rl[:].unsqueeze(2).to_broadcast([128, 4, 64]), op=ALU.mult),
                                         r=[f"c_o{g % 2}", f"c_rl{g % 2}"], w=[kyq])
                            items.append([s0, None, s1, None, s2])
                    run_pipeline(items, 5)
                    for cch in range(4):
                        for q4 in range(4):
                            S.op("pe", lambda: T.transpose(out=ptr[:, q4, :], in_=yq[:, q4, cch * 128:(cch + 1) * 128], identity=ident[:]),
                                 r=[kyq], w=["c_ptr"])
                        ys = yts[cch % 2]
                        S.op("act", lambda: A.activation(out=ys[:], in_=ptr[:].rearrange("p a b -> p (a b)"), func=AF.Copy), r=["c_ptr"], w=[f"c_yts{cch % 2}"])
                        S.dma("act", YT[8 + cch, :, Q * 512:(Q + 1) * 512], ys[:], r=[f"c_yts{cch % 2}"], w=["YT"])
                S.barrier()
                ph.close()

            for ph in ([ExitStack()] if 'pm' in PH else []):
                wbr = [sb(ph, f"m_wbr{i}", [128, 4, D], BF16) for i in range(3)]
                wo = sb(ph, "m_wo", [128, 8, D], BF16)
                yt2 = [sb(ph, f"m_yt{i}", [128, 12, 512], BF16) for i in range(2)]
                gt2 = [sb(ph, f"m_gt{i}", [128, 24, 512], BF16) for i in range(2)]
                m0_2 = [sb(ph, f"m_m0{i}", [128, 512], F32) for i in range(2)]
                m1_2 = [sb(ph, f"m_m1{i}", [128, 512], F32) for i in range(2)]
                m2_2 = [sb(ph, f"m_m2{i}", [128, 512], F32) for i in range(2)]
                mT2 = [sb(ph, f"m_mT{i}", [128, 8, 512], BF16) for i in range(2)]
                xb2 = [sb(ph, f"m_x{i}", [128, D], F32) for i in range(2)]
                ps3 = [ps(ph, f"m_ps{i}", [128, 512], F32) for i in range(6)]
                pso = [ps(ph, f"m_pso{i}", [128, 512], F32) for i in range(2)]
                for i in range(3):
                    load_w_bf16(wbr[i][:], w_br[i][l].rearrange("(c p) n -> p c n", p=128), "m_w")
                load_w_bf16(wo[:], w_out[l].rearrange("(c p) n -> p c n", p=128), "m_w")
                iom = [0]
                items = []
                for Tq in range(NQ):
                    def sm(Tq=Tq):
                        mT = mT2[Tq % 2]
                        kmT = f"m_mT{Tq % 2}"
                        yt = yt2[Tq % 2]
                        gt = gt2[Tq % 2]
                        kin = f"m_in{Tq % 2}"
                        for b3 in range(3):
                            S.dma("sp", yt[:, b3 * 4:(b3 + 1) * 4, :], YT[b3 * 4:(b3 + 1) * 4, :, Tq * 512:(Tq + 1) * 512].rearrange("j p t -> p j t"), r=["YT"], w=[kin])
                            S.dma("sp", gt[:, b3 * 8:(b3 + 1) * 8, :], GT[b3 * 8:(b3 + 1) * 8, :, Tq * 512:(Tq + 1) * 512].rearrange("j p t -> p j t"), r=["GT"], w=[kin])
                        for fc in range(8):
                            fp_ = fc % 2
                            m0, m1, m2 = m0_2[fp_], m1_2[fp_], m2_2[fp_]
                            for br in range(3):
                                pb3 = fp_ * 3 + br
                                for kc in range(4):
                                    S.op("pe", lambda: T.matmul(ps3[pb3][:], lhsT=wbr[br][:, kc, fc * 128:(fc + 1) * 128], rhs=yt[:, br * 4 + kc, :],
                                                                start=(kc == 0), stop=(kc == 3)), r=["m_w", kin], w=[f"m_ps{pb3}"])
                            S.op("dve", lambda: V.tensor_tensor(out=m0[:], in0=ps3[fp_ * 3][:], in1=gt[:, fc, :], op=ALU.mult), r=[f"m_ps{fp_ * 3}", kin], w=[f"m_m0{fp_}"])
                            S.op("dve", lambda: V.tensor_tensor(out=m1[:], in0=ps3[fp_ * 3 + 1][:], in1=gt[:, 8 + fc, :], op=ALU.mult), r=[f"m_ps{fp_ * 3 + 1}", kin], w=[f"m_m1{fp_}"])
                            S.op("dve", lambda: V.tensor_tensor(out=m2[:], in0=ps3[fp_ * 3 + 2][:], in1=gt[:, 16 + fc, :], op=ALU.mult), r=[f"m_ps{fp_ * 3 + 2}", kin], w=[f"m_m2{fp_}"])
                            S.op("pool", lambda: G.tensor_tensor(out=m0[:], in0=m0[:], in1=m1[:], op=ALU.add), r=[f"m_m0{fp_}", f"m_m1{fp_}"], w=[f"m_m0{fp_}"])
                            S.op("pool", lambda: G.tensor_tensor(out=mT[:, fc, :], in0=m0[:], in1=m2[:], op=ALU.add), r=[f"m_m0{fp_}", f"m_m2{fp_}"], w=[kmT])

                    def sw(Tq=Tq):
                        mT = mT2[Tq % 2]
                        kmT = f"m_mT{Tq % 2}"
                        for tt in range(4):
                            t = Tq * 4 + tt
                            xt = xb2[t % 2]
                            kx = f"m_x{t % 2}"
                            S.dma("sp", xt[:], xsrc[t * 128:(t + 1) * 128, :], r=[f"xr{t}"], w=[kx])
                            for nb in range(2):
                                pq = pso[iom[0] % 2]
                                kpq = f"m_pso{iom[0] % 2}"
                                iom[0] += 1
                                for fc in range(8):
                                    S.op("pe", lambda: T.matmul(pq[:], lhsT=mT[:, fc, tt * 128:(tt + 1) * 128], rhs=wo[:, fc, nb * 512:(nb + 1) * 512],
                                                                start=(fc == 0), stop=(fc == 7)), r=[kmT, "m_w"], w=[kpq])
                                S.op("dve", lambda: V.tensor_tensor(out=xt[:, nb * 512:(nb + 1) * 512], in0=pq[:], in1=xt[:, nb * 512:(nb + 1) * 512], op=ALU.add),
                                     r=[kpq, kx], w=[kx])
                            S.dma("sp", xres[t * 128:(t + 1) * 128, :], xt[:], r=[kx], w=[f"xr{t}"])

                    items.append([sm, None, sw])
                run_pipeline(items, 3)
                S.barrier()
                ph.close()

            for ph in ([ExitStack()] if 'pf' in PH else []):
                wu = sb(ph, "f_wu", [128, 8, 4 * D], BF16)
                wd = sb(ph, "f_wd", [128, 32, D], BF16)
                xb4 = [sb(ph, f"f_x{i}", [128, D], F32) for i in range(4)]
                hb = sb(ph, "f_hb", [128, D], BF16)
                junk = sb(ph, "f_junk", [128, D], BF16)
                ssq = sb(ph, "f_ssq", [128, 8], F32)
                rstd = sb(ph, "f_rstd", [128, 8], F32)
                hmT2 = [sb(ph, f"f_hmT{i}", [128, 8, 256], BF16) for i in range(2)]
                rr2 = [sb(ph, f"f_r{i}", [128, 256], F32) for i in range(2)]
                uT = sb(ph, "f_uT", [128, 32, 256], BF16)
                pst2 = [ps(ph, f"f_pst{i}", [128, 4, 128], BF16) for i in range(2)]
                psu = [ps(ph, f"f_psu{i}", [128, 256], F32) for i in range(2)]
                psd = [ps(ph, f"f_psd{i}", [128, 512], F32) for i in range(2)]
                for c in range(8):
                    load_w_bf16(wu[:, c, :], w_up[l, c * 128:(c + 1) * 128, :], "f_wu")
                for c4 in range(4):
                    load_w_bf16(wd[:, c4 * 8:(c4 + 1) * 8, :], w_down[l, c4 * 1024:(c4 + 1) * 1024, :].rearrange("(c p) n -> p c n", p=128), "f_wd")
                iof = [0]
                items = []
                for tg in range(NT // 2):
                    def s0(tg=tg):
                        hm = hmT2[tg % 2]
                        for t2 in range(2):
                            t = tg * 2 + t2
                            xi = (tg % 2) * 2 + t2
                            xt = xb4[xi]
                            kx = f"f_x{xi}"
                            S.dma("sp", xt[:], xres[t * 128:(t + 1) * 128, :], r=[f"xr{t}"], w=[kx])
                            norm_transpose(xt[:], kx, gM, l, hb, pst2, lambda c, t2=t2, hm=hm: hm[:, c, t2 * 128:(t2 + 1) * 128], f"f_hmT{tg % 2}",
                                           ssq, rstd, junk, "f")

                    def s1(tg=tg):
                        hm = hmT2[tg % 2]
                        khm = f"f_hmT{tg % 2}"
                        for fc in range(32):
                            pu = psu[fc % 2]
                            for c in range(8):
                                S.op("pe", lambda: T.matmul(pu[:], lhsT=wu[:, c, fc * 128:(fc + 1) * 128], rhs=hm[:, c, :], start=(c == 0), stop=(c == 7)),
                                     r=["f_wu", khm], w=[f"f_psu{fc % 2}"])
                            rr = rr2[fc % 2]
                            S.op("act", lambda: A.activation(out=rr[:], in_=pu[:], func=AF.Relu), r=[f"f_psu{fc % 2}"], w=[f"f_r{fc % 2}"])
                            if fc % 2 == 0:
                                S.op("pool", lambda: G.tensor_tensor(out=uT[:, fc, :], in0=rr[:], in1=rr[:], op=ALU.mult), r=[f"f_r{fc % 2}"], w=["f_uT"])
                            else:
                                S.op("dve", lambda: V.tensor_tensor(out=uT[:, fc, :], in0=rr[:], in1=rr[:], op=ALU.mult), r=[f"f_r{fc % 2}"], w=["f_uT"])
                        for t2 in range(2):
                            t = tg * 2 + t2
                            xi = (tg % 2) * 2 + t2
                            xt = xb4[xi]
                            kx = f"f_x{xi}"
                            for nb in range(2):
                                pq = psd[iof[0] % 2]
                                kpq = f"f_psd{iof[0] % 2}"
                                iof[0] += 1
                                for fc in range(32):
                                    S.op("pe", lambda: T.matmul(pq[:], lhsT=uT[:, fc, t2 * 128:(t2 + 1) * 128], rhs=wd[:, fc, nb * 512:(nb + 1) * 512],
                                                                start=(fc == 0), stop=(fc == 31)), r=["f_uT", "f_wd"], w=[kpq])
                                S.op("dve", lambda: V.tensor_tensor(out=xt[:, nb * 512:(nb + 1) * 512], in0=pq[:], in1=xt[:, nb * 512:(nb + 1) * 512], op=ALU.add),
                                     r=[kpq, kx], w=[kx])
                            S.dma("sp", xres[t * 128:(t + 1) * 128, :], xt[:], r=[kx], w=[f"xr{t}"])
                    items.append([s0, s1])
                run_pipeline(items, 2)
                S.barrier()
                ph.close()

            for ph in ([ExitStack()] if 'pp' in PH else []):
                wg = sb(ph, "e_wg", [128, 8, D], BF16)
                wp = sb(ph, "e_wp", [128, 2, D], BF16)
                xb3 = [sb(ph, f"e_x{i}", [128, D], F32) for i in range(4)]
                pb3_ = [sb(ph, f"e_p{i}", [128, 256], F32) for i in range(4)]
                pbb = sb(ph, "e_pbb", [128, 256], BF16)
                hb = sb(ph, "e_hb", [128, D], BF16)
                junk = sb(ph, "e_junk", [128, D], BF16)
                ssq = sb(ph, "e_ssq", [128, 8], F32)
                rstd = sb(ph, "e_rstd", [128, 8], F32)
                hpT2 = [sb(ph, f"e_hpT{i}", [128, 8, 128], BF16) for i in range(3)]
                pT2 = [sb(ph, f"e_pT{i}", [128, 2, 128], BF16) for i in range(3)]
                sg2 = [sb(ph, f"e_sg{i}", [128, 512], F32) for i in range(2)]
                tm2 = [sb(ph, f"e_tm{i}", [128, 512], F32) for i in range(2)]
                pst2 = [ps(ph, f"e_pst{i}", [128, 4, 128], BF16) for i in range(2)]
                ptp = ps(ph, "e_ptp", [128, 2, 128], BF16)
                psg = [ps(ph, f"e_psg{i}", [128, 512], F32) for i in range(2)]
                psp = [ps(ph, f"e_psp{i}", [128, 512], F32) for i in range(2)]
                load_w_bf16(wg[:], w_ple_gate[l].rearrange("(c p) n -> p c n", p=128), "e_w")
                load_w_bf16(wp[:], w_ple_proj[l].rearrange("(c p) n -> p c n", p=128), "e_w")
                dst = out_d if l == depth - 1 else xres
                kdst = "out"
                ioe = [0]
                items = []
                for t in range(NT):
                    def s0(t=t):
                        xt = xb3[t % 4]
                        kx = f"e_x{t % 4}"
                        pb = pb3_[t % 4]
                        hp = hpT2[t % 3]
                        S.dma("sp", xt[:], xres[t * 128:(t + 1) * 128, :], r=[f"xr{t}"], w=[kx])
                        S.dma("sp", pb[:], p_in[l, t * 128:(t + 1) * 128, :], w=[f"e_p{t % 4}"])
                        norm_transpose(xt[:], kx, gP, l, hb, pst2, lambda c, hp=hp: hp[:, c, :], f"e_hpT{t % 3}", ssq, rstd, junk, "e")
                        S.op("pool", lambda: G.tensor_copy(out=pbb[:], in_=pb[:]), r=[f"e_p{t % 4}"], w=["e_pbb"])
                        for c in range(2):
                            S.op("pe", lambda: T.transpose(out=ptp[:, c, :], in_=pbb[:, c * 128:(c + 1) * 128], identity=ident[:]), r=["e_pbb"], w=["e_ptp"])
                        S.op("act", lambda: A.activation(out=pT2[t % 3][:], in_=ptp[:], func=AF.Copy), r=["e_ptp"], w=[f"e_pT{t % 3}"])

                    def s1(t=t):
                        xt = xb3[t % 4]
                        kx = f"e_x{t % 4}"
                        hp = hpT2[t % 3]
                        pT = pT2[t % 3]
                        for nb in range(2):
                            b = ioe[0] % 2
                            ioe[0] += 1
                            for c in range(8):
                                S.op("pe", lambda: T.matmul(psg[b][:], lhsT=hp[:, c, :], rhs=wg[:, c, nb * 512:(nb + 1) * 512], start=(c == 0), stop=(c == 7)),
                                     r=[f"e_hpT{t % 3}", "e_w"], w=[f"e_psg{b}"])
                            for c in range(2):
                                S.op("pe", lambda: T.matmul(psp[b][:], lhsT=pT[:, c, :], rhs=wp[:, c, nb * 512:(nb + 1) * 512], start=(c == 0), stop=(c == 1)),
                                     r=[f"e_pT{t % 3}", "e_w"], w=[f"e_psp{b}"])
                            S.op("act", lambda: A.activation(out=sg2[b][:], in_=psg[b][:], func=AF.Sigmoid), r=[f"e_psg{b}"], w=[f"e_sg{b}"])
                            S.op("dve", lambda: V.tensor_tensor(out=tm2[b][:], in0=psp[b][:], in1=sg2[b][:], op=ALU.mult), r=[f"e_psp{b}", f"e_sg{b}"], w=[f"e_tm{b}"])
                            S.op("pool", lambda: G.tensor_tensor(out=xt[:, nb * 512:(nb + 1) * 512], in0=xt[:, nb * 512:(nb + 1) * 512], in1=tm2[b][:], op=ALU.add),
                                 r=[f"e_tm{b}", kx], w=[kx])
                        S.dma("sp", dst[t * 128:(t + 1) * 128, :], xt[:], r=[kx], w=[kdst if l == depth - 1 else f"xr{t}"])
                    items.append([s0, None, s1])
                run_pipeline(items, 3)
                S.barrier()
                ph.close()
        S.barrier()
        build_nc.ninstr = dict(S.ninstr)
    return nc


def _host_layout(inputs, b):
    f32 = np.float32
    m = {}
    m["x"] = np.ascontiguousarray(inputs["x"][b], dtype=f32)
    m["p"] = np.ascontiguousarray(inputs["p"][:, b], dtype=f32)
    pos = np.asarray(inputs["positions"][b]).astype(np.int32)
    m["pos"] = np.ascontiguousarray(pos.reshape(NT, 128).T)
    m["invf"] = (10000.0 ** (-np.arange(0, 64, 2, dtype=np.float32) / 64)).astype(f32)
    for k in ("attn_norm", "mlp_norm", "ple_norm"):
        a = np.asarray(inputs[k], dtype=f32)
        m[k + "_l"] = np.ascontiguousarray(a.reshape(DEPTH, 8, 128).transpose(2, 0, 1).reshape(128, DEPTH * 8))
    for k in ("a_q_norm", "a_k_norm", "a_lambda", "a_subln", "c_q_norm", "c_k_norm", "idx_k_norm"):
        m[k] = np.ascontiguousarray(np.asarray(inputs[k], dtype=f32).reshape(-1))
    for k in ("w_in", "w_br_a", "w_br_b", "w_br_c", "w_out", "w_up", "w_down", "w_ple_gate", "w_ple_proj"):
        m[k] = np.ascontiguousarray(inputs[k], dtype=f32)
    return m


def kernel(**inputs):
    nc = build_nc()
    in_maps = [_host_layout(inputs, c % 4) for c in range(4)]
    in_maps = in_maps + in_maps
    res = run_bass_kernel_spmd(nc, in_maps, core_ids=list(range(8)))
    out = np.stack([np.asarray(res.results[b]["out"], dtype=np.float32) for b in range(4)], axis=0)
    return out
```
